# Optimizing a Trainium2 kernel written in Bass

```python
import math
import jax, jax.numpy as jnp
from jax import lax
import numpy as np

D_MODEL = 2048
BATCH = 1
SEQ = 8192
DEPTH = 4

W_SSM = 1024
SSM_GROUP = 16
SSM_GROUPS = W_SSM // SSM_GROUP
SSM_STATE = 64
SSM_STEP_MIN = 1e-3
SSM_STEP_MAX = 1e-1
W_RET = 1024
RET_HEADS = 8
RET_HEAD_DIM = W_RET // RET_HEADS
RET_CHUNK = 128
ROPE_BASE = 10000.0
W_RWKV = 1024
RWKV_HEAD_DIM = 64
RWKV_HEADS = W_RWKV // RWKV_HEAD_DIM
RWKV_DECAY_RANK = 64
RWKV_A_RANK = 64
RWKV_GATE_RANK = 128
RWKV_LO = RWKV_DECAY_RANK + RWKV_A_RANK + RWKV_GATE_RANK
RWKV_IN = 3 * W_RWKV + RWKV_LO
N_BRANCH = 3
D_FF = 5632
CONV_WIDTH = 3
OFF_SSM = 0
OFF_RET = OFF_SSM + W_SSM
OFF_RWKV = OFF_RET + 4 * W_RET
OFF_GATE = OFF_RWKV + RWKV_IN
N_IN = OFF_GATE + N_BRANCH * D_MODEL
DEEPNORM_ALPHA = (2.0 * DEPTH) ** 0.25
DEEPNORM_BETA = (8.0 * DEPTH) ** -0.25
LN_EPS = 1e-5
GN_EPS = 1e-5
RWKV_GN_EPS = 64e-5

kernel_name = "hybrid_s5_retnet_rwkv7_deepnorm"


def layer_norm(x, g, b):
    xf = x.astype(jnp.float32)
    mu = xf.mean(-1, keepdims=True)
    var = jnp.mean(jnp.square(xf - mu), -1, keepdims=True)
    return ((xf - mu) * lax.rsqrt(var + LN_EPS) * g + b).astype(x.dtype)


def head_norm(y, g, b, n_heads, eps):
    bsz, t, w = y.shape
    yh = y.astype(jnp.float32).reshape(bsz, t, n_heads, w // n_heads)
    mu = yh.mean(-1, keepdims=True)
    var = jnp.mean(jnp.square(yh - mu), -1, keepdims=True)
    return ((yh - mu) * lax.rsqrt(var + eps)).reshape(bsz, t, w) * g + b


def shift_right(z):
    return jnp.pad(z[:, :-1], ((0, 0), (1, 0), (0, 0)))


def causal_dwconv(h, w):
    kw = w.shape[0]
    t = h.shape[1]
    hp = jnp.pad(h, ((0, 0), (kw - 1, 0), (0, 0)))
    out = hp[:, 0:t] * w[0]
    for j in range(1, kw):
        out = out + hp[:, j:j + t] * w[j]
    return out


def rotary(x, positions):
    half = x.shape[-1] // 2
    inv_freq = ROPE_BASE ** (-jnp.arange(half, dtype=jnp.float32) / half)
    ang = positions.astype(jnp.float32)[..., None] * inv_freq
    cos = jnp.cos(ang)[:, :, None, :]
    sin = jnp.sin(ang)[:, :, None, :]
    x1, x2 = x[..., :half], x[..., half:]
    return jnp.concatenate([x1 * cos - x2 * sin, x2 * cos + x1 * sin], axis=-1)


def s5_branch(u, lam_re, lam_im, log_step, b_re, b_im, c_re, c_im, d_skip, w_glu):
    bsz, t, _ = u.shape
    uf = u.astype(jnp.float32)
    ug = uf.reshape(bsz, t, SSM_GROUPS, SSM_GROUP)
    step = jnp.exp(log_step.astype(jnp.float32))[:, None]
    lr = lam_re.astype(jnp.float32)
    li = lam_im.astype(jnp.float32)
    mag = jnp.exp(lr * step)
    ab_re = mag * jnp.cos(li * step)
    ab_im = mag * jnp.sin(li * step)
    denom = lr * lr + li * li
    f_re = ((ab_re - 1.0) * lr + ab_im * li) / denom
    f_im = (ab_im * lr - (ab_re - 1.0) * li) / denom
    bb_re = f_re[..., None] * b_re - f_im[..., None] * b_im
    bb_im = f_re[..., None] * b_im + f_im[..., None] * b_re
    bu_re = jnp.einsum('btgc,gpc->btgp', ug, bb_re)
    bu_im = jnp.einsum('btgc,gpc->btgp', ug, bb_im)
    a_re = jnp.broadcast_to(ab_re, bu_re.shape)
    a_im = jnp.broadcast_to(ab_im, bu_re.shape)

    def combine(e1, e2):
        a1r, a1i, b1r, b1i = e1
        a2r, a2i, b2r, b2i = e2
        return (a2r * a1r - a2i * a1i,
                a2r * a1i + a2i * a1r,
                a2r * b1r - a2i * b1i + b2r,
                a2r * b1i + a2i * b1r + b2i)

    _, _, s_re, s_im = lax.associative_scan(combine, (a_re, a_im, bu_re, bu_im), axis=1)
    y = (jnp.einsum('btgp,gcp->btgc', s_re, c_re)
         - jnp.einsum('btgp,gcp->btgc', s_im, c_im))
    y = y.reshape(bsz, t, W_SSM) + d_skip * uf
    z = jax.nn.gelu(y).astype(u.dtype)
    h = z @ w_glu
    return h[..., :D_MODEL] * jax.nn.sigmoid(h[..., D_MODEL:])


def retention_branch(q, k, v, g, positions, norm_g, norm_b, w_out):
    bsz, t, _ = q.shape
    c = RET_CHUNK
    n = t // c

    def heads(z):
        return z.astype(jnp.float32).reshape(bsz, t, RET_HEADS, RET_HEAD_DIM)

    qh = rotary(heads(q), positions)
    kh = rotary(heads(k), positions) * (RET_HEAD_DIM ** -0.5)
    vh = heads(v)
    qh = qh.reshape(bsz, n, c, RET_HEADS, RET_HEAD_DIM)
    kh = kh.reshape(bsz, n, c, RET_HEADS, RET_HEAD_DIM)
    vh = vh.reshape(bsz, n, c, RET_HEADS, RET_HEAD_DIM)
    log_gamma = jnp.log1p(-jnp.exp2(-5.0 - jnp.arange(RET_HEADS, dtype=jnp.float32)))
    idx = jnp.arange(c, dtype=jnp.float32)
    rel = idx[:, None] - idx[None, :]
    decay = jnp.where(rel >= 0, jnp.exp(log_gamma[:, None, None] * jnp.maximum(rel, 0.0)), 0.0)
    scores = jnp.einsum('bnihd,bnjhd->bnhij', qh, kh) * decay
    inner = jnp.einsum('bnhij,bnjhe->bnihe', scores, vh)
    k_decay = jnp.exp(log_gamma[None, :] * (c - 1.0 - idx)[:, None])
    kv = jnp.einsum('bnjhd,jh,bnjhe->nbhde', kh, k_decay, vh)
    chunk_decay = jnp.exp(log_gamma * c)[None, :, None, None]

    def step(state, kv_n):
        return chunk_decay * state + kv_n, state

    init = jnp.zeros((bsz, RET_HEADS, RET_HEAD_DIM, RET_HEAD_DIM), jnp.float32)
    _, prev = lax.scan(step, init, kv)
    q_decay = jnp.exp(log_gamma[None, :] * (idx + 1.0)[:, None])
    cross = jnp.einsum('bnihd,nbhde->bnihe', qh, prev) * q_decay[:, :, None]
    o = (inner + cross).reshape(bsz, t, W_RET)
    o = head_norm(o, norm_g, norm_b, RET_HEADS, GN_EPS)
    o = jax.nn.silu(g.astype(jnp.float32)) * o
    return o.astype(q.dtype) @ w_out


def rwkv7_branch(z, mu, w0, w2, a0, a2, g2, k_k, k_a, r_k, norm_g, norm_b, w_out):
    bsz, t, _ = z.shape
    zf = z.astype(jnp.float32)
    zs = zf + mu * (shift_right(zf) - zf)
    r = zs[..., 0:W_RWKV]
    k = zs[..., W_RWKV:2 * W_RWKV]
    v = zs[..., 2 * W_RWKV:3 * W_RWKV]
    o0 = 3 * W_RWKV
    w_lo = zs[..., o0:o0 + RWKV_DECAY_RANK]
    a_lo = zs[..., o0 + RWKV_DECAY_RANK:o0 + RWKV_DECAY_RANK + RWKV_A_RANK]
    g_lo = zs[..., o0 + RWKV_DECAY_RANK + RWKV_A_RANK:RWKV_IN]
    w = -jax.nn.softplus(-(w0 + jnp.tanh(w_lo) @ w2)) - 0.5
    decay = jnp.exp(-jnp.exp(w))
    a = jax.nn.sigmoid(a0 + a_lo @ a2)
    g = jax.nn.sigmoid(g_lo) @ g2

    def heads(y):
        return y.reshape(bsz, t, RWKV_HEADS, RWKV_HEAD_DIM)

    kk = heads(k * k_k)
    kk = kk / jnp.maximum(jnp.sqrt(jnp.sum(kk * kk, -1, keepdims=True)), 1e-12)
    k = k * (1.0 + (a - 1.0) * k_a)
    rh, kh, vh, ah, wh = heads(r), heads(k), heads(v), heads(a), heads(decay)

    def tmajor(y):
        return jnp.moveaxis(y, 1, 0)

    def step(S, inp):
        r_t, w_t, k_t, v_t, kk_t, a_t = inp
        sa = jnp.einsum('bhvk,bhk->bhv', S, -kk_t)
        S = (S * w_t[:, :, None, :] + sa[..., None] * (kk_t * a_t)[:, :, None, :]
             + v_t[..., None] * k_t[:, :, None, :])
        return S, jnp.einsum('bhvk,bhk->bhv', S, r_t)

    init = jnp.zeros((bsz, RWKV_HEADS, RWKV_HEAD_DIM, RWKV_HEAD_DIM), jnp.float32)
    xs = (tmajor(rh), tmajor(wh), tmajor(kh), tmajor(vh), tmajor(kk), tmajor(ah))
    _, y = lax.scan(step, init, xs)
    y = jnp.moveaxis(y, 0, 1).reshape(bsz, t, W_RWKV)
    y = head_norm(y, norm_g, norm_b, RWKV_HEADS, RWKV_GN_EPS)
    bonus = jnp.sum(rh * kh * r_k.reshape(RWKV_HEADS, RWKV_HEAD_DIM), -1, keepdims=True) * vh
    o = (y + bonus.reshape(bsz, t, W_RWKV)) * g
    return o.astype(z.dtype) @ w_out


def conv_ffn(x, w_up, w_conv, w_down):
    h = causal_dwconv(x @ w_up, w_conv)
    return (jax.nn.silu(h[..., :D_FF]) * h[..., D_FF:]) @ w_down


def setup_inputs(seed: int = 0) -> dict:
    key = jax.random.key(seed)
    ks = jax.random.split(key, 40)
    f32 = jnp.float32
    L = DEPTH

    def nrm(k, shape, scale):
        return jax.random.normal(k, shape, f32) * scale

    def unif(k, shape, lo, hi):
        return jax.random.uniform(k, shape, f32, lo, hi)

    inp = {}
    inp["x"] = nrm(ks[0], (BATCH, SEQ, D_MODEL), 1.0)
    inp["positions"] = jnp.broadcast_to(jnp.arange(SEQ, dtype=jnp.int32), (BATCH, SEQ))
    inp["w_in"] = nrm(ks[1], (L, D_MODEL, N_IN), D_MODEL ** -0.5)
    inp["ssm_lambda_re"] = -0.5 + nrm(ks[2], (L, SSM_GROUPS, SSM_STATE), 0.01)
    inp["ssm_lambda_im"] = (math.pi * jnp.arange(SSM_STATE, dtype=f32)
                            + nrm(ks[3], (L, SSM_GROUPS, SSM_STATE), 0.01))
    inp["ssm_log_step"] = unif(ks[4], (L, SSM_GROUPS), math.log(SSM_STEP_MIN), math.log(SSM_STEP_MAX))
    inp["ssm_b_re"] = nrm(ks[5], (L, SSM_GROUPS, SSM_STATE, SSM_GROUP), (2.0 * SSM_GROUP) ** -0.5)
    inp["ssm_b_im"] = nrm(ks[6], (L, SSM_GROUPS, SSM_STATE, SSM_GROUP), (2.0 * SSM_GROUP) ** -0.5)
    inp["ssm_c_re"] = nrm(ks[7], (L, SSM_GROUPS, SSM_GROUP, SSM_STATE), (2.0 / SSM_STATE) ** 0.5)
    inp["ssm_c_im"] = nrm(ks[8], (L, SSM_GROUPS, SSM_GROUP, SSM_STATE), (2.0 / SSM_STATE) ** 0.5)
    inp["ssm_d"] = nrm(ks[9], (L, W_SSM), 1.0)
    inp["ssm_glu"] = nrm(ks[10], (L, W_SSM, 2 * D_MODEL), W_SSM ** -0.5)
    inp["ret_norm_g"] = 1.0 + nrm(ks[11], (L, W_RET), 0.02)
    inp["ret_norm_b"] = nrm(ks[12], (L, W_RET), 0.02)
    inp["ret_out"] = nrm(ks[13], (L, W_RET, D_MODEL), W_RET ** -0.5)
    inp["rwkv_mu"] = unif(ks[14], (L, RWKV_IN), 0.0, 1.0)
    inp["rwkv_w0"] = unif(ks[15], (L, W_RWKV), -6.0, -1.0)
    inp["rwkv_w2"] = nrm(ks[16], (L, RWKV_DECAY_RANK, W_RWKV), 0.1 * RWKV_DECAY_RANK ** -0.5)
    inp["rwkv_a0"] = nrm(ks[17], (L, W_RWKV), 0.1)
    inp["rwkv_a2"] = nrm(ks[18], (L, RWKV_A_RANK, W_RWKV), RWKV_A_RANK ** -0.5)
    inp["rwkv_g2"] = nrm(ks[19], (L, RWKV_GATE_RANK, W_RWKV), RWKV_GATE_RANK ** -0.5)
    inp["rwkv_k_k"] = 0.85 + nrm(ks[20], (L, W_RWKV), 0.05)
    inp["rwkv_k_a"] = 1.0 + nrm(ks[21], (L, W_RWKV), 0.05)
    inp["rwkv_r_k"] = nrm(ks[22], (L, W_RWKV), 0.1)
    inp["rwkv_norm_g"] = 1.0 + nrm(ks[23], (L, W_RWKV), 0.02)
    inp["rwkv_norm_b"] = nrm(ks[24], (L, W_RWKV), 0.02)
    inp["rwkv_out"] = nrm(ks[25], (L, W_RWKV, D_MODEL), W_RWKV ** -0.5)
    inp["w_o"] = nrm(ks[26], (L, D_MODEL, D_MODEL), DEEPNORM_BETA * D_MODEL ** -0.5)
    inp["ln1_g"] = 1.0 + nrm(ks[27], (L, D_MODEL), 0.02)
    inp["ln1_b"] = nrm(ks[28], (L, D_MODEL), 0.02)
    inp["ffn_up"] = nrm(ks[29], (L, D_MODEL, 2 * D_FF), D_MODEL ** -0.5)
    inp["ffn_conv"] = nrm(ks[30], (L, CONV_WIDTH, 2 * D_FF), CONV_WIDTH ** -0.5)
    inp["ffn_down"] = nrm(ks[31], (L, D_FF, D_MODEL), DEEPNORM_BETA * D_FF ** -0.5)
    inp["ln2_g"] = 1.0 + nrm(ks[32], (L, D_MODEL), 0.02)
    inp["ln2_b"] = nrm(ks[33], (L, D_MODEL), 0.02)
    return inp


def reference(x, positions, w_in, ssm_lambda_re, ssm_lambda_im, ssm_log_step, ssm_b_re, ssm_b_im,
              ssm_c_re, ssm_c_im, ssm_d, ssm_glu, ret_norm_g, ret_norm_b, ret_out, rwkv_mu, rwkv_w0,
              rwkv_w2, rwkv_a0, rwkv_a2, rwkv_g2, rwkv_k_k, rwkv_k_a, rwkv_r_k, rwkv_norm_g, rwkv_norm_b,
              rwkv_out, w_o, ln1_g, ln1_b, ffn_up, ffn_conv, ffn_down, ln2_g, ln2_b):
    bsz, t, _ = x.shape
    for l in range(DEPTH):
        proj = x @ w_in[l]
        y_ssm = s5_branch(proj[..., OFF_SSM:OFF_RET], ssm_lambda_re[l], ssm_lambda_im[l],
                          ssm_log_step[l], ssm_b_re[l], ssm_b_im[l], ssm_c_re[l], ssm_c_im[l],
                          ssm_d[l], ssm_glu[l])
        y_ret = retention_branch(proj[..., OFF_RET:OFF_RET + W_RET],
                                 proj[..., OFF_RET + W_RET:OFF_RET + 2 * W_RET],
                                 proj[..., OFF_RET + 2 * W_RET:OFF_RET + 3 * W_RET],
                                 proj[..., OFF_RET + 3 * W_RET:OFF_RWKV],
                                 positions, ret_norm_g[l], ret_norm_b[l], ret_out[l])
        y_rwkv = rwkv7_branch(proj[..., OFF_RWKV:OFF_GATE], rwkv_mu[l], rwkv_w0[l], rwkv_w2[l],
                              rwkv_a0[l], rwkv_a2[l], rwkv_g2[l], rwkv_k_k[l], rwkv_k_a[l],
                              rwkv_r_k[l], rwkv_norm_g[l], rwkv_norm_b[l], rwkv_out[l])
        gates = jax.nn.sigmoid(proj[..., OFF_GATE:].astype(jnp.float32)).reshape(bsz, t, N_BRANCH, D_MODEL)
        merged = gates[:, :, 0] * y_ssm + gates[:, :, 1] * y_ret + gates[:, :, 2] * y_rwkv
        x = layer_norm(DEEPNORM_ALPHA * x + merged.astype(x.dtype) @ w_o[l], ln1_g[l], ln1_b[l])
        x = layer_norm(DEEPNORM_ALPHA * x + conv_ffn(x, ffn_up[l], ffn_conv[l], ffn_down[l]),
                       ln2_g[l], ln2_b[l])
    return x
```

```python
import math
import numpy as np
import ml_dtypes
import concourse.bass as bass
import concourse.mybir as mybir
from concourse.bass_utils import run_bass_kernel_spmd

F32 = mybir.dt.float32
BF16 = mybir.dt.bfloat16
I32 = mybir.dt.int32
AF = mybir.ActivationFunctionType
ALU = mybir.AluOpType
AX = mybir.AxisListType
NPBF = ml_dtypes.bfloat16

D = 2048
NCORE = 8
DEPTH = 4
N_IN = 14592
D_FF = 5632
ALPHA = (2.0 * DEPTH) ** 0.25
EXPM05 = math.exp(-0.5)
TWO_PI = 2.0 * math.pi
DEBUG = False
SUB = 9
KNOB = 0


class T:
    __slots__ = ("ap", "name", "w", "r", "psum")

    def __init__(self, ap, name, psum=False):
        self.ap = ap
        self.name = name
        self.w = None
        self.r = []
        self.psum = psum

    def __getitem__(self, idx):
        return V(self, self.ap[idx])


class V:
    __slots__ = ("t", "ap")

    def __init__(self, t, ap):
        self.t = t
        self.ap = ap

    def __getitem__(self, idx):
        return V(self.t, self.ap[idx])

    def re(self, pat, **kw):
        return V(self.t, self.ap.rearrange(pat, **kw))

    def bc(self, shape):
        return V(self.t, self.ap.to_broadcast(list(shape)))


def _tv(x):
    if isinstance(x, T):
        return x, x.ap
    if isinstance(x, V):
        return x.t, x.ap
    return None, x


class P:
    NRING = 8

    def __init__(self):
        self.nc = bass.Bass("TRN2", target_bir_lowering=False)
        nc = self.nc
        self.eng = {"pe": nc.tensor, "dve": nc.vector, "act": nc.scalar,
                    "pool": nc.gpsimd, "sp": nc.sync}
        self.sems = {}
        self.cnt = {}
        self.seen = {e: {} for e in self.eng}
        self._ctx = []
        for e in self.eng:
            self.sems[e] = self._sem("s_" + e)
            self.cnt[e] = 0
        self.ring = {}
        self.ringn = {}
        for q in ("sp", "pool", "act"):
            self.ring[q] = [self._sem(f"d_{q}{i}") for i in range(self.NRING)]
            self.ringn[q] = 0
        self.n_inst = 0
        self._rr = 0

    def _sem(self, name):
        cm = self.nc.semaphore(name)
        s = cm.__enter__()
        self._ctx.append(cm)
        return s

    def sb(self, name, shape, dt=F32):
        cm = self.nc.sbuf_tensor(name, list(shape), dt)
        t = cm.__enter__()
        self._ctx.append(cm)
        return T(t[:], name)

    def ps(self, name, shape, dt=F32):
        cm = self.nc.psum_tensor(name, list(shape), dt)
        t = cm.__enter__()
        self._ctx.append(cm)
        return T(t[:], name, psum=True)

    def dram(self, name, shape, dt=F32, kind="ExternalInput"):
        t = self.nc.dram_tensor(name, list(shape), dt, kind=kind)
        return T(t.ap(), name)

    def _semobj(self, key):
        if isinstance(key, tuple):
            return self.ring[key[0]][key[1]]
        return self.sems[key]

    def _deps(self, e, reads, writes):
        need = {}

        def add(tok):
            k, v = tok[0], tok[1]
            if need.get(k, 0) < v:
                need[k] = v

        for t in reads:
            if t is None or t.w is None:
                continue
            if t.w[0] == e and e == "pe":
                continue
            add(t.w)
        for t in reads:
            if t is not None and t.psum:
                for rd in t.r:
                    if rd[0] != e:
                        add(rd)
        for t in writes:
            if t is None:
                continue
            for rd in t.r:
                if rd[0] == e:
                    continue
                add(rd)
            if t.w is not None and t.w[0] != e:
                add(t.w)
        eng = self.eng[e]
        seen = self.seen[e]
        for k, v in need.items():
            if seen.get(k, 0) < v:
                eng.wait_ge(self._semobj(k), v)
                seen[k] = v

    def _done(self, tok, reads, writes):
        for t in reads:
            if t is not None:
                t.r.append(tok)
                if len(t.r) > 24:
                    best = {}
                    for rd in t.r:
                        if best.get(rd[0], 0) < rd[1]:
                            best[rd[0]] = rd[1]
                    t.r = [(k, v) for k, v in best.items()]
        for t in writes:
            if t is not None:
                t.w = tok
                t.r = []

    def op(self, e, fn, reads, writes):
        self._deps(e, reads, writes)
        ins = fn()
        self.cnt[e] += 1
        ins.then_inc(self.sems[e], 1)
        self._done((e, self.cnt[e]), reads, writes)
        self.n_inst += 1
        return ins

    def dma(self, out, in_, q="sp", **kw):
        to, apo = _tv(out)
        ti, api = _tv(in_)
        n = self.ringn[q]
        slot = n % self.NRING
        rnd = n // self.NRING
        key = (q, slot)
        eng = self.eng[q]
        if rnd > 0 and self.seen[q].get(key, 0) < 16 * rnd:
            eng.wait_ge(self.ring[q][slot], 16 * rnd)
            self.seen[q][key] = 16 * rnd
        self._deps(q, [ti], [to])
        ins = eng.dma_start(out=apo, in_=api, **kw)
        ins.then_inc(self.ring[q][slot], 16)
        self.ringn[q] = n + 1
        self._done((key, 16 * (rnd + 1)), [ti], [to])
        self.n_inst += 1
        return ins

    def mm(self, out, lhsT, rhs, start=True, stop=True, **kw):
        to, apo = _tv(out)
        tl, apl = _tv(lhsT)
        tr, apr = _tv(rhs)
        return self.op("pe", lambda: self.nc.tensor.matmul(apo, apl, apr, start=start, stop=stop, **kw),
                       [tl, tr], [to])

    def tr(self, out, in_, ident):
        to, apo = _tv(out)
        ti, api = _tv(in_)
        td, apd = _tv(ident)
        return self.op("pe", lambda: self.nc.tensor.transpose(apo, api, apd), [ti, td], [to])

    def act(self, out, in_, func, bias=None, scale=None):
        to, apo = _tv(out)
        ti, api = _tv(in_)
        rd = [ti]
        k = {}
        if bias is not None:
            tb, apb = _tv(bias)
            rd.append(tb)
            k["bias"] = apb
        if scale is not None:
            ts_, aps = _tv(scale)
            rd.append(ts_)
            k["scale"] = aps
        return self.op("act", lambda: self.nc.scalar.activation(apo, api, func, **k), rd, [to])

    def tt(self, out, a, b, op, e="dve"):
        to, apo = _tv(out)
        ta, apa = _tv(a)
        tb, apb = _tv(b)
        eng = self.eng[e]
        return self.op(e, lambda: eng.tensor_tensor(apo, apa, apb, op), [ta, tb], [to])

    def ts(self, out, a, s1, op0, s2=None, op1=None, e="dve"):
        to, apo = _tv(out)
        ta, apa = _tv(a)
        t1, ap1 = _tv(s1)
        t2, ap2 = _tv(s2)
        eng = self.eng[e]
        if op1 is None:
            return self.op(e, lambda: eng.tensor_scalar(apo, apa, ap1, None, op0), [ta, t1], [to])
        return self.op(e, lambda: eng.tensor_scalar(apo, apa, ap1, ap2, op0, op1), [ta, t1, t2], [to])

    def stt(self, out, a, s, b, op0, op1):
        to, apo = _tv(out)
        ta, apa = _tv(a)
        ts_, aps = _tv(s)
        tb, apb = _tv(b)
        return self.op("dve", lambda: self.nc.vector.scalar_tensor_tensor(apo, apa, aps, apb, op0, op1),
                       [ta, ts_, tb], [to])

    def copy(self, out, in_, e="dve"):
        to, apo = _tv(out)
        ti, api = _tv(in_)
        if e == "act":
            return self.op("act", lambda: self.nc.scalar.copy(apo, api), [ti], [to])
        eng = self.eng[e]
        return self.op(e, lambda: eng.tensor_copy(apo, api), [ti], [to])

    def memset(self, out, val, e="dve"):
        to, apo = _tv(out)
        eng = self.eng[e]
        return self.op(e, lambda: eng.memset(apo, val), [], [to])

    def scan(self, out, d0, d1, init, op0, op1):
        to, apo = _tv(out)
        t0, ap0 = _tv(d0)
        t1, ap1 = _tv(d1)
        ti, api = _tv(init)
        return self.op("dve", lambda: self.nc.vector.tensor_tensor_scan(apo, ap0, ap1, api, op0, op1),
                       [t0, t1, ti], [to])

    def recip(self, out, in_):
        to, apo = _tv(out)
        ti, api = _tv(in_)
        return self.op("dve", lambda: self.nc.vector.reciprocal(apo, api), [ti], [to])

    def reduce(self, out, in_, op, axis=AX.X):
        to, apo = _tv(out)
        ti, api = _tv(in_)
        return self.op("dve", lambda: self.nc.vector.tensor_reduce(apo, api, axis, op), [ti], [to])

    def bn_stats(self, out, in_):
        to, apo = _tv(out)
        ti, api = _tv(in_)
        return self.op("dve", lambda: self.nc.vector.bn_stats(apo, api), [ti], [to])

    def bn_aggr(self, out, in_):
        to, apo = _tv(out)
        ti, api = _tv(in_)
        return self.op("dve", lambda: self.nc.vector.bn_aggr(apo, api), [ti], [to])

    def finish(self):
        sp = self.nc.sync
        for e in self.eng:
            if e != "sp" and self.cnt[e] > 0 and self.seen["sp"].get(e, 0) < self.cnt[e]:
                sp.wait_ge(self.sems[e], self.cnt[e])
                self.seen["sp"][e] = self.cnt[e]
        for q in self.ring:
            n = self.ringn[q]
            for slot in range(min(n, self.NRING)):
                last_rnd = (n - 1 - slot) // self.NRING
                v = 16 * (last_rnd + 1)
                key = (q, slot)
                if self.seen["sp"].get(key, 0) < v:
                    sp.wait_ge(self.ring[q][slot], v)
                    self.seen["sp"][key] = v

    def run(self, in_maps):
        self.finish()
        return run_bass_kernel_spmd(self.nc, in_maps, core_ids=list(range(NCORE)))


class Rot:
    def __init__(self, items):
        self.items = items
        self.i = 0

    def next(self):
        x = self.items[self.i % len(self.items)]
        self.i += 1
        return x


def sincos(p, ang, sin_out, cos_out, tmp_f, tmp_i, tmp_k):
    C1 = 6.28125
    C2 = TWO_PI - 6.28125
    p.ts(tmp_f, ang, 1.0 / TWO_PI, ALU.mult)
    p.copy(tmp_i, tmp_f)
    p.copy(tmp_k, tmp_i)
    p.stt(tmp_f, tmp_k, -C1, ang, ALU.mult, ALU.add)
    p.stt(tmp_f, tmp_k, -C2, tmp_f, ALU.mult, ALU.add)
    p.ts(tmp_k, tmp_f, math.pi, ALU.is_gt, -TWO_PI, ALU.mult)
    p.tt(tmp_k, tmp_k, tmp_f, ALU.add)
    p.act(sin_out, tmp_k, AF.Sin)
    p.ts(tmp_f, tmp_f, math.pi / 2, ALU.add)
    p.ts(tmp_k, tmp_f, math.pi, ALU.is_gt, -TWO_PI, ALU.mult)
    p.tt(tmp_k, tmp_k, tmp_f, ALU.add)
    p.act(cos_out, tmp_k, AF.Sin)


A_FAMS = [
    ("uT", 0, 1024, "FM", None),
    ("qT", 1024, 1024, "FM", None),
    ("kT", 2048, 1024, "FM", None),
    ("v", 3072, 1024, "TM", None),
    ("sg", 4096, 1024, "TM", AF.Silu),
    ("zT", 5120, 3328, "FM", None),
    ("gates", 8448, 6144, "TM", AF.Sigmoid),
]


def load_cast_T(p, src_ap_fn, dst, nkc, Tc, stages):
    for kc in range(nkc):
        st = stages.next()
        p.dma(st, src_ap_fn(kc))
        p.copy(dst[:, kc, :], st, e=("dve" if kc % 2 == 0 else "pool"))


def build_A(Tc):
    p = P()
    KC = 16
    xT = p.dram("xT", [D, Tc])
    w = p.dram("w", [D, N_IN])
    outs = {}
    for name, c0, wd, mode, fn in A_FAMS:
        shp = [wd, Tc] if mode == "FM" else [Tc, wd]
        outs[name] = p.dram(name, shp, kind="ExternalOutput")
    xb = p.sb("xb", [128, KC, Tc], BF16)
    xst = Rot([p.sb(f"xst{i}", [128, Tc]) for i in range(2)])
    wst = [[p.sb(f"wst{b}{h}", [128, 8, 512]) for h in range(2)] for b in range(2)]
    wb = [[p.sb(f"wb{b}{h}", [128, 8, 512], BF16) for h in range(2)] for b in range(2)]
    pss = Rot([p.ps(f"ps{i}", [128, 512]) for i in range(4)])
    ost = Rot([p.sb(f"ost{i}", [128, 512]) for i in range(4)])
    xTv = xT.ap.rearrange("(kc p) t -> p kc t", p=128)
    load_cast_T(p, lambda kc: V(xT, xTv[:, kc, :]), xb, KC, Tc, xst)
    wv = w.ap.rearrange("(kc p) n -> p kc n", p=128)
    chunks = []
    for name, c0, wd, mode, fn in A_FAMS:
        o = 0
        while o < wd:
            cw = min(512, wd - o)
            chunks.append((name, c0 + o, o, cw, mode, fn))
            o += cw

    def load_w(j):
        name, c, o, cw, mode, fn = chunks[j]
        b = j % 2
        for h in range(2):
            p.dma(wst[b][h][:, :, 0:cw], V(w, wv[:, 8 * h:8 * h + 8, c:c + cw]))

    def cast_w(j):
        name, c, o, cw, mode, fn = chunks[j]
        b = j % 2
        p.copy(wb[b][0][:, :, 0:cw], wst[b][0][:, :, 0:cw], e="dve")
        p.copy(wb[b][1][:, :, 0:cw], wst[b][1][:, :, 0:cw], e="pool")

    TH = min(512, Tc)
    load_w(0)
    ev = 0
    for j in range(len(chunks)):
        name, c, o, cw, mode, fn = chunks[j]
        b = j % 2
        if j + 1 < len(chunks):
            load_w(j + 1)
        cast_w(j)
        od = outs[name]
        if mode == "FM":
            for sub in range(cw // 128):
                for th in range(Tc // TH):
                    ps = pss.next()
                    for kc in range(KC):
                        p.mm(ps[:, 0:TH], wb[b][kc // 8][:, kc % 8, sub * 128:(sub + 1) * 128],
                             xb[:, kc, th * TH:(th + 1) * TH], start=(kc == 0), stop=(kc == KC - 1))
                    os_ = ost.next()
                    if ev % 2 == 0:
                        p.copy(os_[:, 0:TH], ps[:, 0:TH], e="dve")
                    else:
                        p.copy(os_[:, 0:TH], ps[:, 0:TH], e="act")
                    ev += 1
                    p.dma(od[o + sub * 128:o + (sub + 1) * 128, th * TH:(th + 1) * TH], os_[:, 0:TH])
        else:
            for tt_ in range(Tc // 128):
                ps = pss.next()
                for kc in range(KC):
                    p.mm(ps[:, 0:cw], xb[:, kc, tt_ * 128:(tt_ + 1) * 128],
                         wb[b][kc // 8][:, kc % 8, 0:cw], start=(kc == 0), stop=(kc == KC - 1))
                os_ = ost.next()
                if fn is not None:
                    p.act(os_[:, 0:cw], ps[:, 0:cw], fn)
                else:
                    p.copy(os_[:, 0:cw], ps[:, 0:cw], e=("dve" if ev % 2 == 0 else "act"))
                    ev += 1
                p.dma(od[tt_ * 128:(tt_ + 1) * 128, o:o + cw], os_[:, 0:cw])
    return p


def layer_norm_tile(p, pre, out_sb, lng, lnb, scr, eps=1e-5):
    st, mv, rs = scr["st"], scr["mv"], scr["rs"]
    for c in range(4):
        p.bn_stats(st[:, c, :], pre[:, c * 512:(c + 1) * 512])
    p.bn_aggr(mv, V(st, st.ap.rearrange("p a b -> p (a b)")))
    p.ts(rs, mv[:, 1:2], eps, ALU.add)
    p.act(rs, rs, AF.Sqrt)
    p.recip(rs, rs)
    p.ts(out_sb, pre, mv[:, 0:1], ALU.subtract, rs[:, 0:1], ALU.mult)
    p.tt(out_sb, out_sb, lng, ALU.mult, e="pool")
    p.tt(out_sb, out_sb, lnb, ALU.add, e="pool")


def ln_scratch(p, tag):
    return {"st": p.sb("ln_st" + tag, [128, 4, 6]), "mv": p.sb("ln_mv" + tag, [128, 2]),
            "rs": p.sb("ln_rs" + tag, [128, 1])}


def build_C1(Tc):
    p = P()
    NT = Tc // 128
    zT = p.dram("zT", [1024, Tc], BF16)
    orT = p.dram("orT", [1024, Tc], BF16)
    owT = p.dram("owT", [1024, Tc], BF16)
    gates = p.dram("gates", [Tc, 6144])
    x = p.dram("x", [Tc, D])
    wglu = p.dram("wglu", [1024, 4096])
    wret = p.dram("wret", [1024, D])
    wrw = p.dram("wrw", [1024, D])
    wo = p.dram("wo", [D, D])
    lng_d = p.dram("lng", [128, D])
    lnb_d = p.dram("lnb", [128, D])
    x1 = p.dram("x1", [Tc, D], kind="ExternalOutput")

    acts = {}
    actbuf = p.sb("actbuf", [128, 24, Tc], BF16)
    for i_, (nm, src) in enumerate((("z", zT), ("or", orT), ("ow", owT))):
        t = actbuf[:, 8 * i_:8 * i_ + 8, :]
        p.dma(t, V(src, src.ap.rearrange("(kc p) t -> p kc t", p=128)))
        acts[nm] = t
    lng = p.sb("lng_s", [128, D])
    lnb = p.sb("lnb_s", [128, D])
    p.dma(lng, lng_d)
    p.dma(lnb, lnb_d)
    merged = p.sb("merged", [128, NT, D])
    mTbuf = p.sb("mTbuf", [128, 16, Tc], BF16) if Tc < 512 else None
    mT = mTbuf if mTbuf is not None else actbuf
    wst = Rot([p.sb(f"wst{i}", [128, 8, 512]) for i in range(2)])
    wbB = Rot([p.sb(f"wbB{i}", [128, 8, 512], BF16) for i in range(3)])
    gst = Rot([p.sb(f"gst{i}", [128, 512]) for i in range(2)])
    tmp = Rot([p.sb(f"tmp{i}", [128, 512]) for i in range(2)])
    sig = Rot([p.sb(f"sig{i}", [128, 512]) for i in range(2)])
    pss = Rot([p.ps(f"ps{i}", [128, 512]) for i in range(6)])
    ident = p.sb("ident", [128, 128], BF16)
    idf = p.dram("identf", [128, 128])
    idst = p.sb("idst", [128, 128])
    p.dma(idst, idf)
    p.copy(ident, idst)
    ce = [0]

    def wload(dst, src_t, r0, nkc, c0, cw):
        sv = src_t.ap.rearrange("(kc p) n -> p kc n", p=128)
        for h in range(0, nkc, 8):
            st = wst.next()
            p.dma(st[:, :, 0:cw], V(src_t, sv[:, r0 + h:r0 + h + 8, c0:c0 + cw]))
            e = "dve" if ce[0] % 2 == 0 else "pool"
            ce[0] += 1
            p.copy(dst[:, h:h + 8, 0:cw], st[:, :, 0:cw], e=e)

    for cc in range(4):
        wa = wbB.next()
        wload(wa, wglu, 0, 8, cc * 512, 512)
        wb_ = wbB.next()
        wload(wb_, wglu, 0, 8, 2048 + cc * 512, 512)
        for t in range(NT):
            psa = pss.next()
            psb = pss.next()
            for kc in range(8):
                p.mm(psa, acts["z"][:, kc, t * 128:(t + 1) * 128], wa[:, kc, :], start=(kc == 0), stop=(kc == 7))
            for kc in range(8):
                p.mm(psb, acts["z"][:, kc, t * 128:(t + 1) * 128], wb_[:, kc, :], start=(kc == 0), stop=(kc == 7))
            g = gst.next()
            p.dma(g, gates[t * 128:(t + 1) * 128, cc * 512:(cc + 1) * 512])
            s = sig.next()
            p.act(s, psb, AF.Sigmoid)
            y = tmp.next()
            p.tt(y, psa, s, ALU.mult)
            p.tt(merged[:, t, cc * 512:(cc + 1) * 512], y, g, ALU.mult, e="pool")
    for bi, (nm, wsrc) in enumerate((("or", wret), ("ow", wrw))):
        for cc in range(4):
            wb_ = wbB.next()
            wload(wb_, wsrc, 0, 8, cc * 512, 512)
            for t in range(NT):
                ps = pss.next()
                for kc in range(8):
                    p.mm(ps, acts[nm][:, kc, t * 128:(t + 1) * 128], wb_[:, kc, :], start=(kc == 0), stop=(kc == 7))
                g = gst.next()
                p.dma(g, gates[t * 128:(t + 1) * 128, (bi + 1) * 2048 + cc * 512:(bi + 1) * 2048 + (cc + 1) * 512])
                y = tmp.next()
                p.tt(y, ps, g, ALU.mult)
                mv_ = merged[:, t, cc * 512:(cc + 1) * 512]
                p.tt(mv_, mv_, y, ALU.add, e="pool")
    mb = Rot([p.sb(f"mb{i}", [128, D], BF16) for i in range(1)])
    pst = Rot([p.ps(f"pst{i}", [128, 4, 128], BF16) for i in range(2)])
    for t in range(NT):
        m = mb.next()
        p.copy(m, merged[:, t, :], e="act")
        for q4 in range(4):
            pt = pst.next()
            for i in range(4):
                kc = q4 * 4 + i
                p.tr(pt[:, i, :], m[:, kc * 128:(kc + 1) * 128], ident)
            p.copy(mT[:, q4 * 4:(q4 + 1) * 4, t * 128:(t + 1) * 128], pt, e=("dve" if q4 % 2 == 0 else "act"))
    for cc in range(4):
        wa0 = wbB.next()
        wload(wa0, wo, 0, 8, cc * 512, 512)
        wa1 = wbB.next()
        wload(wa1, wo, 8, 8, cc * 512, 512)
        for t in range(NT):
            ps = pss.next()
            for kc in range(16):
                p.mm(ps, mT[:, kc, t * 128:(t + 1) * 128], (wa0 if kc < 8 else wa1)[:, kc % 8, :],
                     start=(kc == 0), stop=(kc == 15))
            g = gst.next()
            p.dma(g, x[t * 128:(t + 1) * 128, cc * 512:(cc + 1) * 512])
            p.stt(merged[:, t, cc * 512:(cc + 1) * 512], g, ALPHA, ps, ALU.mult, ALU.add)
    if DEBUG:
        dbg = p.dram("dbg", [Tc, D], kind="ExternalOutput")
        for t in range(NT):
            p.dma(dbg[t * 128:(t + 1) * 128, :], merged[:, t, :])
    scr = ln_scratch(p, "1")
    for t in range(NT):
        o = merged[:, t, :]
        layer_norm_tile(p, merged[:, t, :], o, lng, lnb, scr)
        p.dma(x1[t * 128:(t + 1) * 128, :], o)
    return p


def build_C2(Tc):
    p = P()
    NT = Tc // 128
    BL = min(256, Tc)
    NB = Tc // BL
    NJ = D_FF // 128
    x1 = p.dram("x1", [Tc, D])
    x1T = p.dram("x1T", [D, Tc + 2])
    wup = p.dram("wup", [D, 2 * D_FF])
    wcv = p.dram("wcv", [128, 2 * NJ, 3])
    wdn = p.dram("wdn", [D_FF, D])
    lng_d = p.dram("lng", [128, D])
    lnb_d = p.dram("lnb", [128, D])
    x2 = p.dram("x2", [Tc, D], kind="ExternalOutput")

    lng = p.sb("lng_s", [128, D])
    lnb = p.sb("lnb_s", [128, D])
    p.dma(lng, lng_d)
    p.dma(lnb, lnb_d)
    wc = p.sb("wc", [128, 2 * NJ, 3])
    p.dma(wc, wcv)
    acc = p.sb("acc", [128, NT, D])
    for t in range(NT):
        p.dma(acc[:, t, :], x1[t * 128:(t + 1) * 128, :])
    for t in range(NT):
        p.ts(acc[:, t, :], acc[:, t, :], ALPHA, ALU.mult, e=("dve" if t % 2 == 0 else "pool"))
    xb = p.sb("xb", [128, 16, Tc + 2], BF16)
    xst = Rot([p.sb(f"xst{i}", [128, Tc + 2]) for i in range(1)])
    xv = x1T.ap.rearrange("(kc p) t -> p kc t", p=128)
    load_cast_T(p, lambda kc: V(x1T, xv[:, kc, :]), xb, 16, Tc + 2, xst)

    JG = 2
    ust = Rot([p.sb(f"ust{i}", [128, 8, 128]) for i in range(2)])
    ub = Rot([p.sb(f"ub{i}", [128, 16, JG, 2, 128], BF16) for i in range(2)])
    dst_ = Rot([p.sb(f"dst{i}", [128, D]) for i in range(2)])
    db = Rot([p.sb(f"db{i}", [128, JG, D], BF16) for i in range(2)])
    gT = Rot([p.sb(f"gT{i}", [128, JG, Tc], BF16) for i in range(2)])
    ha = Rot([p.sb(f"ha{i}", [128, BL]) for i in range(2)])
    hb = Rot([p.sb(f"hb{i}", [128, BL]) for i in range(2)])
    sa = Rot([p.sb(f"sa{i}", [128, BL]) for i in range(2)])
    psu = Rot([p.ps(f"psu{i}", [128, 512]) for i in range(4)])
    psd = Rot([p.ps(f"psd{i}", [128, 512]) for i in range(4)])
    upv = wup.ap.rearrange("(kc p) n -> p kc n", p=128)
    ce = [0]

    def load_group(jg):
        u = ub.next()
        dd = db.next()
        for ji in range(JG):
            j = jg * JG + ji
            for half, c0 in ((0, j * 128), (1, D_FF + j * 128)):
                for h in range(2):
                    st = ust.next()
                    p.dma(st[:, :, 0:128], V(wup, upv[:, 8 * h:8 * h + 8, c0:c0 + 128]))
                    e = "dve" if ce[0] % 2 == 0 else "pool"
                    ce[0] += 1
                    p.copy(u[:, 8 * h:8 * h + 8, ji, half, :], st[:, :, 0:128], e=e)
            st = dst_.next()
            p.dma(st, wdn[j * 128:(j + 1) * 128, :])
            p.copy(dd[:, ji, :], st, e="pool")
        return u, dd

    nxt = load_group(0)
    for jg in range(NJ // JG):
        u, dd = nxt
        if jg + 1 < NJ // JG:
            nxt = load_group(jg + 1)
        g = gT.next()
        for ji in range(JG):
            j = jg * JG + ji
            for blk in range(NB):
                hs = []
                for half in range(2):
                    ps = psu.next()
                    for kc in range(16):
                        p.mm(ps[:, 0:BL + 2], u[:, kc, ji, half, :], xb[:, kc, blk * BL:blk * BL + BL + 2],
                             start=(kc == 0), stop=(kc == 15))
                    h_ = (ha if half == 0 else hb).next()
                    cj = j + half * NJ
                    p.act(h_, ps[:, 2:BL + 2], AF.Copy, scale=wc[:, cj, 2:3])
                    p.stt(h_, ps[:, 1:BL + 1], wc[:, cj, 1:2], h_, ALU.mult, ALU.add)
                    p.stt(h_, ps[:, 0:BL], wc[:, cj, 0:1], h_, ALU.mult, ALU.add)
                    hs.append(h_)
                s = sa.next()
                p.act(s, hs[0], AF.Silu)
                p.tt(g[:, ji, blk * BL:(blk + 1) * BL], s, hs[1], ALU.mult, e="pool")
        for t in range(NT):
            for cc in range(4):
                ps = psd.next()
                for ji in range(JG):
                    p.mm(ps, g[:, ji, t * 128:(t + 1) * 128], dd[:, ji, cc * 512:(cc + 1) * 512],
                         start=(ji == 0), stop=(ji == JG - 1))
                a = acc[:, t, cc * 512:(cc + 1) * 512]
                p.tt(a, a, ps, ALU.add)
    scr = ln_scratch(p, "2")
    for t in range(NT):
        o = acc[:, t, :]
        layer_norm_tile(p, acc[:, t, :], o, lng, lnb, scr)
        p.dma(x2[t * 128:(t + 1) * 128, :], o)
    return p


def build_B1(Tt):
    p = P()
    TB = min(512, Tt)
    NBK = Tt // TB
    uT = p.dram("uT", [128, Tt])
    zT = p.dram("zT", [128, Tt], BF16, kind="ExternalOutput")
    small = {}
    for nm, shp in (("LRr", [128, 64]), ("LIr", [128, 64]), ("LSr", [128, 64]), ("BreT", [128, 64]),
                    ("BimT", [128, 64]), ("LRc", [128, 8]), ("LIc", [128, 8]), ("LSc", [128, 8]),
                    ("CC", [128, 128]), ("CCs", [128, 128]), ("dsk", [128, 1]), ("identf", [128, 128]),
                    ("S12", [128, 128]), ("sgnA", [128, 1]), ("maskg", [128, 8]), ("iota", [128, TB])):
        dt_ = p.dram(nm, shp)
        st = p.sb("s_" + nm, shp)
        p.dma(st, dt_)
        small[nm] = st
    S = small
    n = [0]

    def tmp(shape, dt=F32):
        n[0] += 1
        return p.sb(f"t{n[0]}", shape, dt)

    sh = [128, 64]
    stp = tmp(sh); ang = tmp(sh); sn = tmp(sh); cs = tmp(sh); tf = tmp(sh); ti = tmp(sh, I32); tk = tmp(sh)
    p.act(stp, S["LSr"], AF.Exp)
    p.tt(ang, S["LIr"], stp, ALU.mult)
    sincos(p, ang, sn, cs, tf, ti, tk)
    mag = tmp(sh)
    p.tt(mag, S["LRr"], stp, ALU.mult)
    p.act(mag, mag, AF.Exp)
    abr1 = tmp(sh); abi = tmp(sh)
    p.tt(abr1, mag, cs, ALU.mult)
    p.ts(abr1, abr1, -1.0, ALU.add)
    p.tt(abi, mag, sn, ALU.mult)
    den = tmp(sh); t2 = tmp(sh)
    p.tt(den, S["LRr"], S["LRr"], ALU.mult)
    p.tt(t2, S["LIr"], S["LIr"], ALU.mult)
    p.tt(den, den, t2, ALU.add)
    p.recip(den, den)
    fre = tmp(sh); fim = tmp(sh)
    p.tt(fre, abr1, S["LRr"], ALU.mult)
    p.tt(t2, abi, S["LIr"], ALU.mult)
    p.tt(fre, fre, t2, ALU.add)
    p.tt(fre, fre, den, ALU.mult)
    p.tt(fim, abi, S["LRr"], ALU.mult)
    p.tt(t2, abr1, S["LIr"], ALU.mult)
    p.tt(fim, fim, t2, ALU.subtract)
    p.tt(fim, fim, den, ALU.mult)
    bbr = tmp(sh); bbi = tmp(sh); nbbr = tmp(sh)
    p.tt(bbr, fre, S["BreT"], ALU.mult)
    p.tt(t2, fim, S["BimT"], ALU.mult)
    p.tt(bbr, bbr, t2, ALU.subtract)
    p.tt(bbi, fre, S["BimT"], ALU.mult)
    p.tt(t2, fim, S["BreT"], ALU.mult)
    p.tt(bbi, bbi, t2, ALU.add)
    p.ts(nbbr, bbr, -1.0, ALU.mult)
    BT = p.sb("BT", [128, 8, 128], BF16)
    BTs = p.sb("BTs", [128, 8, 128], BF16)
    for g in range(8):
        mg = S["maskg"][:, g:g + 1]
        p.ts(BT[:, g, 0:64], bbr, mg, ALU.mult)
        p.ts(BT[:, g, 64:128], bbi, mg, ALU.mult)
        p.ts(BTs[:, g, 0:64], bbi, mg, ALU.mult)
        p.ts(BTs[:, g, 64:128], nbbr, mg, ALU.mult)
    C1a = p.sb("C1a", [128, 8, 128], BF16)
    C2a = p.sb("C2a", [128, 8, 128], BF16)
    p.memset(C1a, 0.0)
    p.memset(C2a, 0.0)
    for g in range(8):
        p.ts(C1a[:, g, 16 * g:16 * g + 16], S["CC"][:, 16 * g:16 * g + 16], S["sgnA"][:, 0:1], ALU.mult)
        p.ts(C2a[:, g, 16 * g:16 * g + 16], S["CCs"][:, 16 * g:16 * g + 16], -1.0, ALU.mult)
    sh8 = [128, 8]
    stc = tmp(sh8); rho = tmp(sh8); theta = tmp(sh8)
    p.act(stc, S["LSc"], AF.Exp)
    p.tt(rho, S["LRc"], stc, ALU.mult)
    p.act(rho, rho, AF.Exp)
    p.tt(theta, S["LIc"], stc, ALU.mult)
    phi = tmp(sh8); sph = tmp(sh8); cph = tmp(sh8); f8 = tmp(sh8); i8 = tmp(sh8, I32); k8 = tmp(sh8)
    p.ts(phi, theta, float(TB), ALU.mult)
    sincos(p, phi, sph, cph, f8, i8, k8)
    p.ts(sph, sph, S["sgnA"][:, 0:1], ALU.mult)
    ROT = p.sb("ROT", [128, 8, 128])
    for g in range(8):
        p.ts(ROT[:, g, :], S["identf"], cph[:, g:g + 1], ALU.mult)
        p.stt(ROT[:, g, :], S["S12"], sph[:, g:g + 1], ROT[:, g, :], ALU.mult, ALU.add)
    SIN = p.sb("SIN", [128, 8, TB])
    COS = p.sb("COS", [128, 8, TB])
    RHO = p.sb("RHO", [128, 8, TB])
    shb = [128, TB]
    angb = tmp(shb); fb = tmp(shb); ib = tmp(shb, I32); kb = tmp(shb)
    for g in range(8):
        p.ts(angb, S["iota"], theta[:, g:g + 1], ALU.mult)
        sincos(p, angb, SIN[:, g, :], COS[:, g, :], fb, ib, kb)
        p.ts(RHO[:, g, :], S["iota"], 0.0, ALU.mult, rho[:, g:g + 1], ALU.add)

    ust = Rot([p.sb(f"ust{i}", [128, TB]) for i in range(2)])
    ubf = Rot([p.sb(f"ubf{i}", [128, TB], BF16) for i in range(2)])
    psa = Rot([p.ps(f"psa{i}", [128, TB]) for i in range(2)])
    psb = Rot([p.ps(f"psb{i}", [128, TB]) for i in range(2)])
    psy = Rot([p.ps(f"psy{i}", [128, TB]) for i in range(2)])
    psr = p.ps("psr", [128, 8])
    v1 = Rot([p.sb(f"v1{i}", [128, TB]) for i in range(2)])
    v2 = Rot([p.sb(f"v2{i}", [128, TB]) for i in range(2)])
    vv = Rot([p.sb(f"vv{i}", [128, TB]) for i in range(2)])
    shat = Rot([p.sb(f"shat{i}", [128, TB]) for i in range(3)])
    w1 = Rot([p.sb(f"w1{i}", [128, TB], BF16) for i in range(2)])
    w2 = Rot([p.sb(f"w2{i}", [128, TB], BF16) for i in range(2)])
    last = p.sb("last", [128, 8])
    init = p.sb("init", [128, 8])
    yb = Rot([p.sb(f"yb{i}", [128, TB]) for i in range(2)])
    gt = Rot([p.sb(f"gt{i}", [128, TB]) for i in range(2)])
    zo = Rot([p.sb(f"zo{i}", [128, TB], BF16) for i in range(2)])
    for b in range(NBK):
        uf = ust.next()
        p.dma(uf, uT[:, b * TB:(b + 1) * TB])
        ub = ubf.next()
        p.copy(ub, uf, e="act")
        if b > 0:
            for g in range(8):
                p.mm(psr[:, g:g + 1], ROT[:, g, :], last[:, g:g + 1])
            p.copy(init, psr)
        py = psy.next()
        for g in range(8):
            pa = psa.next()
            pb = psb.next()
            p.mm(pa, BT[:, g, :], ub)
            p.mm(pb, BTs[:, g, :], ub)
            a1 = v1.next(); a2 = v2.next(); av = vv.next()
            p.tt(a1, pa, COS[:, g, :], ALU.mult)
            p.tt(a2, pb, SIN[:, g, :], ALU.mult)
            p.tt(av, a1, a2, ALU.add, e="pool")
            sh_ = shat.next()
            p.scan(sh_, RHO[:, g, :], av, (0.0 if b == 0 else init[:, g:g + 1]), ALU.mult, ALU.add)
            p.copy(last[:, g:g + 1], sh_[:, TB - 1:TB], e="act")
            b1 = w1.next(); b2 = w2.next()
            p.tt(b1, sh_, COS[:, g, :], ALU.mult, e="pool")
            p.tt(b2, sh_, SIN[:, g, :], ALU.mult, e="pool")
            p.mm(py, C1a[:, g, :], b1, start=(g == 0), stop=False)
            p.mm(py, C2a[:, g, :], b2, start=False, stop=(g == 7))
        y = yb.next()
        p.stt(y, uf, S["dsk"][:, 0:1], py, ALU.mult, ALU.add)
        t_ = gt.next()
        p.tt(t_, y, y, ALU.mult, e="pool")
        p.ts(t_, t_, 0.044715, ALU.mult, 1.0, ALU.add, e="pool")
        p.tt(t_, t_, y, ALU.mult, e="pool")
        p.act(t_, t_, AF.Sigmoid, scale=1.5957691216057308)
        z = zo.next()
        p.tt(z, y, t_, ALU.mult)
        p.dma(zT[:, b * TB:(b + 1) * TB], z)
    return p


def host_B1_inputs(inp, l, c, uT_c, TB):
    gs = slice(8 * c, 8 * c + 8)
    lr = inp["ssm_lambda_re"][l][gs]; li = inp["ssm_lambda_im"][l][gs]; ls = inp["ssm_log_step"][l][gs]
    bre = inp["ssm_b_re"][l][gs]; bim = inp["ssm_b_im"][l][gs]
    cre = inp["ssm_c_re"][l][gs]; cim = inp["ssm_c_im"][l][gs]
    f = np.float32
    rep16 = lambda a: np.ascontiguousarray(np.repeat(a, 16, axis=0)).astype(f)
    m = {"uT": uT_c}
    m["LRr"] = rep16(lr); m["LIr"] = rep16(li)
    m["LSr"] = rep16(np.broadcast_to(ls[:, None], (8, 64)))
    m["BreT"] = np.ascontiguousarray(bre.transpose(0, 2, 1).reshape(128, 64)).astype(f)
    m["BimT"] = np.ascontiguousarray(bim.transpose(0, 2, 1).reshape(128, 64)).astype(f)
    m["LRc"] = np.ascontiguousarray(np.concatenate([lr.T, lr.T], 0)).astype(f)
    m["LIc"] = np.ascontiguousarray(np.concatenate([li.T, li.T], 0)).astype(f)
    m["LSc"] = np.ascontiguousarray(np.broadcast_to(ls[None, :], (128, 8))).astype(f)
    creT = cre.transpose(2, 0, 1).reshape(64, 128); cimT = cim.transpose(2, 0, 1).reshape(64, 128)
    m["CC"] = np.ascontiguousarray(np.concatenate([creT, cimT], 0)).astype(f)
    m["CCs"] = np.ascontiguousarray(np.concatenate([cimT, creT], 0)).astype(f)
    m["dsk"] = np.ascontiguousarray(inp["ssm_d"][l][128 * c:128 * c + 128].reshape(128, 1)).astype(f)
    m["identf"] = np.eye(128, dtype=f)
    s12 = np.zeros((128, 128), f)
    for k in range(64):
        s12[k, k + 64] = 1.0
        s12[k + 64, k] = 1.0
    m["S12"] = s12
    sg = np.ones((128, 1), f); sg[64:] = -1.0
    m["sgnA"] = sg
    mg = np.zeros((128, 8), f)
    for g in range(8):
        mg[16 * g:16 * g + 16, g] = 1.0
    m["maskg"] = mg
    m["iota"] = np.ascontiguousarray(np.broadcast_to(np.arange(TB, dtype=f)[None, :], (128, TB)))
    return m


def build_B2(Tt):
    p = P()
    TB = min(512, Tt)
    NBK = Tt // TB
    NCH = TB // 128
    qT = p.dram("qT", [128, Tt])
    kT = p.dram("kT", [128, Tt])
    vv = p.dram("v", [Tt, 128])
    sg = p.dram("sg", [Tt, 128])
    pos = p.dram("pos", [128, Tt], I32)
    o_out = p.dram("o", [Tt, 128], BF16, kind="ExternalOutput")
    S = {}
    for nm, shp in (("invf", [128, 1]), ("PERM", [128, 128]), ("DmT", [128, 128]), ("qd", [128, 1]),
                    ("kd", [128, 1]), ("cdec", [128, 1]), ("NG", [128, 128]), ("NB", [128, 128]),
                    ("identf", [128, 128])):
        dt_ = p.dram(nm, shp)
        st = p.sb("s_" + nm, shp)
        p.dma(st, dt_)
        S[nm] = st
    identb = p.sb("identb", [128, 128], BF16)
    p.copy(identb, S["identf"])
    state = p.sb("state", [128, 128])
    state_bf = p.sb("state_bf", [128, 128], BF16)
    p.memset(state, 0.0)
    p.memset(state_bf, 0.0)
    shb = [128, TB]
    qf = Rot([p.sb(f"qf{i}", shb) for i in range(2)])
    kf = Rot([p.sb(f"kf{i}", shb) for i in range(2)])
    posi = p.sb("posi", shb, I32)
    ang = p.sb("ang", shb); fb = p.sb("fb", shb); ib = p.sb("ib", shb, I32); kb = p.sb("kb", shb)
    SINt = p.sb("SINt", shb); COSt = p.sb("COSt", shb)
    r1 = p.sb("r1", shb); r2 = p.sb("r2", shb)
    rqT = Rot([p.sb(f"rqT{i}", shb, BF16) for i in range(2)])
    rkT = Rot([p.sb(f"rkT{i}", shb, BF16) for i in range(2)])
    pP = Rot([p.ps(f"pP{i}", shb) for i in range(2)])
    psc = p.ps("psc", [128, 128]); po1 = p.ps("po1", [128, 128]); po2 = p.ps("po2", [128, 128])
    ptr = p.ps("ptr", [128, 128], BF16); pkv = p.ps("pkv", [128, 128])
    vf = Rot([p.sb(f"vf{i}", [128, 128]) for i in range(2)])
    vb = Rot([p.sb(f"vb{i}", [128, 128], BF16) for i in range(2)])
    sgf = Rot([p.sb(f"sgf{i}", [128, 128]) for i in range(2)])
    scm = Rot([p.sb(f"scm{i}", [128, 128], BF16) for i in range(2)])
    insb = Rot([p.sb(f"insb{i}", [128, 128]) for i in range(2)])
    osb = Rot([p.sb(f"osb{i}", [128, 128]) for i in range(2)])
    kdb = Rot([p.sb(f"kdb{i}", [128, 128], BF16) for i in range(2)])
    st6 = p.sb("st6", [128, 6]); mv = p.sb("mv", [128, 2]); rs = p.sb("rs", [128, 1])
    onb = Rot([p.sb(f"onb{i}", [128, 128]) for i in range(2)])
    oo = Rot([p.sb(f"oo{i}", [128, 128], BF16) for i in range(2)])
    for b in range(NBK):
        cs_ = slice(b * TB, (b + 1) * TB)
        q_ = qf.next(); k_ = kf.next()
        p.dma(q_, qT[:, cs_])
        p.dma(k_, kT[:, cs_])
        p.dma(posi, pos[:, cs_])
        p.copy(ang, posi)
        p.ts(ang, ang, S["invf"][:, 0:1], ALU.mult)
        sincos(p, ang, SINt, COSt, fb, ib, kb)
        rots = []
        for src, dstrot in ((q_, rqT), (k_, rkT)):
            pp = pP.next()
            p.mm(pp, S["PERM"], src)
            p.tt(r1, src, COSt, ALU.mult, e="pool")
            p.tt(r2, pp, SINt, ALU.mult)
            dd = dstrot.next()
            p.tt(dd, r1, r2, ALU.add)
            rots.append(dd)
        rq, rk = rots
        for n_ in range(NCH):
            c0 = n_ * 128
            rows = slice(b * TB + c0, b * TB + c0 + 128)
            v_ = vf.next(); s_ = sgf.next()
            p.dma(v_, vv[rows, :])
            p.dma(s_, sg[rows, :])
            vb_ = vb.next()
            p.copy(vb_, v_, e="act")
            p.mm(psc, rk[:, c0:c0 + 128], rq[:, c0:c0 + 128])
            sc = scm.next()
            p.tt(sc, psc, S["DmT"], ALU.mult)
            p.mm(po1, sc, vb_)
            p.mm(po2, rq[:, c0:c0 + 128], state_bf)
            i_ = insb.next()
            p.copy(i_, po1, e="act")
            o_ = osb.next()
            p.stt(o_, po2, S["qd"][:, 0:1], i_, ALU.mult, ALU.add)
            p.tr(ptr, rk[:, c0:c0 + 128], identb)
            kd_ = kdb.next()
            p.ts(kd_, ptr, S["kd"][:, 0:1], ALU.mult)
            p.mm(pkv, kd_, vb_)
            p.stt(state, state, S["cdec"][:, 0:1], pkv, ALU.mult, ALU.add)
            p.copy(state_bf, state, e="act")
            p.bn_stats(st6, o_)
            p.bn_aggr(mv, st6)
            p.ts(rs, mv[:, 1:2], 1e-5, ALU.add)
            p.act(rs, rs, AF.Sqrt)
            p.recip(rs, rs)
            on = onb.next()
            p.ts(on, o_, mv[:, 0:1], ALU.subtract, rs[:, 0:1], ALU.mult)
            p.tt(on, on, S["NG"], ALU.mult, e="pool")
            p.tt(on, on, S["NB"], ALU.add, e="pool")
            ob = oo.next()
            p.tt(ob, on, s_, ALU.mult, e="pool")
            p.dma(o_out[rows, :], ob)
    return p


def host_B2_inputs(inp, l, c, qT_c, kT_c, v_c, sg_c, pos_T):
    f = np.float32
    h = c
    lg = np.log1p(-np.exp2(-5.0 - h))
    idx = np.arange(128, dtype=np.float64)
    rel = idx[None, :] - idx[:, None]
    sc = 128 ** -0.5
    DmT = np.where(rel >= 0, np.exp(lg * np.maximum(rel, 0.0)), 0.0) * sc
    perm = np.zeros((128, 128), f)
    for m_ in range(64):
        perm[m_ + 64, m_] = -1.0
        perm[m_, m_ + 64] = 1.0
    invf = (10000.0 ** (-np.arange(64, dtype=f) / 64)).astype(f)
    rep = lambda v_: np.ascontiguousarray(np.broadcast_to(v_[None, :], (128, v_.shape[0]))).astype(f)
    return {
        "qT": qT_c, "kT": kT_c, "v": v_c, "sg": sg_c, "pos": pos_T,
        "invf": np.concatenate([invf, invf]).reshape(128, 1).astype(f),
        "PERM": perm, "DmT": DmT.astype(f),
        "qd": np.exp(lg * (idx + 1.0)).reshape(128, 1).astype(f),
        "kd": (np.exp(lg * (127.0 - idx)) * sc).reshape(128, 1).astype(f),
        "cdec": np.full((128, 1), np.exp(lg * 128.0), f),
        "NG": rep(inp["ret_norm_g"][l][128 * c:128 * c + 128]),
        "NB": rep(inp["ret_norm_b"][l][128 * c:128 * c + 128]),
        "identf": np.eye(128, dtype=f),
    }


def build_B3(Tt, stage=9):
    p = P()
    TB = min(512, Tt)
    NBK = Tt // TB
    CH = 64
    NCH = TB // CH
    zin = {nm: p.dram(nm, [128, Tt + 1]) for nm in ("zr", "zk", "zv", "zl1", "zl2")}
    o_out = p.dram("o", [Tt, 128], BF16, kind="ExternalOutput")
    S = {}
    for nm, shp in (("MU", [128, 5]), ("PV", [128, 5]), ("w2c", [128, 128]), ("g2c", [128, 128]),
                    ("NG", [64, 128]), ("NB", [64, 128]), ("bones", [128, 128]), ("hsel", [128, 2]),
                    ("cmask", [128, TB]), ("MK1", [64, 512]), ("MK2", [64, 512]), ("MK3", [64, 256]),
                    ("identf", [128, 128]), ("bdmask", [128, 128])):
        dt_ = p.dram(nm, shp)
        st = p.sb("s_" + nm, shp)
        p.dma(st, dt_)
        S[nm] = st
    w2b = p.sb("w2b", [128, 128], BF16); p.copy(w2b, S["w2c"])
    g2b = p.sb("g2b", [128, 128], BF16); p.copy(g2b, S["g2c"])
    hselb = p.sb("hselb", [128, 2], BF16); p.copy(hselb, S["hsel"])
    identb = p.sb("identb", [128, 128], BF16); p.copy(identb, S["identf"])
    banks = [p.ps(f"bank{i}", [128, 512]) for i in range(7)]
    bank7 = p.ps("bank7", [128, 1024], BF16)
    H = p.sb("H", [128, 128]); Hb = p.sb("Hb", [128, 128], BF16)
    p.memset(H, 0.0); p.memset(Hb, 0.0)
    shb = [128, TB]
    n = [0]

    def tmp(shape=shb, dt=F32):
        n[0] += 1
        return p.sb(f"t{n[0]}", shape, dt)

    zst = {nm: Rot([p.sb(f"zst_{nm}{i}", [128, TB + 1]) for i in range(2)]) for nm in zin}
    dsh = tmp(); r_ = tmp(); k_ = tmp(); v_ = tmp(); l1 = tmp(); l2 = tmp()
    l1b = tmp(dt=BF16); sgb = tmp([128, TB + 64], BF16)
    p.memset(sgb, 0.0)
    ld = tmp(); a_ = tmp(); cs = tmp(); e0 = tmp(); e1 = tmp(); e2 = tmp(); tq = tmp()
    kk = tmp(); rn = tmp(); kmod = tmp(); bv = tmp()
    KR = p.sb("KR", [128, NCH + 1, 2, CH], BF16)
    p.memset(KR, 0.0)
    Bh = tmp([128, TB + 64], BF16); Kh = tmp([128, TB + 64], BF16); rkr = tmp([128, TB + 64], BF16); vbf = tmp([128, TB + 64], BF16)
    for t_ in (Bh, Kh, rkr, vbf):
        p.memset(t_, 0.0)
    Vb = p.sb("Vb", [64, NCH, 128], BF16); Vf = p.sb("Vf", [64, NCH, 128])
    BhT = p.sb("BhT", [64, NCH, 128], BF16); KhT = p.sb("KhT", [64, NCH, 128], BF16)
    Gtm = p.sb("Gtm", [64, NCH, 128]); BS = p.sb("BS", [64, NCH * 2])
    A1 = p.sb("A1", [64, NCH + 1, 2, 128], BF16); A2 = p.sb("A2", [64, NCH + 1, 2, 128], BF16)
    p.memset(A1, 0.0); p.memset(A2, 0.0)
    Mm = p.sb("Mm", [64, NCH + 1, 2, 64], BF16)
    p.memset(Mm, 0.0)
    MMs = [Rot([p.sb(f"MM{g}{i}", [64, 9, 128], BF16) for i in range(2)]) for g in range(2)]
    Tts = [Rot([p.sb(f"Tt{g}{i}", [64, 10, 64], BF16) for i in range(2)]) for g in range(2)]
    W0n = Rot([p.sb(f"W0n{i}", [64, 128], BF16) for i in range(2)])
    Ub = Rot([p.sb(f"Ub{i}", [64, 128], BF16) for i in range(2)])
    Ysb = p.sb("Ysb", [64, NCH, 128])
    tH = p.sb("tH", [128, 128])
    xc = p.sb("xc", [64, NCH, 128]); sq_ = p.sb("sq_", [64, NCH, 128])
    s16 = p.sb("s16", [64, NCH * 2]); v16 = p.sb("v16", [64, NCH * 2])
    osb = Rot([p.sb(f"osb{i}", [64, NCH, 128], BF16) for i in range(2)])

    def c3(vw):
        return vw.re("p (c t) -> p c t", t=CH)

    for b in range(NBK):
        zs = {}
        for nm in zin:
            st = zst[nm].next()
            p.dma(st, zin[nm][:, b * TB:b * TB + TB + 1])
            zs[nm] = st
        for i, (nm, dst) in enumerate((("zr", r_), ("zk", k_), ("zv", v_), ("zl1", l1), ("zl2", l2))):
            st = zs[nm]
            p.tt(dsh, st[:, 0:TB], st[:, 1:TB + 1], ALU.subtract, e="pool")
            p.stt(dst, dsh, S["MU"][:, i:i + 1], st[:, 1:TB + 1], ALU.mult, ALU.add)
        p.act(l1b[0:64, :], l1[0:64, :], AF.Tanh)
        p.copy(l1b[64:128, :], l1[64:128, :], e="act")
        p.act(sgb[:, 0:TB], l2, AF.Sigmoid)
        p.mm(banks[0], w2b[0:64, :], l1b[0:64, :])
        p.mm(banks[1], w2b[64:128, :], l1b[64:128, :])
        p.act(ld, banks[0], AF.Sigmoid, bias=S["PV"][:, 0:1])
        p.ts(ld, ld, -EXPM05, ALU.mult)
        p.act(a_, banks[1], AF.Sigmoid, bias=S["PV"][:, 1:2])
        p.scan(cs, S["cmask"], ld, 0.0, ALU.mult, ALU.add)
        p.act(e1, cs, AF.Exp)
        p.act(e2, cs, AF.Exp, scale=-1.0)
        p.tt(tq, cs, ld, ALU.subtract, e="pool")
        p.act(e0, tq, AF.Exp)
        p.ts(kk, k_, S["PV"][:, 2:3], ALU.mult)
        p.tt(tq, kk, kk, ALU.mult, e="pool")
        p.mm(banks[2], S["bones"], tq)
        p.ts(rn, banks[2], 1e-24, ALU.max)
        p.act(rn, rn, AF.Sqrt)
        p.recip(rn, rn)
        p.tt(kk, kk, rn, ALU.mult)
        p.ts(tq, a_, -1.0, ALU.add)
        p.ts(tq, tq, S["PV"][:, 3:4], ALU.mult)
        p.stt(kmod, tq, 1.0, k_, ALU.add, ALU.mult)
        p.tt(bv, a_, kk, ALU.mult, e="pool")
        p.tt(KR[:, 0:NCH, 0, :], c3(kk[:, :]), c3(e0[:, :]), ALU.mult)
        p.tt(KR[:, 0:NCH, 1, :], c3(r_[:, :]), c3(e1[:, :]), ALU.mult)
        p.tt(Bh[:, 0:TB], bv, e2, ALU.mult, e="pool")
        p.tt(Kh[:, 0:TB], kmod, e2, ALU.mult, e="pool")
        p.ts(tq, r_, S["PV"][:, 4:5], ALU.mult)
        p.tt(rkr[:, 0:TB], tq, kmod, ALU.mult)
        if stage < 2:
            continue
        for g4 in range(NCH // 4):
            if g4 == 0:
                p.copy(vbf[:, 0:TB], v_, e="act")
            if KNOB == 1:
                continue
            for i in range(4):
                c = g4 * 4 + i
                p.mm(banks[3][:, i * 128:(i + 1) * 128], vbf[:, c * CH:c * CH + 128], identb)
            if KNOB == 2:
                continue
            p.copy(Vf[:, g4 * 4:(g4 + 1) * 4, :], banks[3][0:64, 0:512].re("p (c e) -> p c e", e=128), e="act")
            if KNOB == 3:
                continue
            p.copy(Vb[:, g4 * 4:(g4 + 1) * 4, :], Vf[:, g4 * 4:(g4 + 1) * 4, :])
            if stage == 2 and SUB < 2:
                continue
            pg = banks[4]
            for i in range(4):
                c = g4 * 4 + i
                p.mm(pg[:, i * 128:(i + 1) * 128], sgb[:, c * CH:c * CH + 128], g2b)
            p.copy(Gtm[:, g4 * 4:(g4 + 1) * 4, :], pg[0:64, :].re("p (c e) -> p c e", e=128), e="act")
            for bi_, (src, dstT) in enumerate(((Bh, BhT), (Kh, KhT))):
                bk_ = banks[5 + bi_]
                for i in range(4):
                    c = g4 * 4 + i
                    p.mm(bk_[:, i * 128:(i + 1) * 128], src[:, c * CH:c * CH + 128], identb)
                p.copy(dstT[:, g4 * 4:(g4 + 1) * 4, :], bk_[0:64, 0:512].re("p (c e) -> p c e", e=128))
        if stage == 2 and SUB < 3:
            continue
        for c in range(NCH):
            p.mm(banks[2][:, 2 * c:2 * c + 2], rkr[:, c * CH:c * CH + 128], hselb)
        p.copy(BS, banks[2][0:64, 0:2 * NCH], e="act")
        if stage < 3:
            continue
        for g4 in range(NCH // 4):
            for ci in range(4):
                c = g4 * 4 + ci
                for h in range(2):
                    ho = 64 * h
                    krv = KR[ho:ho + 64, c, :, :].re("p a t -> p (a t)")
                    p.mm(banks[0 + h][:, ci * 128:(ci + 1) * 128], Bh[ho:ho + 64, c * CH:c * CH + 128], krv)
                    p.mm(banks[2 + h][:, ci * 128:(ci + 1) * 128], Kh[ho:ho + 64, c * CH:c * CH + 128], krv)
                    p.mm(banks[4 + h][:, ci * 64:(ci + 1) * 64], krv, Bh[ho:ho + 64, c * CH:(c + 1) * CH])
            for h in range(2):
                mk1 = S["MK1"][:, :].re("p (c t) -> p c t", t=128)
                mk2 = S["MK2"][:, :].re("p (c t) -> p c t", t=128)
                mk3 = S["MK3"][:, :].re("p (c t) -> p c t", t=64)
                p.tt(A1[:, 4 * g4:4 * g4 + 4, h, :], banks[0 + h][0:64, :].re("p (c t) -> p c t", t=128), mk1, ALU.mult)
                p.tt(A2[:, 4 * g4:4 * g4 + 4, h, :], banks[2 + h][0:64, :].re("p (c t) -> p c t", t=128), mk2, ALU.mult)
                p.tt(Mm[:, 4 * g4:4 * g4 + 4, h, :], banks[4 + h][0:64, 0:256].re("p (c t) -> p c t", t=64), mk3, ALU.mult)
        if stage < 4:
            continue
        Tfin = []
        for G in range(NCH // 4):
            pairs = [(G * 4 + ci, h) for ci in range(4) for h in range(2)]
            Tt = Tts[G].next()
            for pi, (c, h) in enumerate(pairs):
                p.tt(Tt[:, pi, :], A1[:, c, h, 0:64], identb[0:64, 0:64], ALU.add, e="pool")
            p.memset(Tt[:, 8:10, :], 0.0, e="pool")
            Mflat = Mm[:, :, :, :].re("p c h t -> p (c h t)")
            curM = [Mflat[:, (2 * c + h) * 64:(2 * c + h) * 64 + 128] for (c, h) in pairs]
            curMt = [A1[:, c, h, 0:128] for (c, h) in pairs]
            for lev in range(1, 6):
                last = (lev == 5)
                sqb = [banks[0 + 2 * G], banks[1 + 2 * G]]
                for pi in range(8):
                    bk = sqb[pi // 4]
                    col = (pi % 4) * 128
                    p.mm(bk[:, col:col + 64], curMt[pi], curM[pi][:, 0:64])
                    if not last:
                        p.mm(bk[:, col + 64:col + 128], curM[pi], curMt[pi][:, 0:64])
                MMn = MMs[G].next()
                for hb in range(2):
                    p.copy(MMn[:, 4 * hb:4 * hb + 4, :], sqb[hb][0:64, :].re("p (a t) -> p a t", t=128),
                           e=("act" if hb == 0 else "dve"))
                MMf = MMn[:, :, :].re("p a t -> p (a t)")
                curM = [MMf[:, pi * 128:pi * 128 + 128] for pi in range(8)]
                curMt = [MMf[:, pi * 128 + 64:pi * 128 + 192] for pi in range(8)]
                pacc = banks[4 + G]
                for pi in range(8):
                    p.mm(pacc[:, pi * 64:(pi + 1) * 64], curM[pi], Tt[:, pi, :])
                Tn = Tts[G].next()
                p.tt(Tn[:, 0:8, :], Tt[:, 0:8, :], pacc[0:64, :].re("p (a t) -> p a t", t=64), ALU.add)
                Tt = Tn
            Tfin.append(Tt)
        if stage < 5:
            continue
        for c in range(NCH):
            G = c // 4
            Tt = Tfin[G]
            pw = banks[0]; pu = banks[1]; py = banks[2 + c % 2]; ph = banks[6]
            p.mm(pw[:, 0:128], KR[:, c, :, :].re("p a t -> p (a t)"), Hb, start=True, stop=False)
            for h in range(2):
                p.mm(pw[:, 64 * h:64 * h + 64], A2[:, c, h, 0:128], Vb[:, c, 64 * h:64 * h + 64],
                     start=False, stop=(h == 1))
            wn = W0n.next()
            p.act(wn, pw[0:64, 0:128], AF.Copy, scale=-1.0)
            for h in range(2):
                pi = (c % 4) * 2 + h
                p.mm(pu[:, 64 * h:64 * h + 64], Tt[:, pi:pi + 2, :].re("p a t -> p (a t)"), wn[:, 64 * h:64 * h + 64])
            ub = Ub.next()
            p.copy(ub, pu[0:64, 0:128])
            p.mm(py[:, 0:128], KR[:, :, :, :].re("p c a t -> p (c a t)")[:, c * 128 + 64:c * 128 + 192], Hb, start=True, stop=False)
            for h in range(2):
                sl = slice(64 * h, 64 * h + 64)
                a1f = A1[:, :, :, :].re("p c h t -> p (c h t)")
                a2f = A2[:, :, :, :].re("p c h t -> p (c h t)")
                o_ = (2 * c + h) * 128 + 64
                p.mm(py[:, sl], a1f[:, o_:o_ + 128], ub[:, sl], start=False, stop=False)
                p.mm(py[:, sl], a2f[:, o_:o_ + 128], Vb[:, c, sl], start=False, stop=(h == 1))
            p.copy(Ysb[:, c, :], py[0:64, 0:128], e="act")
            p.mm(ph[:, 0:128], BhT[:, c, :], ub, start=True, stop=False)
            p.mm(ph[:, 0:128], KhT[:, c, :], Vb[:, c, :], start=False, stop=True)
            p.tt(tH, ph[:, 0:128], S["bdmask"], ALU.mult)
            p.tt(tH, tH, H, ALU.add)
            pc = e1[:, c * CH + CH - 1:c * CH + CH]
            p.ts(H, tH, pc, ALU.mult)
            p.act(Hb, tH, AF.Copy, scale=pc)
        if stage < 6:
            continue
        y4 = Ysb[:, :, :].re("p c (h e) -> p (c h) e", e=64)
        p.reduce(s16, y4, ALU.add)
        p.ts(s16, s16, 1.0 / 64, ALU.mult)
        x4 = xc[:, :, :].re("p c (h e) -> p (c h) e", e=64)
        q4 = sq_[:, :, :].re("p c (h e) -> p (c h) e", e=64)
        p.tt(x4, y4, s16[:, :].re("p (a o) -> p a o", o=1).bc([64, NCH * 2, 64]), ALU.subtract)
        p.tt(q4, x4, x4, ALU.mult, e="pool")
        p.reduce(v16, q4, ALU.add)
        p.ts(v16, v16, 1.0 / 64, ALU.mult, 64e-5, ALU.add)
        p.act(v16, v16, AF.Sqrt)
        p.recip(v16, v16)
        p.tt(x4, x4, v16[:, :].re("p (a o) -> p a o", o=1).bc([64, NCH * 2, 64]), ALU.mult)
        ngb = S["NG"][:, :].re("p (o e) -> p o e", o=1).bc([64, NCH, 128])
        nbb = S["NB"][:, :].re("p (o e) -> p o e", o=1).bc([64, NCH, 128])
        p.tt(xc, xc, ngb, ALU.mult)
        p.tt(xc, xc, nbb, ALU.add)
        vf4 = Vf[:, :, :].re("p c (h e) -> p (c h) e", e=64)
        p.tt(q4, vf4, BS[:, :].re("p (a o) -> p a o", o=1).bc([64, NCH * 2, 64]), ALU.mult)
        p.tt(xc, xc, sq_, ALU.add)
        ob = osb.next()
        p.tt(ob, xc, Gtm, ALU.mult)
        p.dma(V(o_out, o_out.ap[b * TB:(b + 1) * TB, :].rearrange("(c t) e -> t c e", t=CH)), ob)
    return p


def host_B3_inputs(inp, l, c, zT_full, TB):
    f = np.float32
    Tt = zT_full.shape[1]
    ch = slice(128 * c, 128 * c + 128)

    def pad(a):
        return np.ascontiguousarray(np.concatenate([np.zeros((a.shape[0], 1), f), a], axis=1))

    mu = inp["rwkv_mu"][l]
    m = {"zr": pad(zT_full[0:1024][ch]), "zk": pad(zT_full[1024:2048][ch]), "zv": pad(zT_full[2048:3072][ch]),
         "zl1": pad(zT_full[3072:3200]), "zl2": pad(zT_full[3200:3328])}
    m["MU"] = np.ascontiguousarray(np.stack([mu[0:1024][ch], mu[1024:2048][ch], mu[2048:3072][ch],
                                             mu[3072:3200], mu[3200:3328]], axis=1)).astype(f)
    m["PV"] = np.ascontiguousarray(np.stack([inp["rwkv_w0"][l][ch], inp["rwkv_a0"][l][ch], inp["rwkv_k_k"][l][ch],
                                             inp["rwkv_k_a"][l][ch], inp["rwkv_r_k"][l][ch]], axis=1)).astype(f)
    m["w2c"] = np.ascontiguousarray(np.concatenate([inp["rwkv_w2"][l][:, ch], inp["rwkv_a2"][l][:, ch]], 0)).astype(f)
    m["g2c"] = np.ascontiguousarray(inp["rwkv_g2"][l][:, ch]).astype(f)
    rep = lambda v_: np.ascontiguousarray(np.broadcast_to(v_[None, :], (64, v_.shape[0]))).astype(f)
    m["NG"] = rep(inp["rwkv_norm_g"][l][ch]); m["NB"] = rep(inp["rwkv_norm_b"][l][ch])
    bo = np.zeros((128, 128), f); bo[0:64, 0:64] = 1.0; bo[64:, 64:] = 1.0
    m["bones"] = bo
    m["bdmask"] = bo.copy()
    hs = np.zeros((128, 2), f); hs[0:64, 0] = 1.0; hs[64:, 1] = 1.0
    m["hsel"] = hs
    cm = np.ones((128, TB), f); cm[:, ::64] = 0.0
    m["cmask"] = cm
    s = np.arange(64)[:, None]; t = np.arange(64)[None, :]
    strict_st = (t > s).astype(f); incl_st = (t >= s).astype(f)
    strict_ts = (t < s).astype(f)
    m["MK1"] = np.ascontiguousarray(np.tile(np.concatenate([-strict_st, incl_st], 1), (1, 4)))
    m["MK2"] = np.ascontiguousarray(np.tile(np.concatenate([strict_st, incl_st], 1), (1, 4)))
    m["MK3"] = np.ascontiguousarray(np.tile(-strict_ts, (1, 4)))
    m["identf"] = np.eye(128, dtype=f)
    return m


_PROGS = {}


def _prog(name, builder, *a):
    key = (name,) + a
    if key not in _PROGS:
        _PROGS[key] = builder(*a)
    return _PROGS[key]


def _run(p, maps):
    return p.run(maps).results


def kernel(**inp):
    inp = {k: np.asarray(v) for k, v in inp.items()}
    x = np.ascontiguousarray(inp["x"][0]).astype(np.float32)
    Tt = x.shape[0]
    Tc = Tt // NCORE
    TB = min(512, Tt)
    f = np.float32
    posT = np.ascontiguousarray(np.broadcast_to(inp["positions"][0][None, :], (128, Tt))).astype(np.int32)
    rep = lambda v_: np.ascontiguousarray(np.broadcast_to(v_[None, :], (128, v_.shape[0]))).astype(f)
    sl = lambda c: slice(c * Tc, (c + 1) * Tc)
    hs = lambda c: slice(128 * c, 128 * c + 128)
    eye = np.eye(128, dtype=f)
    for l in range(DEPTH):
        pa = build_A(Tc)
        ra = _run(pa, [{"xT": np.ascontiguousarray(x[sl(c)].T), "w": inp["w_in"][l]} for c in range(NCORE)])
        cat_fm = lambda nm: np.concatenate([ra[c][nm] for c in range(NCORE)], axis=1)
        cat_tm = lambda nm: np.concatenate([ra[c][nm] for c in range(NCORE)], axis=0)
        uT, qT, kT, zT = cat_fm("uT"), cat_fm("qT"), cat_fm("kT"), cat_fm("zT")
        v, sg = cat_tm("v"), cat_tm("sg")
        gates = [ra[c]["gates"] for c in range(NCORE)]
        r1 = _run(build_B1(Tt), [host_B1_inputs(inp, l, c, np.ascontiguousarray(uT[hs(c)]), TB) for c in range(NCORE)])
        zs = np.concatenate([r1[c]["zT"] for c in range(NCORE)], axis=0)
        r2 = _run(build_B2(Tt), [host_B2_inputs(inp, l, c, np.ascontiguousarray(qT[hs(c)]), np.ascontiguousarray(kT[hs(c)]),
                                                np.ascontiguousarray(v[:, hs(c)]), np.ascontiguousarray(sg[:, hs(c)]), posT)
                                 for c in range(NCORE)])
        orT = np.ascontiguousarray(np.concatenate([r2[c]["o"] for c in range(NCORE)], axis=1).T)
        r3 = _run(build_B3(Tt), [host_B3_inputs(inp, l, c, zT, TB) for c in range(NCORE)])
        owT = np.ascontiguousarray(np.concatenate([r3[c]["o"] for c in range(NCORE)], axis=1).T)
        rc1 = _run(build_C1(Tc), [{
            "zT": np.ascontiguousarray(zs[:, sl(c)]), "orT": np.ascontiguousarray(orT[:, sl(c)]),
            "owT": np.ascontiguousarray(owT[:, sl(c)]), "gates": gates[c], "x": np.ascontiguousarray(x[sl(c)]),
            "wglu": inp["ssm_glu"][l], "wret": inp["ret_out"][l], "wrw": inp["rwkv_out"][l], "wo": inp["w_o"][l],
            "lng": rep(inp["ln1_g"][l]), "lnb": rep(inp["ln1_b"][l]), "identf": eye} for c in range(NCORE)])
        x1 = np.concatenate([rc1[c]["x1"] for c in range(NCORE)], axis=0)
        x1p = np.concatenate([np.zeros((2, D), f), x1], axis=0)
        wcv = np.ascontiguousarray(inp["ffn_conv"][l].T.reshape(88, 128, 3).transpose(1, 0, 2)).astype(f)
        rc2 = _run(build_C2(Tc), [{
            "x1": np.ascontiguousarray(x1[sl(c)]), "x1T": np.ascontiguousarray(x1p[c * Tc:(c + 1) * Tc + 2].T),
            "wup": inp["ffn_up"][l], "wcv": wcv, "wdn": inp["ffn_down"][l],
            "lng": rep(inp["ln2_g"][l]), "lnb": rep(inp["ln2_b"][l])} for c in range(NCORE)])
        x = np.concatenate([rc2[c]["x2"] for c in range(NCORE)], axis=0)
    return x[None].astype(np.float32)
```

```python
import math
import numpy as np
import ml_dtypes
import concourse.bass as bass
import concourse.mybir as mybir
from concourse.bass_utils import run_bass_kernel_spmd

F32 = mybir.dt.float32
BF16 = mybir.dt.bfloat16
I32 = mybir.dt.int32
AF = mybir.ActivationFunctionType
ALU = mybir.AluOpType
AX = mybir.AxisListType
NPBF = ml_dtypes.bfloat16

D = 2048
NCORE = 8
DEPTH = 4
N_IN = 14592
D_FF = 5632
ALPHA = (2.0 * DEPTH) ** 0.25
EXPM05 = math.exp(-0.5)
TWO_PI = 2.0 * math.pi
DEBUG = False
SUB = 9
KNOB = 0


class T:
    __slots__ = ("ap", "name", "w", "r", "psum")

    def __init__(self, ap, name, psum=False):
        self.ap = ap
        self.name = name
        self.w = None
        self.r = []
        self.psum = psum

    def __getitem__(self, idx):
        return V(self, self.ap[idx])


class V:
    __slots__ = ("t", "ap")

    def __init__(self, t, ap):
        self.t = t
        self.ap = ap

    def __getitem__(self, idx):
        return V(self.t, self.ap[idx])

    def re(self, pat, **kw):
        return V(self.t, self.ap.rearrange(pat, **kw))

    def bc(self, shape):
        return V(self.t, self.ap.to_broadcast(list(shape)))


def _tv(x):
    if isinstance(x, T):
        return x, x.ap
    if isinstance(x, V):
        return x.t, x.ap
    return None, x


class P:
    NRING = 8

    def __init__(self):
        self.nc = bass.Bass("TRN2", target_bir_lowering=False)
        nc = self.nc
        self.eng = {"pe": nc.tensor, "dve": nc.vector, "act": nc.scalar,
                    "pool": nc.gpsimd, "sp": nc.sync}
        self.sems = {}
        self.cnt = {}
        self.seen = {e: {} for e in self.eng}
        self._ctx = []
        for e in self.eng:
            self.sems[e] = self._sem("s_" + e)
            self.cnt[e] = 0
        self.ring = {}
        self.ringn = {}
        for q in ("sp", "pool", "act"):
            self.ring[q] = [self._sem(f"d_{q}{i}") for i in range(self.NRING)]
            self.ringn[q] = 0
        self.n_inst = 0
        self._rr = 0

    def _sem(self, name):
        cm = self.nc.semaphore(name)
        s = cm.__enter__()
        self._ctx.append(cm)
        return s

    def sb(self, name, shape, dt=F32):
        cm = self.nc.sbuf_tensor(name, list(shape), dt)
        t = cm.__enter__()
        self._ctx.append(cm)
        return T(t[:], name)

    def ps(self, name, shape, dt=F32):
        cm = self.nc.psum_tensor(name, list(shape), dt)
        t = cm.__enter__()
        self._ctx.append(cm)
        return T(t[:], name, psum=True)

    def dram(self, name, shape, dt=F32, kind="ExternalInput"):
        t = self.nc.dram_tensor(name, list(shape), dt, kind=kind)
        return T(t.ap(), name)

    def _semobj(self, key):
        if isinstance(key, tuple):
            return self.ring[key[0]][key[1]]
        return self.sems[key]

    def _deps(self, e, reads, writes):
        need = {}

        def add(tok):
            k, v = tok[0], tok[1]
            if need.get(k, 0) < v:
                need[k] = v

        for t in reads:
            if t is None or t.w is None:
                continue
            if t.w[0] == e and e == "pe":
                continue
            add(t.w)
        for t in reads:
            if t is not None and t.psum:
                for rd in t.r:
                    if rd[0] != e:
                        add(rd)
        for t in writes:
            if t is None:
                continue
            for rd in t.r:
                if rd[0] == e:
                    continue
                add(rd)
            if t.w is not None and t.w[0] != e:
                add(t.w)
        eng = self.eng[e]
        seen = self.seen[e]
        for k, v in need.items():
            if seen.get(k, 0) < v:
                eng.wait_ge(self._semobj(k), v)
                seen[k] = v

    def _done(self, tok, reads, writes):
        for t in reads:
            if t is not None:
                t.r.append(tok)
                if len(t.r) > 24:
                    best = {}
                    for rd in t.r:
                        if best.get(rd[0], 0) < rd[1]:
                            best[rd[0]] = rd[1]
                    t.r = [(k, v) for k, v in best.items()]
        for t in writes:
            if t is not None:
                t.w = tok
                t.r = []

    def op(self, e, fn, reads, writes):
        self._deps(e, reads, writes)
        ins = fn()
        self.cnt[e] += 1
        ins.then_inc(self.sems[e], 1)
        self._done((e, self.cnt[e]), reads, writes)
        self.n_inst += 1
        return ins

    def dma(self, out, in_, q="sp", **kw):
        to, apo = _tv(out)
        ti, api = _tv(in_)
        n = self.ringn[q]
        slot = n % self.NRING
        rnd = n // self.NRING
        key = (q, slot)
        eng = self.eng[q]
        if rnd > 0 and self.seen[q].get(key, 0) < 16 * rnd:
            eng.wait_ge(self.ring[q][slot], 16 * rnd)
            self.seen[q][key] = 16 * rnd
        self._deps(q, [ti], [to])
        ins = eng.dma_start(out=apo, in_=api, **kw)
        ins.then_inc(self.ring[q][slot], 16)
        self.ringn[q] = n + 1
        self._done((key, 16 * (rnd + 1)), [ti], [to])
        self.n_inst += 1
        return ins

    def mm(self, out, lhsT, rhs, start=True, stop=True, **kw):
        to, apo = _tv(out)
        tl, apl = _tv(lhsT)
        tr, apr = _tv(rhs)
        return self.op("pe", lambda: self.nc.tensor.matmul(apo, apl, apr, start=start, stop=stop, **kw),
                       [tl, tr], [to])

    def tr(self, out, in_, ident):
        to, apo = _tv(out)
        ti, api = _tv(in_)
        td, apd = _tv(ident)
        return self.op("pe", lambda: self.nc.tensor.transpose(apo, api, apd), [ti, td], [to])

    def act(self, out, in_, func, bias=None, scale=None):
        to, apo = _tv(out)
        ti, api = _tv(in_)
        rd = [ti]
        k = {}
        if bias is not None:
            tb, apb = _tv(bias)
            rd.append(tb)
            k["bias"] = apb
        if scale is not None:
            ts_, aps = _tv(scale)
            rd.append(ts_)
            k["scale"] = aps
        return self.op("act", lambda: self.nc.scalar.activation(apo, api, func, **k), rd, [to])

    def tt(self, out, a, b, op, e="dve"):
        to, apo = _tv(out)
        ta, apa = _tv(a)
        tb, apb = _tv(b)
        eng = self.eng[e]
        return self.op(e, lambda: eng.tensor_tensor(apo, apa, apb, op), [ta, tb], [to])

    def ts(self, out, a, s1, op0, s2=None, op1=None, e="dve"):
        to, apo = _tv(out)
        ta, apa = _tv(a)
        t1, ap1 = _tv(s1)
        t2, ap2 = _tv(s2)
        eng = self.eng[e]
        if op1 is None:
            return self.op(e, lambda: eng.tensor_scalar(apo, apa, ap1, None, op0), [ta, t1], [to])
        return self.op(e, lambda: eng.tensor_scalar(apo, apa, ap1, ap2, op0, op1), [ta, t1, t2], [to])

    def stt(self, out, a, s, b, op0, op1):
        to, apo = _tv(out)
        ta, apa = _tv(a)
        ts_, aps = _tv(s)
        tb, apb = _tv(b)
        return self.op("dve", lambda: self.nc.vector.scalar_tensor_tensor(apo, apa, aps, apb, op0, op1),
                       [ta, ts_, tb], [to])

    def copy(self, out, in_, e="dve"):
        to, apo = _tv(out)
        ti, api = _tv(in_)
        if e == "act":
            return self.op("act", lambda: self.nc.scalar.copy(apo, api), [ti], [to])
        eng = self.eng[e]
        return self.op(e, lambda: eng.tensor_copy(apo, api), [ti], [to])

    def memset(self, out, val, e="dve"):
        to, apo = _tv(out)
        eng = self.eng[e]
        return self.op(e, lambda: eng.memset(apo, val), [], [to])

    def scan(self, out, d0, d1, init, op0, op1):
        to, apo = _tv(out)
        t0, ap0 = _tv(d0)
        t1, ap1 = _tv(d1)
        ti, api = _tv(init)
        return self.op("dve", lambda: self.nc.vector.tensor_tensor_scan(apo, ap0, ap1, api, op0, op1),
                       [t0, t1, ti], [to])

    def recip(self, out, in_):
        to, apo = _tv(out)
        ti, api = _tv(in_)
        return self.op("dve", lambda: self.nc.vector.reciprocal(apo, api), [ti], [to])

    def reduce(self, out, in_, op, axis=AX.X):
        to, apo = _tv(out)
        ti, api = _tv(in_)
        return self.op("dve", lambda: self.nc.vector.tensor_reduce(apo, api, axis, op), [ti], [to])

    def bn_stats(self, out, in_):
        to, apo = _tv(out)
        ti, api = _tv(in_)
        return self.op("dve", lambda: self.nc.vector.bn_stats(apo, api), [ti], [to])

    def bn_aggr(self, out, in_):
        to, apo = _tv(out)
        ti, api = _tv(in_)
        return self.op("dve", lambda: self.nc.vector.bn_aggr(apo, api), [ti], [to])

    def finish(self):
        sp = self.nc.sync
        for e in self.eng:
            if e != "sp" and self.cnt[e] > 0 and self.seen["sp"].get(e, 0) < self.cnt[e]:
                sp.wait_ge(self.sems[e], self.cnt[e])
                self.seen["sp"][e] = self.cnt[e]
        for q in self.ring:
            n = self.ringn[q]
            for slot in range(min(n, self.NRING)):
                last_rnd = (n - 1 - slot) // self.NRING
                v = 16 * (last_rnd + 1)
                key = (q, slot)
                if self.seen["sp"].get(key, 0) < v:
                    sp.wait_ge(self.ring[q][slot], v)
                    self.seen["sp"][key] = v

    def run(self, in_maps):
        self.finish()
        return run_bass_kernel_spmd(self.nc, in_maps, core_ids=list(range(NCORE)))


class Rot:
    def __init__(self, items):
        self.items = items
        self.i = 0

    def next(self):
        x = self.items[self.i % len(self.items)]
        self.i += 1
        return x


def sincos(p, ang, sin_out, cos_out, tmp_f, tmp_i, tmp_k):
    C1 = 6.28125
    C2 = TWO_PI - 6.28125
    p.ts(tmp_f, ang, 1.0 / TWO_PI, ALU.mult)
    p.copy(tmp_i, tmp_f)
    p.copy(tmp_k, tmp_i)
    p.stt(tmp_f, tmp_k, -C1, ang, ALU.mult, ALU.add)
    p.stt(tmp_f, tmp_k, -C2, tmp_f, ALU.mult, ALU.add)
    p.ts(tmp_k, tmp_f, math.pi, ALU.is_gt, -TWO_PI, ALU.mult)
    p.tt(tmp_k, tmp_k, tmp_f, ALU.add)
    p.act(sin_out, tmp_k, AF.Sin)
    p.ts(tmp_f, tmp_f, math.pi / 2, ALU.add)
    p.ts(tmp_k, tmp_f, math.pi, ALU.is_gt, -TWO_PI, ALU.mult)
    p.tt(tmp_k, tmp_k, tmp_f, ALU.add)
    p.act(cos_out, tmp_k, AF.Sin)


A_FAMS = [
    ("uT", 0, 1024, "FM", None),
    ("qT", 1024, 1024, "FM", None),
    ("kT", 2048, 1024, "FM", None),
    ("v", 3072, 1024, "TM", None),
    ("sg", 4096, 1024, "TM", AF.Silu),
    ("zT", 5120, 3328, "FM", None),
    ("gates", 8448, 6144, "TM", AF.Sigmoid),
]


def load_cast_T(p, src_ap_fn, dst, nkc, Tc, stages):
    for kc in range(nkc):
        st = stages.next()
        p.dma(st, src_ap_fn(kc))
        p.copy(dst[:, kc, :], st, e=("dve" if kc % 2 == 0 else "pool"))


def build_A(Tc):
    p = P()
    KC = 16
    xT = p.dram("xT", [D, Tc])
    w = p.dram("w", [D, N_IN])
    outs = {}
    for name, c0, wd, mode, fn in A_FAMS:
        shp = [wd, Tc] if mode == "FM" else [Tc, wd]
        outs[name] = p.dram(name, shp, kind="ExternalOutput")
    xb = p.sb("xb", [128, KC, Tc], BF16)
    xst = Rot([p.sb(f"xst{i}", [128, Tc]) for i in range(2)])
    wst = [[p.sb(f"wst{b}{h}", [128, 8, 512]) for h in range(2)] for b in range(2)]
    wb = [[p.sb(f"wb{b}{h}", [128, 8, 512], BF16) for h in range(2)] for b in range(2)]
    pss = Rot([p.ps(f"ps{i}", [128, 512]) for i in range(4)])
    ost = Rot([p.sb(f"ost{i}", [128, 512]) for i in range(4)])
    xTv = xT.ap.rearrange("(kc p) t -> p kc t", p=128)
    load_cast_T(p, lambda kc: V(xT, xTv[:, kc, :]), xb, KC, Tc, xst)
    wv = w.ap.rearrange("(kc p) n -> p kc n", p=128)
    chunks = []
    for name, c0, wd, mode, fn in A_FAMS:
        o = 0
        while o < wd:
            cw = min(512, wd - o)
            chunks.append((name, c0 + o, o, cw, mode, fn))
            o += cw

    def load_w(j):
        name, c, o, cw, mode, fn = chunks[j]
        b = j % 2
        for h in range(2):
            p.dma(wst[b][h][:, :, 0:cw], V(w, wv[:, 8 * h:8 * h + 8, c:c + cw]))

    def cast_w(j):
        name, c, o, cw, mode, fn = chunks[j]
        b = j % 2
        p.copy(wb[b][0][:, :, 0:cw], wst[b][0][:, :, 0:cw], e="dve")
        p.copy(wb[b][1][:, :, 0:cw], wst[b][1][:, :, 0:cw], e="pool")

    TH = min(512, Tc)
    load_w(0)
    ev = 0
    for j in range(len(chunks)):
        name, c, o, cw, mode, fn = chunks[j]
        b = j % 2
        if j + 1 < len(chunks):
            load_w(j + 1)
        cast_w(j)
        od = outs[name]
        if mode == "FM":
            for sub in range(cw // 128):
                for th in range(Tc // TH):
                    ps = pss.next()
                    for kc in range(KC):
                        p.mm(ps[:, 0:TH], wb[b][kc // 8][:, kc % 8, sub * 128:(sub + 1) * 128],
                             xb[:, kc, th * TH:(th + 1) * TH], start=(kc == 0), stop=(kc == KC - 1))
                    os_ = ost.next()
                    if ev % 2 == 0:
                        p.copy(os_[:, 0:TH], ps[:, 0:TH], e="dve")
                    else:
                        p.copy(os_[:, 0:TH], ps[:, 0:TH], e="act")
                    ev += 1
                    p.dma(od[o + sub * 128:o + (sub + 1) * 128, th * TH:(th + 1) * TH], os_[:, 0:TH])
        else:
            for tt_ in range(Tc // 128):
                ps = pss.next()
                for kc in range(KC):
                    p.mm(ps[:, 0:cw], xb[:, kc, tt_ * 128:(tt_ + 1) * 128],
                         wb[b][kc // 8][:, kc % 8, 0:cw], start=(kc == 0), stop=(kc == KC - 1))
                os_ = ost.next()
                if fn is not None:
                    p.act(os_[:, 0:cw], ps[:, 0:cw], fn)
                else:
                    p.copy(os_[:, 0:cw], ps[:, 0:cw], e=("dve" if ev % 2 == 0 else "act"))
                    ev += 1
                p.dma(od[tt_ * 128:(tt_ + 1) * 128, o:o + cw], os_[:, 0:cw])
    return p


def layer_norm_tile(p, pre, out_sb, lng, lnb, scr, eps=1e-5):
    st, mv, rs = scr["st"], scr["mv"], scr["rs"]
    for c in range(4):
        p.bn_stats(st[:, c, :], pre[:, c * 512:(c + 1) * 512])
    p.bn_aggr(mv, V(st, st.ap.rearrange("p a b -> p (a b)")))
    p.ts(rs, mv[:, 1:2], eps, ALU.add)
    p.act(rs, rs, AF.Sqrt)
    p.recip(rs, rs)
    p.ts(out_sb, pre, mv[:, 0:1], ALU.subtract, rs[:, 0:1], ALU.mult)
    p.tt(out_sb, out_sb, lng, ALU.mult, e="pool")
    p.tt(out_sb, out_sb, lnb, ALU.add, e="pool")


def ln_scratch(p, tag):
    return {"st": p.sb("ln_st" + tag, [128, 4, 6]), "mv": p.sb("ln_mv" + tag, [128, 2]),
            "rs": p.sb("ln_rs" + tag, [128, 1])}


def build_C1(Tc):
    p = P()
    NT = Tc // 128
    zT = p.dram("zT", [1024, Tc], BF16)
    orT = p.dram("orT", [1024, Tc], BF16)
    owT = p.dram("owT", [1024, Tc], BF16)
    gates = p.dram("gates", [Tc, 6144])
    x = p.dram("x", [Tc, D])
    wglu = p.dram("wglu", [1024, 4096])
    wret = p.dram("wret", [1024, D])
    wrw = p.dram("wrw", [1024, D])
    wo = p.dram("wo", [D, D])
    lng_d = p.dram("lng", [128, D])
    lnb_d = p.dram("lnb", [128, D])
    x1 = p.dram("x1", [Tc, D], kind="ExternalOutput")

    acts = {}
    actbuf = p.sb("actbuf", [128, 24, Tc], BF16)
    for i_, (nm, src) in enumerate((("z", zT), ("or", orT), ("ow", owT))):
        t = actbuf[:, 8 * i_:8 * i_ + 8, :]
        p.dma(t, V(src, src.ap.rearrange("(kc p) t -> p kc t", p=128)))
        acts[nm] = t
    lng = p.sb("lng_s", [128, D])
    lnb = p.sb("lnb_s", [128, D])
    p.dma(lng, lng_d)
    p.dma(lnb, lnb_d)
    merged = p.sb("merged", [128, NT, D])
    mTbuf = p.sb("mTbuf", [128, 16, Tc], BF16) if Tc < 512 else None
    mT = mTbuf if mTbuf is not None else actbuf
    wst = Rot([p.sb(f"wst{i}", [128, 8, 512]) for i in range(2)])
    wbB = Rot([p.sb(f"wbB{i}", [128, 8, 512], BF16) for i in range(3)])
    gst = Rot([p.sb(f"gst{i}", [128, 512]) for i in range(2)])
    tmp = Rot([p.sb(f"tmp{i}", [128, 512]) for i in range(2)])
    sig = Rot([p.sb(f"sig{i}", [128, 512]) for i in range(2)])
    pss = Rot([p.ps(f"ps{i}", [128, 512]) for i in range(6)])
    ident = p.sb("ident", [128, 128], BF16)
    idf = p.dram("identf", [128, 128])
    idst = p.sb("idst", [128, 128])
    p.dma(idst, idf)
    p.copy(ident, idst)
    ce = [0]

    def wload(dst, src_t, r0, nkc, c0, cw):
        sv = src_t.ap.rearrange("(kc p) n -> p kc n", p=128)
        for h in range(0, nkc, 8):
            st = wst.next()
            p.dma(st[:, :, 0:cw], V(src_t, sv[:, r0 + h:r0 + h + 8, c0:c0 + cw]))
            e = "dve" if ce[0] % 2 == 0 else "pool"
            ce[0] += 1
            p.copy(dst[:, h:h + 8, 0:cw], st[:, :, 0:cw], e=e)

    for cc in range(4):
        wa = wbB.next()
        wload(wa, wglu, 0, 8, cc * 512, 512)
        wb_ = wbB.next()
        wload(wb_, wglu, 0, 8, 2048 + cc * 512, 512)
        for t in range(NT):
            psa = pss.next()
            psb = pss.next()
            for kc in range(8):
                p.mm(psa, acts["z"][:, kc, t * 128:(t + 1) * 128], wa[:, kc, :], start=(kc == 0), stop=(kc == 7))
            for kc in range(8):
                p.mm(psb, acts["z"][:, kc, t * 128:(t + 1) * 128], wb_[:, kc, :], start=(kc == 0), stop=(kc == 7))
            g = gst.next()
            p.dma(g, gates[t * 128:(t + 1) * 128, cc * 512:(cc + 1) * 512])
            s = sig.next()
            p.act(s, psb, AF.Sigmoid)
            y = tmp.next()
            p.tt(y, psa, s, ALU.mult)
            p.tt(merged[:, t, cc * 512:(cc + 1) * 512], y, g, ALU.mult, e="pool")
    for bi, (nm, wsrc) in enumerate((("or", wret), ("ow", wrw))):
        for cc in range(4):
            wb_ = wbB.next()
            wload(wb_, wsrc, 0, 8, cc * 512, 512)
            for t in range(NT):
                ps = pss.next()
                for kc in range(8):
                    p.mm(ps, acts[nm][:, kc, t * 128:(t + 1) * 128], wb_[:, kc, :], start=(kc == 0), stop=(kc == 7))
                g = gst.next()
                p.dma(g, gates[t * 128:(t + 1) * 128, (bi + 1) * 2048 + cc * 512:(bi + 1) * 2048 + (cc + 1) * 512])
                y = tmp.next()
                p.tt(y, ps, g, ALU.mult)
                mv_ = merged[:, t, cc * 512:(cc + 1) * 512]
                p.tt(mv_, mv_, y, ALU.add, e="pool")
    mb = Rot([p.sb(f"mb{i}", [128, D], BF16) for i in range(1)])
    pst = Rot([p.ps(f"pst{i}", [128, 4, 128], BF16) for i in range(2)])
    for t in range(NT):
        m = mb.next()
        p.copy(m, merged[:, t, :], e="act")
        for q4 in range(4):
            pt = pst.next()
            for i in range(4):
                kc = q4 * 4 + i
                p.tr(pt[:, i, :], m[:, kc * 128:(kc + 1) * 128], ident)
            p.copy(mT[:, q4 * 4:(q4 + 1) * 4, t * 128:(t + 1) * 128], pt, e=("dve" if q4 % 2 == 0 else "act"))
    for cc in range(4):
        wa0 = wbB.next()
        wload(wa0, wo, 0, 8, cc * 512, 512)
        wa1 = wbB.next()
        wload(wa1, wo, 8, 8, cc * 512, 512)
        for t in range(NT):
            ps = pss.next()
            for kc in range(16):
                p.mm(ps, mT[:, kc, t * 128:(t + 1) * 128], (wa0 if kc < 8 else wa1)[:, kc % 8, :],
                     start=(kc == 0), stop=(kc == 15))
            g = gst.next()
            p.dma(g, x[t * 128:(t + 1) * 128, cc * 512:(cc + 1) * 512])
            p.stt(merged[:, t, cc * 512:(cc + 1) * 512], g, ALPHA, ps, ALU.mult, ALU.add)
    if DEBUG:
        dbg = p.dram("dbg", [Tc, D], kind="ExternalOutput")
        for t in range(NT):
            p.dma(dbg[t * 128:(t + 1) * 128, :], merged[:, t, :])
    scr = ln_scratch(p, "1")
    for t in range(NT):
        o = merged[:, t, :]
        layer_norm_tile(p, merged[:, t, :], o, lng, lnb, scr)
        p.dma(x1[t * 128:(t + 1) * 128, :], o)
    return p


def build_PREP(L):
    p = P()
    CW = 2048
    w = p.dram("w", [128, L])
    wb = p.dram("wb", [128, L], BF16, kind="ExternalOutput")
    st = Rot([p.sb(f"st{i}", [128, CW]) for i in range(3)])
    ob = Rot([p.sb(f"ob{i}", [128, CW], BF16) for i in range(3)])
    engs = ["dve", "pool", "act"]
    for i, c0 in enumerate(range(0, L, CW)):
        cw = min(CW, L - c0)
        s_ = st.next(); o_ = ob.next()
        p.dma(s_[:, 0:cw], w[:, c0:c0 + cw])
        p.copy(o_[:, 0:cw], s_[:, 0:cw], e=engs[i % 3])
        p.dma(wb[:, c0:c0 + cw], o_[:, 0:cw])
    return p


def build_C2(Tc):
    p = P()
    NT = Tc // 128
    BL = min(256, Tc)
    NB = Tc // BL
    NJ = D_FF // 128
    x1 = p.dram("x1", [Tc, D])
    x1T = p.dram("x1T", [D, Tc + 2])
    wup = p.dram("wupT", [D_FF // 256, 128, 16 * 2 * 2 * 128], BF16)
    wcv = p.dram("wcv", [128, 2 * NJ, 3])
    wdn = p.dram("wdnT", [D_FF // 256, 128, 2 * D], BF16)
    lng_d = p.dram("lng", [128, D])
    lnb_d = p.dram("lnb", [128, D])
    x2 = p.dram("x2", [Tc, D], kind="ExternalOutput")

    lng = p.sb("lng_s", [128, D])
    lnb = p.sb("lnb_s", [128, D])
    p.dma(lng, lng_d)
    p.dma(lnb, lnb_d)
    wc = p.sb("wc", [128, 2 * NJ, 3])
    p.dma(wc, wcv)
    acc = p.sb("acc", [128, NT, D])
    for t in range(NT):
        p.dma(acc[:, t, :], x1[t * 128:(t + 1) * 128, :])
    for t in range(NT):
        p.ts(acc[:, t, :], acc[:, t, :], ALPHA, ALU.mult, e=("dve" if t % 2 == 0 else "pool"))
    xb = p.sb("xb", [128, 16, Tc + 2], BF16)
    xst = Rot([p.sb(f"xst{i}", [128, Tc + 2]) for i in range(1)])
    xv = x1T.ap.rearrange("(kc p) t -> p kc t", p=128)
    load_cast_T(p, lambda kc: V(x1T, xv[:, kc, :]), xb, 16, Tc + 2, xst)

    JG = 2
    ub = Rot([p.sb(f"ub{i}", [128, 16, JG, 2, 128], BF16) for i in range(2)])
    db = Rot([p.sb(f"db{i}", [128, JG, D], BF16) for i in range(2)])
    gT = Rot([p.sb(f"gT{i}", [128, JG, Tc], BF16) for i in range(2)])
    ha = Rot([p.sb(f"ha{i}", [128, BL]) for i in range(2)])
    hb = Rot([p.sb(f"hb{i}", [128, BL]) for i in range(2)])
    sa = Rot([p.sb(f"sa{i}", [128, BL]) for i in range(2)])
    psu = Rot([p.ps(f"psu{i}", [128, 512]) for i in range(4)])
    psd = Rot([p.ps(f"psd{i}", [128, 512]) for i in range(4)])

    def load_group(jg):
        u = ub.next()
        dd = db.next()
        uv = u[:, :, :, :, :].re("p k j h c -> p (k j h c)")
        p.dma(uv[:, 0:4096], wup[jg, :, 0:4096])
        p.dma(uv[:, 4096:8192], wup[jg, :, 4096:8192])
        p.dma(dd[:, :, :].re("p j n -> p (j n)"), wdn[jg, :, :])
        return u, dd

    nxt = load_group(0)
    for jg in range(NJ // JG):
        u, dd = nxt
        if jg + 1 < NJ // JG:
            nxt = load_group(jg + 1)
        g = gT.next()
        for ji in range(JG):
            j = jg * JG + ji
            for blk in range(NB):
                hs = []
                for half in range(2):
                    ps = psu.next()
                    for kc in range(16):
                        p.mm(ps[:, 0:BL + 2], u[:, kc, ji, half, :], xb[:, kc, blk * BL:blk * BL + BL + 2],
                             start=(kc == 0), stop=(kc == 15))
                    h_ = (ha if half == 0 else hb).next()
                    cj = j + half * NJ
                    p.act(h_, ps[:, 2:BL + 2], AF.Copy, scale=wc[:, cj, 2:3])
                    p.stt(h_, ps[:, 1:BL + 1], wc[:, cj, 1:2], h_, ALU.mult, ALU.add)
                    p.stt(h_, ps[:, 0:BL], wc[:, cj, 0:1], h_, ALU.mult, ALU.add)
                    hs.append(h_)
                s = sa.next()
                p.act(s, hs[0], AF.Silu)
                p.tt(g[:, ji, blk * BL:(blk + 1) * BL], s, hs[1], ALU.mult, e="pool")
        for t in range(NT):
            for cc in range(4):
                ps = psd.next()
                for ji in range(JG):
                    p.mm(ps, g[:, ji, t * 128:(t + 1) * 128], dd[:, ji, cc * 512:(cc + 1) * 512],
                         start=(ji == 0), stop=(ji == JG - 1))
                a = acc[:, t, cc * 512:(cc + 1) * 512]
                p.tt(a, a, ps, ALU.add)
    scr = ln_scratch(p, "2")
    for t in range(NT):
        o = acc[:, t, :]
        layer_norm_tile(p, acc[:, t, :], o, lng, lnb, scr)
        p.dma(x2[t * 128:(t + 1) * 128, :], o)
    return p


def build_B1(Tt):
    p = P()
    TB = min(512, Tt)
    NBK = Tt // TB
    uT = p.dram("uT", [128, Tt])
    zT = p.dram("zT", [128, Tt], BF16, kind="ExternalOutput")
    small = {}
    for nm, shp in (("LRr", [128, 64]), ("LIr", [128, 64]), ("LSr", [128, 64]), ("BreT", [128, 64]),
                    ("BimT", [128, 64]), ("LRc", [128, 8]), ("LIc", [128, 8]), ("LSc", [128, 8]),
                    ("CC", [128, 128]), ("CCs", [128, 128]), ("dsk", [128, 1]), ("identf", [128, 128]),
                    ("S12", [128, 128]), ("sgnA", [128, 1]), ("maskg", [128, 8]), ("iota", [128, TB])):
        dt_ = p.dram(nm, shp)
        st = p.sb("s_" + nm, shp)
        p.dma(st, dt_)
        small[nm] = st
    S = small
    n = [0]

    def tmp(shape, dt=F32):
        n[0] += 1
        return p.sb(f"t{n[0]}", shape, dt)

    sh = [128, 64]
    stp = tmp(sh); ang = tmp(sh); sn = tmp(sh); cs = tmp(sh); tf = tmp(sh); ti = tmp(sh, I32); tk = tmp(sh)
    p.act(stp, S["LSr"], AF.Exp)
    p.tt(ang, S["LIr"], stp, ALU.mult)
    sincos(p, ang, sn, cs, tf, ti, tk)
    mag = tmp(sh)
    p.tt(mag, S["LRr"], stp, ALU.mult)
    p.act(mag, mag, AF.Exp)
    abr1 = tmp(sh); abi = tmp(sh)
    p.tt(abr1, mag, cs, ALU.mult)
    p.ts(abr1, abr1, -1.0, ALU.add)
    p.tt(abi, mag, sn, ALU.mult)
    den = tmp(sh); t2 = tmp(sh)
    p.tt(den, S["LRr"], S["LRr"], ALU.mult)
    p.tt(t2, S["LIr"], S["LIr"], ALU.mult)
    p.tt(den, den, t2, ALU.add)
    p.recip(den, den)
    fre = tmp(sh); fim = tmp(sh)
    p.tt(fre, abr1, S["LRr"], ALU.mult)
    p.tt(t2, abi, S["LIr"], ALU.mult)
    p.tt(fre, fre, t2, ALU.add)
    p.tt(fre, fre, den, ALU.mult)
    p.tt(fim, abi, S["LRr"], ALU.mult)
    p.tt(t2, abr1, S["LIr"], ALU.mult)
    p.tt(fim, fim, t2, ALU.subtract)
    p.tt(fim, fim, den, ALU.mult)
    bbr = tmp(sh); bbi = tmp(sh); nbbr = tmp(sh)
    p.tt(bbr, fre, S["BreT"], ALU.mult)
    p.tt(t2, fim, S["BimT"], ALU.mult)
    p.tt(bbr, bbr, t2, ALU.subtract)
    p.tt(bbi, fre, S["BimT"], ALU.mult)
    p.tt(t2, fim, S["BreT"], ALU.mult)
    p.tt(bbi, bbi, t2, ALU.add)
    p.ts(nbbr, bbr, -1.0, ALU.mult)
    BT = p.sb("BT", [128, 8, 128], BF16)
    BTs = p.sb("BTs", [128, 8, 128], BF16)
    for g in range(8):
        mg = S["maskg"][:, g:g + 1]
        p.ts(BT[:, g, 0:64], bbr, mg, ALU.mult)
        p.ts(BT[:, g, 64:128], bbi, mg, ALU.mult)
        p.ts(BTs[:, g, 0:64], bbi, mg, ALU.mult)
        p.ts(BTs[:, g, 64:128], nbbr, mg, ALU.mult)
    C1a = p.sb("C1a", [128, 8, 128], BF16)
    C2a = p.sb("C2a", [128, 8, 128], BF16)
    p.memset(C1a, 0.0)
    p.memset(C2a, 0.0)
    for g in range(8):
        p.ts(C1a[:, g, 16 * g:16 * g + 16], S["CC"][:, 16 * g:16 * g + 16], S["sgnA"][:, 0:1], ALU.mult)
        p.ts(C2a[:, g, 16 * g:16 * g + 16], S["CCs"][:, 16 * g:16 * g + 16], -1.0, ALU.mult)
    sh8 = [128, 8]
    stc = tmp(sh8); rho = tmp(sh8); theta = tmp(sh8)
    p.act(stc, S["LSc"], AF.Exp)
    p.tt(rho, S["LRc"], stc, ALU.mult)
    p.act(rho, rho, AF.Exp)
    p.tt(theta, S["LIc"], stc, ALU.mult)
    phi = tmp(sh8); sph = tmp(sh8); cph = tmp(sh8); f8 = tmp(sh8); i8 = tmp(sh8, I32); k8 = tmp(sh8)
    p.ts(phi, theta, float(TB), ALU.mult)
    sincos(p, phi, sph, cph, f8, i8, k8)
    p.ts(sph, sph, S["sgnA"][:, 0:1], ALU.mult)
    ROT = p.sb("ROT", [128, 8, 128])
    for g in range(8):
        p.ts(ROT[:, g, :], S["identf"], cph[:, g:g + 1], ALU.mult)
        p.stt(ROT[:, g, :], S["S12"], sph[:, g:g + 1], ROT[:, g, :], ALU.mult, ALU.add)
    SIN = p.sb("SIN", [128, 8, TB])
    COS = p.sb("COS", [128, 8, TB])
    RHO = p.sb("RHO", [128, 8, TB])
    shb = [128, TB]
    angb = tmp(shb); fb = tmp(shb); ib = tmp(shb, I32); kb = tmp(shb)
    for g in range(8):
        p.ts(angb, S["iota"], theta[:, g:g + 1], ALU.mult)
        sincos(p, angb, SIN[:, g, :], COS[:, g, :], fb, ib, kb)
        p.ts(RHO[:, g, :], S["iota"], 0.0, ALU.mult, rho[:, g:g + 1], ALU.add)

    ust = Rot([p.sb(f"ust{i}", [128, TB]) for i in range(2)])
    ubf = Rot([p.sb(f"ubf{i}", [128, TB], BF16) for i in range(2)])
    psa = Rot([p.ps(f"psa{i}", [128, TB]) for i in range(2)])
    psb = Rot([p.ps(f"psb{i}", [128, TB]) for i in range(2)])
    psy = Rot([p.ps(f"psy{i}", [128, TB]) for i in range(2)])
    psr = p.ps("psr", [128, 8])
    v1 = Rot([p.sb(f"v1{i}", [128, TB]) for i in range(2)])
    v2 = Rot([p.sb(f"v2{i}", [128, TB]) for i in range(2)])
    vv = Rot([p.sb(f"vv{i}", [128, TB]) for i in range(2)])
    shat = Rot([p.sb(f"shat{i}", [128, TB]) for i in range(3)])
    w1 = Rot([p.sb(f"w1{i}", [128, TB], BF16) for i in range(2)])
    w2 = Rot([p.sb(f"w2{i}", [128, TB], BF16) for i in range(2)])
    last = p.sb("last", [128, 8])
    init = p.sb("init", [128, 8])
    yb = Rot([p.sb(f"yb{i}", [128, TB]) for i in range(2)])
    gt = Rot([p.sb(f"gt{i}", [128, TB]) for i in range(2)])
    zo = Rot([p.sb(f"zo{i}", [128, TB], BF16) for i in range(2)])
    vv3 = Rot([p.sb(f"vv3{i}", [128, TB]) for i in range(3)])
    w13 = Rot([p.sb(f"w13{i}", [128, TB], BF16) for i in range(3)])
    w23 = Rot([p.sb(f"w23{i}", [128, TB], BF16) for i in range(3)])
    items = [(b, g) for b in range(NBK) for g in range(8)]
    blk = {}
    stA = {}
    stB = {}

    def block_begin(b):
        uf = ust.next()
        p.dma(uf, uT[:, b * TB:(b + 1) * TB])
        ub = ubf.next()
        p.copy(ub, uf, e="act")
        blk[b] = {"uf": uf, "ub": ub, "py": psy.next()}

    def stage_A(b, g):
        if g == 0:
            block_begin(b)
        ub = blk[b]["ub"]
        pa = psa.next()
        pb = psb.next()
        p.mm(pa, BT[:, g, :], ub)
        p.mm(pb, BTs[:, g, :], ub)
        a1 = v1.next(); a2 = v2.next(); av = vv3.next()
        p.tt(a1, pa, COS[:, g, :], ALU.mult)
        p.tt(a2, pb, SIN[:, g, :], ALU.mult)
        p.tt(av, a1, a2, ALU.add, e="pool")
        stA[(b, g)] = av

    def stage_B(b, g):
        av = stA.pop((b, g))
        if b > 0 and g == 0:
            for g_ in range(8):
                p.mm(psr[:, g_:g_ + 1], ROT[:, g_, :], last[:, g_:g_ + 1])
            p.copy(init, psr)
        sh_ = shat.next()
        p.scan(sh_, RHO[:, g, :], av, (0.0 if b == 0 else init[:, g:g + 1]), ALU.mult, ALU.add)
        p.copy(last[:, g:g + 1], sh_[:, TB - 1:TB], e="act")
        b1 = w13.next(); b2 = w23.next()
        p.tt(b1, sh_, COS[:, g, :], ALU.mult, e="pool")
        p.tt(b2, sh_, SIN[:, g, :], ALU.mult)
        stB[(b, g)] = (b1, b2)

    def stage_C(b, g):
        b1, b2 = stB.pop((b, g))
        py = blk[b]["py"]
        p.mm(py, C1a[:, g, :], b1, start=(g == 0), stop=False)
        p.mm(py, C2a[:, g, :], b2, start=False, stop=(g == 7))
        if g == 7:
            uf = blk[b]["uf"]
            y = yb.next()
            p.stt(y, uf, S["dsk"][:, 0:1], py, ALU.mult, ALU.add)
            t_ = gt.next()
            p.tt(t_, y, y, ALU.mult, e="pool")
            p.ts(t_, t_, 0.044715, ALU.mult, 1.0, ALU.add, e="pool")
            p.tt(t_, t_, y, ALU.mult, e="pool")
            p.act(t_, t_, AF.Sigmoid, scale=1.5957691216057308)
            z = zo.next()
            p.tt(z, y, t_, ALU.mult)
            p.dma(zT[:, b * TB:(b + 1) * TB], z)
            del blk[b]

    n_it = len(items)
    for i in range(-1, n_it + 1):
        if 0 <= i + 1 < n_it:
            stage_A(*items[i + 1])
        if 0 <= i < n_it:
            stage_B(*items[i])
        if 0 <= i - 1 < n_it:
            stage_C(*items[i - 1])
    return p


def host_B1_inputs(inp, l, c, uT_c, TB):
    gs = slice(8 * c, 8 * c + 8)
    lr = inp["ssm_lambda_re"][l][gs]; li = inp["ssm_lambda_im"][l][gs]; ls = inp["ssm_log_step"][l][gs]
    bre = inp["ssm_b_re"][l][gs]; bim = inp["ssm_b_im"][l][gs]
    cre = inp["ssm_c_re"][l][gs]; cim = inp["ssm_c_im"][l][gs]
    f = np.float32
    rep16 = lambda a: np.ascontiguousarray(np.repeat(a, 16, axis=0)).astype(f)
    m = {"uT": uT_c}
    m["LRr"] = rep16(lr); m["LIr"] = rep16(li)
    m["LSr"] = rep16(np.broadcast_to(ls[:, None], (8, 64)))
    m["BreT"] = np.ascontiguousarray(bre.transpose(0, 2, 1).reshape(128, 64)).astype(f)
    m["BimT"] = np.ascontiguousarray(bim.transpose(0, 2, 1).reshape(128, 64)).astype(f)
    m["LRc"] = np.ascontiguousarray(np.concatenate([lr.T, lr.T], 0)).astype(f)
    m["LIc"] = np.ascontiguousarray(np.concatenate([li.T, li.T], 0)).astype(f)
    m["LSc"] = np.ascontiguousarray(np.broadcast_to(ls[None, :], (128, 8))).astype(f)
    creT = cre.transpose(2, 0, 1).reshape(64, 128); cimT = cim.transpose(2, 0, 1).reshape(64, 128)
    m["CC"] = np.ascontiguousarray(np.concatenate([creT, cimT], 0)).astype(f)
    m["CCs"] = np.ascontiguousarray(np.concatenate([cimT, creT], 0)).astype(f)
    m["dsk"] = np.ascontiguousarray(inp["ssm_d"][l][128 * c:128 * c + 128].reshape(128, 1)).astype(f)
    m["identf"] = np.eye(128, dtype=f)
    s12 = np.zeros((128, 128), f)
    for k in range(64):
        s12[k, k + 64] = 1.0
        s12[k + 64, k] = 1.0
    m["S12"] = s12
    sg = np.ones((128, 1), f); sg[64:] = -1.0
    m["sgnA"] = sg
    mg = np.zeros((128, 8), f)
    for g in range(8):
        mg[16 * g:16 * g + 16, g] = 1.0
    m["maskg"] = mg
    m["iota"] = np.ascontiguousarray(np.broadcast_to(np.arange(TB, dtype=f)[None, :], (128, TB)))
    return m


def build_B2(Tt):
    p = P()
    TB = min(512, Tt)
    NBK = Tt // TB
    NCH = TB // 128
    qT = p.dram("qT", [128, Tt])
    kT = p.dram("kT", [128, Tt])
    vv = p.dram("v", [Tt, 128])
    sg = p.dram("sg", [Tt, 128])
    pos = p.dram("pos", [128, Tt], I32)
    o_out = p.dram("o", [Tt, 128], BF16, kind="ExternalOutput")
    S = {}
    for nm, shp in (("invf", [128, 1]), ("PERM", [128, 128]), ("DmT", [128, 128]), ("qd", [128, 1]),
                    ("kd", [128, 1]), ("cdec", [128, 1]), ("NG", [128, 128]), ("NB", [128, 128]),
                    ("identf", [128, 128])):
        dt_ = p.dram(nm, shp)
        st = p.sb("s_" + nm, shp)
        p.dma(st, dt_)
        S[nm] = st
    identb = p.sb("identb", [128, 128], BF16)
    p.copy(identb, S["identf"])
    state = p.sb("state", [128, 128])
    state_bf = p.sb("state_bf", [128, 128], BF16)
    p.memset(state, 0.0)
    p.memset(state_bf, 0.0)
    shb = [128, TB]
    qf = Rot([p.sb(f"qf{i}", shb) for i in range(2)])
    kf = Rot([p.sb(f"kf{i}", shb) for i in range(2)])
    posi = p.sb("posi", shb, I32)
    ang = p.sb("ang", shb); fb = p.sb("fb", shb); ib = p.sb("ib", shb, I32); kb = p.sb("kb", shb)
    SINt = p.sb("SINt", shb); COSt = p.sb("COSt", shb)
    r1 = p.sb("r1", shb); r2 = p.sb("r2", shb)
    rqT = Rot([p.sb(f"rqT{i}", shb, BF16) for i in range(2)])
    rkT = Rot([p.sb(f"rkT{i}", shb, BF16) for i in range(2)])
    pP = Rot([p.ps(f"pP{i}", shb) for i in range(2)])
    psc = p.ps("psc", [128, 128]); po1 = p.ps("po1", [128, 128]); po2 = p.ps("po2", [128, 128])
    ptr = p.ps("ptr", [128, 128], BF16); pkv = p.ps("pkv", [128, 128])
    vf = Rot([p.sb(f"vf{i}", [128, 128]) for i in range(2)])
    vb = Rot([p.sb(f"vb{i}", [128, 128], BF16) for i in range(2)])
    sgf = Rot([p.sb(f"sgf{i}", [128, 128]) for i in range(2)])
    scm = Rot([p.sb(f"scm{i}", [128, 128], BF16) for i in range(2)])
    insb = Rot([p.sb(f"insb{i}", [128, 128]) for i in range(2)])
    osb = Rot([p.sb(f"osb{i}", [128, 128]) for i in range(2)])
    kdb = Rot([p.sb(f"kdb{i}", [128, 128], BF16) for i in range(2)])
    st6 = p.sb("st6", [128, 6]); mv = p.sb("mv", [128, 2]); rs = p.sb("rs", [128, 1])
    onb = Rot([p.sb(f"onb{i}", [128, 128]) for i in range(2)])
    oo = Rot([p.sb(f"oo{i}", [128, 128], BF16) for i in range(2)])
    for b in range(NBK):
        cs_ = slice(b * TB, (b + 1) * TB)
        q_ = qf.next(); k_ = kf.next()
        p.dma(q_, qT[:, cs_])
        p.dma(k_, kT[:, cs_])
        p.dma(posi, pos[:, cs_])
        p.copy(ang, posi)
        p.ts(ang, ang, S["invf"][:, 0:1], ALU.mult)
        sincos(p, ang, SINt, COSt, fb, ib, kb)
        rots = []
        for src, dstrot in ((q_, rqT), (k_, rkT)):
            pp = pP.next()
            p.mm(pp, S["PERM"], src)
            p.tt(r1, src, COSt, ALU.mult, e="pool")
            p.tt(r2, pp, SINt, ALU.mult)
            dd = dstrot.next()
            p.tt(dd, r1, r2, ALU.add)
            rots.append(dd)
        rq, rk = rots
        for n_ in range(NCH):
            c0 = n_ * 128
            rows = slice(b * TB + c0, b * TB + c0 + 128)
            v_ = vf.next(); s_ = sgf.next()
            p.dma(v_, vv[rows, :])
            p.dma(s_, sg[rows, :])
            vb_ = vb.next()
            p.copy(vb_, v_, e="act")
            p.mm(psc, rk[:, c0:c0 + 128], rq[:, c0:c0 + 128])
            sc = scm.next()
            p.tt(sc, psc, S["DmT"], ALU.mult)
            p.mm(po1, sc, vb_)
            p.mm(po2, rq[:, c0:c0 + 128], state_bf)
            i_ = insb.next()
            p.copy(i_, po1, e="act")
            o_ = osb.next()
            p.stt(o_, po2, S["qd"][:, 0:1], i_, ALU.mult, ALU.add)
            p.tr(ptr, rk[:, c0:c0 + 128], identb)
            kd_ = kdb.next()
            p.ts(kd_, ptr, S["kd"][:, 0:1], ALU.mult)
            p.mm(pkv, kd_, vb_)
            p.stt(state, state, S["cdec"][:, 0:1], pkv, ALU.mult, ALU.add)
            p.copy(state_bf, state, e="act")
            p.bn_stats(st6, o_)
            p.bn_aggr(mv, st6)
            p.ts(rs, mv[:, 1:2], 1e-5, ALU.add)
            p.act(rs, rs, AF.Sqrt)
            p.recip(rs, rs)
            on = onb.next()
            p.ts(on, o_, mv[:, 0:1], ALU.subtract, rs[:, 0:1], ALU.mult)
            p.tt(on, on, S["NG"], ALU.mult, e="pool")
            p.tt(on, on, S["NB"], ALU.add, e="pool")
            ob = oo.next()
            p.tt(ob, on, s_, ALU.mult, e="pool")
            p.dma(o_out[rows, :], ob)
    return p


def host_B2_inputs(inp, l, c, qT_c, kT_c, v_c, sg_c, pos_T):
    f = np.float32
    h = c
    lg = np.log1p(-np.exp2(-5.0 - h))
    idx = np.arange(128, dtype=np.float64)
    rel = idx[None, :] - idx[:, None]
    sc = 128 ** -0.5
    DmT = np.where(rel >= 0, np.exp(lg * np.maximum(rel, 0.0)), 0.0) * sc
    perm = np.zeros((128, 128), f)
    for m_ in range(64):
        perm[m_ + 64, m_] = -1.0
        perm[m_, m_ + 64] = 1.0
    invf = (10000.0 ** (-np.arange(64, dtype=f) / 64)).astype(f)
    rep = lambda v_: np.ascontiguousarray(np.broadcast_to(v_[None, :], (128, v_.shape[0]))).astype(f)
    return {
        "qT": qT_c, "kT": kT_c, "v": v_c, "sg": sg_c, "pos": pos_T,
        "invf": np.concatenate([invf, invf]).reshape(128, 1).astype(f),
        "PERM": perm, "DmT": DmT.astype(f),
        "qd": np.exp(lg * (idx + 1.0)).reshape(128, 1).astype(f),
        "kd": (np.exp(lg * (127.0 - idx)) * sc).reshape(128, 1).astype(f),
        "cdec": np.full((128, 1), np.exp(lg * 128.0), f),
        "NG": rep(inp["ret_norm_g"][l][128 * c:128 * c + 128]),
        "NB": rep(inp["ret_norm_b"][l][128 * c:128 * c + 128]),
        "identf": np.eye(128, dtype=f),
    }


def build_B3(Tt, stage=9):
    p = P()
    TB = min(512, Tt)
    NBK = Tt // TB
    CH = 64
    NCH = TB // CH
    zin = {nm: p.dram(nm, [128, Tt + 1]) for nm in ("zr", "zk", "zv", "zl1", "zl2")}
    o_out = p.dram("o", [Tt, 128], BF16, kind="ExternalOutput")
    S = {}
    for nm, shp in (("MU", [128, 5]), ("PV", [128, 5]), ("w2c", [128, 128]), ("g2c", [128, 128]),
                    ("NG", [64, 128]), ("NB", [64, 128]), ("bones", [128, 128]), ("hsel", [128, 2]),
                    ("cmask", [128, TB]), ("MK1", [64, 512]), ("MK2", [64, 512]), ("MK3", [64, 256]),
                    ("identf", [128, 128]), ("bdmask", [128, 128])):
        dt_ = p.dram(nm, shp)
        st = p.sb("s_" + nm, shp)
        p.dma(st, dt_)
        S[nm] = st
    w2b = p.sb("w2b", [128, 128], BF16); p.copy(w2b, S["w2c"])
    g2b = p.sb("g2b", [128, 128], BF16); p.copy(g2b, S["g2c"])
    hselb = p.sb("hselb", [128, 2], BF16); p.copy(hselb, S["hsel"])
    identb = p.sb("identb", [128, 128], BF16); p.copy(identb, S["identf"])
    banks = [p.ps(f"bank{i}", [128, 512]) for i in range(7)]
    bank7 = p.ps("bank7", [128, 1024], BF16)
    H = p.sb("H", [128, 128]); Hb = p.sb("Hb", [128, 128], BF16)
    p.memset(H, 0.0); p.memset(Hb, 0.0)
    shb = [128, TB]
    n = [0]

    def tmp(shape=shb, dt=F32):
        n[0] += 1
        return p.sb(f"t{n[0]}", shape, dt)

    zst = {nm: Rot([p.sb(f"zst_{nm}{i}", [128, TB + 1]) for i in range(2)]) for nm in zin}
    dsh = tmp(); r_ = tmp(); k_ = tmp(); v_ = tmp(); l1 = tmp(); l2 = tmp()
    l1b = tmp(dt=BF16); sgb = tmp([128, TB + 64], BF16)
    p.memset(sgb, 0.0)
    ld = tmp(); a_ = tmp(); cs = tmp(); e0 = tmp(); e1 = tmp(); e2 = tmp(); tq = tmp()
    kk = tmp(); rn = tmp(); kmod = tmp(); bv = tmp()
    KR = p.sb("KR", [128, NCH + 1, 2, CH], BF16)
    p.memset(KR, 0.0)
    Bh = tmp([128, TB + 64], BF16); Kh = tmp([128, TB + 64], BF16); rkr = tmp([128, TB + 64], BF16); vbf = tmp([128, TB + 64], BF16)
    for t_ in (Bh, Kh, rkr, vbf):
        p.memset(t_, 0.0)
    Vb = p.sb("Vb", [64, NCH, 128], BF16); Vf = p.sb("Vf", [64, NCH, 128])
    BhT = p.sb("BhT", [64, NCH, 128], BF16); KhT = p.sb("KhT", [64, NCH, 128], BF16)
    Gtm = p.sb("Gtm", [64, NCH, 128]); BS = p.sb("BS", [64, NCH * 2])
    A1 = p.sb("A1", [64, NCH + 1, 2, 128], BF16); A2 = p.sb("A2", [64, NCH + 1, 2, 128], BF16)
    p.memset(A1, 0.0); p.memset(A2, 0.0)
    Mm = p.sb("Mm", [64, NCH + 1, 2, 64], BF16)
    p.memset(Mm, 0.0)
    MMs = [Rot([p.sb(f"MM{g}{i}", [64, 9, 128], BF16) for i in range(2)]) for g in range(2)]
    Tts = [Rot([p.sb(f"Tt{g}{i}", [64, 10, 64], BF16) for i in range(2)]) for g in range(2)]
    W0n = Rot([p.sb(f"W0n{i}", [64, 128], BF16) for i in range(2)])
    Ub = Rot([p.sb(f"Ub{i}", [64, 128], BF16) for i in range(2)])
    Ysb = p.sb("Ysb", [64, NCH, 128])
    tH = p.sb("tH", [128, 128])
    xc = p.sb("xc", [64, NCH, 128]); sq_ = p.sb("sq_", [64, NCH, 128])
    s16 = p.sb("s16", [64, NCH * 2]); v16 = p.sb("v16", [64, NCH * 2])
    osb = Rot([p.sb(f"osb{i}", [64, NCH, 128], BF16) for i in range(2)])

    def c3(vw):
        return vw.re("p (c t) -> p c t", t=CH)

    for b in range(NBK):
        zs = {}
        for nm in zin:
            st = zst[nm].next()
            p.dma(st, zin[nm][:, b * TB:b * TB + TB + 1])
            zs[nm] = st
        for i, (nm, dst) in enumerate((("zr", r_), ("zk", k_), ("zv", v_), ("zl1", l1), ("zl2", l2))):
            st = zs[nm]
            p.tt(dsh, st[:, 0:TB], st[:, 1:TB + 1], ALU.subtract, e="pool")
            p.stt(dst, dsh, S["MU"][:, i:i + 1], st[:, 1:TB + 1], ALU.mult, ALU.add)
        p.act(l1b[0:64, :], l1[0:64, :], AF.Tanh)
        p.copy(l1b[64:128, :], l1[64:128, :], e="act")
        p.act(sgb[:, 0:TB], l2, AF.Sigmoid)
        p.mm(banks[0], w2b[0:64, :], l1b[0:64, :])
        p.mm(banks[1], w2b[64:128, :], l1b[64:128, :])
        p.act(ld, banks[0], AF.Sigmoid, bias=S["PV"][:, 0:1])
        p.ts(ld, ld, -EXPM05, ALU.mult)
        p.act(a_, banks[1], AF.Sigmoid, bias=S["PV"][:, 1:2])
        p.scan(cs, S["cmask"], ld, 0.0, ALU.mult, ALU.add)
        p.act(e1, cs, AF.Exp)
        p.act(e2, cs, AF.Exp, scale=-1.0)
        p.tt(tq, cs, ld, ALU.subtract, e="pool")
        p.act(e0, tq, AF.Exp)
        p.ts(kk, k_, S["PV"][:, 2:3], ALU.mult)
        p.tt(tq, kk, kk, ALU.mult, e="pool")
        p.mm(banks[2], S["bones"], tq)
        p.ts(rn, banks[2], 1e-24, ALU.max)
        p.act(rn, rn, AF.Sqrt)
        p.recip(rn, rn)
        p.tt(kk, kk, rn, ALU.mult)
        p.ts(tq, a_, -1.0, ALU.add)
        p.ts(tq, tq, S["PV"][:, 3:4], ALU.mult)
        p.stt(kmod, tq, 1.0, k_, ALU.add, ALU.mult)
        p.tt(bv, a_, kk, ALU.mult, e="pool")
        p.tt(KR[:, 0:NCH, 0, :], c3(kk[:, :]), c3(e0[:, :]), ALU.mult)
        p.tt(KR[:, 0:NCH, 1, :], c3(r_[:, :]), c3(e1[:, :]), ALU.mult)
        p.tt(Bh[:, 0:TB], bv, e2, ALU.mult, e="pool")
        p.tt(Kh[:, 0:TB], kmod, e2, ALU.mult, e="pool")
        p.ts(tq, r_, S["PV"][:, 4:5], ALU.mult)
        p.tt(rkr[:, 0:TB], tq, kmod, ALU.mult)
        if stage < 2:
            continue
        for g4 in range(NCH // 4):
            if g4 == 0:
                p.copy(vbf[:, 0:TB], v_, e="act")
            if KNOB == 1:
                continue
            for i in range(4):
                c = g4 * 4 + i
                p.mm(banks[3][:, i * 128:(i + 1) * 128], vbf[:, c * CH:c * CH + 128], identb)
            if KNOB == 2:
                continue
            p.copy(Vf[:, g4 * 4:(g4 + 1) * 4, :], banks[3][0:64, 0:512].re("p (c e) -> p c e", e=128), e="act")
            if KNOB == 3:
                continue
            p.copy(Vb[:, g4 * 4:(g4 + 1) * 4, :], Vf[:, g4 * 4:(g4 + 1) * 4, :])
            if stage == 2 and SUB < 2:
                continue
            pg = banks[4]
            for i in range(4):
                c = g4 * 4 + i
                p.mm(pg[:, i * 128:(i + 1) * 128], sgb[:, c * CH:c * CH + 128], g2b)
            p.copy(Gtm[:, g4 * 4:(g4 + 1) * 4, :], pg[0:64, :].re("p (c e) -> p c e", e=128), e="act")
            for bi_, (src, dstT) in enumerate(((Bh, BhT), (Kh, KhT))):
                bk_ = banks[5 + bi_]
                for i in range(4):
                    c = g4 * 4 + i
                    p.mm(bk_[:, i * 128:(i + 1) * 128], src[:, c * CH:c * CH + 128], identb)
                p.copy(dstT[:, g4 * 4:(g4 + 1) * 4, :], bk_[0:64, 0:512].re("p (c e) -> p c e", e=128))
        if stage == 2 and SUB < 3:
            continue
        for c in range(NCH):
            p.mm(banks[2][:, 2 * c:2 * c + 2], rkr[:, c * CH:c * CH + 128], hselb)
        p.copy(BS, banks[2][0:64, 0:2 * NCH], e="act")
        if stage < 3:
            continue
        for g4 in range(NCH // 4):
            for ci in range(4):
                c = g4 * 4 + ci
                for h in range(2):
                    ho = 64 * h
                    krv = KR[ho:ho + 64, c, :, :].re("p a t -> p (a t)")
                    p.mm(banks[0 + h][:, ci * 128:(ci + 1) * 128], Bh[ho:ho + 64, c * CH:c * CH + 128], krv)
                    p.mm(banks[2 + h][:, ci * 128:(ci + 1) * 128], Kh[ho:ho + 64, c * CH:c * CH + 128], krv)
                    p.mm(banks[4 + h][:, ci * 64:(ci + 1) * 64], krv, Bh[ho:ho + 64, c * CH:(c + 1) * CH])
            for h in range(2):
                mk1 = S["MK1"][:, :].re("p (c t) -> p c t", t=128)
                mk2 = S["MK2"][:, :].re("p (c t) -> p c t", t=128)
                mk3 = S["MK3"][:, :].re("p (c t) -> p c t", t=64)
                p.tt(A1[:, 4 * g4:4 * g4 + 4, h, :], banks[0 + h][0:64, :].re("p (c t) -> p c t", t=128), mk1, ALU.mult)
                p.tt(A2[:, 4 * g4:4 * g4 + 4, h, :], banks[2 + h][0:64, :].re("p (c t) -> p c t", t=128), mk2, ALU.mult)
                p.tt(Mm[:, 4 * g4:4 * g4 + 4, h, :], banks[4 + h][0:64, 0:256].re("p (c t) -> p c t", t=64), mk3, ALU.mult)
        if stage < 4:
            continue
        NG_ = NCH // 4
        pairs_g = [[(G * 4 + ci, h) for ci in range(4) for h in range(2)] for G in range(NG_)]
        Tcur = []
        curM = []
        curMt = []
        Mflat = Mm[:, :, :, :].re("p c h t -> p (c h t)")
        for G in range(NG_):
            Tt = Tts[G].next()
            for pi, (c, h) in enumerate(pairs_g[G]):
                p.tt(Tt[:, pi, :], A1[:, c, h, 0:64], identb[0:64, 0:64], ALU.add, e="pool")
            p.memset(Tt[:, 8:10, :], 0.0, e="pool")
            Tcur.append(Tt)
            curM.append([Mflat[:, (2 * c + h) * 64:(2 * c + h) * 64 + 128] for (c, h) in pairs_g[G]])
            curMt.append([A1[:, c, h, 0:128] for (c, h) in pairs_g[G]])
        for lev in range(1, 6):
            last = (lev == 5)
            MMn_g = []
            for G in range(NG_):
                sqb = [banks[0 + 2 * G], banks[1 + 2 * G]]
                for pi in range(8):
                    bk = sqb[pi // 4]
                    col = (pi % 4) * 128
                    p.mm(bk[:, col:col + 64], curMt[G][pi], curM[G][pi][:, 0:64])
                    if not last:
                        p.mm(bk[:, col + 64:col + 128], curM[G][pi], curMt[G][pi][:, 0:64])
            for G in range(NG_):
                sqb = [banks[0 + 2 * G], banks[1 + 2 * G]]
                MMn = MMs[G].next()
                for hb in range(2):
                    p.copy(MMn[:, 4 * hb:4 * hb + 4, :], sqb[hb][0:64, :].re("p (a t) -> p a t", t=128),
                           e=("act" if hb == 0 else "dve"))
                MMf = MMn[:, :, :].re("p a t -> p (a t)")
                curM[G] = [MMf[:, pi * 128:pi * 128 + 128] for pi in range(8)]
                curMt[G] = [MMf[:, pi * 128 + 64:pi * 128 + 192] for pi in range(8)]
            for G in range(NG_):
                pacc = banks[4 + G]
                for pi in range(8):
                    p.mm(pacc[:, pi * 64:(pi + 1) * 64], curM[G][pi], Tcur[G][:, pi, :])
            for G in range(NG_):
                pacc = banks[4 + G]
                Tn = Tts[G].next()
                p.tt(Tn[:, 0:8, :], Tcur[G][:, 0:8, :], pacc[0:64, :].re("p (a t) -> p a t", t=64), ALU.add)
                Tcur[G] = Tn
        Tfin = Tcur
        if stage < 5:
            continue
        for c in range(NCH):
            G = c // 4
            Tt = Tfin[G]
            pw = banks[0]; pu = banks[1]; py = banks[2 + c % 2]; ph = banks[6]
            p.mm(pw[:, 0:128], KR[:, c, :, :].re("p a t -> p (a t)"), Hb, start=True, stop=False)
            for h in range(2):
                p.mm(pw[:, 64 * h:64 * h + 64], A2[:, c, h, 0:128], Vb[:, c, 64 * h:64 * h + 64],
                     start=False, stop=(h == 1))
            wn = W0n.next()
            p.act(wn, pw[0:64, 0:128], AF.Copy, scale=-1.0)
            for h in range(2):
                pi = (c % 4) * 2 + h
                p.mm(pu[:, 64 * h:64 * h + 64], Tt[:, pi:pi + 2, :].re("p a t -> p (a t)"), wn[:, 64 * h:64 * h + 64])
            ub = Ub.next()
            p.copy(ub, pu[0:64, 0:128])
            p.mm(py[:, 0:128], KR[:, :, :, :].re("p c a t -> p (c a t)")[:, c * 128 + 64:c * 128 + 192], Hb, start=True, stop=False)
            for h in range(2):
                sl = slice(64 * h, 64 * h + 64)
                a1f = A1[:, :, :, :].re("p c h t -> p (c h t)")
                a2f = A2[:, :, :, :].re("p c h t -> p (c h t)")
                o_ = (2 * c + h) * 128 + 64
                p.mm(py[:, sl], a1f[:, o_:o_ + 128], ub[:, sl], start=False, stop=False)
                p.mm(py[:, sl], a2f[:, o_:o_ + 128], Vb[:, c, sl], start=False, stop=(h == 1))
            p.copy(Ysb[:, c, :], py[0:64, 0:128], e="act")
            p.mm(ph[:, 0:128], BhT[:, c, :], ub, start=True, stop=False)
            p.mm(ph[:, 0:128], KhT[:, c, :], Vb[:, c, :], start=False, stop=True)
            p.tt(tH, ph[:, 0:128], S["bdmask"], ALU.mult)
            p.tt(tH, tH, H, ALU.add)
            pc = e1[:, c * CH + CH - 1:c * CH + CH]
            p.ts(H, tH, pc, ALU.mult)
            p.act(Hb, tH, AF.Copy, scale=pc)
        if stage < 6:
            continue
        y4 = Ysb[:, :, :].re("p c (h e) -> p (c h) e", e=64)
        p.reduce(s16, y4, ALU.add)
        p.ts(s16, s16, 1.0 / 64, ALU.mult)
        x4 = xc[:, :, :].re("p c (h e) -> p (c h) e", e=64)
        q4 = sq_[:, :, :].re("p c (h e) -> p (c h) e", e=64)
        p.tt(x4, y4, s16[:, :].re("p (a o) -> p a o", o=1).bc([64, NCH * 2, 64]), ALU.subtract)
        p.tt(q4, x4, x4, ALU.mult, e="pool")
        p.reduce(v16, q4, ALU.add)
        p.ts(v16, v16, 1.0 / 64, ALU.mult, 64e-5, ALU.add)
        p.act(v16, v16, AF.Sqrt)
        p.recip(v16, v16)
        p.tt(x4, x4, v16[:, :].re("p (a o) -> p a o", o=1).bc([64, NCH * 2, 64]), ALU.mult)
        ngb = S["NG"][:, :].re("p (o e) -> p o e", o=1).bc([64, NCH, 128])
        nbb = S["NB"][:, :].re("p (o e) -> p o e", o=1).bc([64, NCH, 128])
        p.tt(xc, xc, ngb, ALU.mult)
        p.tt(xc, xc, nbb, ALU.add)
        vf4 = Vf[:, :, :].re("p c (h e) -> p (c h) e", e=64)
        p.tt(q4, vf4, BS[:, :].re("p (a o) -> p a o", o=1).bc([64, NCH * 2, 64]), ALU.mult)
        p.tt(xc, xc, sq_, ALU.add)
        ob = osb.next()
        p.tt(ob, xc, Gtm, ALU.mult)
        p.dma(V(o_out, o_out.ap[b * TB:(b + 1) * TB, :].rearrange("(c t) e -> t c e", t=CH)), ob)
    return p


def host_B3_inputs(inp, l, c, zT_full, TB):
    f = np.float32
    Tt = zT_full.shape[1]
    ch = slice(128 * c, 128 * c + 128)

    def pad(a):
        return np.ascontiguousarray(np.concatenate([np.zeros((a.shape[0], 1), f), a], axis=1))

    mu = inp["rwkv_mu"][l]
    m = {"zr": pad(zT_full[0:1024][ch]), "zk": pad(zT_full[1024:2048][ch]), "zv": pad(zT_full[2048:3072][ch]),
         "zl1": pad(zT_full[3072:3200]), "zl2": pad(zT_full[3200:3328])}
    m["MU"] = np.ascontiguousarray(np.stack([mu[0:1024][ch], mu[1024:2048][ch], mu[2048:3072][ch],
                                             mu[3072:3200], mu[3200:3328]], axis=1)).astype(f)
    m["PV"] = np.ascontiguousarray(np.stack([inp["rwkv_w0"][l][ch], inp["rwkv_a0"][l][ch], inp["rwkv_k_k"][l][ch],
                                             inp["rwkv_k_a"][l][ch], inp["rwkv_r_k"][l][ch]], axis=1)).astype(f)
    m["w2c"] = np.ascontiguousarray(np.concatenate([inp["rwkv_w2"][l][:, ch], inp["rwkv_a2"][l][:, ch]], 0)).astype(f)
    m["g2c"] = np.ascontiguousarray(inp["rwkv_g2"][l][:, ch]).astype(f)
    rep = lambda v_: np.ascontiguousarray(np.broadcast_to(v_[None, :], (64, v_.shape[0]))).astype(f)
    m["NG"] = rep(inp["rwkv_norm_g"][l][ch]); m["NB"] = rep(inp["rwkv_norm_b"][l][ch])
    bo = np.zeros((128, 128), f); bo[0:64, 0:64] = 1.0; bo[64:, 64:] = 1.0
    m["bones"] = bo
    m["bdmask"] = bo.copy()
    hs = np.zeros((128, 2), f); hs[0:64, 0] = 1.0; hs[64:, 1] = 1.0
    m["hsel"] = hs
    cm = np.ones((128, TB), f); cm[:, ::64] = 0.0
    m["cmask"] = cm
    s = np.arange(64)[:, None]; t = np.arange(64)[None, :]
    strict_st = (t > s).astype(f); incl_st = (t >= s).astype(f)
    strict_ts = (t < s).astype(f)
    m["MK1"] = np.ascontiguousarray(np.tile(np.concatenate([-strict_st, incl_st], 1), (1, 4)))
    m["MK2"] = np.ascontiguousarray(np.tile(np.concatenate([strict_st, incl_st], 1), (1, 4)))
    m["MK3"] = np.ascontiguousarray(np.tile(-strict_ts, (1, 4)))
    m["identf"] = np.eye(128, dtype=f)
    return m


_PROGS = {}


def _prog(name, builder, *a):
    key = (name,) + a
    if key not in _PROGS:
        _PROGS[key] = builder(*a)
    return _PROGS[key]


def _run(p, maps):
    return p.run(maps).results


def kernel(**inp):
    inp = {k: np.asarray(v) for k, v in inp.items()}
    x = np.ascontiguousarray(inp["x"][0]).astype(np.float32)
    Tt = x.shape[0]
    Tc = Tt // NCORE
    TB = min(512, Tt)
    f = np.float32
    posT = np.ascontiguousarray(np.broadcast_to(inp["positions"][0][None, :], (128, Tt))).astype(np.int32)
    rep = lambda v_: np.ascontiguousarray(np.broadcast_to(v_[None, :], (128, v_.shape[0]))).astype(f)
    sl = lambda c: slice(c * Tc, (c + 1) * Tc)
    hs = lambda c: slice(128 * c, 128 * c + 128)
    eye = np.eye(128, dtype=f)
    flat = np.concatenate([np.concatenate([inp["ffn_up"][l].ravel(), inp["ffn_down"][l].ravel()])
                           for l in range(DEPTH)])
    Lp = flat.size // (NCORE * 128)
    rp = _run(build_PREP(Lp), [{"w": np.ascontiguousarray(flat[c * 128 * Lp:(c + 1) * 128 * Lp].reshape(128, Lp))}
                               for c in range(NCORE)])
    flat_b = np.concatenate([rp[c]["wb"].reshape(-1) for c in range(NCORE)])
    del flat
    nup, ndn = D * 2 * D_FF, D_FF * D
    wupT, wdnT = [], []
    for l in range(DEPTH):
        o_ = l * (nup + ndn)
        up = flat_b[o_:o_ + nup].reshape(16, 128, 2, D_FF // 256, 2, 128)
        wupT.append(np.ascontiguousarray(up.transpose(3, 1, 0, 4, 2, 5)).reshape(D_FF // 256, 128, 8192))
        dn = flat_b[o_ + nup:o_ + nup + ndn].reshape(D_FF // 256, 2, 128, D)
        wdnT.append(np.ascontiguousarray(dn.transpose(0, 2, 1, 3)).reshape(D_FF // 256, 128, 2 * D))
    for l in range(DEPTH):
        pa = build_A(Tc)
        ra = _run(pa, [{"xT": np.ascontiguousarray(x[sl(c)].T), "w": inp["w_in"][l]} for c in range(NCORE)])
        cat_fm = lambda nm: np.concatenate([ra[c][nm] for c in range(NCORE)], axis=1)
        cat_tm = lambda nm: np.concatenate([ra[c][nm] for c in range(NCORE)], axis=0)
        uT, qT, kT, zT = cat_fm("uT"), cat_fm("qT"), cat_fm("kT"), cat_fm("zT")
        v, sg = cat_tm("v"), cat_tm("sg")
        gates = [ra[c]["gates"] for c in range(NCORE)]
        r1 = _run(build_B1(Tt), [host_B1_inputs(inp, l, c, np.ascontiguousarray(uT[hs(c)]), TB) for c in range(NCORE)])
        zs = np.concatenate([r1[c]["zT"] for c in range(NCORE)], axis=0)
        r2 = _run(build_B2(Tt), [host_B2_inputs(inp, l, c, np.ascontiguousarray(qT[hs(c)]), np.ascontiguousarray(kT[hs(c)]),
                                                np.ascontiguousarray(v[:, hs(c)]), np.ascontiguousarray(sg[:, hs(c)]), posT)
                                 for c in range(NCORE)])
        orT = np.ascontiguousarray(np.concatenate([r2[c]["o"] for c in range(NCORE)], axis=1).T)
        r3 = _run(build_B3(Tt), [host_B3_inputs(inp, l, c, zT, TB) for c in range(NCORE)])
        owT = np.ascontiguousarray(np.concatenate([r3[c]["o"] for c in range(NCORE)], axis=1).T)
        rc1 = _run(build_C1(Tc), [{
            "zT": np.ascontiguousarray(zs[:, sl(c)]), "orT": np.ascontiguousarray(orT[:, sl(c)]),
            "owT": np.ascontiguousarray(owT[:, sl(c)]), "gates": gates[c], "x": np.ascontiguousarray(x[sl(c)]),
            "wglu": inp["ssm_glu"][l], "wret": inp["ret_out"][l], "wrw": inp["rwkv_out"][l], "wo": inp["w_o"][l],
            "lng": rep(inp["ln1_g"][l]), "lnb": rep(inp["ln1_b"][l]), "identf": eye} for c in range(NCORE)])
        x1 = np.concatenate([rc1[c]["x1"] for c in range(NCORE)], axis=0)
        x1p = np.concatenate([np.zeros((2, D), f), x1], axis=0)
        wcv = np.ascontiguousarray(inp["ffn_conv"][l].T.reshape(88, 128, 3).transpose(1, 0, 2)).astype(f)
        rc2 = _run(build_C2(Tc), [{
            "x1": np.ascontiguousarray(x1[sl(c)]), "x1T": np.ascontiguousarray(x1p[c * Tc:(c + 1) * Tc + 2].T),
            "wupT": wupT[l], "wcv": wcv, "wdnT": wdnT[l],
            "lng": rep(inp["ln2_g"][l]), "lnb": rep(inp["ln2_b"][l])} for c in range(NCORE)])
        x = np.concatenate([rc2[c]["x2"] for c in range(NCORE)], axis=0)
    return x[None].astype(np.float32)
```

```python
import math
import numpy as np
import ml_dtypes
import concourse.bass as bass
import concourse.mybir as mybir
from concourse.bass_utils import run_bass_kernel_spmd

F32 = mybir.dt.float32
BF16 = mybir.dt.bfloat16
I32 = mybir.dt.int32
AF = mybir.ActivationFunctionType
ALU = mybir.AluOpType
AX = mybir.AxisListType
NPBF = ml_dtypes.bfloat16

D = 2048
NCORE = 8
DEPTH = 4
N_IN = 14592
D_FF = 5632
ALPHA = (2.0 * DEPTH) ** 0.25
EXPM05 = math.exp(-0.5)
TWO_PI = 2.0 * math.pi
DEBUG = False
SUB = 9
KNOB = 0


class T:
    __slots__ = ("ap", "name", "w", "r", "psum")

    def __init__(self, ap, name, psum=False):
        self.ap = ap
        self.name = name
        self.w = None
        self.r = []
        self.psum = psum

    def __getitem__(self, idx):
        return V(self, self.ap[idx])


class V:
    __slots__ = ("t", "ap")

    def __init__(self, t, ap):
        self.t = t
        self.ap = ap

    def __getitem__(self, idx):
        return V(self.t, self.ap[idx])

    def re(self, pat, **kw):
        return V(self.t, self.ap.rearrange(pat, **kw))

    def bc(self, shape):
        return V(self.t, self.ap.to_broadcast(list(shape)))


def _tv(x):
    if isinstance(x, T):
        return x, x.ap
    if isinstance(x, V):
        return x.t, x.ap
    return None, x


class P:
    NRING = 8

    def __init__(self):
        self.nc = bass.Bass("TRN2", target_bir_lowering=False)
        nc = self.nc
        self.eng = {"pe": nc.tensor, "dve": nc.vector, "act": nc.scalar,
                    "pool": nc.gpsimd, "sp": nc.sync}
        self.sems = {}
        self.cnt = {}
        self.seen = {e: {} for e in self.eng}
        self._ctx = []
        for e in self.eng:
            self.sems[e] = self._sem("s_" + e)
            self.cnt[e] = 0
        self.ring = {}
        self.ringn = {}
        for q in ("sp", "pool", "act"):
            self.ring[q] = [self._sem(f"d_{q}{i}") for i in range(self.NRING)]
            self.ringn[q] = 0
        self.n_inst = 0
        self._rr = 0

    def _sem(self, name):
        cm = self.nc.semaphore(name)
        s = cm.__enter__()
        self._ctx.append(cm)
        return s

    def sb(self, name, shape, dt=F32):
        cm = self.nc.sbuf_tensor(name, list(shape), dt)
        t = cm.__enter__()
        self._ctx.append(cm)
        return T(t[:], name)

    def ps(self, name, shape, dt=F32):
        cm = self.nc.psum_tensor(name, list(shape), dt)
        t = cm.__enter__()
        self._ctx.append(cm)
        return T(t[:], name, psum=True)

    def dram(self, name, shape, dt=F32, kind="ExternalInput"):
        t = self.nc.dram_tensor(name, list(shape), dt, kind=kind)
        return T(t.ap(), name)

    def _semobj(self, key):
        if isinstance(key, tuple):
            return self.ring[key[0]][key[1]]
        return self.sems[key]

    def _deps(self, e, reads, writes):
        need = {}

        def add(tok):
            k, v = tok[0], tok[1]
            if need.get(k, 0) < v:
                need[k] = v

        for t in reads:
            if t is None or t.w is None:
                continue
            if t.w[0] == e and e == "pe":
                continue
            add(t.w)
        for t in reads:
            if t is not None and t.psum:
                for rd in t.r:
                    if rd[0] != e:
                        add(rd)
        for t in writes:
            if t is None:
                continue
            for rd in t.r:
                if rd[0] == e:
                    continue
                add(rd)
            if t.w is not None and t.w[0] != e:
                add(t.w)
        eng = self.eng[e]
        seen = self.seen[e]
        for k, v in need.items():
            if seen.get(k, 0) < v:
                eng.wait_ge(self._semobj(k), v)
                seen[k] = v

    def _done(self, tok, reads, writes):
        for t in reads:
            if t is not None:
                t.r.append(tok)
                if len(t.r) > 24:
                    best = {}
                    for rd in t.r:
                        if best.get(rd[0], 0) < rd[1]:
                            best[rd[0]] = rd[1]
                    t.r = [(k, v) for k, v in best.items()]
        for t in writes:
            if t is not None:
                t.w = tok
                t.r = []

    def op(self, e, fn, reads, writes):
        self._deps(e, reads, writes)
        ins = fn()
        self.cnt[e] += 1
        ins.then_inc(self.sems[e], 1)
        self._done((e, self.cnt[e]), reads, writes)
        self.n_inst += 1
        return ins

    def dma(self, out, in_, q="sp", **kw):
        to, apo = _tv(out)
        ti, api = _tv(in_)
        n = self.ringn[q]
        slot = n % self.NRING
        rnd = n // self.NRING
        key = (q, slot)
        eng = self.eng[q]
        if rnd > 0 and self.seen[q].get(key, 0) < 16 * rnd:
            eng.wait_ge(self.ring[q][slot], 16 * rnd)
            self.seen[q][key] = 16 * rnd
        self._deps(q, [ti], [to])
        ins = eng.dma_start(out=apo, in_=api, **kw)
        ins.then_inc(self.ring[q][slot], 16)
        self.ringn[q] = n + 1
        self._done((key, 16 * (rnd + 1)), [ti], [to])
        self.n_inst += 1
        return ins

    def mm(self, out, lhsT, rhs, start=True, stop=True, **kw):
        to, apo = _tv(out)
        tl, apl = _tv(lhsT)
        tr, apr = _tv(rhs)
        return self.op("pe", lambda: self.nc.tensor.matmul(apo, apl, apr, start=start, stop=stop, **kw),
                       [tl, tr], [to])

    def tr(self, out, in_, ident):
        to, apo = _tv(out)
        ti, api = _tv(in_)
        td, apd = _tv(ident)
        return self.op("pe", lambda: self.nc.tensor.transpose(apo, api, apd), [ti, td], [to])

    def act(self, out, in_, func, bias=None, scale=None):
        to, apo = _tv(out)
        ti, api = _tv(in_)
        rd = [ti]
        k = {}
        if bias is not None:
            tb, apb = _tv(bias)
            rd.append(tb)
            k["bias"] = apb
        if scale is not None:
            ts_, aps = _tv(scale)
            rd.append(ts_)
            k["scale"] = aps
        return self.op("act", lambda: self.nc.scalar.activation(apo, api, func, **k), rd, [to])

    def tt(self, out, a, b, op, e="dve"):
        to, apo = _tv(out)
        ta, apa = _tv(a)
        tb, apb = _tv(b)
        eng = self.eng[e]
        return self.op(e, lambda: eng.tensor_tensor(apo, apa, apb, op), [ta, tb], [to])

    def ts(self, out, a, s1, op0, s2=None, op1=None, e="dve"):
        to, apo = _tv(out)
        ta, apa = _tv(a)
        t1, ap1 = _tv(s1)
        t2, ap2 = _tv(s2)
        eng = self.eng[e]
        if op1 is None:
            return self.op(e, lambda: eng.tensor_scalar(apo, apa, ap1, None, op0), [ta, t1], [to])
        return self.op(e, lambda: eng.tensor_scalar(apo, apa, ap1, ap2, op0, op1), [ta, t1, t2], [to])

    def stt(self, out, a, s, b, op0, op1):
        to, apo = _tv(out)
        ta, apa = _tv(a)
        ts_, aps = _tv(s)
        tb, apb = _tv(b)
        return self.op("dve", lambda: self.nc.vector.scalar_tensor_tensor(apo, apa, aps, apb, op0, op1),
                       [ta, ts_, tb], [to])

    def copy(self, out, in_, e="dve"):
        to, apo = _tv(out)
        ti, api = _tv(in_)
        if e == "act":
            return self.op("act", lambda: self.nc.scalar.copy(apo, api), [ti], [to])
        eng = self.eng[e]
        return self.op(e, lambda: eng.tensor_copy(apo, api), [ti], [to])

    def memset(self, out, val, e="dve"):
        to, apo = _tv(out)
        eng = self.eng[e]
        return self.op(e, lambda: eng.memset(apo, val), [], [to])

    def scan(self, out, d0, d1, init, op0, op1):
        to, apo = _tv(out)
        t0, ap0 = _tv(d0)
        t1, ap1 = _tv(d1)
        ti, api = _tv(init)
        return self.op("dve", lambda: self.nc.vector.tensor_tensor_scan(apo, ap0, ap1, api, op0, op1),
                       [t0, t1, ti], [to])

    def recip(self, out, in_):
        to, apo = _tv(out)
        ti, api = _tv(in_)
        return self.op("dve", lambda: self.nc.vector.reciprocal(apo, api), [ti], [to])

    def reduce(self, out, in_, op, axis=AX.X):
        to, apo = _tv(out)
        ti, api = _tv(in_)
        return self.op("dve", lambda: self.nc.vector.tensor_reduce(apo, api, axis, op), [ti], [to])

    def bn_stats(self, out, in_):
        to, apo = _tv(out)
        ti, api = _tv(in_)
        return self.op("dve", lambda: self.nc.vector.bn_stats(apo, api), [ti], [to])

    def bn_aggr(self, out, in_):
        to, apo = _tv(out)
        ti, api = _tv(in_)
        return self.op("dve", lambda: self.nc.vector.bn_aggr(apo, api), [ti], [to])

    def finish(self):
        sp = self.nc.sync
        for e in self.eng:
            if e != "sp" and self.cnt[e] > 0 and self.seen["sp"].get(e, 0) < self.cnt[e]:
                sp.wait_ge(self.sems[e], self.cnt[e])
                self.seen["sp"][e] = self.cnt[e]
        for q in self.ring:
            n = self.ringn[q]
            for slot in range(min(n, self.NRING)):
                last_rnd = (n - 1 - slot) // self.NRING
                v = 16 * (last_rnd + 1)
                key = (q, slot)
                if self.seen["sp"].get(key, 0) < v:
                    sp.wait_ge(self.ring[q][slot], v)
                    self.seen["sp"][key] = v

    def run(self, in_maps):
        self.finish()
        return run_bass_kernel_spmd(self.nc, in_maps, core_ids=list(range(NCORE)))


class Rot:
    def __init__(self, items):
        self.items = items
        self.i = 0

    def next(self):
        x = self.items[self.i % len(self.items)]
        self.i += 1
        return x


def sincos(p, ang, sin_out, cos_out, tmp_f, tmp_i, tmp_k):
    C1 = 6.28125
    C2 = TWO_PI - 6.28125
    p.ts(tmp_f, ang, 1.0 / TWO_PI, ALU.mult)
    p.copy(tmp_i, tmp_f)
    p.copy(tmp_k, tmp_i)
    p.stt(tmp_f, tmp_k, -C1, ang, ALU.mult, ALU.add)
    p.stt(tmp_f, tmp_k, -C2, tmp_f, ALU.mult, ALU.add)
    p.ts(tmp_k, tmp_f, math.pi, ALU.is_gt, -TWO_PI, ALU.mult)
    p.tt(tmp_k, tmp_k, tmp_f, ALU.add)
    p.act(sin_out, tmp_k, AF.Sin)
    p.ts(tmp_f, tmp_f, math.pi / 2, ALU.add)
    p.ts(tmp_k, tmp_f, math.pi, ALU.is_gt, -TWO_PI, ALU.mult)
    p.tt(tmp_k, tmp_k, tmp_f, ALU.add)
    p.act(cos_out, tmp_k, AF.Sin)


A_FAMS = [
    ("uT", 0, 1024, "FM", None),
    ("qT", 1024, 1024, "FM", None),
    ("kT", 2048, 1024, "FM", None),
    ("v", 3072, 1024, "TM", None),
    ("sg", 4096, 1024, "TM", AF.Silu),
    ("zT", 5120, 3328, "FM", None),
    ("gates", 8448, 6144, "TM", AF.Sigmoid),
]


def load_cast_T(p, src_ap_fn, dst, nkc, Tc, stages):
    for kc in range(nkc):
        st = stages.next()
        p.dma(st, src_ap_fn(kc))
        p.copy(dst[:, kc, :], st, e=("dve" if kc % 2 == 0 else "pool"))


def build_A(Tc):
    p = P()
    KC = 16
    xT = p.dram("xT", [D, Tc])
    w = p.dram("w", [D, N_IN])
    outs = {}
    for name, c0, wd, mode, fn in A_FAMS:
        shp = [wd, Tc] if mode == "FM" else [Tc, wd]
        outs[name] = p.dram(name, shp, kind="ExternalOutput")
    xb = p.sb("xb", [128, KC, Tc], BF16)
    xst = Rot([p.sb(f"xst{i}", [128, Tc]) for i in range(2)])
    wst = [[p.sb(f"wst{b}{h}", [128, 8, 512]) for h in range(2)] for b in range(2)]
    wb = [[p.sb(f"wb{b}{h}", [128, 8, 512], BF16) for h in range(2)] for b in range(2)]
    pss = Rot([p.ps(f"ps{i}", [128, 512]) for i in range(4)])
    ost = Rot([p.sb(f"ost{i}", [128, 512]) for i in range(4)])
    xTv = xT.ap.rearrange("(kc p) t -> p kc t", p=128)
    load_cast_T(p, lambda kc: V(xT, xTv[:, kc, :]), xb, KC, Tc, xst)
    wv = w.ap.rearrange("(kc p) n -> p kc n", p=128)
    chunks = []
    for name, c0, wd, mode, fn in A_FAMS:
        o = 0
        while o < wd:
            cw = min(512, wd - o)
            chunks.append((name, c0 + o, o, cw, mode, fn))
            o += cw

    def load_w(j):
        name, c, o, cw, mode, fn = chunks[j]
        b = j % 2
        for h in range(2):
            p.dma(wst[b][h][:, :, 0:cw], V(w, wv[:, 8 * h:8 * h + 8, c:c + cw]))

    def cast_w(j):
        name, c, o, cw, mode, fn = chunks[j]
        b = j % 2
        p.copy(wb[b][0][:, :, 0:cw], wst[b][0][:, :, 0:cw], e="dve")
        p.copy(wb[b][1][:, :, 0:cw], wst[b][1][:, :, 0:cw], e="pool")

    TH = min(512, Tc)
    load_w(0)
    ev = 0
    for j in range(len(chunks)):
        name, c, o, cw, mode, fn = chunks[j]
        b = j % 2
        if j + 1 < len(chunks):
            load_w(j + 1)
        cast_w(j)
        od = outs[name]
        if mode == "FM":
            for sub in range(cw // 128):
                for th in range(Tc // TH):
                    ps = pss.next()
                    for kc in range(KC):
                        p.mm(ps[:, 0:TH], wb[b][kc // 8][:, kc % 8, sub * 128:(sub + 1) * 128],
                             xb[:, kc, th * TH:(th + 1) * TH], start=(kc == 0), stop=(kc == KC - 1))
                    os_ = ost.next()
                    if ev % 2 == 0:
                        p.copy(os_[:, 0:TH], ps[:, 0:TH], e="dve")
                    else:
                        p.copy(os_[:, 0:TH], ps[:, 0:TH], e="act")
                    ev += 1
                    p.dma(od[o + sub * 128:o + (sub + 1) * 128, th * TH:(th + 1) * TH], os_[:, 0:TH])
        else:
            for tt_ in range(Tc // 128):
                ps = pss.next()
                for kc in range(KC):
                    p.mm(ps[:, 0:cw], xb[:, kc, tt_ * 128:(tt_ + 1) * 128],
                         wb[b][kc // 8][:, kc % 8, 0:cw], start=(kc == 0), stop=(kc == KC - 1))
                os_ = ost.next()
                if fn is not None:
                    p.act(os_[:, 0:cw], ps[:, 0:cw], fn)
                else:
                    p.copy(os_[:, 0:cw], ps[:, 0:cw], e=("dve" if ev % 2 == 0 else "act"))
                    ev += 1
                p.dma(od[tt_ * 128:(tt_ + 1) * 128, o:o + cw], os_[:, 0:cw])
    return p


def layer_norm_tile(p, pre, out_sb, lng, lnb, scr, eps=1e-5):
    st, mv, rs = scr["st"], scr["mv"], scr["rs"]
    for c in range(4):
        p.bn_stats(st[:, c, :], pre[:, c * 512:(c + 1) * 512])
    p.bn_aggr(mv, V(st, st.ap.rearrange("p a b -> p (a b)")))
    p.ts(rs, mv[:, 1:2], eps, ALU.add)
    p.act(rs, rs, AF.Sqrt)
    p.recip(rs, rs)
    p.ts(out_sb, pre, mv[:, 0:1], ALU.subtract, rs[:, 0:1], ALU.mult)
    p.tt(out_sb, out_sb, lng, ALU.mult, e="pool")
    p.tt(out_sb, out_sb, lnb, ALU.add, e="pool")


def ln_scratch(p, tag):
    return {"st": p.sb("ln_st" + tag, [128, 4, 6]), "mv": p.sb("ln_mv" + tag, [128, 2]),
            "rs": p.sb("ln_rs" + tag, [128, 1])}


def build_C1(Tc):
    p = P()
    NT = Tc // 128
    zT = p.dram("zT", [1024, Tc], BF16)
    orT = p.dram("orT", [1024, Tc], BF16)
    owT = p.dram("owT", [1024, Tc], BF16)
    gates = p.dram("gates", [Tc, 6144])
    x = p.dram("x", [Tc, D])
    wglu = p.dram("wglu", [1024, 4096])
    wret = p.dram("wret", [1024, D])
    wrw = p.dram("wrw", [1024, D])
    wo = p.dram("wo", [D, D])
    lng_d = p.dram("lng", [128, D])
    lnb_d = p.dram("lnb", [128, D])
    x1 = p.dram("x1", [Tc, D], kind="ExternalOutput")

    acts = {}
    actbuf = p.sb("actbuf", [128, 24, Tc], BF16)
    for i_, (nm, src) in enumerate((("z", zT), ("or", orT), ("ow", owT))):
        t = actbuf[:, 8 * i_:8 * i_ + 8, :]
        p.dma(t, V(src, src.ap.rearrange("(kc p) t -> p kc t", p=128)))
        acts[nm] = t
    lng = p.sb("lng_s", [128, D])
    lnb = p.sb("lnb_s", [128, D])
    p.dma(lng, lng_d)
    p.dma(lnb, lnb_d)
    merged = p.sb("merged", [128, NT, D])
    mTbuf = p.sb("mTbuf", [128, 16, Tc], BF16) if Tc < 512 else None
    mT = mTbuf if mTbuf is not None else actbuf
    wst = Rot([p.sb(f"wst{i}", [128, 8, 512]) for i in range(2)])
    wbB = Rot([p.sb(f"wbB{i}", [128, 8, 512], BF16) for i in range(3)])
    gst = Rot([p.sb(f"gst{i}", [128, 512]) for i in range(2)])
    tmp = Rot([p.sb(f"tmp{i}", [128, 512]) for i in range(2)])
    sig = Rot([p.sb(f"sig{i}", [128, 512]) for i in range(2)])
    pss = Rot([p.ps(f"ps{i}", [128, 512]) for i in range(6)])
    ident = p.sb("ident", [128, 128], BF16)
    idf = p.dram("identf", [128, 128])
    idst = p.sb("idst", [128, 128])
    p.dma(idst, idf)
    p.copy(ident, idst)
    ce = [0]

    def wload(dst, src_t, r0, nkc, c0, cw):
        sv = src_t.ap.rearrange("(kc p) n -> p kc n", p=128)
        for h in range(0, nkc, 8):
            st = wst.next()
            p.dma(st[:, :, 0:cw], V(src_t, sv[:, r0 + h:r0 + h + 8, c0:c0 + cw]))
            e = "dve" if ce[0] % 2 == 0 else "pool"
            ce[0] += 1
            p.copy(dst[:, h:h + 8, 0:cw], st[:, :, 0:cw], e=e)

    for cc in range(4):
        wa = wbB.next()
        wload(wa, wglu, 0, 8, cc * 512, 512)
        wb_ = wbB.next()
        wload(wb_, wglu, 0, 8, 2048 + cc * 512, 512)
        for t in range(NT):
            psa = pss.next()
            psb = pss.next()
            for kc in range(8):
                p.mm(psa, acts["z"][:, kc, t * 128:(t + 1) * 128], wa[:, kc, :], start=(kc == 0), stop=(kc == 7))
            for kc in range(8):
                p.mm(psb, acts["z"][:, kc, t * 128:(t + 1) * 128], wb_[:, kc, :], start=(kc == 0), stop=(kc == 7))
            g = gst.next()
            p.dma(g, gates[t * 128:(t + 1) * 128, cc * 512:(cc + 1) * 512])
            s = sig.next()
            p.act(s, psb, AF.Sigmoid)
            y = tmp.next()
            p.tt(y, psa, s, ALU.mult)
            p.tt(merged[:, t, cc * 512:(cc + 1) * 512], y, g, ALU.mult, e="pool")
    for bi, (nm, wsrc) in enumerate((("or", wret), ("ow", wrw))):
        for cc in range(4):
            wb_ = wbB.next()
            wload(wb_, wsrc, 0, 8, cc * 512, 512)
            for t in range(NT):
                ps = pss.next()
                for kc in range(8):
                    p.mm(ps, acts[nm][:, kc, t * 128:(t + 1) * 128], wb_[:, kc, :], start=(kc == 0), stop=(kc == 7))
                g = gst.next()
                p.dma(g, gates[t * 128:(t + 1) * 128, (bi + 1) * 2048 + cc * 512:(bi + 1) * 2048 + (cc + 1) * 512])
                y = tmp.next()
                p.tt(y, ps, g, ALU.mult)
                mv_ = merged[:, t, cc * 512:(cc + 1) * 512]
                p.tt(mv_, mv_, y, ALU.add, e="pool")
    mb = Rot([p.sb(f"mb{i}", [128, D], BF16) for i in range(1)])
    pst = Rot([p.ps(f"pst{i}", [128, 4, 128], BF16) for i in range(2)])
    for t in range(NT):
        m = mb.next()
        p.copy(m, merged[:, t, :], e="act")
        for q4 in range(4):
            pt = pst.next()
            for i in range(4):
                kc = q4 * 4 + i
                p.tr(pt[:, i, :], m[:, kc * 128:(kc + 1) * 128], ident)
            p.copy(mT[:, q4 * 4:(q4 + 1) * 4, t * 128:(t + 1) * 128], pt, e=("dve" if q4 % 2 == 0 else "act"))
    for cc in range(4):
        wa0 = wbB.next()
        wload(wa0, wo, 0, 8, cc * 512, 512)
        wa1 = wbB.next()
        wload(wa1, wo, 8, 8, cc * 512, 512)
        for t in range(NT):
            ps = pss.next()
            for kc in range(16):
                p.mm(ps, mT[:, kc, t * 128:(t + 1) * 128], (wa0 if kc < 8 else wa1)[:, kc % 8, :],
                     start=(kc == 0), stop=(kc == 15))
            g = gst.next()
            p.dma(g, x[t * 128:(t + 1) * 128, cc * 512:(cc + 1) * 512])
            p.stt(merged[:, t, cc * 512:(cc + 1) * 512], g, ALPHA, ps, ALU.mult, ALU.add)
    if DEBUG:
        dbg = p.dram("dbg", [Tc, D], kind="ExternalOutput")
        for t in range(NT):
            p.dma(dbg[t * 128:(t + 1) * 128, :], merged[:, t, :])
    scr = ln_scratch(p, "1")
    for t in range(NT):
        o = merged[:, t, :]
        layer_norm_tile(p, merged[:, t, :], o, lng, lnb, scr)
        p.dma(x1[t * 128:(t + 1) * 128, :], o)
    return p


def build_PREP(L):
    p = P()
    CW = 2048
    w = p.dram("w", [128, L])
    wb = p.dram("wb", [128, L], BF16, kind="ExternalOutput")
    st = Rot([p.sb(f"st{i}", [128, CW]) for i in range(3)])
    ob = Rot([p.sb(f"ob{i}", [128, CW], BF16) for i in range(3)])
    engs = ["dve", "pool", "act"]
    for i, c0 in enumerate(range(0, L, CW)):
        cw = min(CW, L - c0)
        s_ = st.next(); o_ = ob.next()
        p.dma(s_[:, 0:cw], w[:, c0:c0 + cw])
        p.copy(o_[:, 0:cw], s_[:, 0:cw], e=engs[i % 3])
        p.dma(wb[:, c0:c0 + cw], o_[:, 0:cw])
    return p


def build_C2(Tc):
    p = P()
    NT = Tc // 128
    BL = min(256, Tc)
    NB = Tc // BL
    NJ = D_FF // 128
    x1 = p.dram("x1", [Tc, D])
    x1T = p.dram("x1T", [D, Tc + 2])
    wup = p.dram("wupT", [D_FF // 256, 128, 16 * 2 * 2 * 128], BF16)
    wcv = p.dram("wcv", [128, 2 * NJ, 3])
    wdn = p.dram("wdnT", [D_FF // 256, 128, 2 * D], BF16)
    lng_d = p.dram("lng", [128, D])
    lnb_d = p.dram("lnb", [128, D])
    x2 = p.dram("x2", [Tc, D], kind="ExternalOutput")

    lng = p.sb("lng_s", [128, D])
    lnb = p.sb("lnb_s", [128, D])
    p.dma(lng, lng_d)
    p.dma(lnb, lnb_d)
    wc = p.sb("wc", [128, 2 * NJ, 3])
    p.dma(wc, wcv)
    acc = p.sb("acc", [128, NT, D])
    for t in range(NT):
        p.dma(acc[:, t, :], x1[t * 128:(t + 1) * 128, :])
    for t in range(NT):
        p.ts(acc[:, t, :], acc[:, t, :], ALPHA, ALU.mult, e=("dve" if t % 2 == 0 else "pool"))
    xb = p.sb("xb", [128, 16, Tc + 2], BF16)
    xst = Rot([p.sb(f"xst{i}", [128, Tc + 2]) for i in range(1)])
    xv = x1T.ap.rearrange("(kc p) t -> p kc t", p=128)
    load_cast_T(p, lambda kc: V(x1T, xv[:, kc, :]), xb, 16, Tc + 2, xst)

    JG = 2
    ub = Rot([p.sb(f"ub{i}", [128, 16, JG, 2, 128], BF16) for i in range(2)])
    db = Rot([p.sb(f"db{i}", [128, JG, D], BF16) for i in range(2)])
    gT = Rot([p.sb(f"gT{i}", [128, JG, Tc], BF16) for i in range(2)])
    ha = Rot([p.sb(f"ha{i}", [128, BL]) for i in range(2)])
    hb = Rot([p.sb(f"hb{i}", [128, BL]) for i in range(2)])
    sa = Rot([p.sb(f"sa{i}", [128, BL]) for i in range(2)])
    psu = Rot([p.ps(f"psu{i}", [128, 512]) for i in range(4)])
    psd = Rot([p.ps(f"psd{i}", [128, 512]) for i in range(4)])

    def load_group(jg):
        u = ub.next()
        dd = db.next()
        uv = u[:, :, :, :, :].re("p k j h c -> p (k j h c)")
        p.dma(uv[:, 0:4096], wup[jg, :, 0:4096])
        p.dma(uv[:, 4096:8192], wup[jg, :, 4096:8192])
        p.dma(dd[:, :, :].re("p j n -> p (j n)"), wdn[jg, :, :])
        return u, dd

    nxt = load_group(0)
    for jg in range(NJ // JG):
        u, dd = nxt
        if jg + 1 < NJ // JG:
            nxt = load_group(jg + 1)
        g = gT.next()
        for ji in range(JG):
            j = jg * JG + ji
            for blk in range(NB):
                hs = []
                for half in range(2):
                    ps = psu.next()
                    for kc in range(16):
                        p.mm(ps[:, 0:BL + 2], u[:, kc, ji, half, :], xb[:, kc, blk * BL:blk * BL + BL + 2],
                             start=(kc == 0), stop=(kc == 15))
                    h_ = (ha if half == 0 else hb).next()
                    cj = j + half * NJ
                    p.act(h_, ps[:, 2:BL + 2], AF.Copy, scale=wc[:, cj, 2:3])
                    p.stt(h_, ps[:, 1:BL + 1], wc[:, cj, 1:2], h_, ALU.mult, ALU.add)
                    p.stt(h_, ps[:, 0:BL], wc[:, cj, 0:1], h_, ALU.mult, ALU.add)
                    hs.append(h_)
                s = sa.next()
                p.act(s, hs[0], AF.Silu)
                p.tt(g[:, ji, blk * BL:(blk + 1) * BL], s, hs[1], ALU.mult, e="pool")
        for t in range(NT):
            for cc in range(4):
                ps = psd.next()
                for ji in range(JG):
                    p.mm(ps, g[:, ji, t * 128:(t + 1) * 128], dd[:, ji, cc * 512:(cc + 1) * 512],
                         start=(ji == 0), stop=(ji == JG - 1))
                a = acc[:, t, cc * 512:(cc + 1) * 512]
                p.tt(a, a, ps, ALU.add)
    scr = ln_scratch(p, "2")
    for t in range(NT):
        o = acc[:, t, :]
        layer_norm_tile(p, acc[:, t, :], o, lng, lnb, scr)
        p.dma(x2[t * 128:(t + 1) * 128, :], o)
    return p


def build_B1(Tt):
    p = P()
    TB = min(512, Tt)
    NBK = Tt // TB
    uT = p.dram("uT", [128, Tt])
    zT = p.dram("zT", [128, Tt], BF16, kind="ExternalOutput")
    small = {}
    for nm, shp in (("LRr", [128, 64]), ("LIr", [128, 64]), ("LSr", [128, 64]), ("BreT", [128, 64]),
                    ("BimT", [128, 64]), ("LRc", [128, 8]), ("LIc", [128, 8]), ("LSc", [128, 8]),
                    ("CC", [128, 128]), ("CCs", [128, 128]), ("dsk", [128, 1]), ("identf", [128, 128]),
                    ("S12", [128, 128]), ("sgnA", [128, 1]), ("maskg", [128, 8]), ("iota", [128, TB])):
        dt_ = p.dram(nm, shp)
        st = p.sb("s_" + nm, shp)
        p.dma(st, dt_)
        small[nm] = st
    S = small
    n = [0]

    def tmp(shape, dt=F32):
        n[0] += 1
        return p.sb(f"t{n[0]}", shape, dt)

    sh = [128, 64]
    stp = tmp(sh); ang = tmp(sh); sn = tmp(sh); cs = tmp(sh); tf = tmp(sh); ti = tmp(sh, I32); tk = tmp(sh)
    p.act(stp, S["LSr"], AF.Exp)
    p.tt(ang, S["LIr"], stp, ALU.mult)
    sincos(p, ang, sn, cs, tf, ti, tk)
    mag = tmp(sh)
    p.tt(mag, S["LRr"], stp, ALU.mult)
    p.act(mag, mag, AF.Exp)
    abr1 = tmp(sh); abi = tmp(sh)
    p.tt(abr1, mag, cs, ALU.mult)
    p.ts(abr1, abr1, -1.0, ALU.add)
    p.tt(abi, mag, sn, ALU.mult)
    den = tmp(sh); t2 = tmp(sh)
    p.tt(den, S["LRr"], S["LRr"], ALU.mult)
    p.tt(t2, S["LIr"], S["LIr"], ALU.mult)
    p.tt(den, den, t2, ALU.add)
    p.recip(den, den)
    fre = tmp(sh); fim = tmp(sh)
    p.tt(fre, abr1, S["LRr"], ALU.mult)
    p.tt(t2, abi, S["LIr"], ALU.mult)
    p.tt(fre, fre, t2, ALU.add)
    p.tt(fre, fre, den, ALU.mult)
    p.tt(fim, abi, S["LRr"], ALU.mult)
    p.tt(t2, abr1, S["LIr"], ALU.mult)
    p.tt(fim, fim, t2, ALU.subtract)
    p.tt(fim, fim, den, ALU.mult)
    bbr = tmp(sh); bbi = tmp(sh); nbbr = tmp(sh)
    p.tt(bbr, fre, S["BreT"], ALU.mult)
    p.tt(t2, fim, S["BimT"], ALU.mult)
    p.tt(bbr, bbr, t2, ALU.subtract)
    p.tt(bbi, fre, S["BimT"], ALU.mult)
    p.tt(t2, fim, S["BreT"], ALU.mult)
    p.tt(bbi, bbi, t2, ALU.add)
    p.ts(nbbr, bbr, -1.0, ALU.mult)
    BT = p.sb("BT", [128, 8, 128], BF16)
    BTs = p.sb("BTs", [128, 8, 128], BF16)
    for g in range(8):
        mg = S["maskg"][:, g:g + 1]
        p.ts(BT[:, g, 0:64], bbr, mg, ALU.mult)
        p.ts(BT[:, g, 64:128], bbi, mg, ALU.mult)
        p.ts(BTs[:, g, 0:64], bbi, mg, ALU.mult)
        p.ts(BTs[:, g, 64:128], nbbr, mg, ALU.mult)
    C1a = p.sb("C1a", [128, 8, 128], BF16)
    C2a = p.sb("C2a", [128, 8, 128], BF16)
    p.memset(C1a, 0.0)
    p.memset(C2a, 0.0)
    for g in range(8):
        p.ts(C1a[:, g, 16 * g:16 * g + 16], S["CC"][:, 16 * g:16 * g + 16], S["sgnA"][:, 0:1], ALU.mult)
        p.ts(C2a[:, g, 16 * g:16 * g + 16], S["CCs"][:, 16 * g:16 * g + 16], -1.0, ALU.mult)
    sh8 = [128, 8]
    stc = tmp(sh8); rho = tmp(sh8); theta = tmp(sh8)
    p.act(stc, S["LSc"], AF.Exp)
    p.tt(rho, S["LRc"], stc, ALU.mult)
    p.act(rho, rho, AF.Exp)
    p.tt(theta, S["LIc"], stc, ALU.mult)
    phi = tmp(sh8); sph = tmp(sh8); cph = tmp(sh8); f8 = tmp(sh8); i8 = tmp(sh8, I32); k8 = tmp(sh8)
    p.ts(phi, theta, float(TB), ALU.mult)
    sincos(p, phi, sph, cph, f8, i8, k8)
    p.ts(sph, sph, S["sgnA"][:, 0:1], ALU.mult)
    ROT = p.sb("ROT", [128, 8, 128])
    for g in range(8):
        p.ts(ROT[:, g, :], S["identf"], cph[:, g:g + 1], ALU.mult)
        p.stt(ROT[:, g, :], S["S12"], sph[:, g:g + 1], ROT[:, g, :], ALU.mult, ALU.add)
    SIN = p.sb("SIN", [128, 8, TB])
    COS = p.sb("COS", [128, 8, TB])
    RHO = p.sb("RHO", [128, 8, TB])
    shb = [128, TB]
    angb = tmp(shb); fb = tmp(shb); ib = tmp(shb, I32); kb = tmp(shb)
    for g in range(8):
        p.ts(angb, S["iota"], theta[:, g:g + 1], ALU.mult)
        sincos(p, angb, SIN[:, g, :], COS[:, g, :], fb, ib, kb)
        p.ts(RHO[:, g, :], S["iota"], 0.0, ALU.mult, rho[:, g:g + 1], ALU.add)

    ust = Rot([p.sb(f"ust{i}", [128, TB]) for i in range(2)])
    ubf = Rot([p.sb(f"ubf{i}", [128, TB], BF16) for i in range(2)])
    psa = Rot([p.ps(f"psa{i}", [128, TB]) for i in range(2)])
    psb = Rot([p.ps(f"psb{i}", [128, TB]) for i in range(2)])
    psy = Rot([p.ps(f"psy{i}", [128, TB]) for i in range(2)])
    psr = p.ps("psr", [128, 8])
    v1 = Rot([p.sb(f"v1{i}", [128, TB]) for i in range(2)])
    v2 = Rot([p.sb(f"v2{i}", [128, TB]) for i in range(2)])
    vv = Rot([p.sb(f"vv{i}", [128, TB]) for i in range(2)])
    shat = Rot([p.sb(f"shat{i}", [128, TB]) for i in range(3)])
    w1 = Rot([p.sb(f"w1{i}", [128, TB], BF16) for i in range(2)])
    w2 = Rot([p.sb(f"w2{i}", [128, TB], BF16) for i in range(2)])
    last = p.sb("last", [128, 8])
    init = p.sb("init", [128, 8])
    yb = Rot([p.sb(f"yb{i}", [128, TB]) for i in range(2)])
    gt = Rot([p.sb(f"gt{i}", [128, TB]) for i in range(2)])
    zo = Rot([p.sb(f"zo{i}", [128, TB], BF16) for i in range(2)])
    vv3 = Rot([p.sb(f"vv3{i}", [128, TB]) for i in range(3)])
    w13 = Rot([p.sb(f"w13{i}", [128, TB], BF16) for i in range(3)])
    w23 = Rot([p.sb(f"w23{i}", [128, TB], BF16) for i in range(3)])
    items = [(b, g) for b in range(NBK) for g in range(8)]
    blk = {}
    stA = {}
    stB = {}

    def block_begin(b):
        uf = ust.next()
        p.dma(uf, uT[:, b * TB:(b + 1) * TB])
        ub = ubf.next()
        p.copy(ub, uf, e="act")
        blk[b] = {"uf": uf, "ub": ub, "py": psy.next()}

    def stage_A(b, g):
        if g == 0:
            block_begin(b)
        ub = blk[b]["ub"]
        pa = psa.next()
        pb = psb.next()
        p.mm(pa, BT[:, g, :], ub)
        p.mm(pb, BTs[:, g, :], ub)
        a1 = v1.next(); a2 = v2.next(); av = vv3.next()
        p.tt(a1, pa, COS[:, g, :], ALU.mult)
        p.tt(a2, pb, SIN[:, g, :], ALU.mult)
        p.tt(av, a1, a2, ALU.add, e="pool")
        stA[(b, g)] = av

    def stage_B(b, g):
        av = stA.pop((b, g))
        if b > 0 and g == 0:
            for g_ in range(8):
                p.mm(psr[:, g_:g_ + 1], ROT[:, g_, :], last[:, g_:g_ + 1])
            p.copy(init, psr)
        sh_ = shat.next()
        p.scan(sh_, RHO[:, g, :], av, (0.0 if b == 0 else init[:, g:g + 1]), ALU.mult, ALU.add)
        p.copy(last[:, g:g + 1], sh_[:, TB - 1:TB], e="act")
        b1 = w13.next(); b2 = w23.next()
        p.tt(b1, sh_, COS[:, g, :], ALU.mult, e="pool")
        p.tt(b2, sh_, SIN[:, g, :], ALU.mult)
        stB[(b, g)] = (b1, b2)

    def stage_C(b, g):
        b1, b2 = stB.pop((b, g))
        py = blk[b]["py"]
        p.mm(py, C1a[:, g, :], b1, start=(g == 0), stop=False)
        p.mm(py, C2a[:, g, :], b2, start=False, stop=(g == 7))
        if g == 7:
            uf = blk[b]["uf"]
            y = yb.next()
            p.stt(y, uf, S["dsk"][:, 0:1], py, ALU.mult, ALU.add)
            t_ = gt.next()
            p.tt(t_, y, y, ALU.mult, e="pool")
            p.ts(t_, t_, 0.044715, ALU.mult, 1.0, ALU.add, e="pool")
            p.tt(t_, t_, y, ALU.mult, e="pool")
            p.act(t_, t_, AF.Sigmoid, scale=1.5957691216057308)
            z = zo.next()
            p.tt(z, y, t_, ALU.mult)
            p.dma(zT[:, b * TB:(b + 1) * TB], z)
            del blk[b]

    n_it = len(items)
    for i in range(-1, n_it + 1):
        if 0 <= i + 1 < n_it:
            stage_A(*items[i + 1])
        if 0 <= i < n_it:
            stage_B(*items[i])
        if 0 <= i - 1 < n_it:
            stage_C(*items[i - 1])
    return p


def host_B1_inputs(inp, l, c, uT_c, TB):
    gs = slice(8 * c, 8 * c + 8)
    lr = inp["ssm_lambda_re"][l][gs]; li = inp["ssm_lambda_im"][l][gs]; ls = inp["ssm_log_step"][l][gs]
    bre = inp["ssm_b_re"][l][gs]; bim = inp["ssm_b_im"][l][gs]
    cre = inp["ssm_c_re"][l][gs]; cim = inp["ssm_c_im"][l][gs]
    f = np.float32
    rep16 = lambda a: np.ascontiguousarray(np.repeat(a, 16, axis=0)).astype(f)
    m = {"uT": uT_c}
    m["LRr"] = rep16(lr); m["LIr"] = rep16(li)
    m["LSr"] = rep16(np.broadcast_to(ls[:, None], (8, 64)))
    m["BreT"] = np.ascontiguousarray(bre.transpose(0, 2, 1).reshape(128, 64)).astype(f)
    m["BimT"] = np.ascontiguousarray(bim.transpose(0, 2, 1).reshape(128, 64)).astype(f)
    m["LRc"] = np.ascontiguousarray(np.concatenate([lr.T, lr.T], 0)).astype(f)
    m["LIc"] = np.ascontiguousarray(np.concatenate([li.T, li.T], 0)).astype(f)
    m["LSc"] = np.ascontiguousarray(np.broadcast_to(ls[None, :], (128, 8))).astype(f)
    creT = cre.transpose(2, 0, 1).reshape(64, 128); cimT = cim.transpose(2, 0, 1).reshape(64, 128)
    m["CC"] = np.ascontiguousarray(np.concatenate([creT, cimT], 0)).astype(f)
    m["CCs"] = np.ascontiguousarray(np.concatenate([cimT, creT], 0)).astype(f)
    m["dsk"] = np.ascontiguousarray(inp["ssm_d"][l][128 * c:128 * c + 128].reshape(128, 1)).astype(f)
    m["identf"] = np.eye(128, dtype=f)
    s12 = np.zeros((128, 128), f)
    for k in range(64):
        s12[k, k + 64] = 1.0
        s12[k + 64, k] = 1.0
    m["S12"] = s12
    sg = np.ones((128, 1), f); sg[64:] = -1.0
    m["sgnA"] = sg
    mg = np.zeros((128, 8), f)
    for g in range(8):
        mg[16 * g:16 * g + 16, g] = 1.0
    m["maskg"] = mg
    m["iota"] = np.ascontiguousarray(np.broadcast_to(np.arange(TB, dtype=f)[None, :], (128, TB)))
    return m


def build_B2(Tt):
    p = P()
    TB = min(512, Tt)
    NBK = Tt // TB
    NCH = TB // 128
    qT = p.dram("qT", [128, Tt])
    kT = p.dram("kT", [128, Tt])
    vv = p.dram("v", [Tt, 128])
    sg = p.dram("sg", [Tt, 128])
    pos = p.dram("pos", [128, Tt], I32)
    o_out = p.dram("o", [Tt, 128], BF16, kind="ExternalOutput")
    S = {}
    for nm, shp in (("invf", [128, 1]), ("PERM", [128, 128]), ("DmT", [128, 128]), ("qd", [128, 1]),
                    ("kd", [128, 1]), ("cdec", [128, 1]), ("NG", [128, 128]), ("NB", [128, 128]),
                    ("identf", [128, 128])):
        dt_ = p.dram(nm, shp)
        st = p.sb("s_" + nm, shp)
        p.dma(st, dt_)
        S[nm] = st
    identb = p.sb("identb", [128, 128], BF16)
    p.copy(identb, S["identf"])
    state = p.sb("state", [128, 128])
    state_bf = p.sb("state_bf", [128, 128], BF16)
    p.memset(state, 0.0)
    p.memset(state_bf, 0.0)
    shb = [128, TB]
    qf = Rot([p.sb(f"qf{i}", shb) for i in range(2)])
    kf = Rot([p.sb(f"kf{i}", shb) for i in range(2)])
    posi = p.sb("posi", shb, I32)
    ang = p.sb("ang", shb); fb = p.sb("fb", shb); ib = p.sb("ib", shb, I32); kb = p.sb("kb", shb)
    SINt = p.sb("SINt", shb); COSt = p.sb("COSt", shb)
    r1 = p.sb("r1", shb); r2 = p.sb("r2", shb)
    rqT = Rot([p.sb(f"rqT{i}", shb, BF16) for i in range(2)])
    rkT = Rot([p.sb(f"rkT{i}", shb, BF16) for i in range(2)])
    pP = Rot([p.ps(f"pP{i}", shb) for i in range(1)])
    psc = p.ps("psc", [128, 128]); po2 = p.ps("po2", [128, 128])
    po1 = Rot([p.ps(f"po1{i}", [128, 128]) for i in range(2)])
    ptr = p.ps("ptr", [128, 128], BF16)
    pkv = Rot([p.ps(f"pkv{i}", [128, 128]) for i in range(2)])
    vf = Rot([p.sb(f"vf{i}", [128, 128]) for i in range(2)])
    vb = Rot([p.sb(f"vb{i}", [128, 128], BF16) for i in range(2)])
    sgf = Rot([p.sb(f"sgf{i}", [128, 128]) for i in range(2)])
    scm = Rot([p.sb(f"scm{i}", [128, 128], BF16) for i in range(2)])
    insb = Rot([p.sb(f"insb{i}", [128, 128]) for i in range(2)])
    osb = Rot([p.sb(f"osb{i}", [128, 128]) for i in range(2)])
    kdb = Rot([p.sb(f"kdb{i}", [128, 128], BF16) for i in range(2)])
    st6 = p.sb("st6", [128, 6]); mv = p.sb("mv", [128, 2]); rs = p.sb("rs", [128, 1])
    onb = Rot([p.sb(f"onb{i}", [128, 128]) for i in range(2)])
    oo = Rot([p.sb(f"oo{i}", [128, 128], BF16) for i in range(2)])
    rot_blk = {}
    live = {}

    def rotary(b):
        cs_ = slice(b * TB, (b + 1) * TB)
        q_ = qf.next(); k_ = kf.next()
        p.dma(q_, qT[:, cs_])
        p.dma(k_, kT[:, cs_])
        p.dma(posi, pos[:, cs_])
        p.copy(ang, posi)
        p.ts(ang, ang, S["invf"][:, 0:1], ALU.mult)
        sincos(p, ang, SINt, COSt, fb, ib, kb)
        rots = []
        for src, dstrot in ((q_, rqT), (k_, rkT)):
            pp = pP.next()
            p.mm(pp, S["PERM"], src)
            p.tt(r1, src, COSt, ALU.mult, e="pool")
            p.tt(r2, pp, SINt, ALU.mult)
            dd = dstrot.next()
            p.tt(dd, r1, r2, ALU.add)
            rots.append(dd)
        rot_blk[b] = rots

    def s1(b, n_):
        if n_ == 0:
            rotary(b)
        rq, rk = rot_blk[b]
        c0 = n_ * 128
        rows = slice(b * TB + c0, b * TB + c0 + 128)
        v_ = vf.next(); s_ = sgf.next()
        p.dma(v_, vv[rows, :])
        p.dma(s_, sg[rows, :])
        vb_ = vb.next()
        p.copy(vb_, v_, e="act")
        p.mm(psc, rk[:, c0:c0 + 128], rq[:, c0:c0 + 128])
        sc = scm.next()
        p.tt(sc, psc, S["DmT"], ALU.mult)
        p1 = po1.next()
        p.mm(p1, sc, vb_)
        i_ = insb.next()
        p.copy(i_, p1, e="act")
        p.tr(ptr, rk[:, c0:c0 + 128], identb)
        kd_ = kdb.next()
        p.ts(kd_, ptr, S["kd"][:, 0:1], ALU.mult)
        pk = pkv.next()
        p.mm(pk, kd_, vb_)
        live[(b, n_)] = (rq, c0, rows, s_, i_, pk)

    def s2(b, n_):
        rq, c0, rows, s_, i_, pk = live.pop((b, n_))
        p.mm(po2, rq[:, c0:c0 + 128], state_bf)
        o_ = osb.next()
        p.stt(o_, po2, S["qd"][:, 0:1], i_, ALU.mult, ALU.add)
        p.stt(state, state, S["cdec"][:, 0:1], pk, ALU.mult, ALU.add)
        p.copy(state_bf, state, e="act")
        p.bn_stats(st6, o_)
        p.bn_aggr(mv, st6)
        p.ts(rs, mv[:, 1:2], 1e-5, ALU.add)
        p.act(rs, rs, AF.Sqrt)
        p.recip(rs, rs)
        on = onb.next()
        p.ts(on, o_, mv[:, 0:1], ALU.subtract, rs[:, 0:1], ALU.mult)
        p.tt(on, on, S["NG"], ALU.mult, e="pool")
        p.tt(on, on, S["NB"], ALU.add, e="pool")
        ob = oo.next()
        p.tt(ob, on, s_, ALU.mult, e="pool")
        p.dma(o_out[rows, :], ob)
        if n_ == NCH - 1:
            del rot_blk[b]

    items = [(b, n_) for b in range(NBK) for n_ in range(NCH)]
    s1(*items[0])
    for i in range(len(items)):
        if i + 1 < len(items):
            s1(*items[i + 1])
        s2(*items[i])
    return p


def host_B2_inputs(inp, l, c, qT_c, kT_c, v_c, sg_c, pos_T):
    f = np.float32
    h = c
    lg = np.log1p(-np.exp2(-5.0 - h))
    idx = np.arange(128, dtype=np.float64)
    rel = idx[None, :] - idx[:, None]
    sc = 128 ** -0.5
    DmT = np.where(rel >= 0, np.exp(lg * np.maximum(rel, 0.0)), 0.0) * sc
    perm = np.zeros((128, 128), f)
    for m_ in range(64):
        perm[m_ + 64, m_] = -1.0
        perm[m_, m_ + 64] = 1.0
    invf = (10000.0 ** (-np.arange(64, dtype=f) / 64)).astype(f)
    rep = lambda v_: np.ascontiguousarray(np.broadcast_to(v_[None, :], (128, v_.shape[0]))).astype(f)
    return {
        "qT": qT_c, "kT": kT_c, "v": v_c, "sg": sg_c, "pos": pos_T,
        "invf": np.concatenate([invf, invf]).reshape(128, 1).astype(f),
        "PERM": perm, "DmT": DmT.astype(f),
        "qd": np.exp(lg * (idx + 1.0)).reshape(128, 1).astype(f),
        "kd": (np.exp(lg * (127.0 - idx)) * sc).reshape(128, 1).astype(f),
        "cdec": np.full((128, 1), np.exp(lg * 128.0), f),
        "NG": rep(inp["ret_norm_g"][l][128 * c:128 * c + 128]),
        "NB": rep(inp["ret_norm_b"][l][128 * c:128 * c + 128]),
        "identf": np.eye(128, dtype=f),
    }


def build_B3(Tt):
    p = P()
    TB = min(512, Tt)
    NBK = Tt // TB
    CH = 64
    NCH = TB // CH
    NG_ = NCH // 4
    zin = {nm: p.dram(nm, [128, Tt + 1]) for nm in ("zr", "zk", "zv", "zl1", "zl2")}
    o_out = p.dram("o", [Tt, 128], BF16, kind="ExternalOutput")
    S = {}
    for nm, shp in (("MU", [128, 5]), ("PV", [128, 5]), ("w2c", [128, 128]), ("g2c", [128, 128]),
                    ("NG", [64, 128]), ("NB", [64, 128]), ("bones", [128, 128]), ("hsel", [128, 2]),
                    ("cmask", [128, TB]), ("MK1", [64, 512]), ("MK2", [64, 512]), ("MK3", [64, 256]),
                    ("identf", [128, 128]), ("bdmask", [128, 128])):
        dt_ = p.dram(nm, shp)
        st = p.sb("s_" + nm, shp)
        p.dma(st, dt_)
        S[nm] = st
    w2b = p.sb("w2b", [128, 128], BF16); p.copy(w2b, S["w2c"])
    g2b = p.sb("g2b", [128, 128], BF16); p.copy(g2b, S["g2c"])
    hselb = p.sb("hselb", [128, 2], BF16); p.copy(hselb, S["hsel"])
    identb = p.sb("identb", [128, 128], BF16); p.copy(identb, S["identf"])
    bk = [p.ps(f"bank{i}", [128, 512]) for i in range(8)]
    H = p.sb("H", [128, 128]); Hb = p.sb("Hb", [128, 128], BF16)
    p.memset(H, 0.0); p.memset(Hb, 0.0)
    shb = [128, TB]
    n = [0]

    def tmp(shape=shb, dt=F32):
        n[0] += 1
        return p.sb(f"t{n[0]}", shape, dt)

    zst = {nm: Rot([p.sb(f"zst_{nm}{i}", [128, TB + 1]) for i in range(2)]) for nm in zin}
    dsh = tmp(); r_ = tmp(); k_ = tmp(); v_ = tmp(); l1 = tmp(); l2 = tmp()
    l1b = tmp(dt=BF16); sgb = tmp([128, TB + 64], BF16)
    p.memset(sgb, 0.0)
    ld = tmp(); a_ = tmp(); cs = tmp(); e0 = tmp(); e1 = tmp(); e2 = tmp(); tq = tmp()
    kk = tmp(); rn = tmp(); kmod = tmp(); bv = tmp()
    rkr = tmp([128, TB + 64], BF16); vbf = tmp([128, TB + 64], BF16)
    for t_ in (rkr, vbf):
        p.memset(t_, 0.0)
    Mm = p.sb("Mm", [64, NCH + 1, 2, 64], BF16)
    p.memset(Mm, 0.0)
    MMs = [Rot([p.sb(f"MM{g}{i}", [64, 9, 128], BF16) for i in range(2)]) for g in range(NG_)]
    Tts = [Rot([p.sb(f"Tt{g}{i}", [64, 10, 64], BF16) for i in range(2)]) for g in range(NG_)]
    D1 = []
    for i3 in range(3):
        d_ = {"KR": p.sb(f"KR{i3}", [128, NCH + 1, 2, CH], BF16),
              "Bh": p.sb(f"Bh{i3}", [128, TB + 64], BF16), "Kh": p.sb(f"Kh{i3}", [128, TB + 64], BF16),
              "Vb": p.sb(f"Vb{i3}", [64, NCH, 128], BF16), "Vf": p.sb(f"Vf{i3}", [64, NCH, 128]),
              "BhT": p.sb(f"BhT{i3}", [64, NCH, 128], BF16), "KhT": p.sb(f"KhT{i3}", [64, NCH, 128], BF16),
              "Gtm": p.sb(f"Gtm{i3}", [64, NCH, 128]), "BS": p.sb(f"BS{i3}", [64, NCH * 2]),
              "PC": p.sb(f"PC{i3}", [128, NCH])}
        for nm in ("KR", "Bh", "Kh"):
            p.memset(d_[nm], 0.0)
        D1.append(d_)
    D2 = []
    for par in range(2):
        d_ = {"A1": p.sb(f"A1{par}", [64, NCH + 1, 2, 128], BF16),
              "A2": p.sb(f"A2{par}", [64, NCH + 1, 2, 128], BF16),
              "Tf": [p.sb(f"Tf{par}{g}", [64, 10, 64], BF16) for g in range(NG_)]}
        for nm in ("A1", "A2"):
            p.memset(d_[nm], 0.0)
        for g in range(NG_):
            p.memset(d_["Tf"][g], 0.0)
        D2.append(d_)
    W0n = Rot([p.sb(f"W0n{i}", [64, 128], BF16) for i in range(2)])
    Ub = Rot([p.sb(f"Ub{i}", [64, 128], BF16) for i in range(2)])
    Ysb = p.sb("Ysb", [64, NCH, 128])
    tH = p.sb("tH", [128, 128])
    xc = p.sb("xc", [64, NCH, 128]); sq_ = p.sb("sq_", [64, NCH, 128])
    s16 = p.sb("s16", [64, NCH * 2]); v16 = p.sb("v16", [64, NCH * 2])
    osb = Rot([p.sb(f"osb{i}", [64, NCH, 128], BF16) for i in range(2)])
    mk1 = S["MK1"][:, :].re("p (c t) -> p c t", t=128)
    mk2 = S["MK2"][:, :].re("p (c t) -> p c t", t=128)
    mk3 = S["MK3"][:, :].re("p (c t) -> p c t", t=64)

    def c3(vw):
        return vw.re("p (c t) -> p c t", t=CH)

    def pre1(b):
        D_ = D1[b % 3]
        KR, Bh, Kh = D_["KR"], D_["Bh"], D_["Kh"]
        zs = {}
        for nm in zin:
            st = zst[nm].next()
            p.dma(st, zin[nm][:, b * TB:b * TB + TB + 1])
            zs[nm] = st
        for i, (nm, dst) in enumerate((("zr", r_), ("zk", k_), ("zv", v_), ("zl1", l1), ("zl2", l2))):
            st = zs[nm]
            p.tt(dsh, st[:, 0:TB], st[:, 1:TB + 1], ALU.subtract, e="pool")
            p.stt(dst, dsh, S["MU"][:, i:i + 1], st[:, 1:TB + 1], ALU.mult, ALU.add)
        yield
        p.act(l1b[0:64, :], l1[0:64, :], AF.Tanh)
        p.copy(l1b[64:128, :], l1[64:128, :], e="act")
        p.act(sgb[:, 0:TB], l2, AF.Sigmoid)
        p.mm(bk[4], w2b[0:64, :], l1b[0:64, :])
        p.mm(bk[5], w2b[64:128, :], l1b[64:128, :])
        p.act(ld, bk[4], AF.Sigmoid, bias=S["PV"][:, 0:1])
        p.ts(ld, ld, -EXPM05, ALU.mult)
        p.act(a_, bk[5], AF.Sigmoid, bias=S["PV"][:, 1:2])
        p.scan(cs, S["cmask"], ld, 0.0, ALU.mult, ALU.add)
        p.act(e1, cs, AF.Exp)
        p.act(e2, cs, AF.Exp, scale=-1.0)
        p.tt(tq, cs, ld, ALU.subtract, e="pool")
        p.act(e0, tq, AF.Exp)
        p.copy(D_["PC"], e1[:, :].re("p (c t) -> p c t", t=CH)[:, :, CH - 1], e="pool")
        yield
        p.ts(kk, k_, S["PV"][:, 2:3], ALU.mult)
        p.tt(tq, kk, kk, ALU.mult, e="pool")
        p.mm(bk[6], S["bones"], tq)
        p.ts(rn, bk[6], 1e-24, ALU.max)
        p.act(rn, rn, AF.Sqrt)
        p.recip(rn, rn)
        p.tt(kk, kk, rn, ALU.mult)
        p.ts(tq, a_, -1.0, ALU.add)
        p.ts(tq, tq, S["PV"][:, 3:4], ALU.mult)
        p.stt(kmod, tq, 1.0, k_, ALU.add, ALU.mult)
        p.tt(bv, a_, kk, ALU.mult, e="pool")
        yield
        p.tt(KR[:, 0:NCH, 0, :], c3(kk[:, :]), c3(e0[:, :]), ALU.mult)
        p.tt(KR[:, 0:NCH, 1, :], c3(r_[:, :]), c3(e1[:, :]), ALU.mult)
        p.tt(Bh[:, 0:TB], bv, e2, ALU.mult, e="pool")
        p.tt(Kh[:, 0:TB], kmod, e2, ALU.mult, e="pool")
        p.ts(tq, r_, S["PV"][:, 4:5], ALU.mult)
        p.tt(rkr[:, 0:TB], tq, kmod, ALU.mult)
        p.copy(vbf[:, 0:TB], v_, e="act")
        yield
        for g4 in range(NG_):
            sl4 = slice(g4 * 4, (g4 + 1) * 4)
            for i in range(4):
                c = g4 * 4 + i
                p.mm(bk[4][:, i * 128:(i + 1) * 128], vbf[:, c * CH:c * CH + 128], identb)
            p.copy(D_["Vf"][:, sl4, :], bk[4][0:64, 0:512].re("p (c e) -> p c e", e=128), e="act")
            p.copy(D_["Vb"][:, sl4, :], D_["Vf"][:, sl4, :])
            for i in range(4):
                c = g4 * 4 + i
                p.mm(bk[5][:, i * 128:(i + 1) * 128], sgb[:, c * CH:c * CH + 128], g2b)
            p.copy(D_["Gtm"][:, sl4, :], bk[5][0:64, :].re("p (c e) -> p c e", e=128), e="act")
            for bi_, (src, dstT) in enumerate(((Bh, D_["BhT"]), (Kh, D_["KhT"]))):
                bk_ = bk[6 - 2 * bi_]
                for i in range(4):
                    c = g4 * 4 + i
                    p.mm(bk_[:, i * 128:(i + 1) * 128], src[:, c * CH:c * CH + 128], identb)
                p.copy(dstT[:, sl4, :], bk_[0:64, 0:512].re("p (c e) -> p c e", e=128))
            yield
        for c in range(NCH):
            p.mm(bk[5][:, 2 * c:2 * c + 2], rkr[:, c * CH:c * CH + 128], hselb)
        p.copy(D_["BS"], bk[5][0:64, 0:2 * NCH], e="act")
        yield

    def pre2(b):
        D_ = D1[b % 3]
        KR, Bh, Kh = D_["KR"], D_["Bh"], D_["Kh"]
        E_ = D2[b % 2]
        A1, A2 = E_["A1"], E_["A2"]
        for g4 in range(NG_):
            sl4 = slice(g4 * 4, (g4 + 1) * 4)
            for ci in range(4):
                c = g4 * 4 + ci
                for h in range(2):
                    ho = 64 * h
                    krv = KR[ho:ho + 64, c, :, :].re("p a t -> p (a t)")
                    p.mm(bk[0 + h][:, ci * 128:(ci + 1) * 128], Bh[ho:ho + 64, c * CH:c * CH + 128], krv)
                    p.mm(bk[2 + h][:, ci * 128:(ci + 1) * 128], Kh[ho:ho + 64, c * CH:c * CH + 128], krv)
            for h in range(2):
                p.tt(A1[:, sl4, h, :], bk[0 + h][0:64, :].re("p (c t) -> p c t", t=128), mk1, ALU.mult)
                p.tt(A2[:, sl4, h, :], bk[2 + h][0:64, :].re("p (c t) -> p c t", t=128), mk2, ALU.mult)
            for ci in range(4):
                c = g4 * 4 + ci
                for h in range(2):
                    ho = 64 * h
                    krv = KR[ho:ho + 64, c, :, :].re("p a t -> p (a t)")
                    p.mm(bk[0 + h][:, ci * 64:(ci + 1) * 64], krv, Bh[ho:ho + 64, c * CH:(c + 1) * CH])
            for h in range(2):
                p.tt(Mm[:, sl4, h, :], bk[0 + h][0:64, 0:256].re("p (c t) -> p c t", t=64), mk3, ALU.mult)
            yield
        pairs_g = [[(G * 4 + ci, h) for ci in range(4) for h in range(2)] for G in range(NG_)]
        Tcur = []
        curM = []
        curMt = []
        Mflat = Mm[:, :, :, :].re("p c h t -> p (c h t)")
        for G in range(NG_):
            Tt = Tts[G].next()
            for pi, (c, h) in enumerate(pairs_g[G]):
                p.tt(Tt[:, pi, :], A1[:, c, h, 0:64], identb[0:64, 0:64], ALU.add, e="pool")
            Tcur.append(Tt)
            curM.append([Mflat[:, (2 * c + h) * 64:(2 * c + h) * 64 + 128] for (c, h) in pairs_g[G]])
            curMt.append([A1[:, c, h, 0:128] for (c, h) in pairs_g[G]])
        yield
        for lev in range(1, 6):
            last = (lev == 5)
            for G in range(NG_):
                sqb = [bk[(2 * G) % 4], bk[(2 * G + 1) % 4]]
                for pi in range(8):
                    b_ = sqb[pi // 4]
                    col = (pi % 4) * 128
                    p.mm(b_[:, col:col + 64], curMt[G][pi], curM[G][pi][:, 0:64])
                    if not last:
                        p.mm(b_[:, col + 64:col + 128], curM[G][pi], curMt[G][pi][:, 0:64])
            for G in range(NG_):
                sqb = [bk[(2 * G) % 4], bk[(2 * G + 1) % 4]]
                MMn = MMs[G].next()
                for hb in range(2):
                    p.copy(MMn[:, 4 * hb:4 * hb + 4, :], sqb[hb][0:64, :].re("p (a t) -> p a t", t=128),
                           e=("act" if hb == 0 else "dve"))
                MMf = MMn[:, :, :].re("p a t -> p (a t)")
                curM[G] = [MMf[:, pi * 128:pi * 128 + 128] for pi in range(8)]
                curMt[G] = [MMf[:, pi * 128 + 64:pi * 128 + 192] for pi in range(8)]
            for G in range(NG_):
                pacc = bk[(2 * G) % 4]
                for pi in range(8):
                    p.mm(pacc[:, pi * 64:(pi + 1) * 64], curM[G][pi], Tcur[G][:, pi, :])
            for G in range(NG_):
                pacc = bk[(2 * G) % 4]
                Tn = E_["Tf"][G] if last else Tts[G].next()
                p.tt(Tn[:, 0:8, :], Tcur[G][:, 0:8, :], pacc[0:64, :].re("p (a t) -> p a t", t=64), ALU.add)
                Tcur[G] = Tn
            yield

    def chain(b):
        D_ = D1[b % 3]
        E_ = D2[b % 2]
        KR, Vb, BhT, KhT = D_["KR"], D_["Vb"], D_["BhT"], D_["KhT"]
        A1, A2 = E_["A1"], E_["A2"]
        krf = KR[:, :, :, :].re("p c a t -> p (c a t)")
        a1f = A1[:, :, :, :].re("p c h t -> p (c h t)")
        a2f = A2[:, :, :, :].re("p c h t -> p (c h t)")
        pw, pu, py, ph = bk[7][:, 0:128], bk[7][:, 128:256], bk[7][:, 256:384], bk[7][:, 384:512]
        for c in range(NCH):
            Tt = E_["Tf"][c // 4]
            p.mm(pw[:, 0:128], krf[:, c * 128:c * 128 + 128], Hb, start=True, stop=False)
            for h in range(2):
                p.mm(pw[:, 64 * h:64 * h + 64], A2[:, c, h, 0:128], Vb[:, c, 64 * h:64 * h + 64],
                     start=False, stop=(h == 1))
            wn = W0n.next()
            p.act(wn, pw[0:64, 0:128], AF.Copy, scale=-1.0)
            for h in range(2):
                pi = (c % 4) * 2 + h
                p.mm(pu[:, 64 * h:64 * h + 64], Tt[:, pi:pi + 2, :].re("p a t -> p (a t)"), wn[:, 64 * h:64 * h + 64])
            ub = Ub.next()
            p.copy(ub, pu[0:64, 0:128])
            p.mm(py[:, 0:128], krf[:, c * 128 + 64:c * 128 + 192], Hb, start=True, stop=False)
            for h in range(2):
                sl = slice(64 * h, 64 * h + 64)
                o_ = (2 * c + h) * 128 + 64
                p.mm(py[:, sl], a1f[:, o_:o_ + 128], ub[:, sl], start=False, stop=False)
                p.mm(py[:, sl], a2f[:, o_:o_ + 128], Vb[:, c, sl], start=False, stop=(h == 1))
            p.copy(Ysb[:, c, :], py[0:64, 0:128], e="act")
            p.mm(ph[:, 0:128], BhT[:, c, :], ub, start=True, stop=False)
            p.mm(ph[:, 0:128], KhT[:, c, :], Vb[:, c, :], start=False, stop=True)
            p.tt(tH, ph[:, 0:128], S["bdmask"], ALU.mult)
            p.tt(tH, tH, H, ALU.add)
            pc = D_["PC"][:, c:c + 1]
            p.ts(H, tH, pc, ALU.mult)
            p.act(Hb, tH, AF.Copy, scale=pc)
            yield
        y4 = Ysb[:, :, :].re("p c (h e) -> p (c h) e", e=64)
        p.reduce(s16, y4, ALU.add)
        p.ts(s16, s16, 1.0 / 64, ALU.mult)
        x4 = xc[:, :, :].re("p c (h e) -> p (c h) e", e=64)
        q4 = sq_[:, :, :].re("p c (h e) -> p (c h) e", e=64)
        p.tt(x4, y4, s16[:, :].re("p (a o) -> p a o", o=1).bc([64, NCH * 2, 64]), ALU.subtract)
        p.tt(q4, x4, x4, ALU.mult, e="pool")
        p.reduce(v16, q4, ALU.add)
        p.ts(v16, v16, 1.0 / 64, ALU.mult, 64e-5, ALU.add)
        p.act(v16, v16, AF.Sqrt)
        p.recip(v16, v16)
        p.tt(x4, x4, v16[:, :].re("p (a o) -> p a o", o=1).bc([64, NCH * 2, 64]), ALU.mult)
        ngb = S["NG"][:, :].re("p (o e) -> p o e", o=1).bc([64, NCH, 128])
        nbb = S["NB"][:, :].re("p (o e) -> p o e", o=1).bc([64, NCH, 128])
        p.tt(xc, xc, ngb, ALU.mult)
        p.tt(xc, xc, nbb, ALU.add)
        vf4 = D_["Vf"][:, :, :].re("p c (h e) -> p (c h) e", e=64)
        p.tt(q4, vf4, D_["BS"][:, :].re("p (a o) -> p a o", o=1).bc([64, NCH * 2, 64]), ALU.mult)
        p.tt(xc, xc, sq_, ALU.add)
        ob = osb.next()
        p.tt(ob, xc, D_["Gtm"], ALU.mult)
        p.dma(V(o_out, o_out.ap[b * TB:(b + 1) * TB, :].rearrange("(c t) e -> t c e", t=CH)), ob)
        yield

    def drain(gens):
        gens = [g for g in gens if g is not None]
        while gens:
            for g in list(gens):
                try:
                    next(g)
                except StopIteration:
                    gens.remove(g)

    drain([pre1(0)])
    drain([pre2(0), pre1(1) if NBK > 1 else None])
    for b in range(NBK):
        drain([chain(b), pre2(b + 1) if b + 1 < NBK else None, pre1(b + 2) if b + 2 < NBK else None])
    return p


def host_B3_inputs(inp, l, c, zT_full, TB):
    f = np.float32
    Tt = zT_full.shape[1]
    ch = slice(128 * c, 128 * c + 128)

    def pad(a):
        return np.ascontiguousarray(np.concatenate([np.zeros((a.shape[0], 1), f), a], axis=1))

    mu = inp["rwkv_mu"][l]
    m = {"zr": pad(zT_full[0:1024][ch]), "zk": pad(zT_full[1024:2048][ch]), "zv": pad(zT_full[2048:3072][ch]),
         "zl1": pad(zT_full[3072:3200]), "zl2": pad(zT_full[3200:3328])}
    m["MU"] = np.ascontiguousarray(np.stack([mu[0:1024][ch], mu[1024:2048][ch], mu[2048:3072][ch],
                                             mu[3072:3200], mu[3200:3328]], axis=1)).astype(f)
    m["PV"] = np.ascontiguousarray(np.stack([inp["rwkv_w0"][l][ch], inp["rwkv_a0"][l][ch], inp["rwkv_k_k"][l][ch],
                                             inp["rwkv_k_a"][l][ch], inp["rwkv_r_k"][l][ch]], axis=1)).astype(f)
    m["w2c"] = np.ascontiguousarray(np.concatenate([inp["rwkv_w2"][l][:, ch], inp["rwkv_a2"][l][:, ch]], 0)).astype(f)
    m["g2c"] = np.ascontiguousarray(inp["rwkv_g2"][l][:, ch]).astype(f)
    rep = lambda v_: np.ascontiguousarray(np.broadcast_to(v_[None, :], (64, v_.shape[0]))).astype(f)
    m["NG"] = rep(inp["rwkv_norm_g"][l][ch]); m["NB"] = rep(inp["rwkv_norm_b"][l][ch])
    bo = np.zeros((128, 128), f); bo[0:64, 0:64] = 1.0; bo[64:, 64:] = 1.0
    m["bones"] = bo
    m["bdmask"] = bo.copy()
    hs = np.zeros((128, 2), f); hs[0:64, 0] = 1.0; hs[64:, 1] = 1.0
    m["hsel"] = hs
    cm = np.ones((128, TB), f); cm[:, ::64] = 0.0
    m["cmask"] = cm
    s = np.arange(64)[:, None]; t = np.arange(64)[None, :]
    strict_st = (t > s).astype(f); incl_st = (t >= s).astype(f)
    strict_ts = (t < s).astype(f)
    m["MK1"] = np.ascontiguousarray(np.tile(np.concatenate([-strict_st, incl_st], 1), (1, 4)))
    m["MK2"] = np.ascontiguousarray(np.tile(np.concatenate([strict_st, incl_st], 1), (1, 4)))
    m["MK3"] = np.ascontiguousarray(np.tile(-strict_ts, (1, 4)))
    m["identf"] = np.eye(128, dtype=f)
    return m


_PROGS = {}


def _prog(name, builder, *a):
    key = (name,) + a
    if key not in _PROGS:
        _PROGS[key] = builder(*a)
    return _PROGS[key]


def _run(p, maps):
    return p.run(maps).results


def kernel(**inp):
    inp = {k: np.asarray(v) for k, v in inp.items()}
    x = np.ascontiguousarray(inp["x"][0]).astype(np.float32)
    Tt = x.shape[0]
    Tc = Tt // NCORE
    TB = min(512, Tt)
    f = np.float32
    posT = np.ascontiguousarray(np.broadcast_to(inp["positions"][0][None, :], (128, Tt))).astype(np.int32)
    rep = lambda v_: np.ascontiguousarray(np.broadcast_to(v_[None, :], (128, v_.shape[0]))).astype(f)
    sl = lambda c: slice(c * Tc, (c + 1) * Tc)
    hs = lambda c: slice(128 * c, 128 * c + 128)
    eye = np.eye(128, dtype=f)
    flat = np.concatenate([np.concatenate([inp["ffn_up"][l].ravel(), inp["ffn_down"][l].ravel()])
                           for l in range(DEPTH)])
    Lp = flat.size // (NCORE * 128)
    rp = _run(build_PREP(Lp), [{"w": np.ascontiguousarray(flat[c * 128 * Lp:(c + 1) * 128 * Lp].reshape(128, Lp))}
                               for c in range(NCORE)])
    flat_b = np.concatenate([rp[c]["wb"].reshape(-1) for c in range(NCORE)])
    del flat
    nup, ndn = D * 2 * D_FF, D_FF * D
    wupT, wdnT = [], []
    for l in range(DEPTH):
        o_ = l * (nup + ndn)
        up = flat_b[o_:o_ + nup].reshape(16, 128, 2, D_FF // 256, 2, 128)
        wupT.append(np.ascontiguousarray(up.transpose(3, 1, 0, 4, 2, 5)).reshape(D_FF // 256, 128, 8192))
        dn = flat_b[o_ + nup:o_ + nup + ndn].reshape(D_FF // 256, 2, 128, D)
        wdnT.append(np.ascontiguousarray(dn.transpose(0, 2, 1, 3)).reshape(D_FF // 256, 128, 2 * D))
    for l in range(DEPTH):
        pa = build_A(Tc)
        ra = _run(pa, [{"xT": np.ascontiguousarray(x[sl(c)].T), "w": inp["w_in"][l]} for c in range(NCORE)])
        cat_fm = lambda nm: np.concatenate([ra[c][nm] for c in range(NCORE)], axis=1)
        cat_tm = lambda nm: np.concatenate([ra[c][nm] for c in range(NCORE)], axis=0)
        uT, qT, kT, zT = cat_fm("uT"), cat_fm("qT"), cat_fm("kT"), cat_fm("zT")
        v, sg = cat_tm("v"), cat_tm("sg")
        gates = [ra[c]["gates"] for c in range(NCORE)]
        r1 = _run(build_B1(Tt), [host_B1_inputs(inp, l, c, np.ascontiguousarray(uT[hs(c)]), TB) for c in range(NCORE)])
        zs = np.concatenate([r1[c]["zT"] for c in range(NCORE)], axis=0)
        r2 = _run(build_B2(Tt), [host_B2_inputs(inp, l, c, np.ascontiguousarray(qT[hs(c)]), np.ascontiguousarray(kT[hs(c)]),
                                                np.ascontiguousarray(v[:, hs(c)]), np.ascontiguousarray(sg[:, hs(c)]), posT)
                                 for c in range(NCORE)])
        orT = np.ascontiguousarray(np.concatenate([r2[c]["o"] for c in range(NCORE)], axis=1).T)
        r3 = _run(build_B3(Tt), [host_B3_inputs(inp, l, c, zT, TB) for c in range(NCORE)])
        owT = np.ascontiguousarray(np.concatenate([r3[c]["o"] for c in range(NCORE)], axis=1).T)
        rc1 = _run(build_C1(Tc), [{
            "zT": np.ascontiguousarray(zs[:, sl(c)]), "orT": np.ascontiguousarray(orT[:, sl(c)]),
            "owT": np.ascontiguousarray(owT[:, sl(c)]), "gates": gates[c], "x": np.ascontiguousarray(x[sl(c)]),
            "wglu": inp["ssm_glu"][l], "wret": inp["ret_out"][l], "wrw": inp["rwkv_out"][l], "wo": inp["w_o"][l],
            "lng": rep(inp["ln1_g"][l]), "lnb": rep(inp["ln1_b"][l]), "identf": eye} for c in range(NCORE)])
        x1 = np.concatenate([rc1[c]["x1"] for c in range(NCORE)], axis=0)
        x1p = np.concatenate([np.zeros((2, D), f), x1], axis=0)
        wcv = np.ascontiguousarray(inp["ffn_conv"][l].T.reshape(88, 128, 3).transpose(1, 0, 2)).astype(f)
        rc2 = _run(build_C2(Tc), [{
            "x1": np.ascontiguousarray(x1[sl(c)]), "x1T": np.ascontiguousarray(x1p[c * Tc:(c + 1) * Tc + 2].T),
            "wupT": wupT[l], "wcv": wcv, "wdnT": wdnT[l],
            "lng": rep(inp["ln2_g"][l]), "lnb": rep(inp["ln2_b"][l])} for c in range(NCORE)])
        x = np.concatenate([rc2[c]["x2"] for c in range(NCORE)], axis=0)
    return x[None].astype(np.float32)
```

```python
import math
import numpy as np
import ml_dtypes
import concourse.bass as bass
import concourse.mybir as mybir
from concourse.bass_utils import run_bass_kernel_spmd

F32 = mybir.dt.float32
BF16 = mybir.dt.bfloat16
I32 = mybir.dt.int32
AF = mybir.ActivationFunctionType
ALU = mybir.AluOpType
AX = mybir.AxisListType
NPBF = ml_dtypes.bfloat16

D = 2048
NCORE = 8
DEPTH = 4
N_IN = 14592
D_FF = 5632
ALPHA = (2.0 * DEPTH) ** 0.25
EXPM05 = math.exp(-0.5)
TWO_PI = 2.0 * math.pi
DEBUG = False
SUB = 9
KNOB = 0
STQ = "act"


class T:
    __slots__ = ("ap", "name", "w", "r", "psum")

    def __init__(self, ap, name, psum=False):
        self.ap = ap
        self.name = name
        self.w = None
        self.r = []
        self.psum = psum

    def __getitem__(self, idx):
        return V(self, self.ap[idx])


class V:
    __slots__ = ("t", "ap")

    def __init__(self, t, ap):
        self.t = t
        self.ap = ap

    def __getitem__(self, idx):
        return V(self.t, self.ap[idx])

    def re(self, pat, **kw):
        return V(self.t, self.ap.rearrange(pat, **kw))

    def bc(self, shape):
        return V(self.t, self.ap.to_broadcast(list(shape)))


def _tv(x):
    if isinstance(x, T):
        return x, x.ap
    if isinstance(x, V):
        return x.t, x.ap
    return None, x


class P:
    NRING = 8

    def __init__(self):
        self.nc = bass.Bass("TRN2", target_bir_lowering=False)
        nc = self.nc
        self.eng = {"pe": nc.tensor, "dve": nc.vector, "act": nc.scalar,
                    "pool": nc.gpsimd, "sp": nc.sync}
        self.sems = {}
        self.cnt = {}
        self.seen = {e: {} for e in self.eng}
        self._ctx = []
        for e in self.eng:
            self.sems[e] = self._sem("s_" + e)
            self.cnt[e] = 0
        self.ring = {}
        self.ringn = {}
        for q in ("sp", "pool", "act"):
            self.ring[q] = [self._sem(f"d_{q}{i}") for i in range(self.NRING)]
            self.ringn[q] = 0
        self.n_inst = 0
        self._rr = 0

    def _sem(self, name):
        cm = self.nc.semaphore(name)
        s = cm.__enter__()
        self._ctx.append(cm)
        return s

    def sb(self, name, shape, dt=F32):
        cm = self.nc.sbuf_tensor(name, list(shape), dt)
        t = cm.__enter__()
        self._ctx.append(cm)
        return T(t[:], name)

    def ps(self, name, shape, dt=F32):
        cm = self.nc.psum_tensor(name, list(shape), dt)
        t = cm.__enter__()
        self._ctx.append(cm)
        return T(t[:], name, psum=True)

    def dram(self, name, shape, dt=F32, kind="ExternalInput"):
        t = self.nc.dram_tensor(name, list(shape), dt, kind=kind)
        return T(t.ap(), name)

    def _semobj(self, key):
        if isinstance(key, tuple):
            return self.ring[key[0]][key[1]]
        return self.sems[key]

    def _deps(self, e, reads, writes):
        need = {}

        def add(tok):
            k, v = tok[0], tok[1]
            if need.get(k, 0) < v:
                need[k] = v

        for t in reads:
            if t is None or t.w is None:
                continue
            if t.w[0] == e and e == "pe":
                continue
            add(t.w)
        for t in reads:
            if t is not None and t.psum:
                for rd in t.r:
                    if rd[0] != e:
                        add(rd)
        for t in writes:
            if t is None:
                continue
            for rd in t.r:
                if rd[0] == e:
                    continue
                add(rd)
            if t.w is not None and t.w[0] != e:
                add(t.w)
        eng = self.eng[e]
        seen = self.seen[e]
        for k, v in need.items():
            if seen.get(k, 0) < v:
                eng.wait_ge(self._semobj(k), v)
                seen[k] = v

    def _done(self, tok, reads, writes):
        for t in reads:
            if t is not None:
                t.r.append(tok)
                if len(t.r) > 24:
                    best = {}
                    for rd in t.r:
                        if best.get(rd[0], 0) < rd[1]:
                            best[rd[0]] = rd[1]
                    t.r = [(k, v) for k, v in best.items()]
        for t in writes:
            if t is not None:
                t.w = tok
                t.r = []

    def op(self, e, fn, reads, writes):
        self._deps(e, reads, writes)
        ins = fn()
        self.cnt[e] += 1
        ins.then_inc(self.sems[e], 1)
        self._done((e, self.cnt[e]), reads, writes)
        self.n_inst += 1
        return ins

    def dma(self, out, in_, q="sp", **kw):
        to, apo = _tv(out)
        ti, api = _tv(in_)
        n = self.ringn[q]
        slot = n % self.NRING
        rnd = n // self.NRING
        key = (q, slot)
        eng = self.eng[q]
        if rnd > 0 and self.seen[q].get(key, 0) < 16 * rnd:
            eng.wait_ge(self.ring[q][slot], 16 * rnd)
            self.seen[q][key] = 16 * rnd
        self._deps(q, [ti], [to])
        ins = eng.dma_start(out=apo, in_=api, **kw)
        ins.then_inc(self.ring[q][slot], 16)
        self.ringn[q] = n + 1
        self._done((key, 16 * (rnd + 1)), [ti], [to])
        self.n_inst += 1
        return ins

    def mm(self, out, lhsT, rhs, start=True, stop=True, **kw):
        to, apo = _tv(out)
        tl, apl = _tv(lhsT)
        tr, apr = _tv(rhs)
        return self.op("pe", lambda: self.nc.tensor.matmul(apo, apl, apr, start=start, stop=stop, **kw),
                       [tl, tr], [to])

    def tr(self, out, in_, ident):
        to, apo = _tv(out)
        ti, api = _tv(in_)
        td, apd = _tv(ident)
        return self.op("pe", lambda: self.nc.tensor.transpose(apo, api, apd), [ti, td], [to])

    def act(self, out, in_, func, bias=None, scale=None):
        to, apo = _tv(out)
        ti, api = _tv(in_)
        rd = [ti]
        k = {}
        if bias is not None:
            tb, apb = _tv(bias)
            rd.append(tb)
            k["bias"] = apb
        if scale is not None:
            ts_, aps = _tv(scale)
            rd.append(ts_)
            k["scale"] = aps
        return self.op("act", lambda: self.nc.scalar.activation(apo, api, func, **k), rd, [to])

    def tt(self, out, a, b, op, e="dve"):
        to, apo = _tv(out)
        ta, apa = _tv(a)
        tb, apb = _tv(b)
        eng = self.eng[e]
        return self.op(e, lambda: eng.tensor_tensor(apo, apa, apb, op), [ta, tb], [to])

    def ts(self, out, a, s1, op0, s2=None, op1=None, e="dve"):
        to, apo = _tv(out)
        ta, apa = _tv(a)
        t1, ap1 = _tv(s1)
        t2, ap2 = _tv(s2)
        eng = self.eng[e]
        if op1 is None:
            return self.op(e, lambda: eng.tensor_scalar(apo, apa, ap1, None, op0), [ta, t1], [to])
        return self.op(e, lambda: eng.tensor_scalar(apo, apa, ap1, ap2, op0, op1), [ta, t1, t2], [to])

    def stt(self, out, a, s, b, op0, op1):
        to, apo = _tv(out)
        ta, apa = _tv(a)
        ts_, aps = _tv(s)
        tb, apb = _tv(b)
        return self.op("dve", lambda: self.nc.vector.scalar_tensor_tensor(apo, apa, aps, apb, op0, op1),
                       [ta, ts_, tb], [to])

    def copy(self, out, in_, e="dve"):
        to, apo = _tv(out)
        ti, api = _tv(in_)
        if e == "act":
            return self.op("act", lambda: self.nc.scalar.copy(apo, api), [ti], [to])
        eng = self.eng[e]
        return self.op(e, lambda: eng.tensor_copy(apo, api), [ti], [to])

    def memset(self, out, val, e="dve"):
        to, apo = _tv(out)
        eng = self.eng[e]
        return self.op(e, lambda: eng.memset(apo, val), [], [to])

    def scan(self, out, d0, d1, init, op0, op1):
        to, apo = _tv(out)
        t0, ap0 = _tv(d0)
        t1, ap1 = _tv(d1)
        ti, api = _tv(init)
        return self.op("dve", lambda: self.nc.vector.tensor_tensor_scan(apo, ap0, ap1, api, op0, op1),
                       [t0, t1, ti], [to])

    def recip(self, out, in_):
        to, apo = _tv(out)
        ti, api = _tv(in_)
        return self.op("dve", lambda: self.nc.vector.reciprocal(apo, api), [ti], [to])

    def reduce(self, out, in_, op, axis=AX.X):
        to, apo = _tv(out)
        ti, api = _tv(in_)
        return self.op("dve", lambda: self.nc.vector.tensor_reduce(apo, api, axis, op), [ti], [to])

    def bn_stats(self, out, in_):
        to, apo = _tv(out)
        ti, api = _tv(in_)
        return self.op("dve", lambda: self.nc.vector.bn_stats(apo, api), [ti], [to])

    def bn_aggr(self, out, in_):
        to, apo = _tv(out)
        ti, api = _tv(in_)
        return self.op("dve", lambda: self.nc.vector.bn_aggr(apo, api), [ti], [to])

    def finish(self):
        sp = self.nc.sync
        for e in self.eng:
            if e != "sp" and self.cnt[e] > 0 and self.seen["sp"].get(e, 0) < self.cnt[e]:
                sp.wait_ge(self.sems[e], self.cnt[e])
                self.seen["sp"][e] = self.cnt[e]
        for q in self.ring:
            n = self.ringn[q]
            for slot in range(min(n, self.NRING)):
                last_rnd = (n - 1 - slot) // self.NRING
                v = 16 * (last_rnd + 1)
                key = (q, slot)
                if self.seen["sp"].get(key, 0) < v:
                    sp.wait_ge(self.ring[q][slot], v)
                    self.seen["sp"][key] = v

    def run(self, in_maps):
        self.finish()
        return run_bass_kernel_spmd(self.nc, in_maps, core_ids=list(range(NCORE)))


class Rot:
    def __init__(self, items):
        self.items = items
        self.i = 0

    def next(self):
        x = self.items[self.i % len(self.items)]
        self.i += 1
        return x


def sincos(p, ang, sin_out, cos_out, tmp_f, tmp_i, tmp_k):
    C1 = 6.28125
    C2 = TWO_PI - 6.28125
    p.ts(tmp_f, ang, 1.0 / TWO_PI, ALU.mult)
    p.copy(tmp_i, tmp_f)
    p.copy(tmp_k, tmp_i)
    p.stt(tmp_f, tmp_k, -C1, ang, ALU.mult, ALU.add)
    p.stt(tmp_f, tmp_k, -C2, tmp_f, ALU.mult, ALU.add)
    p.ts(tmp_k, tmp_f, math.pi, ALU.is_gt, -TWO_PI, ALU.mult)
    p.tt(tmp_k, tmp_k, tmp_f, ALU.add)
    p.act(sin_out, tmp_k, AF.Sin)
    p.ts(tmp_f, tmp_f, math.pi / 2, ALU.add)
    p.ts(tmp_k, tmp_f, math.pi, ALU.is_gt, -TWO_PI, ALU.mult)
    p.tt(tmp_k, tmp_k, tmp_f, ALU.add)
    p.act(cos_out, tmp_k, AF.Sin)


A_FAMS = [
    ("uT", 0, 1024, "FM", None),
    ("qT", 1024, 1024, "FM", None),
    ("kT", 2048, 1024, "FM", None),
    ("v", 3072, 1024, "TM", None),
    ("sg", 4096, 1024, "TM", AF.Silu),
    ("zT", 5120, 3328, "FM", None),
    ("gates", 8448, 6144, "TM", AF.Sigmoid),
]


def load_cast_T(p, src_ap_fn, dst, nkc, Tc, stages):
    for kc in range(nkc):
        st = stages.next()
        p.dma(st, src_ap_fn(kc))
        p.copy(dst[:, kc, :], st, e=("dve" if kc % 2 == 0 else "pool"))


def build_A(Tc):
    p = P()
    KC = 16
    xT = p.dram("xT", [D, Tc])
    w = p.dram("w", [D, N_IN])
    outs = {}
    for name, c0, wd, mode, fn in A_FAMS:
        shp = [wd, Tc] if mode == "FM" else [Tc, wd]
        outs[name] = p.dram(name, shp, kind="ExternalOutput")
    xb = p.sb("xb", [128, KC, Tc], BF16)
    xst = Rot([p.sb(f"xst{i}", [128, Tc]) for i in range(2)])
    wst = [[p.sb(f"wst{b}{h}", [128, 8, 512]) for h in range(2)] for b in range(2)]
    wb = [[p.sb(f"wb{b}{h}", [128, 8, 512], BF16) for h in range(2)] for b in range(2)]
    pss = Rot([p.ps(f"ps{i}", [128, 512]) for i in range(4)])
    ost = Rot([p.sb(f"ost{i}", [128, 512]) for i in range(4)])
    xTv = xT.ap.rearrange("(kc p) t -> p kc t", p=128)
    load_cast_T(p, lambda kc: V(xT, xTv[:, kc, :]), xb, KC, Tc, xst)
    wv = w.ap.rearrange("(kc p) n -> p kc n", p=128)
    chunks = []
    for name, c0, wd, mode, fn in A_FAMS:
        o = 0
        while o < wd:
            cw = min(512, wd - o)
            chunks.append((name, c0 + o, o, cw, mode, fn))
            o += cw

    def load_w(j):
        name, c, o, cw, mode, fn = chunks[j]
        b = j % 2
        for h in range(2):
            p.dma(wst[b][h][:, :, 0:cw], V(w, wv[:, 8 * h:8 * h + 8, c:c + cw]))

    def cast_w(j):
        name, c, o, cw, mode, fn = chunks[j]
        b = j % 2
        p.copy(wb[b][0][:, :, 0:cw], wst[b][0][:, :, 0:cw], e="dve")
        p.copy(wb[b][1][:, :, 0:cw], wst[b][1][:, :, 0:cw], e="pool")

    TH = min(512, Tc)
    load_w(0)
    ev = 0
    for j in range(len(chunks)):
        name, c, o, cw, mode, fn = chunks[j]
        b = j % 2
        if j + 1 < len(chunks):
            load_w(j + 1)
        cast_w(j)
        od = outs[name]
        if mode == "FM":
            for sub in range(cw // 128):
                for th in range(Tc // TH):
                    ps = pss.next()
                    for kc in range(KC):
                        p.mm(ps[:, 0:TH], wb[b][kc // 8][:, kc % 8, sub * 128:(sub + 1) * 128],
                             xb[:, kc, th * TH:(th + 1) * TH], start=(kc == 0), stop=(kc == KC - 1))
                    os_ = ost.next()
                    if ev % 2 == 0:
                        p.copy(os_[:, 0:TH], ps[:, 0:TH], e="dve")
                    else:
                        p.copy(os_[:, 0:TH], ps[:, 0:TH], e="act")
                    ev += 1
                    p.dma(od[o + sub * 128:o + (sub + 1) * 128, th * TH:(th + 1) * TH], os_[:, 0:TH], q=STQ)
        else:
            for tt_ in range(Tc // 128):
                ps = pss.next()
                for kc in range(KC):
                    p.mm(ps[:, 0:cw], xb[:, kc, tt_ * 128:(tt_ + 1) * 128],
                         wb[b][kc // 8][:, kc % 8, 0:cw], start=(kc == 0), stop=(kc == KC - 1))
                os_ = ost.next()
                if fn is not None:
                    p.act(os_[:, 0:cw], ps[:, 0:cw], fn)
                else:
                    p.copy(os_[:, 0:cw], ps[:, 0:cw], e=("dve" if ev % 2 == 0 else "act"))
                    ev += 1
                p.dma(od[tt_ * 128:(tt_ + 1) * 128, o:o + cw], os_[:, 0:cw], q=STQ)
    return p


def layer_norm_tile(p, pre, out_sb, lng, lnb, scr, eps=1e-5):
    st, mv, rs = scr["st"], scr["mv"], scr["rs"]
    for c in range(4):
        p.bn_stats(st[:, c, :], pre[:, c * 512:(c + 1) * 512])
    p.bn_aggr(mv, V(st, st.ap.rearrange("p a b -> p (a b)")))
    p.ts(rs, mv[:, 1:2], eps, ALU.add)
    p.act(rs, rs, AF.Sqrt)
    p.recip(rs, rs)
    p.ts(out_sb, pre, mv[:, 0:1], ALU.subtract, rs[:, 0:1], ALU.mult)
    p.tt(out_sb, out_sb, lng, ALU.mult, e="pool")
    p.tt(out_sb, out_sb, lnb, ALU.add, e="pool")


def ln_scratch(p, tag):
    return {"st": p.sb("ln_st" + tag, [128, 4, 6]), "mv": p.sb("ln_mv" + tag, [128, 2]),
            "rs": p.sb("ln_rs" + tag, [128, 1])}


def build_C1(Tc):
    p = P()
    NT = Tc // 128
    zT = p.dram("zT", [1024, Tc], BF16)
    orT = p.dram("orT", [1024, Tc], BF16)
    owT = p.dram("owT", [1024, Tc], BF16)
    gates = p.dram("gates", [Tc, 6144])
    x = p.dram("x", [Tc, D])
    wglu = p.dram("wglu", [1024, 4096])
    wret = p.dram("wret", [1024, D])
    wrw = p.dram("wrw", [1024, D])
    wo = p.dram("wo", [D, D])
    lng_d = p.dram("lng", [128, D])
    lnb_d = p.dram("lnb", [128, D])
    x1 = p.dram("x1", [Tc, D], kind="ExternalOutput")

    acts = {}
    actbuf = p.sb("actbuf", [128, 24, Tc], BF16)
    for i_, (nm, src) in enumerate((("z", zT), ("or", orT), ("ow", owT))):
        t = actbuf[:, 8 * i_:8 * i_ + 8, :]
        p.dma(t, V(src, src.ap.rearrange("(kc p) t -> p kc t", p=128)))
        acts[nm] = t
    lng = p.sb("lng_s", [128, D])
    lnb = p.sb("lnb_s", [128, D])
    p.dma(lng, lng_d)
    p.dma(lnb, lnb_d)
    merged = p.sb("merged", [128, NT, D])
    mTbuf = p.sb("mTbuf", [128, 16, Tc], BF16) if Tc < 512 else None
    mT = mTbuf if mTbuf is not None else actbuf
    wst = Rot([p.sb(f"wst{i}", [128, 8, 512]) for i in range(2)])
    wbB = Rot([p.sb(f"wbB{i}", [128, 8, 512], BF16) for i in range(3)])
    gst = Rot([p.sb(f"gst{i}", [128, 512]) for i in range(2)])
    tmp = Rot([p.sb(f"tmp{i}", [128, 512]) for i in range(2)])
    sig = Rot([p.sb(f"sig{i}", [128, 512]) for i in range(2)])
    pss = Rot([p.ps(f"ps{i}", [128, 512]) for i in range(6)])
    ident = p.sb("ident", [128, 128], BF16)
    idf = p.dram("identf", [128, 128])
    idst = p.sb("idst", [128, 128])
    p.dma(idst, idf)
    p.copy(ident, idst)
    ce = [0]

    def wload(dst, src_t, r0, nkc, c0, cw):
        sv = src_t.ap.rearrange("(kc p) n -> p kc n", p=128)
        for h in range(0, nkc, 8):
            st = wst.next()
            p.dma(st[:, :, 0:cw], V(src_t, sv[:, r0 + h:r0 + h + 8, c0:c0 + cw]))
            e = "dve" if ce[0] % 2 == 0 else "pool"
            ce[0] += 1
            p.copy(dst[:, h:h + 8, 0:cw], st[:, :, 0:cw], e=e)

    for cc in range(4):
        wa = wbB.next()
        wload(wa, wglu, 0, 8, cc * 512, 512)
        wb_ = wbB.next()
        wload(wb_, wglu, 0, 8, 2048 + cc * 512, 512)
        for t in range(NT):
            psa = pss.next()
            psb = pss.next()
            for kc in range(8):
                p.mm(psa, acts["z"][:, kc, t * 128:(t + 1) * 128], wa[:, kc, :], start=(kc == 0), stop=(kc == 7))
            for kc in range(8):
                p.mm(psb, acts["z"][:, kc, t * 128:(t + 1) * 128], wb_[:, kc, :], start=(kc == 0), stop=(kc == 7))
            g = gst.next()
            p.dma(g, gates[t * 128:(t + 1) * 128, cc * 512:(cc + 1) * 512])
            s = sig.next()
            p.act(s, psb, AF.Sigmoid)
            y = tmp.next()
            p.tt(y, psa, s, ALU.mult)
            p.tt(merged[:, t, cc * 512:(cc + 1) * 512], y, g, ALU.mult, e="pool")
    for bi, (nm, wsrc) in enumerate((("or", wret), ("ow", wrw))):
        for cc in range(4):
            wb_ = wbB.next()
            wload(wb_, wsrc, 0, 8, cc * 512, 512)
            for t in range(NT):
                ps = pss.next()
                for kc in range(8):
                    p.mm(ps, acts[nm][:, kc, t * 128:(t + 1) * 128], wb_[:, kc, :], start=(kc == 0), stop=(kc == 7))
                g = gst.next()
                p.dma(g, gates[t * 128:(t + 1) * 128, (bi + 1) * 2048 + cc * 512:(bi + 1) * 2048 + (cc + 1) * 512])
                y = tmp.next()
                p.tt(y, ps, g, ALU.mult)
                mv_ = merged[:, t, cc * 512:(cc + 1) * 512]
                p.tt(mv_, mv_, y, ALU.add, e="pool")
    mb = Rot([p.sb(f"mb{i}", [128, D], BF16) for i in range(1)])
    pst = Rot([p.ps(f"pst{i}", [128, 4, 128], BF16) for i in range(2)])
    for t in range(NT):
        m = mb.next()
        p.copy(m, merged[:, t, :], e="act")
        for q4 in range(4):
            pt = pst.next()
            for i in range(4):
                kc = q4 * 4 + i
                p.tr(pt[:, i, :], m[:, kc * 128:(kc + 1) * 128], ident)
            p.copy(mT[:, q4 * 4:(q4 + 1) * 4, t * 128:(t + 1) * 128], pt, e=("dve" if q4 % 2 == 0 else "act"))
    for cc in range(4):
        wa0 = wbB.next()
        wload(wa0, wo, 0, 8, cc * 512, 512)
        wa1 = wbB.next()
        wload(wa1, wo, 8, 8, cc * 512, 512)
        for t in range(NT):
            ps = pss.next()
            for kc in range(16):
                p.mm(ps, mT[:, kc, t * 128:(t + 1) * 128], (wa0 if kc < 8 else wa1)[:, kc % 8, :],
                     start=(kc == 0), stop=(kc == 15))
            g = gst.next()
            p.dma(g, x[t * 128:(t + 1) * 128, cc * 512:(cc + 1) * 512])
            p.stt(merged[:, t, cc * 512:(cc + 1) * 512], g, ALPHA, ps, ALU.mult, ALU.add)
    if DEBUG:
        dbg = p.dram("dbg", [Tc, D], kind="ExternalOutput")
        for t in range(NT):
            p.dma(dbg[t * 128:(t + 1) * 128, :], merged[:, t, :])
    scr = ln_scratch(p, "1")
    for t in range(NT):
        o = merged[:, t, :]
        layer_norm_tile(p, merged[:, t, :], o, lng, lnb, scr)
        p.dma(x1[t * 128:(t + 1) * 128, :], o)
    return p


def build_PREP(L):
    p = P()
    CW = 2048
    w = p.dram("w", [128, L])
    wb = p.dram("wb", [128, L], BF16, kind="ExternalOutput")
    st = Rot([p.sb(f"st{i}", [128, CW]) for i in range(3)])
    ob = Rot([p.sb(f"ob{i}", [128, CW], BF16) for i in range(3)])
    engs = ["dve", "pool", "act"]
    for i, c0 in enumerate(range(0, L, CW)):
        cw = min(CW, L - c0)
        s_ = st.next(); o_ = ob.next()
        p.dma(s_[:, 0:cw], w[:, c0:c0 + cw])
        p.copy(o_[:, 0:cw], s_[:, 0:cw], e=engs[i % 3])
        p.dma(wb[:, c0:c0 + cw], o_[:, 0:cw])
    return p


def build_C2(Tc):
    p = P()
    NT = Tc // 128
    BL = min(256, Tc)
    NB = Tc // BL
    NJ = D_FF // 128
    x1 = p.dram("x1", [Tc, D])
    x1T = p.dram("x1T", [D, Tc + 2])
    wup = p.dram("wupT", [D_FF // 256, 128, 16 * 2 * 2 * 128], BF16)
    wcv = p.dram("wcv", [128, 2 * NJ, 3])
    wdn = p.dram("wdnT", [D_FF // 256, 128, 2 * D], BF16)
    lng_d = p.dram("lng", [128, D])
    lnb_d = p.dram("lnb", [128, D])
    x2 = p.dram("x2", [Tc, D], kind="ExternalOutput")

    lng = p.sb("lng_s", [128, D])
    lnb = p.sb("lnb_s", [128, D])
    p.dma(lng, lng_d)
    p.dma(lnb, lnb_d)
    wc = p.sb("wc", [128, 2 * NJ, 3])
    p.dma(wc, wcv)
    acc = p.sb("acc", [128, NT, D])
    for t in range(NT):
        p.dma(acc[:, t, :], x1[t * 128:(t + 1) * 128, :])
    for t in range(NT):
        p.ts(acc[:, t, :], acc[:, t, :], ALPHA, ALU.mult, e=("dve" if t % 2 == 0 else "pool"))
    xb = p.sb("xb", [128, 16, Tc + 2], BF16)
    xst = Rot([p.sb(f"xst{i}", [128, Tc + 2]) for i in range(1)])
    xv = x1T.ap.rearrange("(kc p) t -> p kc t", p=128)
    load_cast_T(p, lambda kc: V(x1T, xv[:, kc, :]), xb, 16, Tc + 2, xst)

    JG = 2
    ub = Rot([p.sb(f"ub{i}", [128, 16, JG, 2, 128], BF16) for i in range(2)])
    db = Rot([p.sb(f"db{i}", [128, JG, D], BF16) for i in range(2)])
    gT = Rot([p.sb(f"gT{i}", [128, JG, Tc], BF16) for i in range(2)])
    ha = Rot([p.sb(f"ha{i}", [128, BL]) for i in range(2)])
    hb = Rot([p.sb(f"hb{i}", [128, BL]) for i in range(2)])
    sa = Rot([p.sb(f"sa{i}", [128, BL]) for i in range(2)])
    psu = Rot([p.ps(f"psu{i}", [128, 512]) for i in range(4)])
    psd = Rot([p.ps(f"psd{i}", [128, 512]) for i in range(4)])

    def load_group(jg):
        u = ub.next()
        dd = db.next()
        uv = u[:, :, :, :, :].re("p k j h c -> p (k j h c)")
        p.dma(uv[:, 0:4096], wup[jg, :, 0:4096])
        p.dma(uv[:, 4096:8192], wup[jg, :, 4096:8192])
        p.dma(dd[:, :, :].re("p j n -> p (j n)"), wdn[jg, :, :])
        return u, dd

    nxt = load_group(0)
    for jg in range(NJ // JG):
        u, dd = nxt
        if jg + 1 < NJ // JG:
            nxt = load_group(jg + 1)
        g = gT.next()
        for ji in range(JG):
            j = jg * JG + ji
            for blk in range(NB):
                hs = []
                for half in range(2):
                    ps = psu.next()
                    for kc in range(16):
                        p.mm(ps[:, 0:BL + 2], u[:, kc, ji, half, :], xb[:, kc, blk * BL:blk * BL + BL + 2],
                             start=(kc == 0), stop=(kc == 15))
                    h_ = (ha if half == 0 else hb).next()
                    cj = j + half * NJ
                    p.act(h_, ps[:, 2:BL + 2], AF.Copy, scale=wc[:, cj, 2:3])
                    p.stt(h_, ps[:, 1:BL + 1], wc[:, cj, 1:2], h_, ALU.mult, ALU.add)
                    p.stt(h_, ps[:, 0:BL], wc[:, cj, 0:1], h_, ALU.mult, ALU.add)
                    hs.append(h_)
                s = sa.next()
                p.act(s, hs[0], AF.Silu)
                p.tt(g[:, ji, blk * BL:(blk + 1) * BL], s, hs[1], ALU.mult, e="pool")
        for t in range(NT):
            for cc in range(4):
                ps = psd.next()
                for ji in range(JG):
                    p.mm(ps, g[:, ji, t * 128:(t + 1) * 128], dd[:, ji, cc * 512:(cc + 1) * 512],
                         start=(ji == 0), stop=(ji == JG - 1))
                a = acc[:, t, cc * 512:(cc + 1) * 512]
                p.tt(a, a, ps, ALU.add)
    scr = ln_scratch(p, "2")
    for t in range(NT):
        o = acc[:, t, :]
        layer_norm_tile(p, acc[:, t, :], o, lng, lnb, scr)
        p.dma(x2[t * 128:(t + 1) * 128, :], o)
    return p


def build_B1(Tt):
    p = P()
    TB = min(512, Tt)
    NBK = Tt // TB
    uT = p.dram("uT", [128, Tt])
    zT = p.dram("zT", [128, Tt], BF16, kind="ExternalOutput")
    small = {}
    for nm, shp in (("LRr", [128, 64]), ("LIr", [128, 64]), ("LSr", [128, 64]), ("BreT", [128, 64]),
                    ("BimT", [128, 64]), ("LRc", [128, 8]), ("LIc", [128, 8]), ("LSc", [128, 8]),
                    ("CC", [128, 128]), ("CCs", [128, 128]), ("dsk", [128, 1]), ("identf", [128, 128]),
                    ("S12", [128, 128]), ("sgnA", [128, 1]), ("maskg", [128, 8]), ("iota", [128, TB])):
        dt_ = p.dram(nm, shp)
        st = p.sb("s_" + nm, shp)
        p.dma(st, dt_)
        small[nm] = st
    S = small
    n = [0]

    def tmp(shape, dt=F32):
        n[0] += 1
        return p.sb(f"t{n[0]}", shape, dt)

    sh = [128, 64]
    stp = tmp(sh); ang = tmp(sh); sn = tmp(sh); cs = tmp(sh); tf = tmp(sh); ti = tmp(sh, I32); tk = tmp(sh)
    p.act(stp, S["LSr"], AF.Exp)
    p.tt(ang, S["LIr"], stp, ALU.mult)
    sincos(p, ang, sn, cs, tf, ti, tk)
    mag = tmp(sh)
    p.tt(mag, S["LRr"], stp, ALU.mult)
    p.act(mag, mag, AF.Exp)
    abr1 = tmp(sh); abi = tmp(sh)
    p.tt(abr1, mag, cs, ALU.mult)
    p.ts(abr1, abr1, -1.0, ALU.add)
    p.tt(abi, mag, sn, ALU.mult)
    den = tmp(sh); t2 = tmp(sh)
    p.tt(den, S["LRr"], S["LRr"], ALU.mult)
    p.tt(t2, S["LIr"], S["LIr"], ALU.mult)
    p.tt(den, den, t2, ALU.add)
    p.recip(den, den)
    fre = tmp(sh); fim = tmp(sh)
    p.tt(fre, abr1, S["LRr"], ALU.mult)
    p.tt(t2, abi, S["LIr"], ALU.mult)
    p.tt(fre, fre, t2, ALU.add)
    p.tt(fre, fre, den, ALU.mult)
    p.tt(fim, abi, S["LRr"], ALU.mult)
    p.tt(t2, abr1, S["LIr"], ALU.mult)
    p.tt(fim, fim, t2, ALU.subtract)
    p.tt(fim, fim, den, ALU.mult)
    bbr = tmp(sh); bbi = tmp(sh); nbbr = tmp(sh)
    p.tt(bbr, fre, S["BreT"], ALU.mult)
    p.tt(t2, fim, S["BimT"], ALU.mult)
    p.tt(bbr, bbr, t2, ALU.subtract)
    p.tt(bbi, fre, S["BimT"], ALU.mult)
    p.tt(t2, fim, S["BreT"], ALU.mult)
    p.tt(bbi, bbi, t2, ALU.add)
    p.ts(nbbr, bbr, -1.0, ALU.mult)
    BT = p.sb("BT", [128, 8, 128], BF16)
    BTs = p.sb("BTs", [128, 8, 128], BF16)
    for g in range(8):
        mg = S["maskg"][:, g:g + 1]
        p.ts(BT[:, g, 0:64], bbr, mg, ALU.mult)
        p.ts(BT[:, g, 64:128], bbi, mg, ALU.mult)
        p.ts(BTs[:, g, 0:64], bbi, mg, ALU.mult)
        p.ts(BTs[:, g, 64:128], nbbr, mg, ALU.mult)
    C1a = p.sb("C1a", [128, 8, 128], BF16)
    C2a = p.sb("C2a", [128, 8, 128], BF16)
    p.memset(C1a, 0.0)
    p.memset(C2a, 0.0)
    for g in range(8):
        p.ts(C1a[:, g, 16 * g:16 * g + 16], S["CC"][:, 16 * g:16 * g + 16], S["sgnA"][:, 0:1], ALU.mult)
        p.ts(C2a[:, g, 16 * g:16 * g + 16], S["CCs"][:, 16 * g:16 * g + 16], -1.0, ALU.mult)
    sh8 = [128, 8]
    stc = tmp(sh8); rho = tmp(sh8); theta = tmp(sh8)
    p.act(stc, S["LSc"], AF.Exp)
    p.tt(rho, S["LRc"], stc, ALU.mult)
    p.act(rho, rho, AF.Exp)
    p.tt(theta, S["LIc"], stc, ALU.mult)
    phi = tmp(sh8); sph = tmp(sh8); cph = tmp(sh8); f8 = tmp(sh8); i8 = tmp(sh8, I32); k8 = tmp(sh8)
    p.ts(phi, theta, float(TB), ALU.mult)
    sincos(p, phi, sph, cph, f8, i8, k8)
    p.ts(sph, sph, S["sgnA"][:, 0:1], ALU.mult)
    ROT = p.sb("ROT", [128, 8, 128])
    for g in range(8):
        p.ts(ROT[:, g, :], S["identf"], cph[:, g:g + 1], ALU.mult)
        p.stt(ROT[:, g, :], S["S12"], sph[:, g:g + 1], ROT[:, g, :], ALU.mult, ALU.add)
    SIN = p.sb("SIN", [128, 8, TB])
    COS = p.sb("COS", [128, 8, TB])
    RHO = p.sb("RHO", [128, 8, TB])
    shb = [128, TB]
    angb = tmp(shb); fb = tmp(shb); ib = tmp(shb, I32); kb = tmp(shb)
    for g in range(8):
        p.ts(angb, S["iota"], theta[:, g:g + 1], ALU.mult)
        sincos(p, angb, SIN[:, g, :], COS[:, g, :], fb, ib, kb)
        p.ts(RHO[:, g, :], S["iota"], 0.0, ALU.mult, rho[:, g:g + 1], ALU.add)

    ust = Rot([p.sb(f"ust{i}", [128, TB]) for i in range(2)])
    ubf = Rot([p.sb(f"ubf{i}", [128, TB], BF16) for i in range(2)])
    psa = Rot([p.ps(f"psa{i}", [128, TB]) for i in range(2)])
    psb = Rot([p.ps(f"psb{i}", [128, TB]) for i in range(2)])
    psy = Rot([p.ps(f"psy{i}", [128, TB]) for i in range(2)])
    psr = p.ps("psr", [128, 8])
    v1 = Rot([p.sb(f"v1{i}", [128, TB]) for i in range(2)])
    v2 = Rot([p.sb(f"v2{i}", [128, TB]) for i in range(2)])
    vv = Rot([p.sb(f"vv{i}", [128, TB]) for i in range(2)])
    shat = Rot([p.sb(f"shat{i}", [128, TB]) for i in range(3)])
    w1 = Rot([p.sb(f"w1{i}", [128, TB], BF16) for i in range(2)])
    w2 = Rot([p.sb(f"w2{i}", [128, TB], BF16) for i in range(2)])
    last = p.sb("last", [128, 8])
    init = p.sb("init", [128, 8])
    yb = Rot([p.sb(f"yb{i}", [128, TB]) for i in range(2)])
    gt = Rot([p.sb(f"gt{i}", [128, TB]) for i in range(2)])
    zo = Rot([p.sb(f"zo{i}", [128, TB], BF16) for i in range(2)])
    vv3 = Rot([p.sb(f"vv3{i}", [128, TB]) for i in range(3)])
    w13 = Rot([p.sb(f"w13{i}", [128, TB], BF16) for i in range(3)])
    w23 = Rot([p.sb(f"w23{i}", [128, TB], BF16) for i in range(3)])
    items = [(b, g) for b in range(NBK) for g in range(8)]
    blk = {}
    stA = {}
    stB = {}

    def block_begin(b):
        uf = ust.next()
        p.dma(uf, uT[:, b * TB:(b + 1) * TB])
        ub = ubf.next()
        p.copy(ub, uf, e="act")
        blk[b] = {"uf": uf, "ub": ub, "py": psy.next()}

    def stage_A(b, g):
        if g == 0:
            block_begin(b)
        ub = blk[b]["ub"]
        pa = psa.next()
        pb = psb.next()
        p.mm(pa, BT[:, g, :], ub)
        p.mm(pb, BTs[:, g, :], ub)
        a1 = v1.next(); a2 = v2.next(); av = vv3.next()
        p.tt(a1, pa, COS[:, g, :], ALU.mult)
        p.tt(a2, pb, SIN[:, g, :], ALU.mult)
        p.tt(av, a1, a2, ALU.add, e="pool")
        stA[(b, g)] = av

    def stage_B(b, g):
        av = stA.pop((b, g))
        if b > 0 and g == 0:
            for g_ in range(8):
                p.mm(psr[:, g_:g_ + 1], ROT[:, g_, :], last[:, g_:g_ + 1])
            p.copy(init, psr)
        sh_ = shat.next()
        p.scan(sh_, RHO[:, g, :], av, (0.0 if b == 0 else init[:, g:g + 1]), ALU.mult, ALU.add)
        p.copy(last[:, g:g + 1], sh_[:, TB - 1:TB], e="act")
        b1 = w13.next(); b2 = w23.next()
        p.tt(b1, sh_, COS[:, g, :], ALU.mult, e="pool")
        p.tt(b2, sh_, SIN[:, g, :], ALU.mult)
        stB[(b, g)] = (b1, b2)

    def stage_C(b, g):
        b1, b2 = stB.pop((b, g))
        py = blk[b]["py"]
        p.mm(py, C1a[:, g, :], b1, start=(g == 0), stop=False)
        p.mm(py, C2a[:, g, :], b2, start=False, stop=(g == 7))
        if g == 7:
            uf = blk[b]["uf"]
            y = yb.next()
            p.stt(y, uf, S["dsk"][:, 0:1], py, ALU.mult, ALU.add)
            t_ = gt.next()
            p.tt(t_, y, y, ALU.mult, e="pool")
            p.ts(t_, t_, 0.044715, ALU.mult, 1.0, ALU.add, e="pool")
            p.tt(t_, t_, y, ALU.mult, e="pool")
            p.act(t_, t_, AF.Sigmoid, scale=1.5957691216057308)
            z = zo.next()
            p.tt(z, y, t_, ALU.mult)
            p.dma(zT[:, b * TB:(b + 1) * TB], z, q="act")
            del blk[b]

    n_it = len(items)
    for i in range(-1, n_it + 1):
        if 0 <= i + 1 < n_it:
            stage_A(*items[i + 1])
        if 0 <= i < n_it:
            stage_B(*items[i])
        if 0 <= i - 1 < n_it:
            stage_C(*items[i - 1])
    return p


def host_B1_inputs(inp, l, c, uT_c, TB):
    gs = slice(8 * c, 8 * c + 8)
    lr = inp["ssm_lambda_re"][l][gs]; li = inp["ssm_lambda_im"][l][gs]; ls = inp["ssm_log_step"][l][gs]
    bre = inp["ssm_b_re"][l][gs]; bim = inp["ssm_b_im"][l][gs]
    cre = inp["ssm_c_re"][l][gs]; cim = inp["ssm_c_im"][l][gs]
    f = np.float32
    rep16 = lambda a: np.ascontiguousarray(np.repeat(a, 16, axis=0)).astype(f)
    m = {"uT": uT_c}
    m["LRr"] = rep16(lr); m["LIr"] = rep16(li)
    m["LSr"] = rep16(np.broadcast_to(ls[:, None], (8, 64)))
    m["BreT"] = np.ascontiguousarray(bre.transpose(0, 2, 1).reshape(128, 64)).astype(f)
    m["BimT"] = np.ascontiguousarray(bim.transpose(0, 2, 1).reshape(128, 64)).astype(f)
    m["LRc"] = np.ascontiguousarray(np.concatenate([lr.T, lr.T], 0)).astype(f)
    m["LIc"] = np.ascontiguousarray(np.concatenate([li.T, li.T], 0)).astype(f)
    m["LSc"] = np.ascontiguousarray(np.broadcast_to(ls[None, :], (128, 8))).astype(f)
    creT = cre.transpose(2, 0, 1).reshape(64, 128); cimT = cim.transpose(2, 0, 1).reshape(64, 128)
    m["CC"] = np.ascontiguousarray(np.concatenate([creT, cimT], 0)).astype(f)
    m["CCs"] = np.ascontiguousarray(np.concatenate([cimT, creT], 0)).astype(f)
    m["dsk"] = np.ascontiguousarray(inp["ssm_d"][l][128 * c:128 * c + 128].reshape(128, 1)).astype(f)
    m["identf"] = np.eye(128, dtype=f)
    s12 = np.zeros((128, 128), f)
    for k in range(64):
        s12[k, k + 64] = 1.0
        s12[k + 64, k] = 1.0
    m["S12"] = s12
    sg = np.ones((128, 1), f); sg[64:] = -1.0
    m["sgnA"] = sg
    mg = np.zeros((128, 8), f)
    for g in range(8):
        mg[16 * g:16 * g + 16, g] = 1.0
    m["maskg"] = mg
    m["iota"] = np.ascontiguousarray(np.broadcast_to(np.arange(TB, dtype=f)[None, :], (128, TB)))
    return m


def build_B2(Tt):
    p = P()
    TB = min(512, Tt)
    NBK = Tt // TB
    NCH = TB // 128
    qT = p.dram("qT", [128, Tt])
    kT = p.dram("kT", [128, Tt])
    vv = p.dram("v", [Tt, 128])
    sg = p.dram("sg", [Tt, 128])
    pos = p.dram("pos", [128, Tt], I32)
    o_out = p.dram("o", [Tt, 128], BF16, kind="ExternalOutput")
    S = {}
    for nm, shp in (("invf", [128, 1]), ("PERM", [128, 128]), ("DmT", [128, 128]), ("qd", [128, 1]),
                    ("kd", [128, 1]), ("cdec", [128, 1]), ("NG", [128, 128]), ("NB", [128, 128]),
                    ("identf", [128, 128])):
        dt_ = p.dram(nm, shp)
        st = p.sb("s_" + nm, shp)
        p.dma(st, dt_)
        S[nm] = st
    identb = p.sb("identb", [128, 128], BF16)
    p.copy(identb, S["identf"])
    state = p.sb("state", [128, 128])
    state_bf = p.sb("state_bf", [128, 128], BF16)
    p.memset(state, 0.0)
    p.memset(state_bf, 0.0)
    shb = [128, TB]
    qf = Rot([p.sb(f"qf{i}", shb) for i in range(2)])
    kf = Rot([p.sb(f"kf{i}", shb) for i in range(2)])
    posi = p.sb("posi", shb, I32)
    ang = p.sb("ang", shb); fb = p.sb("fb", shb); ib = p.sb("ib", shb, I32); kb = p.sb("kb", shb)
    SINt = p.sb("SINt", shb); COSt = p.sb("COSt", shb)
    r1 = p.sb("r1", shb); r2 = p.sb("r2", shb)
    rqT = Rot([p.sb(f"rqT{i}", shb, BF16) for i in range(2)])
    rkT = Rot([p.sb(f"rkT{i}", shb, BF16) for i in range(2)])
    pP = Rot([p.ps(f"pP{i}", shb) for i in range(1)])
    psc = p.ps("psc", [128, 128]); po2 = p.ps("po2", [128, 128])
    po1 = Rot([p.ps(f"po1{i}", [128, 128]) for i in range(2)])
    ptr = p.ps("ptr", [128, 128], BF16)
    pkv = Rot([p.ps(f"pkv{i}", [128, 128]) for i in range(2)])
    vf = Rot([p.sb(f"vf{i}", [128, 128]) for i in range(2)])
    vb = Rot([p.sb(f"vb{i}", [128, 128], BF16) for i in range(2)])
    sgf = Rot([p.sb(f"sgf{i}", [128, 128]) for i in range(4)])
    scm = Rot([p.sb(f"scm{i}", [128, 128], BF16) for i in range(2)])
    insb = Rot([p.sb(f"insb{i}", [128, 128]) for i in range(2)])
    osb = Rot([p.sb(f"osb{i}", [128, 128]) for i in range(3)])
    kdb = Rot([p.sb(f"kdb{i}", [128, 128], BF16) for i in range(2)])
    st6r = Rot([p.sb(f"st6{i}", [128, 6]) for i in range(3)])
    mvr = Rot([p.sb(f"mv{i}", [128, 2]) for i in range(3)])
    rsr = Rot([p.sb(f"rs{i}", [128, 1]) for i in range(3)])
    onb = Rot([p.sb(f"onb{i}", [128, 128]) for i in range(2)])
    oo = Rot([p.sb(f"oo{i}", [128, 128], BF16) for i in range(2)])
    rot_blk = {}
    live = {}
    live3 = {}

    def rotary(b):
        cs_ = slice(b * TB, (b + 1) * TB)
        q_ = qf.next(); k_ = kf.next()
        p.dma(q_, qT[:, cs_])
        p.dma(k_, kT[:, cs_])
        p.dma(posi, pos[:, cs_])
        p.copy(ang, posi)
        p.ts(ang, ang, S["invf"][:, 0:1], ALU.mult)
        sincos(p, ang, SINt, COSt, fb, ib, kb)
        rots = []
        for src, dstrot in ((q_, rqT), (k_, rkT)):
            pp = pP.next()
            p.mm(pp, S["PERM"], src)
            p.tt(r1, src, COSt, ALU.mult, e="pool")
            p.tt(r2, pp, SINt, ALU.mult)
            dd = dstrot.next()
            p.tt(dd, r1, r2, ALU.add)
            rots.append(dd)
        rot_blk[b] = rots

    def s1(b, n_):
        if n_ == 0:
            rotary(b)
        rq, rk = rot_blk[b]
        c0 = n_ * 128
        rows = slice(b * TB + c0, b * TB + c0 + 128)
        v_ = vf.next(); s_ = sgf.next()
        p.dma(v_, vv[rows, :])
        p.dma(s_, sg[rows, :])
        vb_ = vb.next()
        p.copy(vb_, v_, e="act")
        p.mm(psc, rk[:, c0:c0 + 128], rq[:, c0:c0 + 128])
        sc = scm.next()
        p.tt(sc, psc, S["DmT"], ALU.mult)
        p1 = po1.next()
        p.mm(p1, sc, vb_)
        i_ = insb.next()
        p.copy(i_, p1, e="act")
        p.tr(ptr, rk[:, c0:c0 + 128], identb)
        kd_ = kdb.next()
        p.ts(kd_, ptr, S["kd"][:, 0:1], ALU.mult)
        pk = pkv.next()
        p.mm(pk, kd_, vb_)
        live[(b, n_)] = (rq, c0, rows, s_, i_, pk)

    def s2(b, n_):
        rq, c0, rows, s_, i_, pk = live.pop((b, n_))
        p.mm(po2, rq[:, c0:c0 + 128], state_bf)
        o_ = osb.next()
        p.stt(o_, po2, S["qd"][:, 0:1], i_, ALU.mult, ALU.add)
        p.stt(state, state, S["cdec"][:, 0:1], pk, ALU.mult, ALU.add)
        p.copy(state_bf, state, e="act")
        st6 = st6r.next(); mv = mvr.next(); rs = rsr.next()
        p.bn_stats(st6, o_)
        p.bn_aggr(mv, st6)
        p.ts(rs, mv[:, 1:2], 1e-5, ALU.add)
        p.act(rs, rs, AF.Sqrt)
        live3[(b, n_)] = (rows, s_, o_, mv, rs)
        if n_ == NCH - 1:
            del rot_blk[b]

    def s3(b, n_):
        rows, s_, o_, mv, rs = live3.pop((b, n_))
        p.recip(rs, rs)
        on = onb.next()
        p.ts(on, o_, mv[:, 0:1], ALU.subtract, rs[:, 0:1], ALU.mult)
        p.tt(on, on, S["NG"], ALU.mult)
        p.tt(on, on, S["NB"], ALU.add)
        ob = oo.next()
        p.tt(ob, on, s_, ALU.mult)
        p.dma(o_out[rows, :], ob, q="pool")

    items = [(b, n_) for b in range(NBK) for n_ in range(NCH)]
    s1(*items[0])
    for i in range(len(items)):
        if i + 1 < len(items):
            s1(*items[i + 1])
        s2(*items[i])
        if i >= 1:
            s3(*items[i - 1])
    s3(*items[-1])
    return p


def host_B2_inputs(inp, l, c, qT_c, kT_c, v_c, sg_c, pos_T):
    f = np.float32
    h = c
    lg = np.log1p(-np.exp2(-5.0 - h))
    idx = np.arange(128, dtype=np.float64)
    rel = idx[None, :] - idx[:, None]
    sc = 128 ** -0.5
    DmT = np.where(rel >= 0, np.exp(lg * np.maximum(rel, 0.0)), 0.0) * sc
    perm = np.zeros((128, 128), f)
    for m_ in range(64):
        perm[m_ + 64, m_] = -1.0
        perm[m_, m_ + 64] = 1.0
    invf = (10000.0 ** (-np.arange(64, dtype=f) / 64)).astype(f)
    rep = lambda v_: np.ascontiguousarray(np.broadcast_to(v_[None, :], (128, v_.shape[0]))).astype(f)
    return {
        "qT": qT_c, "kT": kT_c, "v": v_c, "sg": sg_c, "pos": pos_T,
        "invf": np.concatenate([invf, invf]).reshape(128, 1).astype(f),
        "PERM": perm, "DmT": DmT.astype(f),
        "qd": np.exp(lg * (idx + 1.0)).reshape(128, 1).astype(f),
        "kd": (np.exp(lg * (127.0 - idx)) * sc).reshape(128, 1).astype(f),
        "cdec": np.full((128, 1), np.exp(lg * 128.0), f),
        "NG": rep(inp["ret_norm_g"][l][128 * c:128 * c + 128]),
        "NB": rep(inp["ret_norm_b"][l][128 * c:128 * c + 128]),
        "identf": np.eye(128, dtype=f),
    }


def build_B3(Tt):
    p = P()
    TB = min(512, Tt)
    NBK = Tt // TB
    CH = 64
    NCH = TB // CH
    NG_ = NCH // 4
    zin = {nm: p.dram(nm, [128, Tt + 1]) for nm in ("zr", "zk", "zv", "zl1", "zl2")}
    o_out = p.dram("o", [Tt, 128], BF16, kind="ExternalOutput")
    S = {}
    for nm, shp in (("MU", [128, 5]), ("PV", [128, 5]), ("w2c", [128, 128]), ("g2c", [128, 128]),
                    ("NG", [64, 128]), ("NB", [64, 128]), ("bones", [128, 128]), ("hsel", [128, 2]),
                    ("cmask", [128, TB]), ("MK1", [64, 512]), ("MK2", [64, 512]), ("MK3", [64, 256]),
                    ("identf", [128, 128]), ("bdmask", [128, 128])):
        dt_ = p.dram(nm, shp)
        st = p.sb("s_" + nm, shp)
        p.dma(st, dt_)
        S[nm] = st
    w2b = p.sb("w2b", [128, 128], BF16); p.copy(w2b, S["w2c"])
    g2b = p.sb("g2b", [128, 128], BF16); p.copy(g2b, S["g2c"])
    hselb = p.sb("hselb", [128, 2], BF16); p.copy(hselb, S["hsel"])
    identb = p.sb("identb", [128, 128], BF16); p.copy(identb, S["identf"])
    bk = [p.ps(f"bank{i}", [128, 512]) for i in range(8)]
    H = p.sb("H", [128, 128]); Hb = p.sb("Hb", [128, 128], BF16)
    p.memset(H, 0.0); p.memset(Hb, 0.0)
    shb = [128, TB]
    n = [0]

    def tmp(shape=shb, dt=F32):
        n[0] += 1
        return p.sb(f"t{n[0]}", shape, dt)

    zst = {nm: Rot([p.sb(f"zst_{nm}{i}", [128, TB + 1]) for i in range(2)]) for nm in zin}
    dsh = tmp(); r_ = tmp(); k_ = tmp(); v_ = tmp(); l1 = tmp(); l2 = tmp()
    l1b = tmp(dt=BF16); sgb = tmp([128, TB + 64], BF16)
    p.memset(sgb, 0.0)
    ld = tmp(); a_ = tmp(); cs = tmp(); e0 = tmp(); e1 = tmp(); e2 = tmp(); tq = tmp()
    kk = tmp(); rn = tmp(); kmod = tmp(); bv = tmp()
    rkr = tmp([128, TB + 64], BF16); vbf = tmp([128, TB + 64], BF16)
    for t_ in (rkr, vbf):
        p.memset(t_, 0.0)
    Mm = p.sb("Mm", [64, NCH + 1, 2, 64], BF16)
    p.memset(Mm, 0.0)
    MMs = [Rot([p.sb(f"MM{g}{i}", [64, 9, 128], BF16) for i in range(2)]) for g in range(NG_)]
    Tts = [Rot([p.sb(f"Tt{g}{i}", [64, 10, 64], BF16) for i in range(2)]) for g in range(NG_)]
    D1 = []
    for i3 in range(3):
        d_ = {"KR": p.sb(f"KR{i3}", [128, NCH + 1, 2, CH], BF16),
              "Bh": p.sb(f"Bh{i3}", [128, TB + 64], BF16), "Kh": p.sb(f"Kh{i3}", [128, TB + 64], BF16),
              "Vb": p.sb(f"Vb{i3}", [64, NCH, 128], BF16), "Vf": p.sb(f"Vf{i3}", [64, NCH, 128]),
              "BhT": p.sb(f"BhT{i3}", [64, NCH, 128], BF16), "KhT": p.sb(f"KhT{i3}", [64, NCH, 128], BF16),
              "Gtm": p.sb(f"Gtm{i3}", [64, NCH, 128]), "BS": p.sb(f"BS{i3}", [64, NCH * 2]),
              "PC": p.sb(f"PC{i3}", [128, NCH])}
        for nm in ("KR", "Bh", "Kh"):
            p.memset(d_[nm], 0.0)
        D1.append(d_)
    D2 = []
    for par in range(2):
        d_ = {"A1": p.sb(f"A1{par}", [64, NCH + 1, 2, 128], BF16),
              "A2": p.sb(f"A2{par}", [64, NCH + 1, 2, 128], BF16),
              "Tf": [p.sb(f"Tf{par}{g}", [64, 10, 64], BF16) for g in range(NG_)]}
        for nm in ("A1", "A2"):
            p.memset(d_[nm], 0.0)
        for g in range(NG_):
            p.memset(d_["Tf"][g], 0.0)
        D2.append(d_)
    W0n = Rot([p.sb(f"W0n{i}", [64, 128], BF16) for i in range(2)])
    Ub = Rot([p.sb(f"Ub{i}", [64, 128], BF16) for i in range(2)])
    Ysb = p.sb("Ysb", [64, NCH, 128])
    tH = p.sb("tH", [128, 128])
    xc = p.sb("xc", [64, NCH, 128]); sq_ = p.sb("sq_", [64, NCH, 128])
    s16 = p.sb("s16", [64, NCH * 2]); v16 = p.sb("v16", [64, NCH * 2])
    osb = Rot([p.sb(f"osb{i}", [64, NCH, 128], BF16) for i in range(2)])
    mk1 = S["MK1"][:, :].re("p (c t) -> p c t", t=128)
    mk2 = S["MK2"][:, :].re("p (c t) -> p c t", t=128)
    mk3 = S["MK3"][:, :].re("p (c t) -> p c t", t=64)

    def c3(vw):
        return vw.re("p (c t) -> p c t", t=CH)

    def pre1(b):
        D_ = D1[b % 3]
        KR, Bh, Kh = D_["KR"], D_["Bh"], D_["Kh"]
        zs = {}
        for nm in zin:
            st = zst[nm].next()
            p.dma(st, zin[nm][:, b * TB:b * TB + TB + 1])
            zs[nm] = st
        for i, (nm, dst) in enumerate((("zr", r_), ("zk", k_), ("zv", v_), ("zl1", l1), ("zl2", l2))):
            st = zs[nm]
            p.tt(dsh, st[:, 0:TB], st[:, 1:TB + 1], ALU.subtract, e="pool")
            p.stt(dst, dsh, S["MU"][:, i:i + 1], st[:, 1:TB + 1], ALU.mult, ALU.add)
        yield
        p.act(l1b[0:64, :], l1[0:64, :], AF.Tanh)
        p.copy(l1b[64:128, :], l1[64:128, :], e="act")
        p.act(sgb[:, 0:TB], l2, AF.Sigmoid)
        p.mm(bk[4], w2b[0:64, :], l1b[0:64, :])
        p.mm(bk[5], w2b[64:128, :], l1b[64:128, :])
        p.act(ld, bk[4], AF.Sigmoid, bias=S["PV"][:, 0:1])
        p.ts(ld, ld, -EXPM05, ALU.mult)
        p.act(a_, bk[5], AF.Sigmoid, bias=S["PV"][:, 1:2])
        p.scan(cs, S["cmask"], ld, 0.0, ALU.mult, ALU.add)
        p.act(e1, cs, AF.Exp)
        p.act(e2, cs, AF.Exp, scale=-1.0)
        p.tt(tq, cs, ld, ALU.subtract, e="pool")
        p.act(e0, tq, AF.Exp)
        p.copy(D_["PC"], e1[:, :].re("p (c t) -> p c t", t=CH)[:, :, CH - 1], e="pool")
        yield
        p.ts(kk, k_, S["PV"][:, 2:3], ALU.mult)
        p.tt(tq, kk, kk, ALU.mult, e="pool")
        p.mm(bk[6], S["bones"], tq)
        p.ts(rn, bk[6], 1e-24, ALU.max)
        p.act(rn, rn, AF.Sqrt)
        p.recip(rn, rn)
        p.tt(kk, kk, rn, ALU.mult)
        p.ts(tq, a_, -1.0, ALU.add)
        p.ts(tq, tq, S["PV"][:, 3:4], ALU.mult)
        p.stt(kmod, tq, 1.0, k_, ALU.add, ALU.mult)
        p.tt(bv, a_, kk, ALU.mult, e="pool")
        yield
        p.tt(KR[:, 0:NCH, 0, :], c3(kk[:, :]), c3(e0[:, :]), ALU.mult)
        p.tt(KR[:, 0:NCH, 1, :], c3(r_[:, :]), c3(e1[:, :]), ALU.mult)
        p.tt(Bh[:, 0:TB], bv, e2, ALU.mult, e="pool")
        p.tt(Kh[:, 0:TB], kmod, e2, ALU.mult, e="pool")
        p.ts(tq, r_, S["PV"][:, 4:5], ALU.mult)
        p.tt(rkr[:, 0:TB], tq, kmod, ALU.mult)
        p.copy(vbf[:, 0:TB], v_, e="act")
        yield
        for g4 in range(NG_):
            sl4 = slice(g4 * 4, (g4 + 1) * 4)
            for i in range(4):
                c = g4 * 4 + i
                p.mm(bk[4][:, i * 128:(i + 1) * 128], vbf[:, c * CH:c * CH + 128], identb)
            p.copy(D_["Vf"][:, sl4, :], bk[4][0:64, 0:512].re("p (c e) -> p c e", e=128), e="act")
            p.copy(D_["Vb"][:, sl4, :], D_["Vf"][:, sl4, :])
            for i in range(4):
                c = g4 * 4 + i
                p.mm(bk[5][:, i * 128:(i + 1) * 128], sgb[:, c * CH:c * CH + 128], g2b)
            p.copy(D_["Gtm"][:, sl4, :], bk[5][0:64, :].re("p (c e) -> p c e", e=128), e="act")
            for bi_, (src, dstT) in enumerate(((Bh, D_["BhT"]), (Kh, D_["KhT"]))):
                bk_ = bk[6 - 2 * bi_]
                for i in range(4):
                    c = g4 * 4 + i
                    p.mm(bk_[:, i * 128:(i + 1) * 128], src[:, c * CH:c * CH + 128], identb)
                p.copy(dstT[:, sl4, :], bk_[0:64, 0:512].re("p (c e) -> p c e", e=128))
            yield
        for c in range(NCH):
            p.mm(bk[5][:, 2 * c:2 * c + 2], rkr[:, c * CH:c * CH + 128], hselb)
        p.copy(D_["BS"], bk[5][0:64, 0:2 * NCH], e="act")
        yield

    def pre2(b):
        D_ = D1[b % 3]
        KR, Bh, Kh = D_["KR"], D_["Bh"], D_["Kh"]
        E_ = D2[b % 2]
        A1, A2 = E_["A1"], E_["A2"]
        for g4 in range(NG_):
            sl4 = slice(g4 * 4, (g4 + 1) * 4)
            for ci in range(4):
                c = g4 * 4 + ci
                for h in range(2):
                    ho = 64 * h
                    krv = KR[ho:ho + 64, c, :, :].re("p a t -> p (a t)")
                    p.mm(bk[0 + h][:, ci * 128:(ci + 1) * 128], Bh[ho:ho + 64, c * CH:c * CH + 128], krv)
                    p.mm(bk[2 + h][:, ci * 128:(ci + 1) * 128], Kh[ho:ho + 64, c * CH:c * CH + 128], krv)
            for h in range(2):
                p.tt(A1[:, sl4, h, :], bk[0 + h][0:64, :].re("p (c t) -> p c t", t=128), mk1, ALU.mult)
                p.tt(A2[:, sl4, h, :], bk[2 + h][0:64, :].re("p (c t) -> p c t", t=128), mk2, ALU.mult)
            for ci in range(4):
                c = g4 * 4 + ci
                for h in range(2):
                    ho = 64 * h
                    krv = KR[ho:ho + 64, c, :, :].re("p a t -> p (a t)")
                    p.mm(bk[0 + h][:, ci * 64:(ci + 1) * 64], krv, Bh[ho:ho + 64, c * CH:(c + 1) * CH])
            for h in range(2):
                p.tt(Mm[:, sl4, h, :], bk[0 + h][0:64, 0:256].re("p (c t) -> p c t", t=64), mk3, ALU.mult)
            yield
        pairs_g = [[(G * 4 + ci, h) for ci in range(4) for h in range(2)] for G in range(NG_)]
        Tcur = []
        curM = []
        curMt = []
        Mflat = Mm[:, :, :, :].re("p c h t -> p (c h t)")
        for G in range(NG_):
            Tt = Tts[G].next()
            for pi, (c, h) in enumerate(pairs_g[G]):
                p.tt(Tt[:, pi, :], A1[:, c, h, 0:64], identb[0:64, 0:64], ALU.add, e="pool")
            Tcur.append(Tt)
            curM.append([Mflat[:, (2 * c + h) * 64:(2 * c + h) * 64 + 128] for (c, h) in pairs_g[G]])
            curMt.append([A1[:, c, h, 0:128] for (c, h) in pairs_g[G]])
        yield
        for lev in range(1, 6):
            last = (lev == 5)
            for G in range(NG_):
                sqb = [bk[(2 * G) % 4], bk[(2 * G + 1) % 4]]
                for pi in range(8):
                    b_ = sqb[pi // 4]
                    col = (pi % 4) * 128
                    p.mm(b_[:, col:col + 64], curMt[G][pi], curM[G][pi][:, 0:64])
                    if not last:
                        p.mm(b_[:, col + 64:col + 128], curM[G][pi], curMt[G][pi][:, 0:64])
            for G in range(NG_):
                sqb = [bk[(2 * G) % 4], bk[(2 * G + 1) % 4]]
                MMn = MMs[G].next()
                for hb in range(2):
                    p.copy(MMn[:, 4 * hb:4 * hb + 4, :], sqb[hb][0:64, :].re("p (a t) -> p a t", t=128),
                           e=("act" if hb == 0 else "dve"))
                MMf = MMn[:, :, :].re("p a t -> p (a t)")
                curM[G] = [MMf[:, pi * 128:pi * 128 + 128] for pi in range(8)]
                curMt[G] = [MMf[:, pi * 128 + 64:pi * 128 + 192] for pi in range(8)]
            for G in range(NG_):
                pacc = bk[(2 * G) % 4]
                for pi in range(8):
                    p.mm(pacc[:, pi * 64:(pi + 1) * 64], curM[G][pi], Tcur[G][:, pi, :])
            for G in range(NG_):
                pacc = bk[(2 * G) % 4]
                Tn = E_["Tf"][G] if last else Tts[G].next()
                p.tt(Tn[:, 0:8, :], Tcur[G][:, 0:8, :], pacc[0:64, :].re("p (a t) -> p a t", t=64), ALU.add)
                Tcur[G] = Tn
            yield

    def chain(b):
        D_ = D1[b % 3]
        E_ = D2[b % 2]
        KR, Vb, BhT, KhT = D_["KR"], D_["Vb"], D_["BhT"], D_["KhT"]
        A1, A2 = E_["A1"], E_["A2"]
        krf = KR[:, :, :, :].re("p c a t -> p (c a t)")
        a1f = A1[:, :, :, :].re("p c h t -> p (c h t)")
        a2f = A2[:, :, :, :].re("p c h t -> p (c h t)")
        pw, pu, py, ph = bk[7][:, 0:128], bk[7][:, 128:256], bk[7][:, 256:384], bk[7][:, 384:512]
        for c in range(NCH):
            Tt = E_["Tf"][c // 4]
            p.mm(pw[:, 0:128], krf[:, c * 128:c * 128 + 128], Hb, start=True, stop=False)
            for h in range(2):
                p.mm(pw[:, 64 * h:64 * h + 64], A2[:, c, h, 0:128], Vb[:, c, 64 * h:64 * h + 64],
                     start=False, stop=(h == 1))
            wn = W0n.next()
            p.act(wn, pw[0:64, 0:128], AF.Copy, scale=-1.0)
            for h in range(2):
                pi = (c % 4) * 2 + h
                p.mm(pu[:, 64 * h:64 * h + 64], Tt[:, pi:pi + 2, :].re("p a t -> p (a t)"), wn[:, 64 * h:64 * h + 64])
            ub = Ub.next()
            p.copy(ub, pu[0:64, 0:128])
            p.mm(py[:, 0:128], krf[:, c * 128 + 64:c * 128 + 192], Hb, start=True, stop=False)
            for h in range(2):
                sl = slice(64 * h, 64 * h + 64)
                o_ = (2 * c + h) * 128 + 64
                p.mm(py[:, sl], a1f[:, o_:o_ + 128], ub[:, sl], start=False, stop=False)
                p.mm(py[:, sl], a2f[:, o_:o_ + 128], Vb[:, c, sl], start=False, stop=(h == 1))
            p.copy(Ysb[:, c, :], py[0:64, 0:128], e="act")
            p.mm(ph[:, 0:128], BhT[:, c, :], ub, start=True, stop=False)
            p.mm(ph[:, 0:128], KhT[:, c, :], Vb[:, c, :], start=False, stop=True)
            p.tt(tH, ph[:, 0:128], S["bdmask"], ALU.mult)
            p.tt(tH, tH, H, ALU.add)
            pc = D_["PC"][:, c:c + 1]
            p.ts(H, tH, pc, ALU.mult)
            p.act(Hb, tH, AF.Copy, scale=pc)
            yield
        y4 = Ysb[:, :, :].re("p c (h e) -> p (c h) e", e=64)
        p.reduce(s16, y4, ALU.add)
        p.ts(s16, s16, 1.0 / 64, ALU.mult)
        x4 = xc[:, :, :].re("p c (h e) -> p (c h) e", e=64)
        q4 = sq_[:, :, :].re("p c (h e) -> p (c h) e", e=64)
        p.tt(x4, y4, s16[:, :].re("p (a o) -> p a o", o=1).bc([64, NCH * 2, 64]), ALU.subtract)
        p.tt(q4, x4, x4, ALU.mult, e="pool")
        p.reduce(v16, q4, ALU.add)
        p.ts(v16, v16, 1.0 / 64, ALU.mult, 64e-5, ALU.add)
        p.act(v16, v16, AF.Sqrt)
        p.recip(v16, v16)
        p.tt(x4, x4, v16[:, :].re("p (a o) -> p a o", o=1).bc([64, NCH * 2, 64]), ALU.mult)
        ngb = S["NG"][:, :].re("p (o e) -> p o e", o=1).bc([64, NCH, 128])
        nbb = S["NB"][:, :].re("p (o e) -> p o e", o=1).bc([64, NCH, 128])
        p.tt(xc, xc, ngb, ALU.mult)
        p.tt(xc, xc, nbb, ALU.add)
        vf4 = D_["Vf"][:, :, :].re("p c (h e) -> p (c h) e", e=64)
        p.tt(q4, vf4, D_["BS"][:, :].re("p (a o) -> p a o", o=1).bc([64, NCH * 2, 64]), ALU.mult)
        p.tt(xc, xc, sq_, ALU.add)
        ob = osb.next()
        p.tt(ob, xc, D_["Gtm"], ALU.mult)
        p.dma(V(o_out, o_out.ap[b * TB:(b + 1) * TB, :].rearrange("(c t) e -> t c e", t=CH)), ob)
        yield

    def drain(gens):
        gens = [g for g in gens if g is not None]
        while gens:
            for g in list(gens):
                try:
                    next(g)
                except StopIteration:
                    gens.remove(g)

    drain([pre1(0)])
    drain([pre2(0), pre1(1) if NBK > 1 else None])
    for b in range(NBK):
        drain([chain(b), pre2(b + 1) if b + 1 < NBK else None, pre1(b + 2) if b + 2 < NBK else None])
    return p


def host_B3_inputs(inp, l, c, zT_full, TB):
    f = np.float32
    Tt = zT_full.shape[1]
    ch = slice(128 * c, 128 * c + 128)

    def pad(a):
        return np.ascontiguousarray(np.concatenate([np.zeros((a.shape[0], 1), f), a], axis=1))

    mu = inp["rwkv_mu"][l]
    m = {"zr": pad(zT_full[0:1024][ch]), "zk": pad(zT_full[1024:2048][ch]), "zv": pad(zT_full[2048:3072][ch]),
         "zl1": pad(zT_full[3072:3200]), "zl2": pad(zT_full[3200:3328])}
    m["MU"] = np.ascontiguousarray(np.stack([mu[0:1024][ch], mu[1024:2048][ch], mu[2048:3072][ch],
                                             mu[3072:3200], mu[3200:3328]], axis=1)).astype(f)
    m["PV"] = np.ascontiguousarray(np.stack([inp["rwkv_w0"][l][ch], inp["rwkv_a0"][l][ch], inp["rwkv_k_k"][l][ch],
                                             inp["rwkv_k_a"][l][ch], inp["rwkv_r_k"][l][ch]], axis=1)).astype(f)
    m["w2c"] = np.ascontiguousarray(np.concatenate([inp["rwkv_w2"][l][:, ch], inp["rwkv_a2"][l][:, ch]], 0)).astype(f)
    m["g2c"] = np.ascontiguousarray(inp["rwkv_g2"][l][:, ch]).astype(f)
    rep = lambda v_: np.ascontiguousarray(np.broadcast_to(v_[None, :], (64, v_.shape[0]))).astype(f)
    m["NG"] = rep(inp["rwkv_norm_g"][l][ch]); m["NB"] = rep(inp["rwkv_norm_b"][l][ch])
    bo = np.zeros((128, 128), f); bo[0:64, 0:64] = 1.0; bo[64:, 64:] = 1.0
    m["bones"] = bo
    m["bdmask"] = bo.copy()
    hs = np.zeros((128, 2), f); hs[0:64, 0] = 1.0; hs[64:, 1] = 1.0
    m["hsel"] = hs
    cm = np.ones((128, TB), f); cm[:, ::64] = 0.0
    m["cmask"] = cm
    s = np.arange(64)[:, None]; t = np.arange(64)[None, :]
    strict_st = (t > s).astype(f); incl_st = (t >= s).astype(f)
    strict_ts = (t < s).astype(f)
    m["MK1"] = np.ascontiguousarray(np.tile(np.concatenate([-strict_st, incl_st], 1), (1, 4)))
    m["MK2"] = np.ascontiguousarray(np.tile(np.concatenate([strict_st, incl_st], 1), (1, 4)))
    m["MK3"] = np.ascontiguousarray(np.tile(-strict_ts, (1, 4)))
    m["identf"] = np.eye(128, dtype=f)
    return m


_PROGS = {}


def _prog(name, builder, *a):
    key = (name,) + a
    if key not in _PROGS:
        _PROGS[key] = builder(*a)
    return _PROGS[key]


def _run(p, maps):
    return p.run(maps).results


def kernel(**inp):
    inp = {k: np.asarray(v) for k, v in inp.items()}
    x = np.ascontiguousarray(inp["x"][0]).astype(np.float32)
    Tt = x.shape[0]
    Tc = Tt // NCORE
    TB = min(512, Tt)
    f = np.float32
    posT = np.ascontiguousarray(np.broadcast_to(inp["positions"][0][None, :], (128, Tt))).astype(np.int32)
    rep = lambda v_: np.ascontiguousarray(np.broadcast_to(v_[None, :], (128, v_.shape[0]))).astype(f)
    sl = lambda c: slice(c * Tc, (c + 1) * Tc)
    hs = lambda c: slice(128 * c, 128 * c + 128)
    eye = np.eye(128, dtype=f)
    flat = np.concatenate([np.concatenate([inp["ffn_up"][l].ravel(), inp["ffn_down"][l].ravel()])
                           for l in range(DEPTH)])
    Lp = flat.size // (NCORE * 128)
    rp = _run(build_PREP(Lp), [{"w": np.ascontiguousarray(flat[c * 128 * Lp:(c + 1) * 128 * Lp].reshape(128, Lp))}
                               for c in range(NCORE)])
    flat_b = np.concatenate([rp[c]["wb"].reshape(-1) for c in range(NCORE)])
    del flat
    nup, ndn = D * 2 * D_FF, D_FF * D
    wupT, wdnT = [], []
    for l in range(DEPTH):
        o_ = l * (nup + ndn)
        up = flat_b[o_:o_ + nup].reshape(16, 128, 2, D_FF // 256, 2, 128)
        wupT.append(np.ascontiguousarray(up.transpose(3, 1, 0, 4, 2, 5)).reshape(D_FF // 256, 128, 8192))
        dn = flat_b[o_ + nup:o_ + nup + ndn].reshape(D_FF // 256, 2, 128, D)
        wdnT.append(np.ascontiguousarray(dn.transpose(0, 2, 1, 3)).reshape(D_FF // 256, 128, 2 * D))
    for l in range(DEPTH):
        pa = build_A(Tc)
        ra = _run(pa, [{"xT": np.ascontiguousarray(x[sl(c)].T), "w": inp["w_in"][l]} for c in range(NCORE)])
        cat_fm = lambda nm: np.concatenate([ra[c][nm] for c in range(NCORE)], axis=1)
        cat_tm = lambda nm: np.concatenate([ra[c][nm] for c in range(NCORE)], axis=0)
        uT, qT, kT, zT = cat_fm("uT"), cat_fm("qT"), cat_fm("kT"), cat_fm("zT")
        v, sg = cat_tm("v"), cat_tm("sg")
        gates = [ra[c]["gates"] for c in range(NCORE)]
        r1 = _run(build_B1(Tt), [host_B1_inputs(inp, l, c, np.ascontiguousarray(uT[hs(c)]), TB) for c in range(NCORE)])
        zs = np.concatenate([r1[c]["zT"] for c in range(NCORE)], axis=0)
        r2 = _run(build_B2(Tt), [host_B2_inputs(inp, l, c, np.ascontiguousarray(qT[hs(c)]), np.ascontiguousarray(kT[hs(c)]),
                                                np.ascontiguousarray(v[:, hs(c)]), np.ascontiguousarray(sg[:, hs(c)]), posT)
                                 for c in range(NCORE)])
        orT = np.ascontiguousarray(np.concatenate([r2[c]["o"] for c in range(NCORE)], axis=1).T)
        r3 = _run(build_B3(Tt), [host_B3_inputs(inp, l, c, zT, TB) for c in range(NCORE)])
        owT = np.ascontiguousarray(np.concatenate([r3[c]["o"] for c in range(NCORE)], axis=1).T)
        rc1 = _run(build_C1(Tc), [{
            "zT": np.ascontiguousarray(zs[:, sl(c)]), "orT": np.ascontiguousarray(orT[:, sl(c)]),
            "owT": np.ascontiguousarray(owT[:, sl(c)]), "gates": gates[c], "x": np.ascontiguousarray(x[sl(c)]),
            "wglu": inp["ssm_glu"][l], "wret": inp["ret_out"][l], "wrw": inp["rwkv_out"][l], "wo": inp["w_o"][l],
            "lng": rep(inp["ln1_g"][l]), "lnb": rep(inp["ln1_b"][l]), "identf": eye} for c in range(NCORE)])
        x1 = np.concatenate([rc1[c]["x1"] for c in range(NCORE)], axis=0)
        x1p = np.concatenate([np.zeros((2, D), f), x1], axis=0)
        wcv = np.ascontiguousarray(inp["ffn_conv"][l].T.reshape(88, 128, 3).transpose(1, 0, 2)).astype(f)
        rc2 = _run(build_C2(Tc), [{
            "x1": np.ascontiguousarray(x1[sl(c)]), "x1T": np.ascontiguousarray(x1p[c * Tc:(c + 1) * Tc + 2].T),
            "wupT": wupT[l], "wcv": wcv, "wdnT": wdnT[l],
            "lng": rep(inp["ln2_g"][l]), "lnb": rep(inp["ln2_b"][l])} for c in range(NCORE)])
        x = np.concatenate([rc2[c]["x2"] for c in range(NCORE)], axis=0)
    return x[None].astype(np.float32)
```

```python
import math
import numpy as np
import ml_dtypes
import concourse.bass as bass
import concourse.mybir as mybir
from concourse.bass_utils import run_bass_kernel_spmd

F32 = mybir.dt.float32
BF16 = mybir.dt.bfloat16
I32 = mybir.dt.int32
AF = mybir.ActivationFunctionType
ALU = mybir.AluOpType
AX = mybir.AxisListType
NPBF = ml_dtypes.bfloat16

D = 2048
NCORE = 8
DEPTH = 4
N_IN = 14592
D_FF = 5632
ALPHA = (2.0 * DEPTH) ** 0.25
EXPM05 = math.exp(-0.5)
TWO_PI = 2.0 * math.pi
DEBUG = False
SUB = 9
KNOB = 0
STQ = "act"


class T:
    __slots__ = ("ap", "name", "w", "r", "psum")

    def __init__(self, ap, name, psum=False):
        self.ap = ap
        self.name = name
        self.w = None
        self.r = []
        self.psum = psum

    def __getitem__(self, idx):
        return V(self, self.ap[idx])


class V:
    __slots__ = ("t", "ap")

    def __init__(self, t, ap):
        self.t = t
        self.ap = ap

    def __getitem__(self, idx):
        return V(self.t, self.ap[idx])

    def re(self, pat, **kw):
        return V(self.t, self.ap.rearrange(pat, **kw))

    def bc(self, shape):
        return V(self.t, self.ap.to_broadcast(list(shape)))


def _tv(x):
    if isinstance(x, T):
        return x, x.ap
    if isinstance(x, V):
        return x.t, x.ap
    return None, x


class P:
    NRING = 8

    def __init__(self):
        self.nc = bass.Bass("TRN2", target_bir_lowering=False)
        nc = self.nc
        self.eng = {"pe": nc.tensor, "dve": nc.vector, "act": nc.scalar,
                    "pool": nc.gpsimd, "sp": nc.sync}
        self.sems = {}
        self.cnt = {}
        self.seen = {e: {} for e in self.eng}
        self._ctx = []
        for e in self.eng:
            self.sems[e] = self._sem("s_" + e)
            self.cnt[e] = 0
        self.ring = {}
        self.ringn = {}
        for q in ("sp", "pool", "act"):
            self.ring[q] = [self._sem(f"d_{q}{i}") for i in range(self.NRING)]
            self.ringn[q] = 0
        self.n_inst = 0
        self._rr = 0

    def _sem(self, name):
        cm = self.nc.semaphore(name)
        s = cm.__enter__()
        self._ctx.append(cm)
        return s

    def sb(self, name, shape, dt=F32):
        cm = self.nc.sbuf_tensor(name, list(shape), dt)
        t = cm.__enter__()
        self._ctx.append(cm)
        return T(t[:], name)

    def ps(self, name, shape, dt=F32):
        cm = self.nc.psum_tensor(name, list(shape), dt)
        t = cm.__enter__()
        self._ctx.append(cm)
        return T(t[:], name, psum=True)

    def dram(self, name, shape, dt=F32, kind="ExternalInput"):
        t = self.nc.dram_tensor(name, list(shape), dt, kind=kind)
        return T(t.ap(), name)

    def _semobj(self, key):
        if isinstance(key, tuple):
            return self.ring[key[0]][key[1]]
        return self.sems[key]

    def _deps(self, e, reads, writes):
        need = {}

        def add(tok):
            k, v = tok[0], tok[1]
            if need.get(k, 0) < v:
                need[k] = v

        for t in reads:
            if t is None or t.w is None:
                continue
            if t.w[0] == e and e == "pe":
                continue
            add(t.w)
        for t in reads:
            if t is not None and t.psum:
                for rd in t.r:
                    if rd[0] != e:
                        add(rd)
        for t in writes:
            if t is None:
                continue
            for rd in t.r:
                if rd[0] == e:
                    continue
                add(rd)
            if t.w is not None and t.w[0] != e:
                add(t.w)
        eng = self.eng[e]
        seen = self.seen[e]
        for k, v in need.items():
            if seen.get(k, 0) < v:
                eng.wait_ge(self._semobj(k), v)
                seen[k] = v

    def _done(self, tok, reads, writes):
        for t in reads:
            if t is not None:
                t.r.append(tok)
                if len(t.r) > 24:
                    best = {}
                    for rd in t.r:
                        if best.get(rd[0], 0) < rd[1]:
                            best[rd[0]] = rd[1]
                    t.r = [(k, v) for k, v in best.items()]
        for t in writes:
            if t is not None:
                t.w = tok
                t.r = []

    def op(self, e, fn, reads, writes):
        self._deps(e, reads, writes)
        ins = fn()
        self.cnt[e] += 1
        ins.then_inc(self.sems[e], 1)
        self._done((e, self.cnt[e]), reads, writes)
        self.n_inst += 1
        return ins

    def dma(self, out, in_, q="sp", **kw):
        to, apo = _tv(out)
        ti, api = _tv(in_)
        n = self.ringn[q]
        slot = n % self.NRING
        rnd = n // self.NRING
        key = (q, slot)
        eng = self.eng[q]
        if rnd > 0 and self.seen[q].get(key, 0) < 16 * rnd:
            eng.wait_ge(self.ring[q][slot], 16 * rnd)
            self.seen[q][key] = 16 * rnd
        self._deps(q, [ti], [to])
        ins = eng.dma_start(out=apo, in_=api, **kw)
        ins.then_inc(self.ring[q][slot], 16)
        self.ringn[q] = n + 1
        self._done((key, 16 * (rnd + 1)), [ti], [to])
        self.n_inst += 1
        return ins

    def mm(self, out, lhsT, rhs, start=True, stop=True, **kw):
        to, apo = _tv(out)
        tl, apl = _tv(lhsT)
        tr, apr = _tv(rhs)
        return self.op("pe", lambda: self.nc.tensor.matmul(apo, apl, apr, start=start, stop=stop, **kw),
                       [tl, tr], [to])

    def tr(self, out, in_, ident):
        to, apo = _tv(out)
        ti, api = _tv(in_)
        td, apd = _tv(ident)
        return self.op("pe", lambda: self.nc.tensor.transpose(apo, api, apd), [ti, td], [to])

    def act(self, out, in_, func, bias=None, scale=None):
        to, apo = _tv(out)
        ti, api = _tv(in_)
        rd = [ti]
        k = {}
        if bias is not None:
            tb, apb = _tv(bias)
            rd.append(tb)
            k["bias"] = apb
        if scale is not None:
            ts_, aps = _tv(scale)
            rd.append(ts_)
            k["scale"] = aps
        return self.op("act", lambda: self.nc.scalar.activation(apo, api, func, **k), rd, [to])

    def tt(self, out, a, b, op, e="dve"):
        to, apo = _tv(out)
        ta, apa = _tv(a)
        tb, apb = _tv(b)
        eng = self.eng[e]
        return self.op(e, lambda: eng.tensor_tensor(apo, apa, apb, op), [ta, tb], [to])

    def ts(self, out, a, s1, op0, s2=None, op1=None, e="dve"):
        to, apo = _tv(out)
        ta, apa = _tv(a)
        t1, ap1 = _tv(s1)
        t2, ap2 = _tv(s2)
        eng = self.eng[e]
        if op1 is None:
            return self.op(e, lambda: eng.tensor_scalar(apo, apa, ap1, None, op0), [ta, t1], [to])
        return self.op(e, lambda: eng.tensor_scalar(apo, apa, ap1, ap2, op0, op1), [ta, t1, t2], [to])

    def stt(self, out, a, s, b, op0, op1):
        to, apo = _tv(out)
        ta, apa = _tv(a)
        ts_, aps = _tv(s)
        tb, apb = _tv(b)
        return self.op("dve", lambda: self.nc.vector.scalar_tensor_tensor(apo, apa, aps, apb, op0, op1),
                       [ta, ts_, tb], [to])

    def copy(self, out, in_, e="dve"):
        to, apo = _tv(out)
        ti, api = _tv(in_)
        if e == "act":
            return self.op("act", lambda: self.nc.scalar.copy(apo, api), [ti], [to])
        eng = self.eng[e]
        return self.op(e, lambda: eng.tensor_copy(apo, api), [ti], [to])

    def memset(self, out, val, e="dve"):
        to, apo = _tv(out)
        eng = self.eng[e]
        return self.op(e, lambda: eng.memset(apo, val), [], [to])

    def scan(self, out, d0, d1, init, op0, op1):
        to, apo = _tv(out)
        t0, ap0 = _tv(d0)
        t1, ap1 = _tv(d1)
        ti, api = _tv(init)
        return self.op("dve", lambda: self.nc.vector.tensor_tensor_scan(apo, ap0, ap1, api, op0, op1),
                       [t0, t1, ti], [to])

    def recip(self, out, in_):
        to, apo = _tv(out)
        ti, api = _tv(in_)
        return self.op("dve", lambda: self.nc.vector.reciprocal(apo, api), [ti], [to])

    def reduce(self, out, in_, op, axis=AX.X):
        to, apo = _tv(out)
        ti, api = _tv(in_)
        return self.op("dve", lambda: self.nc.vector.tensor_reduce(apo, api, axis, op), [ti], [to])

    def bn_stats(self, out, in_):
        to, apo = _tv(out)
        ti, api = _tv(in_)
        return self.op("dve", lambda: self.nc.vector.bn_stats(apo, api), [ti], [to])

    def bn_aggr(self, out, in_):
        to, apo = _tv(out)
        ti, api = _tv(in_)
        return self.op("dve", lambda: self.nc.vector.bn_aggr(apo, api), [ti], [to])

    def finish(self):
        sp = self.nc.sync
        for e in self.eng:
            if e != "sp" and self.cnt[e] > 0 and self.seen["sp"].get(e, 0) < self.cnt[e]:
                sp.wait_ge(self.sems[e], self.cnt[e])
                self.seen["sp"][e] = self.cnt[e]
        for q in self.ring:
            n = self.ringn[q]
            for slot in range(min(n, self.NRING)):
                last_rnd = (n - 1 - slot) // self.NRING
                v = 16 * (last_rnd + 1)
                key = (q, slot)
                if self.seen["sp"].get(key, 0) < v:
                    sp.wait_ge(self.ring[q][slot], v)
                    self.seen["sp"][key] = v

    def run(self, in_maps):
        self.finish()
        return run_bass_kernel_spmd(self.nc, in_maps, core_ids=list(range(NCORE)))


class Rot:
    def __init__(self, items):
        self.items = items
        self.i = 0

    def next(self):
        x = self.items[self.i % len(self.items)]
        self.i += 1
        return x


def sincos(p, ang, sin_out, cos_out, tmp_f, tmp_i, tmp_k):
    C1 = 6.28125
    C2 = TWO_PI - 6.28125
    p.ts(tmp_f, ang, 1.0 / TWO_PI, ALU.mult)
    p.copy(tmp_i, tmp_f)
    p.copy(tmp_k, tmp_i)
    p.stt(tmp_f, tmp_k, -C1, ang, ALU.mult, ALU.add)
    p.stt(tmp_f, tmp_k, -C2, tmp_f, ALU.mult, ALU.add)
    p.ts(tmp_k, tmp_f, math.pi, ALU.is_gt, -TWO_PI, ALU.mult)
    p.tt(tmp_k, tmp_k, tmp_f, ALU.add)
    p.act(sin_out, tmp_k, AF.Sin)
    p.ts(tmp_f, tmp_f, math.pi / 2, ALU.add)
    p.ts(tmp_k, tmp_f, math.pi, ALU.is_gt, -TWO_PI, ALU.mult)
    p.tt(tmp_k, tmp_k, tmp_f, ALU.add)
    p.act(cos_out, tmp_k, AF.Sin)


A_FAMS = [
    ("uT", 0, 1024, "FM", None),
    ("qT", 1024, 1024, "FM", None),
    ("kT", 2048, 1024, "FM", None),
    ("v", 3072, 1024, "TM", None),
    ("sg", 4096, 1024, "TM", AF.Silu),
    ("zT", 5120, 3328, "FM", None),
    ("gates", 8448, 6144, "TM", AF.Sigmoid),
]


def load_cast_T(p, src_ap_fn, dst, nkc, Tc, stages):
    for kc in range(nkc):
        st = stages.next()
        p.dma(st, src_ap_fn(kc))
        p.copy(dst[:, kc, :], st, e=("dve" if kc % 2 == 0 else "pool"))


def build_A(Tc):
    p = P()
    KC = 16
    xT = p.dram("xT", [D, Tc])
    w = p.dram("w", [D, N_IN])
    outs = {}
    for name, c0, wd, mode, fn in A_FAMS:
        shp = [wd, Tc] if mode == "FM" else [Tc, wd]
        outs[name] = p.dram(name, shp, kind="ExternalOutput")
    xb = p.sb("xb", [128, KC, Tc], BF16)
    xst = Rot([p.sb(f"xst{i}", [128, Tc]) for i in range(2)])
    wst = [[p.sb(f"wst{b}{h}", [128, 8, 512]) for h in range(2)] for b in range(2)]
    wb = [[p.sb(f"wb{b}{h}", [128, 8, 512], BF16) for h in range(2)] for b in range(2)]
    pss = Rot([p.ps(f"ps{i}", [128, 512]) for i in range(4)])
    ost = Rot([p.sb(f"ost{i}", [128, 512]) for i in range(4)])
    xTv = xT.ap.rearrange("(kc p) t -> p kc t", p=128)
    load_cast_T(p, lambda kc: V(xT, xTv[:, kc, :]), xb, KC, Tc, xst)
    wv = w.ap.rearrange("(kc p) n -> p kc n", p=128)
    chunks = []
    for name, c0, wd, mode, fn in A_FAMS:
        o = 0
        while o < wd:
            cw = min(512, wd - o)
            chunks.append((name, c0 + o, o, cw, mode, fn))
            o += cw

    def load_w(j):
        name, c, o, cw, mode, fn = chunks[j]
        b = j % 2
        for h in range(2):
            p.dma(wst[b][h][:, :, 0:cw], V(w, wv[:, 8 * h:8 * h + 8, c:c + cw]))

    def cast_w(j):
        name, c, o, cw, mode, fn = chunks[j]
        b = j % 2
        p.copy(wb[b][0][:, :, 0:cw], wst[b][0][:, :, 0:cw], e="dve")
        p.copy(wb[b][1][:, :, 0:cw], wst[b][1][:, :, 0:cw], e="pool")

    TH = min(512, Tc)
    load_w(0)
    ev = 0
    for j in range(len(chunks)):
        name, c, o, cw, mode, fn = chunks[j]
        b = j % 2
        if j + 1 < len(chunks):
            load_w(j + 1)
        cast_w(j)
        od = outs[name]
        if mode == "FM":
            for sub in range(cw // 128):
                for th in range(Tc // TH):
                    ps = pss.next()
                    for kc in range(KC):
                        p.mm(ps[:, 0:TH], wb[b][kc // 8][:, kc % 8, sub * 128:(sub + 1) * 128],
                             xb[:, kc, th * TH:(th + 1) * TH], start=(kc == 0), stop=(kc == KC - 1))
                    os_ = ost.next()
                    if ev % 2 == 0:
                        p.copy(os_[:, 0:TH], ps[:, 0:TH], e="dve")
                    else:
                        p.copy(os_[:, 0:TH], ps[:, 0:TH], e="act")
                    ev += 1
                    p.dma(od[o + sub * 128:o + (sub + 1) * 128, th * TH:(th + 1) * TH], os_[:, 0:TH], q=STQ)
        else:
            for tt_ in range(Tc // 128):
                ps = pss.next()
                for kc in range(KC):
                    p.mm(ps[:, 0:cw], xb[:, kc, tt_ * 128:(tt_ + 1) * 128],
                         wb[b][kc // 8][:, kc % 8, 0:cw], start=(kc == 0), stop=(kc == KC - 1))
                os_ = ost.next()
                if fn is not None:
                    p.act(os_[:, 0:cw], ps[:, 0:cw], fn)
                else:
                    p.copy(os_[:, 0:cw], ps[:, 0:cw], e=("dve" if ev % 2 == 0 else "act"))
                    ev += 1
                p.dma(od[tt_ * 128:(tt_ + 1) * 128, o:o + cw], os_[:, 0:cw], q=STQ)
    return p


def layer_norm_tile(p, pre, out_sb, lng, lnb, scr, eps=1e-5):
    st, mv, rs = scr["st"], scr["mv"], scr["rs"]
    for c in range(4):
        p.bn_stats(st[:, c, :], pre[:, c * 512:(c + 1) * 512])
    p.bn_aggr(mv, V(st, st.ap.rearrange("p a b -> p (a b)")))
    p.ts(rs, mv[:, 1:2], eps, ALU.add)
    p.act(rs, rs, AF.Sqrt)
    p.recip(rs, rs)
    p.ts(out_sb, pre, mv[:, 0:1], ALU.subtract, rs[:, 0:1], ALU.mult)
    p.tt(out_sb, out_sb, lng, ALU.mult, e="pool")
    p.tt(out_sb, out_sb, lnb, ALU.add, e="pool")


def ln_scratch(p, tag):
    return {"st": p.sb("ln_st" + tag, [128, 4, 6]), "mv": p.sb("ln_mv" + tag, [128, 2]),
            "rs": p.sb("ln_rs" + tag, [128, 1])}


def build_C1(Tc):
    p = P()
    NT = Tc // 128
    zT = p.dram("zT", [1024, Tc], BF16)
    orT = p.dram("orT", [1024, Tc], BF16)
    owT = p.dram("owT", [1024, Tc], BF16)
    gates = p.dram("gates", [Tc, 6144])
    x = p.dram("x", [Tc, D])
    wglu = p.dram("wglu", [1024, 4096])
    wret = p.dram("wret", [1024, D])
    wrw = p.dram("wrw", [1024, D])
    wo = p.dram("wo", [D, D])
    lng_d = p.dram("lng", [128, D])
    lnb_d = p.dram("lnb", [128, D])
    x1 = p.dram("x1", [Tc, D], kind="ExternalOutput")

    acts = {}
    actbuf = p.sb("actbuf", [128, 24, Tc], BF16)
    for i_, (nm, src) in enumerate((("z", zT), ("or", orT), ("ow", owT))):
        t = actbuf[:, 8 * i_:8 * i_ + 8, :]
        p.dma(t, V(src, src.ap.rearrange("(kc p) t -> p kc t", p=128)))
        acts[nm] = t
    lng = p.sb("lng_s", [128, D])
    lnb = p.sb("lnb_s", [128, D])
    p.dma(lng, lng_d)
    p.dma(lnb, lnb_d)
    merged = p.sb("merged", [128, NT, D])
    mTbuf = p.sb("mTbuf", [128, 16, Tc], BF16) if Tc < 512 else None
    mT = mTbuf if mTbuf is not None else actbuf
    wst = Rot([p.sb(f"wst{i}", [128, 8, 512]) for i in range(2)])
    wbB = Rot([p.sb(f"wbB{i}", [128, 8, 512], BF16) for i in range(3)])
    gst = Rot([p.sb(f"gst{i}", [128, 512]) for i in range(2)])
    tmp = Rot([p.sb(f"tmp{i}", [128, 512]) for i in range(2)])
    sig = Rot([p.sb(f"sig{i}", [128, 512]) for i in range(2)])
    pss = Rot([p.ps(f"ps{i}", [128, 512]) for i in range(6)])
    ident = p.sb("ident", [128, 128], BF16)
    idf = p.dram("identf", [128, 128])
    idst = p.sb("idst", [128, 128])
    p.dma(idst, idf)
    p.copy(ident, idst)
    ce = [0]

    def wload(dst, src_t, r0, nkc, c0, cw):
        sv = src_t.ap.rearrange("(kc p) n -> p kc n", p=128)
        for h in range(0, nkc, 8):
            st = wst.next()
            p.dma(st[:, :, 0:cw], V(src_t, sv[:, r0 + h:r0 + h + 8, c0:c0 + cw]))
            e = "dve" if ce[0] % 2 == 0 else "pool"
            ce[0] += 1
            p.copy(dst[:, h:h + 8, 0:cw], st[:, :, 0:cw], e=e)

    for cc in range(4):
        wa = wbB.next()
        wload(wa, wglu, 0, 8, cc * 512, 512)
        wb_ = wbB.next()
        wload(wb_, wglu, 0, 8, 2048 + cc * 512, 512)
        for t in range(NT):
            psa = pss.next()
            psb = pss.next()
            for kc in range(8):
                p.mm(psa, acts["z"][:, kc, t * 128:(t + 1) * 128], wa[:, kc, :], start=(kc == 0), stop=(kc == 7))
            for kc in range(8):
                p.mm(psb, acts["z"][:, kc, t * 128:(t + 1) * 128], wb_[:, kc, :], start=(kc == 0), stop=(kc == 7))
            g = gst.next()
            p.dma(g, gates[t * 128:(t + 1) * 128, cc * 512:(cc + 1) * 512])
            s = sig.next()
            p.act(s, psb, AF.Sigmoid)
            y = tmp.next()
            p.tt(y, psa, s, ALU.mult)
            p.tt(merged[:, t, cc * 512:(cc + 1) * 512], y, g, ALU.mult, e="pool")
    for bi, (nm, wsrc) in enumerate((("or", wret), ("ow", wrw))):
        for cc in range(4):
            wb_ = wbB.next()
            wload(wb_, wsrc, 0, 8, cc * 512, 512)
            for t in range(NT):
                ps = pss.next()
                for kc in range(8):
                    p.mm(ps, acts[nm][:, kc, t * 128:(t + 1) * 128], wb_[:, kc, :], start=(kc == 0), stop=(kc == 7))
                g = gst.next()
                p.dma(g, gates[t * 128:(t + 1) * 128, (bi + 1) * 2048 + cc * 512:(bi + 1) * 2048 + (cc + 1) * 512])
                y = tmp.next()
                p.tt(y, ps, g, ALU.mult)
                mv_ = merged[:, t, cc * 512:(cc + 1) * 512]
                p.tt(mv_, mv_, y, ALU.add, e="pool")
    mb = Rot([p.sb(f"mb{i}", [128, D], BF16) for i in range(1)])
    pst = Rot([p.ps(f"pst{i}", [128, 4, 128], BF16) for i in range(2)])
    for t in range(NT):
        m = mb.next()
        p.copy(m, merged[:, t, :], e="act")
        for q4 in range(4):
            pt = pst.next()
            for i in range(4):
                kc = q4 * 4 + i
                p.tr(pt[:, i, :], m[:, kc * 128:(kc + 1) * 128], ident)
            p.copy(mT[:, q4 * 4:(q4 + 1) * 4, t * 128:(t + 1) * 128], pt, e=("dve" if q4 % 2 == 0 else "act"))
    for cc in range(4):
        wa0 = wbB.next()
        wload(wa0, wo, 0, 8, cc * 512, 512)
        wa1 = wbB.next()
        wload(wa1, wo, 8, 8, cc * 512, 512)
        for t in range(NT):
            ps = pss.next()
            for kc in range(16):
                p.mm(ps, mT[:, kc, t * 128:(t + 1) * 128], (wa0 if kc < 8 else wa1)[:, kc % 8, :],
                     start=(kc == 0), stop=(kc == 15))
            g = gst.next()
            p.dma(g, x[t * 128:(t + 1) * 128, cc * 512:(cc + 1) * 512])
            p.stt(merged[:, t, cc * 512:(cc + 1) * 512], g, ALPHA, ps, ALU.mult, ALU.add)
    if DEBUG:
        dbg = p.dram("dbg", [Tc, D], kind="ExternalOutput")
        for t in range(NT):
            p.dma(dbg[t * 128:(t + 1) * 128, :], merged[:, t, :])
    scr = ln_scratch(p, "1")
    for t in range(NT):
        o = merged[:, t, :]
        layer_norm_tile(p, merged[:, t, :], o, lng, lnb, scr)
        p.dma(x1[t * 128:(t + 1) * 128, :], o)
    return p


def build_PREP(L):
    p = P()
    CW = 2048
    w = p.dram("w", [128, L])
    wb = p.dram("wb", [128, L], BF16, kind="ExternalOutput")
    st = Rot([p.sb(f"st{i}", [128, CW]) for i in range(3)])
    ob = Rot([p.sb(f"ob{i}", [128, CW], BF16) for i in range(3)])
    engs = ["dve", "pool"]
    for i, c0 in enumerate(range(0, L, CW)):
        cw = min(CW, L - c0)
        s_ = st.next(); o_ = ob.next()
        p.dma(s_[:, 0:cw], w[:, c0:c0 + cw])
        p.copy(o_[:, 0:cw], s_[:, 0:cw], e=engs[i % 2])
        p.dma(wb[:, c0:c0 + cw], o_[:, 0:cw], q="act")
    return p


def build_C2(Tc):
    p = P()
    NT = Tc // 128
    BL = min(256, Tc)
    NB = Tc // BL
    NJ = D_FF // 128
    x1 = p.dram("x1", [Tc, D])
    x1T = p.dram("x1T", [D, Tc + 2])
    wup = p.dram("wupT", [D_FF // 256, 128, 16 * 2 * 2 * 128], BF16)
    wcv = p.dram("wcv", [128, 2 * NJ, 3])
    wdn = p.dram("wdnT", [D_FF // 256, 128, 2 * D], BF16)
    lng_d = p.dram("lng", [128, D])
    lnb_d = p.dram("lnb", [128, D])
    x2 = p.dram("x2", [Tc, D], kind="ExternalOutput")

    lng = p.sb("lng_s", [128, D])
    lnb = p.sb("lnb_s", [128, D])
    p.dma(lng, lng_d)
    p.dma(lnb, lnb_d)
    wc = p.sb("wc", [128, 2 * NJ, 3])
    p.dma(wc, wcv)
    acc = p.sb("acc", [128, NT, D])
    for t in range(NT):
        p.dma(acc[:, t, :], x1[t * 128:(t + 1) * 128, :])
    for t in range(NT):
        p.ts(acc[:, t, :], acc[:, t, :], ALPHA, ALU.mult, e=("dve" if t % 2 == 0 else "pool"))
    xb = p.sb("xb", [128, 16, Tc + 2], BF16)
    xst = Rot([p.sb(f"xst{i}", [128, Tc + 2]) for i in range(1)])
    xv = x1T.ap.rearrange("(kc p) t -> p kc t", p=128)
    load_cast_T(p, lambda kc: V(x1T, xv[:, kc, :]), xb, 16, Tc + 2, xst)

    JG = 2
    ub = Rot([p.sb(f"ub{i}", [128, 16, JG, 2, 128], BF16) for i in range(2)])
    db = Rot([p.sb(f"db{i}", [128, JG, D], BF16) for i in range(2)])
    gT = Rot([p.sb(f"gT{i}", [128, JG, Tc], BF16) for i in range(2)])
    ha = Rot([p.sb(f"ha{i}", [128, BL]) for i in range(2)])
    hb = Rot([p.sb(f"hb{i}", [128, BL]) for i in range(2)])
    sa = Rot([p.sb(f"sa{i}", [128, BL]) for i in range(2)])
    psu = Rot([p.ps(f"psu{i}", [128, 512]) for i in range(4)])
    psd = Rot([p.ps(f"psd{i}", [128, 512]) for i in range(4)])

    def load_group(jg):
        u = ub.next()
        dd = db.next()
        uv = u[:, :, :, :, :].re("p k j h c -> p (k j h c)")
        p.dma(uv[:, 0:4096], wup[jg, :, 0:4096])
        p.dma(uv[:, 4096:8192], wup[jg, :, 4096:8192])
        p.dma(dd[:, :, :].re("p j n -> p (j n)"), wdn[jg, :, :])
        return u, dd

    nxt = load_group(0)
    for jg in range(NJ // JG):
        u, dd = nxt
        if jg + 1 < NJ // JG:
            nxt = load_group(jg + 1)
        g = gT.next()
        for ji in range(JG):
            j = jg * JG + ji
            for blk in range(NB):
                hs = []
                for half in range(2):
                    ps = psu.next()
                    for kc in range(16):
                        p.mm(ps[:, 0:BL + 2], u[:, kc, ji, half, :], xb[:, kc, blk * BL:blk * BL + BL + 2],
                             start=(kc == 0), stop=(kc == 15))
                    h_ = (ha if half == 0 else hb).next()
                    cj = j + half * NJ
                    p.act(h_, ps[:, 2:BL + 2], AF.Copy, scale=wc[:, cj, 2:3])
                    p.stt(h_, ps[:, 1:BL + 1], wc[:, cj, 1:2], h_, ALU.mult, ALU.add)
                    p.stt(h_, ps[:, 0:BL], wc[:, cj, 0:1], h_, ALU.mult, ALU.add)
                    hs.append(h_)
                s = sa.next()
                p.act(s, hs[0], AF.Silu)
                p.tt(g[:, ji, blk * BL:(blk + 1) * BL], s, hs[1], ALU.mult, e="pool")
        for t in range(NT):
            for cc in range(4):
                ps = psd.next()
                for ji in range(JG):
                    p.mm(ps, g[:, ji, t * 128:(t + 1) * 128], dd[:, ji, cc * 512:(cc + 1) * 512],
                         start=(ji == 0), stop=(ji == JG - 1))
                a = acc[:, t, cc * 512:(cc + 1) * 512]
                p.tt(a, a, ps, ALU.add)
    scr = ln_scratch(p, "2")
    for t in range(NT):
        o = acc[:, t, :]
        layer_norm_tile(p, acc[:, t, :], o, lng, lnb, scr)
        p.dma(x2[t * 128:(t + 1) * 128, :], o)
    return p


def build_B1(Tt):
    p = P()
    TB = min(512, Tt)
    NBK = Tt // TB
    uT = p.dram("uT", [128, Tt])
    zT = p.dram("zT", [128, Tt], BF16, kind="ExternalOutput")
    small = {}
    for nm, shp in (("LRr", [128, 64]), ("LIr", [128, 64]), ("LSr", [128, 64]), ("BreT", [128, 64]),
                    ("BimT", [128, 64]), ("LRc", [128, 8]), ("LIc", [128, 8]), ("LSc", [128, 8]),
                    ("CC", [128, 128]), ("CCs", [128, 128]), ("dsk", [128, 1]), ("identf", [128, 128]),
                    ("S12", [128, 128]), ("sgnA", [128, 1]), ("maskg", [128, 8]), ("iota", [128, TB])):
        dt_ = p.dram(nm, shp)
        st = p.sb("s_" + nm, shp)
        p.dma(st, dt_)
        small[nm] = st
    S = small
    n = [0]

    def tmp(shape, dt=F32):
        n[0] += 1
        return p.sb(f"t{n[0]}", shape, dt)

    sh = [128, 64]
    stp = tmp(sh); ang = tmp(sh); sn = tmp(sh); cs = tmp(sh); tf = tmp(sh); ti = tmp(sh, I32); tk = tmp(sh)
    p.act(stp, S["LSr"], AF.Exp)
    p.tt(ang, S["LIr"], stp, ALU.mult)
    sincos(p, ang, sn, cs, tf, ti, tk)
    mag = tmp(sh)
    p.tt(mag, S["LRr"], stp, ALU.mult)
    p.act(mag, mag, AF.Exp)
    abr1 = tmp(sh); abi = tmp(sh)
    p.tt(abr1, mag, cs, ALU.mult)
    p.ts(abr1, abr1, -1.0, ALU.add)
    p.tt(abi, mag, sn, ALU.mult)
    den = tmp(sh); t2 = tmp(sh)
    p.tt(den, S["LRr"], S["LRr"], ALU.mult)
    p.tt(t2, S["LIr"], S["LIr"], ALU.mult)
    p.tt(den, den, t2, ALU.add)
    p.recip(den, den)
    fre = tmp(sh); fim = tmp(sh)
    p.tt(fre, abr1, S["LRr"], ALU.mult)
    p.tt(t2, abi, S["LIr"], ALU.mult)
    p.tt(fre, fre, t2, ALU.add)
    p.tt(fre, fre, den, ALU.mult)
    p.tt(fim, abi, S["LRr"], ALU.mult)
    p.tt(t2, abr1, S["LIr"], ALU.mult)
    p.tt(fim, fim, t2, ALU.subtract)
    p.tt(fim, fim, den, ALU.mult)
    bbr = tmp(sh); bbi = tmp(sh); nbbr = tmp(sh)
    p.tt(bbr, fre, S["BreT"], ALU.mult)
    p.tt(t2, fim, S["BimT"], ALU.mult)
    p.tt(bbr, bbr, t2, ALU.subtract)
    p.tt(bbi, fre, S["BimT"], ALU.mult)
    p.tt(t2, fim, S["BreT"], ALU.mult)
    p.tt(bbi, bbi, t2, ALU.add)
    p.ts(nbbr, bbr, -1.0, ALU.mult)
    BT = p.sb("BT", [128, 8, 128], BF16)
    BTs = p.sb("BTs", [128, 8, 128], BF16)
    for g in range(8):
        mg = S["maskg"][:, g:g + 1]
        p.ts(BT[:, g, 0:64], bbr, mg, ALU.mult)
        p.ts(BT[:, g, 64:128], bbi, mg, ALU.mult)
        p.ts(BTs[:, g, 0:64], bbi, mg, ALU.mult)
        p.ts(BTs[:, g, 64:128], nbbr, mg, ALU.mult)
    C1a = p.sb("C1a", [128, 8, 128], BF16)
    C2a = p.sb("C2a", [128, 8, 128], BF16)
    p.memset(C1a, 0.0)
    p.memset(C2a, 0.0)
    for g in range(8):
        p.ts(C1a[:, g, 16 * g:16 * g + 16], S["CC"][:, 16 * g:16 * g + 16], S["sgnA"][:, 0:1], ALU.mult)
        p.ts(C2a[:, g, 16 * g:16 * g + 16], S["CCs"][:, 16 * g:16 * g + 16], -1.0, ALU.mult)
    sh8 = [128, 8]
    stc = tmp(sh8); rho = tmp(sh8); theta = tmp(sh8)
    p.act(stc, S["LSc"], AF.Exp)
    p.tt(rho, S["LRc"], stc, ALU.mult)
    p.act(rho, rho, AF.Exp)
    p.tt(theta, S["LIc"], stc, ALU.mult)
    phi = tmp(sh8); sph = tmp(sh8); cph = tmp(sh8); f8 = tmp(sh8); i8 = tmp(sh8, I32); k8 = tmp(sh8)
    p.ts(phi, theta, float(TB), ALU.mult)
    sincos(p, phi, sph, cph, f8, i8, k8)
    p.ts(sph, sph, S["sgnA"][:, 0:1], ALU.mult)
    ROT = p.sb("ROT", [128, 8, 128])
    for g in range(8):
        p.ts(ROT[:, g, :], S["identf"], cph[:, g:g + 1], ALU.mult)
        p.stt(ROT[:, g, :], S["S12"], sph[:, g:g + 1], ROT[:, g, :], ALU.mult, ALU.add)
    SIN = p.sb("SIN", [128, 8, TB])
    COS = p.sb("COS", [128, 8, TB])
    RHO = p.sb("RHO", [128, 8, TB])
    shb = [128, TB]
    angb = tmp(shb); fb = tmp(shb); ib = tmp(shb, I32); kb = tmp(shb)
    for g in range(8):
        p.ts(angb, S["iota"], theta[:, g:g + 1], ALU.mult)
        sincos(p, angb, SIN[:, g, :], COS[:, g, :], fb, ib, kb)
        p.ts(RHO[:, g, :], S["iota"], 0.0, ALU.mult, rho[:, g:g + 1], ALU.add)

    ust = Rot([p.sb(f"ust{i}", [128, TB]) for i in range(2)])
    ubf = Rot([p.sb(f"ubf{i}", [128, TB], BF16) for i in range(2)])
    psa = Rot([p.ps(f"psa{i}", [128, TB]) for i in range(2)])
    psb = Rot([p.ps(f"psb{i}", [128, TB]) for i in range(2)])
    psy = Rot([p.ps(f"psy{i}", [128, TB]) for i in range(2)])
    psr = p.ps("psr", [128, 8])
    v1 = Rot([p.sb(f"v1{i}", [128, TB]) for i in range(2)])
    v2 = Rot([p.sb(f"v2{i}", [128, TB]) for i in range(2)])
    vv = Rot([p.sb(f"vv{i}", [128, TB]) for i in range(2)])
    shat = Rot([p.sb(f"shat{i}", [128, TB]) for i in range(3)])
    w1 = Rot([p.sb(f"w1{i}", [128, TB], BF16) for i in range(2)])
    w2 = Rot([p.sb(f"w2{i}", [128, TB], BF16) for i in range(2)])
    last = p.sb("last", [128, 8])
    init = p.sb("init", [128, 8])
    yb = Rot([p.sb(f"yb{i}", [128, TB]) for i in range(2)])
    gt = Rot([p.sb(f"gt{i}", [128, TB]) for i in range(2)])
    zo = Rot([p.sb(f"zo{i}", [128, TB], BF16) for i in range(2)])
    vv3 = Rot([p.sb(f"vv3{i}", [128, TB]) for i in range(3)])
    w13 = Rot([p.sb(f"w13{i}", [128, TB], BF16) for i in range(3)])
    w23 = Rot([p.sb(f"w23{i}", [128, TB], BF16) for i in range(3)])
    items = [(b, g) for b in range(NBK) for g in range(8)]
    blk = {}
    stA = {}
    stB = {}

    def block_begin(b):
        uf = ust.next()
        p.dma(uf, uT[:, b * TB:(b + 1) * TB])
        ub = ubf.next()
        p.copy(ub, uf, e="act")
        blk[b] = {"uf": uf, "ub": ub, "py": psy.next()}

    def stage_A(b, g):
        if g == 0:
            block_begin(b)
        ub = blk[b]["ub"]
        pa = psa.next()
        pb = psb.next()
        p.mm(pa, BT[:, g, :], ub)
        p.mm(pb, BTs[:, g, :], ub)
        a1 = v1.next(); a2 = v2.next(); av = vv3.next()
        p.tt(a1, pa, COS[:, g, :], ALU.mult)
        p.tt(a2, pb, SIN[:, g, :], ALU.mult)
        p.tt(av, a1, a2, ALU.add, e="pool")
        stA[(b, g)] = av

    def stage_B(b, g):
        av = stA.pop((b, g))
        if b > 0 and g == 0:
            for g_ in range(8):
                p.mm(psr[:, g_:g_ + 1], ROT[:, g_, :], last[:, g_:g_ + 1])
            p.copy(init, psr)
        sh_ = shat.next()
        p.scan(sh_, RHO[:, g, :], av, (0.0 if b == 0 else init[:, g:g + 1]), ALU.mult, ALU.add)
        p.copy(last[:, g:g + 1], sh_[:, TB - 1:TB], e="act")
        b1 = w13.next(); b2 = w23.next()
        p.tt(b1, sh_, COS[:, g, :], ALU.mult, e="pool")
        p.tt(b2, sh_, SIN[:, g, :], ALU.mult)
        stB[(b, g)] = (b1, b2)

    def stage_C(b, g):
        b1, b2 = stB.pop((b, g))
        py = blk[b]["py"]
        p.mm(py, C1a[:, g, :], b1, start=(g == 0), stop=False)
        p.mm(py, C2a[:, g, :], b2, start=False, stop=(g == 7))
        if g == 7:
            uf = blk[b]["uf"]
            y = yb.next()
            p.stt(y, uf, S["dsk"][:, 0:1], py, ALU.mult, ALU.add)
            t_ = gt.next()
            p.tt(t_, y, y, ALU.mult, e="pool")
            p.ts(t_, t_, 0.044715, ALU.mult, 1.0, ALU.add, e="pool")
            p.tt(t_, t_, y, ALU.mult, e="pool")
            p.act(t_, t_, AF.Sigmoid, scale=1.5957691216057308)
            z = zo.next()
            p.tt(z, y, t_, ALU.mult)
            p.dma(zT[:, b * TB:(b + 1) * TB], z, q="act")
            del blk[b]

    n_it = len(items)
    for i in range(-1, n_it + 1):
        if 0 <= i + 1 < n_it:
            stage_A(*items[i + 1])
        if 0 <= i < n_it:
            stage_B(*items[i])
        if 0 <= i - 1 < n_it:
            stage_C(*items[i - 1])
    return p


def host_B1_inputs(inp, l, c, uT_c, TB):
    gs = slice(8 * c, 8 * c + 8)
    lr = inp["ssm_lambda_re"][l][gs]; li = inp["ssm_lambda_im"][l][gs]; ls = inp["ssm_log_step"][l][gs]
    bre = inp["ssm_b_re"][l][gs]; bim = inp["ssm_b_im"][l][gs]
    cre = inp["ssm_c_re"][l][gs]; cim = inp["ssm_c_im"][l][gs]
    f = np.float32
    rep16 = lambda a: np.ascontiguousarray(np.repeat(a, 16, axis=0)).astype(f)
    m = {"uT": uT_c}
    m["LRr"] = rep16(lr); m["LIr"] = rep16(li)
    m["LSr"] = rep16(np.broadcast_to(ls[:, None], (8, 64)))
    m["BreT"] = np.ascontiguousarray(bre.transpose(0, 2, 1).reshape(128, 64)).astype(f)
    m["BimT"] = np.ascontiguousarray(bim.transpose(0, 2, 1).reshape(128, 64)).astype(f)
    m["LRc"] = np.ascontiguousarray(np.concatenate([lr.T, lr.T], 0)).astype(f)
    m["LIc"] = np.ascontiguousarray(np.concatenate([li.T, li.T], 0)).astype(f)
    m["LSc"] = np.ascontiguousarray(np.broadcast_to(ls[None, :], (128, 8))).astype(f)
    creT = cre.transpose(2, 0, 1).reshape(64, 128); cimT = cim.transpose(2, 0, 1).reshape(64, 128)
    m["CC"] = np.ascontiguousarray(np.concatenate([creT, cimT], 0)).astype(f)
    m["CCs"] = np.ascontiguousarray(np.concatenate([cimT, creT], 0)).astype(f)
    m["dsk"] = np.ascontiguousarray(inp["ssm_d"][l][128 * c:128 * c + 128].reshape(128, 1)).astype(f)
    m["identf"] = np.eye(128, dtype=f)
    s12 = np.zeros((128, 128), f)
    for k in range(64):
        s12[k, k + 64] = 1.0
        s12[k + 64, k] = 1.0
    m["S12"] = s12
    sg = np.ones((128, 1), f); sg[64:] = -1.0
    m["sgnA"] = sg
    mg = np.zeros((128, 8), f)
    for g in range(8):
        mg[16 * g:16 * g + 16, g] = 1.0
    m["maskg"] = mg
    m["iota"] = np.ascontiguousarray(np.broadcast_to(np.arange(TB, dtype=f)[None, :], (128, TB)))
    return m


def build_B2(Tt):
    p = P()
    TB = min(512, Tt)
    NBK = Tt // TB
    NCH = TB // 128
    qT = p.dram("qT", [128, Tt])
    kT = p.dram("kT", [128, Tt])
    vv = p.dram("v", [Tt, 128])
    sg = p.dram("sg", [Tt, 128])
    pos = p.dram("pos", [128, Tt], I32)
    o_out = p.dram("o", [Tt, 128], BF16, kind="ExternalOutput")
    S = {}
    for nm, shp in (("invf", [128, 1]), ("PERM", [128, 128]), ("DmT", [128, 128]), ("qd", [128, 1]),
                    ("kd", [128, 1]), ("cdec", [128, 1]), ("NG", [128, 128]), ("NB", [128, 128]),
                    ("identf", [128, 128])):
        dt_ = p.dram(nm, shp)
        st = p.sb("s_" + nm, shp)
        p.dma(st, dt_)
        S[nm] = st
    identb = p.sb("identb", [128, 128], BF16)
    p.copy(identb, S["identf"])
    state = p.sb("state", [128, 128])
    state_bf = p.sb("state_bf", [128, 128], BF16)
    p.memset(state, 0.0)
    p.memset(state_bf, 0.0)
    shb = [128, TB]
    qf = Rot([p.sb(f"qf{i}", shb) for i in range(2)])
    kf = Rot([p.sb(f"kf{i}", shb) for i in range(2)])
    posi = p.sb("posi", shb, I32)
    ang = p.sb("ang", shb); fb = p.sb("fb", shb); ib = p.sb("ib", shb, I32); kb = p.sb("kb", shb)
    SINt = p.sb("SINt", shb); COSt = p.sb("COSt", shb)
    r1 = p.sb("r1", shb); r2 = p.sb("r2", shb)
    rqT = Rot([p.sb(f"rqT{i}", shb, BF16) for i in range(2)])
    rkT = Rot([p.sb(f"rkT{i}", shb, BF16) for i in range(2)])
    pP = Rot([p.ps(f"pP{i}", shb) for i in range(1)])
    psc = p.ps("psc", [128, 128]); po2 = p.ps("po2", [128, 128])
    po1 = Rot([p.ps(f"po1{i}", [128, 128]) for i in range(2)])
    ptr = p.ps("ptr", [128, 128], BF16)
    pkv = Rot([p.ps(f"pkv{i}", [128, 128]) for i in range(2)])
    vf = Rot([p.sb(f"vf{i}", [128, 128]) for i in range(2)])
    vb = Rot([p.sb(f"vb{i}", [128, 128], BF16) for i in range(2)])
    sgf = Rot([p.sb(f"sgf{i}", [128, 128]) for i in range(4)])
    scm = Rot([p.sb(f"scm{i}", [128, 128], BF16) for i in range(2)])
    insb = Rot([p.sb(f"insb{i}", [128, 128]) for i in range(2)])
    osb = Rot([p.sb(f"osb{i}", [128, 128]) for i in range(3)])
    kdb = Rot([p.sb(f"kdb{i}", [128, 128], BF16) for i in range(2)])
    st6r = Rot([p.sb(f"st6{i}", [128, 6]) for i in range(3)])
    mvr = Rot([p.sb(f"mv{i}", [128, 2]) for i in range(3)])
    rsr = Rot([p.sb(f"rs{i}", [128, 1]) for i in range(3)])
    onb = Rot([p.sb(f"onb{i}", [128, 128]) for i in range(2)])
    oo = Rot([p.sb(f"oo{i}", [128, 128], BF16) for i in range(2)])
    rot_blk = {}
    live = {}
    live3 = {}

    def rotary(b):
        cs_ = slice(b * TB, (b + 1) * TB)
        q_ = qf.next(); k_ = kf.next()
        p.dma(q_, qT[:, cs_])
        p.dma(k_, kT[:, cs_])
        p.dma(posi, pos[:, cs_])
        p.copy(ang, posi)
        p.ts(ang, ang, S["invf"][:, 0:1], ALU.mult)
        sincos(p, ang, SINt, COSt, fb, ib, kb)
        rots = []
        for src, dstrot in ((q_, rqT), (k_, rkT)):
            pp = pP.next()
            p.mm(pp, S["PERM"], src)
            p.tt(r1, src, COSt, ALU.mult, e="pool")
            p.tt(r2, pp, SINt, ALU.mult)
            dd = dstrot.next()
            p.tt(dd, r1, r2, ALU.add)
            rots.append(dd)
        rot_blk[b] = rots

    def s1(b, n_):
        if n_ == 0:
            rotary(b)
        rq, rk = rot_blk[b]
        c0 = n_ * 128
        rows = slice(b * TB + c0, b * TB + c0 + 128)
        v_ = vf.next(); s_ = sgf.next()
        p.dma(v_, vv[rows, :])
        p.dma(s_, sg[rows, :])
        vb_ = vb.next()
        p.copy(vb_, v_, e="act")
        p.mm(psc, rk[:, c0:c0 + 128], rq[:, c0:c0 + 128])
        sc = scm.next()
        p.tt(sc, psc, S["DmT"], ALU.mult)
        p1 = po1.next()
        p.mm(p1, sc, vb_)
        i_ = insb.next()
        p.copy(i_, p1, e="act")
        p.tr(ptr, rk[:, c0:c0 + 128], identb)
        kd_ = kdb.next()
        p.ts(kd_, ptr, S["kd"][:, 0:1], ALU.mult)
        pk = pkv.next()
        p.mm(pk, kd_, vb_)
        live[(b, n_)] = (rq, c0, rows, s_, i_, pk)

    def s2(b, n_):
        rq, c0, rows, s_, i_, pk = live.pop((b, n_))
        p.mm(po2, rq[:, c0:c0 + 128], state_bf)
        o_ = osb.next()
        p.stt(o_, po2, S["qd"][:, 0:1], i_, ALU.mult, ALU.add)
        p.stt(state, state, S["cdec"][:, 0:1], pk, ALU.mult, ALU.add)
        p.copy(state_bf, state, e="act")
        st6 = st6r.next(); mv = mvr.next(); rs = rsr.next()
        p.bn_stats(st6, o_)
        p.bn_aggr(mv, st6)
        p.ts(rs, mv[:, 1:2], 1e-5, ALU.add)
        p.act(rs, rs, AF.Sqrt)
        live3[(b, n_)] = (rows, s_, o_, mv, rs)
        if n_ == NCH - 1:
            del rot_blk[b]

    def s3(b, n_):
        rows, s_, o_, mv, rs = live3.pop((b, n_))
        p.recip(rs, rs)
        on = onb.next()
        p.ts(on, o_, mv[:, 0:1], ALU.subtract, rs[:, 0:1], ALU.mult)
        p.tt(on, on, S["NG"], ALU.mult)
        p.tt(on, on, S["NB"], ALU.add)
        ob = oo.next()
        p.tt(ob, on, s_, ALU.mult)
        p.dma(o_out[rows, :], ob, q="pool")

    items = [(b, n_) for b in range(NBK) for n_ in range(NCH)]
    s1(*items[0])
    for i in range(len(items)):
        if i + 1 < len(items):
            s1(*items[i + 1])
        s2(*items[i])
        if i >= 1:
            s3(*items[i - 1])
    s3(*items[-1])
    return p


def host_B2_inputs(inp, l, c, qT_c, kT_c, v_c, sg_c, pos_T):
    f = np.float32
    h = c
    lg = np.log1p(-np.exp2(-5.0 - h))
    idx = np.arange(128, dtype=np.float64)
    rel = idx[None, :] - idx[:, None]
    sc = 128 ** -0.5
    DmT = np.where(rel >= 0, np.exp(lg * np.maximum(rel, 0.0)), 0.0) * sc
    perm = np.zeros((128, 128), f)
    for m_ in range(64):
        perm[m_ + 64, m_] = -1.0
        perm[m_, m_ + 64] = 1.0
    invf = (10000.0 ** (-np.arange(64, dtype=f) / 64)).astype(f)
    rep = lambda v_: np.ascontiguousarray(np.broadcast_to(v_[None, :], (128, v_.shape[0]))).astype(f)
    return {
        "qT": qT_c, "kT": kT_c, "v": v_c, "sg": sg_c, "pos": pos_T,
        "invf": np.concatenate([invf, invf]).reshape(128, 1).astype(f),
        "PERM": perm, "DmT": DmT.astype(f),
        "qd": np.exp(lg * (idx + 1.0)).reshape(128, 1).astype(f),
        "kd": (np.exp(lg * (127.0 - idx)) * sc).reshape(128, 1).astype(f),
        "cdec": np.full((128, 1), np.exp(lg * 128.0), f),
        "NG": rep(inp["ret_norm_g"][l][128 * c:128 * c + 128]),
        "NB": rep(inp["ret_norm_b"][l][128 * c:128 * c + 128]),
        "identf": np.eye(128, dtype=f),
    }


def build_B3(Tt):
    p = P()
    TB = min(512, Tt)
    NBK = Tt // TB
    CH = 64
    NCH = TB // CH
    NG_ = NCH // 4
    zin = {nm: p.dram(nm, [128, Tt + 1]) for nm in ("zr", "zk", "zv", "zl1", "zl2")}
    o_out = p.dram("o", [Tt, 128], BF16, kind="ExternalOutput")
    S = {}
    for nm, shp in (("MU", [128, 5]), ("PV", [128, 5]), ("w2c", [128, 128]), ("g2c", [128, 128]),
                    ("NG", [64, 128]), ("NB", [64, 128]), ("bones", [128, 128]), ("hsel", [128, 2]),
                    ("cmask", [128, TB]), ("MK1", [64, 512]), ("MK2", [64, 512]), ("MK3", [64, 256]),
                    ("identf", [128, 128]), ("bdmask", [128, 128])):
        dt_ = p.dram(nm, shp)
        st = p.sb("s_" + nm, shp)
        p.dma(st, dt_)
        S[nm] = st
    w2b = p.sb("w2b", [128, 128], BF16); p.copy(w2b, S["w2c"])
    g2b = p.sb("g2b", [128, 128], BF16); p.copy(g2b, S["g2c"])
    hselb = p.sb("hselb", [128, 2], BF16); p.copy(hselb, S["hsel"])
    identb = p.sb("identb", [128, 128], BF16); p.copy(identb, S["identf"])
    bk = [p.ps(f"bank{i}", [128, 512]) for i in range(8)]
    H = p.sb("H", [128, 128]); Hb = p.sb("Hb", [128, 128], BF16)
    p.memset(H, 0.0); p.memset(Hb, 0.0)
    shb = [128, TB]
    n = [0]

    def tmp(shape=shb, dt=F32):
        n[0] += 1
        return p.sb(f"t{n[0]}", shape, dt)

    zst = {nm: Rot([p.sb(f"zst_{nm}{i}", [128, TB + 1]) for i in range(2)]) for nm in zin}
    dsh = tmp(); r_ = tmp(); k_ = tmp(); v_ = tmp(); l1 = tmp(); l2 = tmp()
    l1b = tmp(dt=BF16); sgb = tmp([128, TB + 64], BF16)
    p.memset(sgb, 0.0)
    ld = tmp(); a_ = tmp(); cs = tmp(); e0 = tmp(); e1 = tmp(); e2 = tmp(); tq = tmp()
    kk = tmp(); rn = tmp(); kmod = tmp(); bv = tmp()
    rkr = tmp([128, TB + 64], BF16); vbf = tmp([128, TB + 64], BF16)
    for t_ in (rkr, vbf):
        p.memset(t_, 0.0)
    Mm = p.sb("Mm", [64, NCH + 1, 2, 64], BF16)
    p.memset(Mm, 0.0)
    MMs = [Rot([p.sb(f"MM{g}{i}", [64, 9, 128], BF16) for i in range(2)]) for g in range(NG_)]
    Tts = [Rot([p.sb(f"Tt{g}{i}", [64, 10, 64], BF16) for i in range(2)]) for g in range(NG_)]
    D1 = []
    for i3 in range(3):
        d_ = {"KR": p.sb(f"KR{i3}", [128, NCH + 1, 2, CH], BF16),
              "Bh": p.sb(f"Bh{i3}", [128, TB + 64], BF16), "Kh": p.sb(f"Kh{i3}", [128, TB + 64], BF16),
              "Vb": p.sb(f"Vb{i3}", [64, NCH, 128], BF16), "Vf": p.sb(f"Vf{i3}", [64, NCH, 128]),
              "BhT": p.sb(f"BhT{i3}", [64, NCH, 128], BF16), "KhT": p.sb(f"KhT{i3}", [64, NCH, 128], BF16),
              "Gtm": p.sb(f"Gtm{i3}", [64, NCH, 128]), "BS": p.sb(f"BS{i3}", [64, NCH * 2]),
              "PC": p.sb(f"PC{i3}", [128, NCH])}
        for nm in ("KR", "Bh", "Kh"):
            p.memset(d_[nm], 0.0)
        D1.append(d_)
    D2 = []
    for par in range(2):
        d_ = {"A1": p.sb(f"A1{par}", [64, NCH + 1, 2, 128], BF16),
              "A2": p.sb(f"A2{par}", [64, NCH + 1, 2, 128], BF16),
              "Tf": [p.sb(f"Tf{par}{g}", [64, 10, 64], BF16) for g in range(NG_)]}
        for nm in ("A1", "A2"):
            p.memset(d_[nm], 0.0)
        for g in range(NG_):
            p.memset(d_["Tf"][g], 0.0)
        D2.append(d_)
    W0n = Rot([p.sb(f"W0n{i}", [64, 128], BF16) for i in range(2)])
    Ub = Rot([p.sb(f"Ub{i}", [64, 128], BF16) for i in range(2)])
    Ysb = p.sb("Ysb", [64, NCH, 128])
    tH = p.sb("tH", [128, 128])
    xc = p.sb("xc", [64, NCH, 128]); sq_ = p.sb("sq_", [64, NCH, 128])
    s16 = p.sb("s16", [64, NCH * 2]); v16 = p.sb("v16", [64, NCH * 2])
    osb = Rot([p.sb(f"osb{i}", [64, NCH, 128], BF16) for i in range(2)])
    mk1 = S["MK1"][:, :].re("p (c t) -> p c t", t=128)
    mk2 = S["MK2"][:, :].re("p (c t) -> p c t", t=128)
    mk3 = S["MK3"][:, :].re("p (c t) -> p c t", t=64)

    def c3(vw):
        return vw.re("p (c t) -> p c t", t=CH)

    def pre1(b):
        D_ = D1[b % 3]
        KR, Bh, Kh = D_["KR"], D_["Bh"], D_["Kh"]
        zs = {}
        for nm in zin:
            st = zst[nm].next()
            p.dma(st, zin[nm][:, b * TB:b * TB + TB + 1])
            zs[nm] = st
        for i, (nm, dst) in enumerate((("zr", r_), ("zk", k_), ("zv", v_), ("zl1", l1), ("zl2", l2))):
            st = zs[nm]
            p.tt(dsh, st[:, 0:TB], st[:, 1:TB + 1], ALU.subtract, e="pool")
            p.stt(dst, dsh, S["MU"][:, i:i + 1], st[:, 1:TB + 1], ALU.mult, ALU.add)
        yield
        p.act(l1b[0:64, :], l1[0:64, :], AF.Tanh)
        p.copy(l1b[64:128, :], l1[64:128, :], e="act")
        p.act(sgb[:, 0:TB], l2, AF.Sigmoid)
        p.mm(bk[4], w2b[0:64, :], l1b[0:64, :])
        p.mm(bk[5], w2b[64:128, :], l1b[64:128, :])
        p.act(ld, bk[4], AF.Sigmoid, bias=S["PV"][:, 0:1])
        p.ts(ld, ld, -EXPM05, ALU.mult)
        p.act(a_, bk[5], AF.Sigmoid, bias=S["PV"][:, 1:2])
        p.scan(cs, S["cmask"], ld, 0.0, ALU.mult, ALU.add)
        p.act(e1, cs, AF.Exp)
        p.act(e2, cs, AF.Exp, scale=-1.0)
        p.tt(tq, cs, ld, ALU.subtract, e="pool")
        p.act(e0, tq, AF.Exp)
        p.copy(D_["PC"], e1[:, :].re("p (c t) -> p c t", t=CH)[:, :, CH - 1], e="pool")
        yield
        p.ts(kk, k_, S["PV"][:, 2:3], ALU.mult)
        p.tt(tq, kk, kk, ALU.mult, e="pool")
        p.mm(bk[6], S["bones"], tq)
        p.ts(rn, bk[6], 1e-24, ALU.max)
        p.act(rn, rn, AF.Sqrt)
        p.recip(rn, rn)
        p.tt(kk, kk, rn, ALU.mult)
        p.ts(tq, a_, -1.0, ALU.add)
        p.ts(tq, tq, S["PV"][:, 3:4], ALU.mult)
        p.stt(kmod, tq, 1.0, k_, ALU.add, ALU.mult)
        p.tt(bv, a_, kk, ALU.mult, e="pool")
        yield
        p.tt(KR[:, 0:NCH, 0, :], c3(kk[:, :]), c3(e0[:, :]), ALU.mult)
        p.tt(KR[:, 0:NCH, 1, :], c3(r_[:, :]), c3(e1[:, :]), ALU.mult)
        p.tt(Bh[:, 0:TB], bv, e2, ALU.mult, e="pool")
        p.tt(Kh[:, 0:TB], kmod, e2, ALU.mult, e="pool")
        p.ts(tq, r_, S["PV"][:, 4:5], ALU.mult)
        p.tt(rkr[:, 0:TB], tq, kmod, ALU.mult)
        p.copy(vbf[:, 0:TB], v_, e="act")
        yield
        for g4 in range(NG_):
            sl4 = slice(g4 * 4, (g4 + 1) * 4)
            for i in range(4):
                c = g4 * 4 + i
                p.mm(bk[4][:, i * 128:(i + 1) * 128], vbf[:, c * CH:c * CH + 128], identb)
            p.copy(D_["Vf"][:, sl4, :], bk[4][0:64, 0:512].re("p (c e) -> p c e", e=128), e="act")
            p.copy(D_["Vb"][:, sl4, :], D_["Vf"][:, sl4, :])
            for i in range(4):
                c = g4 * 4 + i
                p.mm(bk[5][:, i * 128:(i + 1) * 128], sgb[:, c * CH:c * CH + 128], g2b)
            p.copy(D_["Gtm"][:, sl4, :], bk[5][0:64, :].re("p (c e) -> p c e", e=128), e="act")
            for bi_, (src, dstT) in enumerate(((Bh, D_["BhT"]), (Kh, D_["KhT"]))):
                bk_ = bk[6 - 2 * bi_]
                for i in range(4):
                    c = g4 * 4 + i
                    p.mm(bk_[:, i * 128:(i + 1) * 128], src[:, c * CH:c * CH + 128], identb)
                p.copy(dstT[:, sl4, :], bk_[0:64, 0:512].re("p (c e) -> p c e", e=128))
            yield
        for c in range(NCH):
            p.mm(bk[5][:, 2 * c:2 * c + 2], rkr[:, c * CH:c * CH + 128], hselb)
        p.copy(D_["BS"], bk[5][0:64, 0:2 * NCH], e="act")
        yield

    def pre2(b):
        D_ = D1[b % 3]
        KR, Bh, Kh = D_["KR"], D_["Bh"], D_["Kh"]
        E_ = D2[b % 2]
        A1, A2 = E_["A1"], E_["A2"]
        for g4 in range(NG_):
            sl4 = slice(g4 * 4, (g4 + 1) * 4)
            for ci in range(4):
                c = g4 * 4 + ci
                for h in range(2):
                    ho = 64 * h
                    krv = KR[ho:ho + 64, c, :, :].re("p a t -> p (a t)")
                    p.mm(bk[0 + h][:, ci * 128:(ci + 1) * 128], Bh[ho:ho + 64, c * CH:c * CH + 128], krv)
                    p.mm(bk[2 + h][:, ci * 128:(ci + 1) * 128], Kh[ho:ho + 64, c * CH:c * CH + 128], krv)
            for h in range(2):
                p.tt(A1[:, sl4, h, :], bk[0 + h][0:64, :].re("p (c t) -> p c t", t=128), mk1, ALU.mult)
                p.tt(A2[:, sl4, h, :], bk[2 + h][0:64, :].re("p (c t) -> p c t", t=128), mk2, ALU.mult)
            for ci in range(4):
                c = g4 * 4 + ci
                for h in range(2):
                    ho = 64 * h
                    krv = KR[ho:ho + 64, c, :, :].re("p a t -> p (a t)")
                    p.mm(bk[0 + h][:, ci * 64:(ci + 1) * 64], krv, Bh[ho:ho + 64, c * CH:(c + 1) * CH])
            for h in range(2):
                p.tt(Mm[:, sl4, h, :], bk[0 + h][0:64, 0:256].re("p (c t) -> p c t", t=64), mk3, ALU.mult)
            yield
        pairs_g = [[(G * 4 + ci, h) for ci in range(4) for h in range(2)] for G in range(NG_)]
        Tcur = []
        curM = []
        curMt = []
        Mflat = Mm[:, :, :, :].re("p c h t -> p (c h t)")
        for G in range(NG_):
            Tt = Tts[G].next()
            for pi, (c, h) in enumerate(pairs_g[G]):
                p.tt(Tt[:, pi, :], A1[:, c, h, 0:64], identb[0:64, 0:64], ALU.add, e="pool")
            Tcur.append(Tt)
            curM.append([Mflat[:, (2 * c + h) * 64:(2 * c + h) * 64 + 128] for (c, h) in pairs_g[G]])
            curMt.append([A1[:, c, h, 0:128] for (c, h) in pairs_g[G]])
        yield
        for lev in range(1, 6):
            last = (lev == 5)
            for G in range(NG_):
                sqb = [bk[(2 * G) % 4], bk[(2 * G + 1) % 4]]
                for pi in range(8):
                    b_ = sqb[pi // 4]
                    col = (pi % 4) * 128
                    p.mm(b_[:, col:col + 64], curMt[G][pi], curM[G][pi][:, 0:64])
                    if not last:
                        p.mm(b_[:, col + 64:col + 128], curM[G][pi], curMt[G][pi][:, 0:64])
            for G in range(NG_):
                sqb = [bk[(2 * G) % 4], bk[(2 * G + 1) % 4]]
                MMn = MMs[G].next()
                for hb in range(2):
                    p.copy(MMn[:, 4 * hb:4 * hb + 4, :], sqb[hb][0:64, :].re("p (a t) -> p a t", t=128),
                           e=("act" if hb == 0 else "dve"))
                MMf = MMn[:, :, :].re("p a t -> p (a t)")
                curM[G] = [MMf[:, pi * 128:pi * 128 + 128] for pi in range(8)]
                curMt[G] = [MMf[:, pi * 128 + 64:pi * 128 + 192] for pi in range(8)]
            for G in range(NG_):
                pacc = bk[(2 * G) % 4]
                for pi in range(8):
                    p.mm(pacc[:, pi * 64:(pi + 1) * 64], curM[G][pi], Tcur[G][:, pi, :])
            for G in range(NG_):
                pacc = bk[(2 * G) % 4]
                Tn = E_["Tf"][G] if last else Tts[G].next()
                p.tt(Tn[:, 0:8, :], Tcur[G][:, 0:8, :], pacc[0:64, :].re("p (a t) -> p a t", t=64), ALU.add)
                Tcur[G] = Tn
            yield

    def chain(b):
        D_ = D1[b % 3]
        E_ = D2[b % 2]
        KR, Vb, BhT, KhT = D_["KR"], D_["Vb"], D_["BhT"], D_["KhT"]
        A1, A2 = E_["A1"], E_["A2"]
        krf = KR[:, :, :, :].re("p c a t -> p (c a t)")
        a1f = A1[:, :, :, :].re("p c h t -> p (c h t)")
        a2f = A2[:, :, :, :].re("p c h t -> p (c h t)")
        pw, pu, py, ph = bk[7][:, 0:128], bk[7][:, 128:256], bk[7][:, 256:384], bk[7][:, 384:512]
        for c in range(NCH):
            Tt = E_["Tf"][c // 4]
            p.mm(pw[:, 0:128], krf[:, c * 128:c * 128 + 128], Hb, start=True, stop=False)
            for h in range(2):
                p.mm(pw[:, 64 * h:64 * h + 64], A2[:, c, h, 0:128], Vb[:, c, 64 * h:64 * h + 64],
                     start=False, stop=(h == 1))
            wn = W0n.next()
            p.act(wn, pw[0:64, 0:128], AF.Copy, scale=-1.0)
            for h in range(2):
                pi = (c % 4) * 2 + h
                p.mm(pu[:, 64 * h:64 * h + 64], Tt[:, pi:pi + 2, :].re("p a t -> p (a t)"), wn[:, 64 * h:64 * h + 64])
            ub = Ub.next()
            p.copy(ub, pu[0:64, 0:128])
            p.mm(py[:, 0:128], krf[:, c * 128 + 64:c * 128 + 192], Hb, start=True, stop=False)
            for h in range(2):
                sl = slice(64 * h, 64 * h + 64)
                o_ = (2 * c + h) * 128 + 64
                p.mm(py[:, sl], a1f[:, o_:o_ + 128], ub[:, sl], start=False, stop=False)
                p.mm(py[:, sl], a2f[:, o_:o_ + 128], Vb[:, c, sl], start=False, stop=(h == 1))
            p.copy(Ysb[:, c, :], py[0:64, 0:128], e="act")
            p.mm(ph[:, 0:128], BhT[:, c, :], ub, start=True, stop=False)
            p.mm(ph[:, 0:128], KhT[:, c, :], Vb[:, c, :], start=False, stop=True)
            p.tt(tH, ph[:, 0:128], S["bdmask"], ALU.mult)
            p.tt(tH, tH, H, ALU.add)
            pc = D_["PC"][:, c:c + 1]
            p.ts(H, tH, pc, ALU.mult)
            p.act(Hb, tH, AF.Copy, scale=pc)
            yield
        y4 = Ysb[:, :, :].re("p c (h e) -> p (c h) e", e=64)
        p.reduce(s16, y4, ALU.add)
        p.ts(s16, s16, 1.0 / 64, ALU.mult)
        x4 = xc[:, :, :].re("p c (h e) -> p (c h) e", e=64)
        q4 = sq_[:, :, :].re("p c (h e) -> p (c h) e", e=64)
        p.tt(x4, y4, s16[:, :].re("p (a o) -> p a o", o=1).bc([64, NCH * 2, 64]), ALU.subtract)
        p.tt(q4, x4, x4, ALU.mult, e="pool")
        p.reduce(v16, q4, ALU.add)
        p.ts(v16, v16, 1.0 / 64, ALU.mult, 64e-5, ALU.add)
        p.act(v16, v16, AF.Sqrt)
        p.recip(v16, v16)
        p.tt(x4, x4, v16[:, :].re("p (a o) -> p a o", o=1).bc([64, NCH * 2, 64]), ALU.mult)
        ngb = S["NG"][:, :].re("p (o e) -> p o e", o=1).bc([64, NCH, 128])
        nbb = S["NB"][:, :].re("p (o e) -> p o e", o=1).bc([64, NCH, 128])
        p.tt(xc, xc, ngb, ALU.mult)
        p.tt(xc, xc, nbb, ALU.add)
        vf4 = D_["Vf"][:, :, :].re("p c (h e) -> p (c h) e", e=64)
        p.tt(q4, vf4, D_["BS"][:, :].re("p (a o) -> p a o", o=1).bc([64, NCH * 2, 64]), ALU.mult)
        p.tt(xc, xc, sq_, ALU.add)
        ob = osb.next()
        p.tt(ob, xc, D_["Gtm"], ALU.mult)
        p.dma(V(o_out, o_out.ap[b * TB:(b + 1) * TB, :].rearrange("(c t) e -> t c e", t=CH)), ob)
        yield

    def drain(gens):
        gens = [g for g in gens if g is not None]
        while gens:
            for g in list(gens):
                try:
                    next(g)
                except StopIteration:
                    gens.remove(g)

    drain([pre1(0)])
    drain([pre2(0), pre1(1) if NBK > 1 else None])
    for b in range(NBK):
        drain([chain(b), pre2(b + 1) if b + 1 < NBK else None, pre1(b + 2) if b + 2 < NBK else None])
    return p


def host_B3_inputs(inp, l, c, zT_full, TB):
    f = np.float32
    Tt = zT_full.shape[1]
    ch = slice(128 * c, 128 * c + 128)

    def pad(a):
        return np.ascontiguousarray(np.concatenate([np.zeros((a.shape[0], 1), f), a], axis=1))

    mu = inp["rwkv_mu"][l]
    m = {"zr": pad(zT_full[0:1024][ch]), "zk": pad(zT_full[1024:2048][ch]), "zv": pad(zT_full[2048:3072][ch]),
         "zl1": pad(zT_full[3072:3200]), "zl2": pad(zT_full[3200:3328])}
    m["MU"] = np.ascontiguousarray(np.stack([mu[0:1024][ch], mu[1024:2048][ch], mu[2048:3072][ch],
                                             mu[3072:3200], mu[3200:3328]], axis=1)).astype(f)
    m["PV"] = np.ascontiguousarray(np.stack([inp["rwkv_w0"][l][ch], inp["rwkv_a0"][l][ch], inp["rwkv_k_k"][l][ch],
                                             inp["rwkv_k_a"][l][ch], inp["rwkv_r_k"][l][ch]], axis=1)).astype(f)
    m["w2c"] = np.ascontiguousarray(np.concatenate([inp["rwkv_w2"][l][:, ch], inp["rwkv_a2"][l][:, ch]], 0)).astype(f)
    m["g2c"] = np.ascontiguousarray(inp["rwkv_g2"][l][:, ch]).astype(f)
    rep = lambda v_: np.ascontiguousarray(np.broadcast_to(v_[None, :], (64, v_.shape[0]))).astype(f)
    m["NG"] = rep(inp["rwkv_norm_g"][l][ch]); m["NB"] = rep(inp["rwkv_norm_b"][l][ch])
    bo = np.zeros((128, 128), f); bo[0:64, 0:64] = 1.0; bo[64:, 64:] = 1.0
    m["bones"] = bo
    m["bdmask"] = bo.copy()
    hs = np.zeros((128, 2), f); hs[0:64, 0] = 1.0; hs[64:, 1] = 1.0
    m["hsel"] = hs
    cm = np.ones((128, TB), f); cm[:, ::64] = 0.0
    m["cmask"] = cm
    s = np.arange(64)[:, None]; t = np.arange(64)[None, :]
    strict_st = (t > s).astype(f); incl_st = (t >= s).astype(f)
    strict_ts = (t < s).astype(f)
    m["MK1"] = np.ascontiguousarray(np.tile(np.concatenate([-strict_st, incl_st], 1), (1, 4)))
    m["MK2"] = np.ascontiguousarray(np.tile(np.concatenate([strict_st, incl_st], 1), (1, 4)))
    m["MK3"] = np.ascontiguousarray(np.tile(-strict_ts, (1, 4)))
    m["identf"] = np.eye(128, dtype=f)
    return m


_PROGS = {}


def _prog(name, builder, *a):
    key = (name,) + a
    if key not in _PROGS:
        _PROGS[key] = builder(*a)
    return _PROGS[key]


def _run(p, maps):
    return p.run(maps).results


def kernel(**inp):
    inp = {k: np.asarray(v) for k, v in inp.items()}
    x = np.ascontiguousarray(inp["x"][0]).astype(np.float32)
    Tt = x.shape[0]
    Tc = Tt // NCORE
    TB = min(512, Tt)
    f = np.float32
    posT = np.ascontiguousarray(np.broadcast_to(inp["positions"][0][None, :], (128, Tt))).astype(np.int32)
    rep = lambda v_: np.ascontiguousarray(np.broadcast_to(v_[None, :], (128, v_.shape[0]))).astype(f)
    sl = lambda c: slice(c * Tc, (c + 1) * Tc)
    hs = lambda c: slice(128 * c, 128 * c + 128)
    eye = np.eye(128, dtype=f)
    flat = np.concatenate([np.concatenate([inp["ffn_up"][l].ravel(), inp["ffn_down"][l].ravel()])
                           for l in range(DEPTH)])
    Lp = flat.size // (NCORE * 128)
    rp = _run(build_PREP(Lp), [{"w": np.ascontiguousarray(flat[c * 128 * Lp:(c + 1) * 128 * Lp].reshape(128, Lp))}
                               for c in range(NCORE)])
    flat_b = np.concatenate([rp[c]["wb"].reshape(-1) for c in range(NCORE)])
    del flat
    nup, ndn = D * 2 * D_FF, D_FF * D
    wupT, wdnT = [], []
    for l in range(DEPTH):
        o_ = l * (nup + ndn)
        up = flat_b[o_:o_ + nup].reshape(16, 128, 2, D_FF // 256, 2, 128)
        wupT.append(np.ascontiguousarray(up.transpose(3, 1, 0, 4, 2, 5)).reshape(D_FF // 256, 128, 8192))
        dn = flat_b[o_ + nup:o_ + nup + ndn].reshape(D_FF // 256, 2, 128, D)
        wdnT.append(np.ascontiguousarray(dn.transpose(0, 2, 1, 3)).reshape(D_FF // 256, 128, 2 * D))
    for l in range(DEPTH):
        pa = build_A(Tc)
        ra = _run(pa, [{"xT": np.ascontiguousarray(x[sl(c)].T), "w": inp["w_in"][l]} for c in range(NCORE)])
        cat_fm = lambda nm: np.concatenate([ra[c][nm] for c in range(NCORE)], axis=1)
        cat_tm = lambda nm: np.concatenate([ra[c][nm] for c in range(NCORE)], axis=0)
        uT, qT, kT, zT = cat_fm("uT"), cat_fm("qT"), cat_fm("kT"), cat_fm("zT")
        v, sg = cat_tm("v"), cat_tm("sg")
        gates = [ra[c]["gates"] for c in range(NCORE)]
        r1 = _run(build_B1(Tt), [host_B1_inputs(inp, l, c, np.ascontiguousarray(uT[hs(c)]), TB) for c in range(NCORE)])
        zs = np.concatenate([r1[c]["zT"] for c in range(NCORE)], axis=0)
        r2 = _run(build_B2(Tt), [host_B2_inputs(inp, l, c, np.ascontiguousarray(qT[hs(c)]), np.ascontiguousarray(kT[hs(c)]),
                                                np.ascontiguousarray(v[:, hs(c)]), np.ascontiguousarray(sg[:, hs(c)]), posT)
                                 for c in range(NCORE)])
        orT = np.ascontiguousarray(np.concatenate([r2[c]["o"] for c in range(NCORE)], axis=1).T)
        r3 = _run(build_B3(Tt), [host_B3_inputs(inp, l, c, zT, TB) for c in range(NCORE)])
        owT = np.ascontiguousarray(np.concatenate([r3[c]["o"] for c in range(NCORE)], axis=1).T)
        rc1 = _run(build_C1(Tc), [{
            "zT": np.ascontiguousarray(zs[:, sl(c)]), "orT": np.ascontiguousarray(orT[:, sl(c)]),
            "owT": np.ascontiguousarray(owT[:, sl(c)]), "gates": gates[c], "x": np.ascontiguousarray(x[sl(c)]),
            "wglu": inp["ssm_glu"][l], "wret": inp["ret_out"][l], "wrw": inp["rwkv_out"][l], "wo": inp["w_o"][l],
            "lng": rep(inp["ln1_g"][l]), "lnb": rep(inp["ln1_b"][l]), "identf": eye} for c in range(NCORE)])
        x1 = np.concatenate([rc1[c]["x1"] for c in range(NCORE)], axis=0)
        x1p = np.concatenate([np.zeros((2, D), f), x1], axis=0)
        wcv = np.ascontiguousarray(inp["ffn_conv"][l].T.reshape(88, 128, 3).transpose(1, 0, 2)).astype(f)
        rc2 = _run(build_C2(Tc), [{
            "x1": np.ascontiguousarray(x1[sl(c)]), "x1T": np.ascontiguousarray(x1p[c * Tc:(c + 1) * Tc + 2].T),
            "wupT": wupT[l], "wcv": wcv, "wdnT": wdnT[l],
            "lng": rep(inp["ln2_g"][l]), "lnb": rep(inp["ln2_b"][l])} for c in range(NCORE)])
        x = np.concatenate([rc2[c]["x2"] for c in range(NCORE)], axis=0)
    return x[None].astype(np.float32)
```

```python
import math
import numpy as np
import ml_dtypes
import concourse.bass as bass
import concourse.mybir as mybir
from concourse.bass_utils import run_bass_kernel_spmd

F32 = mybir.dt.float32
BF16 = mybir.dt.bfloat16
I32 = mybir.dt.int32
AF = mybir.ActivationFunctionType
ALU = mybir.AluOpType
AX = mybir.AxisListType
NPBF = ml_dtypes.bfloat16

D = 2048
NCORE = 8
DEPTH = 4
N_IN = 14592
D_FF = 5632
ALPHA = (2.0 * DEPTH) ** 0.25
EXPM05 = math.exp(-0.5)
TWO_PI = 2.0 * math.pi
DEBUG = False
SUB = 9
KNOB = 0
STQ = "act"


class T:
    __slots__ = ("ap", "name", "w", "r", "psum")

    def __init__(self, ap, name, psum=False):
        self.ap = ap
        self.name = name
        self.w = None
        self.r = []
        self.psum = psum

    def __getitem__(self, idx):
        return V(self, self.ap[idx])


class V:
    __slots__ = ("t", "ap")

    def __init__(self, t, ap):
        self.t = t
        self.ap = ap

    def __getitem__(self, idx):
        return V(self.t, self.ap[idx])

    def re(self, pat, **kw):
        return V(self.t, self.ap.rearrange(pat, **kw))

    def bc(self, shape):
        return V(self.t, self.ap.to_broadcast(list(shape)))


def _tv(x):
    if isinstance(x, T):
        return x, x.ap
    if isinstance(x, V):
        return x.t, x.ap
    return None, x


class P:
    NRING = 8

    def __init__(self):
        self.nc = bass.Bass("TRN2", target_bir_lowering=False)
        nc = self.nc
        self.eng = {"pe": nc.tensor, "dve": nc.vector, "act": nc.scalar,
                    "pool": nc.gpsimd, "sp": nc.sync}
        self.sems = {}
        self.cnt = {}
        self.seen = {e: {} for e in self.eng}
        self._ctx = []
        for e in self.eng:
            self.sems[e] = self._sem("s_" + e)
            self.cnt[e] = 0
        self.ring = {}
        self.ringn = {}
        for q in ("sp", "pool", "act"):
            self.ring[q] = [self._sem(f"d_{q}{i}") for i in range(self.NRING)]
            self.ringn[q] = 0
        self.n_inst = 0
        self._rr = 0

    def _sem(self, name):
        cm = self.nc.semaphore(name)
        s = cm.__enter__()
        self._ctx.append(cm)
        return s

    def sb(self, name, shape, dt=F32):
        cm = self.nc.sbuf_tensor(name, list(shape), dt)
        t = cm.__enter__()
        self._ctx.append(cm)
        return T(t[:], name)

    def ps(self, name, shape, dt=F32):
        cm = self.nc.psum_tensor(name, list(shape), dt)
        t = cm.__enter__()
        self._ctx.append(cm)
        return T(t[:], name, psum=True)

    def dram(self, name, shape, dt=F32, kind="ExternalInput"):
        t = self.nc.dram_tensor(name, list(shape), dt, kind=kind)
        return T(t.ap(), name)

    def _semobj(self, key):
        if isinstance(key, tuple):
            return self.ring[key[0]][key[1]]
        return self.sems[key]

    def _deps(self, e, reads, writes):
        need = {}

        def add(tok):
            k, v = tok[0], tok[1]
            if need.get(k, 0) < v:
                need[k] = v

        for t in reads:
            if t is None or t.w is None:
                continue
            if t.w[0] == e and e == "pe":
                continue
            add(t.w)
        for t in reads:
            if t is not None and t.psum:
                for rd in t.r:
                    if rd[0] != e:
                        add(rd)
        for t in writes:
            if t is None:
                continue
            for rd in t.r:
                if rd[0] == e:
                    continue
                add(rd)
            if t.w is not None and t.w[0] != e:
                add(t.w)
        eng = self.eng[e]
        seen = self.seen[e]
        for k, v in need.items():
            if seen.get(k, 0) < v:
                eng.wait_ge(self._semobj(k), v)
                seen[k] = v

    def _done(self, tok, reads, writes):
        for t in reads:
            if t is not None:
                t.r.append(tok)
                if len(t.r) > 24:
                    best = {}
                    for rd in t.r:
                        if best.get(rd[0], 0) < rd[1]:
                            best[rd[0]] = rd[1]
                    t.r = [(k, v) for k, v in best.items()]
        for t in writes:
            if t is not None:
                t.w = tok
                t.r = []

    def op(self, e, fn, reads, writes):
        self._deps(e, reads, writes)
        ins = fn()
        self.cnt[e] += 1
        ins.then_inc(self.sems[e], 1)
        self._done((e, self.cnt[e]), reads, writes)
        self.n_inst += 1
        return ins

    def dma(self, out, in_, q="sp", **kw):
        to, apo = _tv(out)
        ti, api = _tv(in_)
        n = self.ringn[q]
        slot = n % self.NRING
        rnd = n // self.NRING
        key = (q, slot)
        eng = self.eng[q]
        if rnd > 0 and self.seen[q].get(key, 0) < 16 * rnd:
            eng.wait_ge(self.ring[q][slot], 16 * rnd)
            self.seen[q][key] = 16 * rnd
        self._deps(q, [ti], [to])
        ins = eng.dma_start(out=apo, in_=api, **kw)
        ins.then_inc(self.ring[q][slot], 16)
        self.ringn[q] = n + 1
        self._done((key, 16 * (rnd + 1)), [ti], [to])
        self.n_inst += 1
        return ins

    def mm(self, out, lhsT, rhs, start=True, stop=True, **kw):
        to, apo = _tv(out)
        tl, apl = _tv(lhsT)
        tr, apr = _tv(rhs)
        return self.op("pe", lambda: self.nc.tensor.matmul(apo, apl, apr, start=start, stop=stop, **kw),
                       [tl, tr], [to])

    def tr(self, out, in_, ident):
        to, apo = _tv(out)
        ti, api = _tv(in_)
        td, apd = _tv(ident)
        return self.op("pe", lambda: self.nc.tensor.transpose(apo, api, apd), [ti, td], [to])

    def act(self, out, in_, func, bias=None, scale=None):
        to, apo = _tv(out)
        ti, api = _tv(in_)
        rd = [ti]
        k = {}
        if bias is not None:
            tb, apb = _tv(bias)
            rd.append(tb)
            k["bias"] = apb
        if scale is not None:
            ts_, aps = _tv(scale)
            rd.append(ts_)
            k["scale"] = aps
        return self.op("act", lambda: self.nc.scalar.activation(apo, api, func, **k), rd, [to])

    def tt(self, out, a, b, op, e="dve"):
        to, apo = _tv(out)
        ta, apa = _tv(a)
        tb, apb = _tv(b)
        eng = self.eng[e]
        return self.op(e, lambda: eng.tensor_tensor(apo, apa, apb, op), [ta, tb], [to])

    def ts(self, out, a, s1, op0, s2=None, op1=None, e="dve"):
        to, apo = _tv(out)
        ta, apa = _tv(a)
        t1, ap1 = _tv(s1)
        t2, ap2 = _tv(s2)
        eng = self.eng[e]
        if op1 is None:
            return self.op(e, lambda: eng.tensor_scalar(apo, apa, ap1, None, op0), [ta, t1], [to])
        return self.op(e, lambda: eng.tensor_scalar(apo, apa, ap1, ap2, op0, op1), [ta, t1, t2], [to])

    def stt(self, out, a, s, b, op0, op1):
        to, apo = _tv(out)
        ta, apa = _tv(a)
        ts_, aps = _tv(s)
        tb, apb = _tv(b)
        return self.op("dve", lambda: self.nc.vector.scalar_tensor_tensor(apo, apa, aps, apb, op0, op1),
                       [ta, ts_, tb], [to])

    def copy(self, out, in_, e="dve"):
        to, apo = _tv(out)
        ti, api = _tv(in_)
        if e == "act":
            return self.op("act", lambda: self.nc.scalar.copy(apo, api), [ti], [to])
        eng = self.eng[e]
        return self.op(e, lambda: eng.tensor_copy(apo, api), [ti], [to])

    def memset(self, out, val, e="dve"):
        to, apo = _tv(out)
        eng = self.eng[e]
        return self.op(e, lambda: eng.memset(apo, val), [], [to])

    def scan(self, out, d0, d1, init, op0, op1):
        to, apo = _tv(out)
        t0, ap0 = _tv(d0)
        t1, ap1 = _tv(d1)
        ti, api = _tv(init)
        return self.op("dve", lambda: self.nc.vector.tensor_tensor_scan(apo, ap0, ap1, api, op0, op1),
                       [t0, t1, ti], [to])

    def recip(self, out, in_):
        to, apo = _tv(out)
        ti, api = _tv(in_)
        return self.op("dve", lambda: self.nc.vector.reciprocal(apo, api), [ti], [to])

    def reduce(self, out, in_, op, axis=AX.X):
        to, apo = _tv(out)
        ti, api = _tv(in_)
        return self.op("dve", lambda: self.nc.vector.tensor_reduce(apo, api, axis, op), [ti], [to])

    def bn_stats(self, out, in_):
        to, apo = _tv(out)
        ti, api = _tv(in_)
        return self.op("dve", lambda: self.nc.vector.bn_stats(apo, api), [ti], [to])

    def bn_aggr(self, out, in_):
        to, apo = _tv(out)
        ti, api = _tv(in_)
        return self.op("dve", lambda: self.nc.vector.bn_aggr(apo, api), [ti], [to])

    def finish(self):
        sp = self.nc.sync
        for e in self.eng:
            if e != "sp" and self.cnt[e] > 0 and self.seen["sp"].get(e, 0) < self.cnt[e]:
                sp.wait_ge(self.sems[e], self.cnt[e])
                self.seen["sp"][e] = self.cnt[e]
        for q in self.ring:
            n = self.ringn[q]
            for slot in range(min(n, self.NRING)):
                last_rnd = (n - 1 - slot) // self.NRING
                v = 16 * (last_rnd + 1)
                key = (q, slot)
                if self.seen["sp"].get(key, 0) < v:
                    sp.wait_ge(self.ring[q][slot], v)
                    self.seen["sp"][key] = v

    def run(self, in_maps):
        self.finish()
        return run_bass_kernel_spmd(self.nc, in_maps, core_ids=list(range(NCORE)))


class Rot:
    def __init__(self, items):
        self.items = items
        self.i = 0

    def next(self):
        x = self.items[self.i % len(self.items)]
        self.i += 1
        return x


def sincos(p, ang, sin_out, cos_out, tmp_f, tmp_i, tmp_k):
    C1 = 6.28125
    C2 = TWO_PI - 6.28125
    p.ts(tmp_f, ang, 1.0 / TWO_PI, ALU.mult)
    p.copy(tmp_i, tmp_f)
    p.copy(tmp_k, tmp_i)
    p.stt(tmp_f, tmp_k, -C1, ang, ALU.mult, ALU.add)
    p.stt(tmp_f, tmp_k, -C2, tmp_f, ALU.mult, ALU.add)
    p.ts(tmp_k, tmp_f, math.pi, ALU.is_gt, -TWO_PI, ALU.mult)
    p.tt(tmp_k, tmp_k, tmp_f, ALU.add)
    p.act(sin_out, tmp_k, AF.Sin)
    p.ts(tmp_f, tmp_f, math.pi / 2, ALU.add)
    p.ts(tmp_k, tmp_f, math.pi, ALU.is_gt, -TWO_PI, ALU.mult)
    p.tt(tmp_k, tmp_k, tmp_f, ALU.add)
    p.act(cos_out, tmp_k, AF.Sin)


A_FAMS = [
    ("uT", 0, 1024, "FM", None),
    ("qT", 1024, 1024, "FM", None),
    ("kT", 2048, 1024, "FM", None),
    ("v", 3072, 1024, "TM", None),
    ("sg", 4096, 1024, "TM", AF.Silu),
    ("zT", 5120, 3328, "FM", None),
    ("gates", 8448, 6144, "TM", AF.Sigmoid),
]


def load_cast_T(p, src_ap_fn, dst, nkc, Tc, stages):
    for kc in range(nkc):
        st = stages.next()
        p.dma(st, src_ap_fn(kc))
        p.copy(dst[:, kc, :], st, e=("dve" if kc % 2 == 0 else "pool"))


def build_A(Tc):
    p = P()
    KC = 16
    xT = p.dram("xT", [D, Tc])
    w = p.dram("w", [D, N_IN])
    outs = {}
    for name, c0, wd, mode, fn in A_FAMS:
        shp = [wd, Tc] if mode == "FM" else [Tc, wd]
        outs[name] = p.dram(name, shp, kind="ExternalOutput")
    xb = p.sb("xb", [128, KC, Tc], BF16)
    xst = Rot([p.sb(f"xst{i}", [128, Tc]) for i in range(2)])
    wst = [[p.sb(f"wst{b}{h}", [128, 8, 512]) for h in range(2)] for b in range(2)]
    wb = [[p.sb(f"wb{b}{h}", [128, 8, 512], BF16) for h in range(2)] for b in range(2)]
    pss = Rot([p.ps(f"ps{i}", [128, 512]) for i in range(4)])
    ost = Rot([p.sb(f"ost{i}", [128, 512]) for i in range(4)])
    xTv = xT.ap.rearrange("(kc p) t -> p kc t", p=128)
    load_cast_T(p, lambda kc: V(xT, xTv[:, kc, :]), xb, KC, Tc, xst)
    wv = w.ap.rearrange("(kc p) n -> p kc n", p=128)
    chunks = []
    for name, c0, wd, mode, fn in A_FAMS:
        o = 0
        while o < wd:
            cw = min(512, wd - o)
            chunks.append((name, c0 + o, o, cw, mode, fn))
            o += cw

    def load_w(j):
        name, c, o, cw, mode, fn = chunks[j]
        b = j % 2
        for h in range(2):
            p.dma(wst[b][h][:, :, 0:cw], V(w, wv[:, 8 * h:8 * h + 8, c:c + cw]))

    def cast_w(j):
        name, c, o, cw, mode, fn = chunks[j]
        b = j % 2
        p.copy(wb[b][0][:, :, 0:cw], wst[b][0][:, :, 0:cw], e="dve")
        p.copy(wb[b][1][:, :, 0:cw], wst[b][1][:, :, 0:cw], e="pool")

    TH = min(512, Tc)
    load_w(0)
    ev = 0
    for j in range(len(chunks)):
        name, c, o, cw, mode, fn = chunks[j]
        b = j % 2
        if j + 1 < len(chunks):
            load_w(j + 1)
        cast_w(j)
        od = outs[name]
        if mode == "FM":
            for sub in range(cw // 128):
                for th in range(Tc // TH):
                    ps = pss.next()
                    for kc in range(KC):
                        p.mm(ps[:, 0:TH], wb[b][kc // 8][:, kc % 8, sub * 128:(sub + 1) * 128],
                             xb[:, kc, th * TH:(th + 1) * TH], start=(kc == 0), stop=(kc == KC - 1))
                    os_ = ost.next()
                    if ev % 2 == 0:
                        p.copy(os_[:, 0:TH], ps[:, 0:TH], e="dve")
                    else:
                        p.copy(os_[:, 0:TH], ps[:, 0:TH], e="act")
                    ev += 1
                    p.dma(od[o + sub * 128:o + (sub + 1) * 128, th * TH:(th + 1) * TH], os_[:, 0:TH], q=STQ)
        else:
            for tt_ in range(Tc // 128):
                ps = pss.next()
                for kc in range(KC):
                    p.mm(ps[:, 0:cw], xb[:, kc, tt_ * 128:(tt_ + 1) * 128],
                         wb[b][kc // 8][:, kc % 8, 0:cw], start=(kc == 0), stop=(kc == KC - 1))
                os_ = ost.next()
                if fn is not None:
                    p.act(os_[:, 0:cw], ps[:, 0:cw], fn)
                else:
                    p.copy(os_[:, 0:cw], ps[:, 0:cw], e=("dve" if ev % 2 == 0 else "act"))
                    ev += 1
                p.dma(od[tt_ * 128:(tt_ + 1) * 128, o:o + cw], os_[:, 0:cw], q=STQ)
    return p


def layer_norm_tile(p, pre, out_sb, lng, lnb, scr, eps=1e-5):
    st, mv, rs = scr["st"], scr["mv"], scr["rs"]
    for c in range(4):
        p.bn_stats(st[:, c, :], pre[:, c * 512:(c + 1) * 512])
    p.bn_aggr(mv, V(st, st.ap.rearrange("p a b -> p (a b)")))
    p.ts(rs, mv[:, 1:2], eps, ALU.add)
    p.act(rs, rs, AF.Sqrt)
    p.recip(rs, rs)
    p.ts(out_sb, pre, mv[:, 0:1], ALU.subtract, rs[:, 0:1], ALU.mult)
    p.tt(out_sb, out_sb, lng, ALU.mult)
    p.tt(out_sb, out_sb, lnb, ALU.add, e="pool")


def ln_scratch(p, tag):
    return {"st": p.sb("ln_st" + tag, [128, 4, 6]), "mv": p.sb("ln_mv" + tag, [128, 2]),
            "rs": p.sb("ln_rs" + tag, [128, 1])}


def build_C1(Tc):
    p = P()
    NT = Tc // 128
    zT = p.dram("zT", [1024, Tc], BF16)
    orT = p.dram("orT", [1024, Tc], BF16)
    owT = p.dram("owT", [1024, Tc], BF16)
    gates = p.dram("gates", [Tc, 6144])
    x = p.dram("x", [Tc, D])
    wglu = p.dram("wglu", [1024, 4096])
    wret = p.dram("wret", [1024, D])
    wrw = p.dram("wrw", [1024, D])
    wo = p.dram("wo", [D, D])
    lng_d = p.dram("lng", [128, D])
    lnb_d = p.dram("lnb", [128, D])
    x1 = p.dram("x1", [Tc, D], kind="ExternalOutput")

    acts = {}
    actbuf = p.sb("actbuf", [128, 24, Tc], BF16)
    for i_, (nm, src) in enumerate((("z", zT), ("or", orT), ("ow", owT))):
        t = actbuf[:, 8 * i_:8 * i_ + 8, :]
        p.dma(t, V(src, src.ap.rearrange("(kc p) t -> p kc t", p=128)))
        acts[nm] = t
    lng = p.sb("lng_s", [128, D])
    lnb = p.sb("lnb_s", [128, D])
    p.dma(lng, lng_d)
    p.dma(lnb, lnb_d)
    merged = p.sb("merged", [128, NT, D])
    mTbuf = p.sb("mTbuf", [128, 16, Tc], BF16) if Tc < 512 else None
    mT = mTbuf if mTbuf is not None else actbuf
    wst = Rot([p.sb(f"wst{i}", [128, 8, 512]) for i in range(2)])
    wbB = Rot([p.sb(f"wbB{i}", [128, 8, 512], BF16) for i in range(3)])
    gst = Rot([p.sb(f"gst{i}", [128, 512]) for i in range(2)])
    tmp = Rot([p.sb(f"tmp{i}", [128, 512]) for i in range(2)])
    sig = Rot([p.sb(f"sig{i}", [128, 512]) for i in range(2)])
    pss = Rot([p.ps(f"ps{i}", [128, 512]) for i in range(6)])
    ident = p.sb("ident", [128, 128], BF16)
    idf = p.dram("identf", [128, 128])
    idst = p.sb("idst", [128, 128])
    p.dma(idst, idf)
    p.copy(ident, idst)
    ce = [0]

    def wload(dst, src_t, r0, nkc, c0, cw):
        sv = src_t.ap.rearrange("(kc p) n -> p kc n", p=128)
        for h in range(0, nkc, 8):
            st = wst.next()
            p.dma(st[:, :, 0:cw], V(src_t, sv[:, r0 + h:r0 + h + 8, c0:c0 + cw]))
            e = "dve" if ce[0] % 2 == 0 else "pool"
            ce[0] += 1
            p.copy(dst[:, h:h + 8, 0:cw], st[:, :, 0:cw], e=e)

    for cc in range(4):
        wa = wbB.next()
        wload(wa, wglu, 0, 8, cc * 512, 512)
        wb_ = wbB.next()
        wload(wb_, wglu, 0, 8, 2048 + cc * 512, 512)
        for t in range(NT):
            psa = pss.next()
            psb = pss.next()
            for kc in range(8):
                p.mm(psa, acts["z"][:, kc, t * 128:(t + 1) * 128], wa[:, kc, :], start=(kc == 0), stop=(kc == 7))
            for kc in range(8):
                p.mm(psb, acts["z"][:, kc, t * 128:(t + 1) * 128], wb_[:, kc, :], start=(kc == 0), stop=(kc == 7))
            g = gst.next()
            p.dma(g, gates[t * 128:(t + 1) * 128, cc * 512:(cc + 1) * 512])
            s = sig.next()
            p.act(s, psb, AF.Sigmoid)
            y = tmp.next()
            p.tt(y, psa, s, ALU.mult)
            p.tt(merged[:, t, cc * 512:(cc + 1) * 512], y, g, ALU.mult, e="pool")
    for bi, (nm, wsrc) in enumerate((("or", wret), ("ow", wrw))):
        for cc in range(4):
            wb_ = wbB.next()
            wload(wb_, wsrc, 0, 8, cc * 512, 512)
            for t in range(NT):
                ps = pss.next()
                for kc in range(8):
                    p.mm(ps, acts[nm][:, kc, t * 128:(t + 1) * 128], wb_[:, kc, :], start=(kc == 0), stop=(kc == 7))
                g = gst.next()
                p.dma(g, gates[t * 128:(t + 1) * 128, (bi + 1) * 2048 + cc * 512:(bi + 1) * 2048 + (cc + 1) * 512])
                y = tmp.next()
                p.tt(y, ps, g, ALU.mult)
                mv_ = merged[:, t, cc * 512:(cc + 1) * 512]
                p.tt(mv_, mv_, y, ALU.add, e="pool")
    mb = Rot([p.sb(f"mb{i}", [128, D], BF16) for i in range(1)])
    pst = Rot([p.ps(f"pst{i}", [128, 4, 128], BF16) for i in range(2)])
    for t in range(NT):
        m = mb.next()
        p.copy(m, merged[:, t, :], e="act")
        for q4 in range(4):
            pt = pst.next()
            for i in range(4):
                kc = q4 * 4 + i
                p.tr(pt[:, i, :], m[:, kc * 128:(kc + 1) * 128], ident)
            p.copy(mT[:, q4 * 4:(q4 + 1) * 4, t * 128:(t + 1) * 128], pt, e=("dve" if q4 % 2 == 0 else "act"))
    for cc in range(4):
        wa0 = wbB.next()
        wload(wa0, wo, 0, 8, cc * 512, 512)
        wa1 = wbB.next()
        wload(wa1, wo, 8, 8, cc * 512, 512)
        for t in range(NT):
            ps = pss.next()
            for kc in range(16):
                p.mm(ps, mT[:, kc, t * 128:(t + 1) * 128], (wa0 if kc < 8 else wa1)[:, kc % 8, :],
                     start=(kc == 0), stop=(kc == 15))
            g = gst.next()
            p.dma(g, x[t * 128:(t + 1) * 128, cc * 512:(cc + 1) * 512])
            p.stt(merged[:, t, cc * 512:(cc + 1) * 512], g, ALPHA, ps, ALU.mult, ALU.add)
    if DEBUG:
        dbg = p.dram("dbg", [Tc, D], kind="ExternalOutput")
        for t in range(NT):
            p.dma(dbg[t * 128:(t + 1) * 128, :], merged[:, t, :])
    scr = ln_scratch(p, "1")
    for t in range(NT):
        o = merged[:, t, :]
        layer_norm_tile(p, merged[:, t, :], o, lng, lnb, scr)
        p.dma(x1[t * 128:(t + 1) * 128, :], o)
    return p


def build_PREP(L):
    p = P()
    CW = 2048
    w = p.dram("w", [128, L])
    wb = p.dram("wb", [128, L], BF16, kind="ExternalOutput")
    st = Rot([p.sb(f"st{i}", [128, CW]) for i in range(3)])
    ob = Rot([p.sb(f"ob{i}", [128, CW], BF16) for i in range(3)])
    engs = ["dve", "pool"]
    for i, c0 in enumerate(range(0, L, CW)):
        cw = min(CW, L - c0)
        s_ = st.next(); o_ = ob.next()
        p.dma(s_[:, 0:cw], w[:, c0:c0 + cw])
        p.copy(o_[:, 0:cw], s_[:, 0:cw], e=engs[i % 2])
        p.dma(wb[:, c0:c0 + cw], o_[:, 0:cw], q="act")
    return p


def build_C2(Tc):
    p = P()
    NT = Tc // 128
    BL = min(256, Tc)
    NB = Tc // BL
    NJ = D_FF // 128
    x1 = p.dram("x1", [Tc, D])
    x1T = p.dram("x1T", [D, Tc + 2])
    wup = p.dram("wupT", [D_FF // 256, 128, 16 * 2 * 2 * 128], BF16)
    wcv = p.dram("wcv", [128, 2 * NJ, 3])
    wdn = p.dram("wdnT", [D_FF // 256, 128, 2 * D], BF16)
    lng_d = p.dram("lng", [128, D])
    lnb_d = p.dram("lnb", [128, D])
    x2 = p.dram("x2", [Tc, D], kind="ExternalOutput")

    lng = p.sb("lng_s", [128, D])
    lnb = p.sb("lnb_s", [128, D])
    p.dma(lng, lng_d)
    p.dma(lnb, lnb_d)
    wc = p.sb("wc", [128, 2 * NJ, 3])
    p.dma(wc, wcv)
    acc = p.sb("acc", [128, NT, D])
    xb = p.sb("xb", [128, 16, Tc + 2], BF16)
    xst = Rot([p.sb(f"xst{i}", [128, Tc + 2]) for i in range(1)])
    xv = x1T.ap.rearrange("(kc p) t -> p kc t", p=128)
    load_cast_T(p, lambda kc: V(x1T, xv[:, kc, :]), xb, 16, Tc + 2, xst)

    JG = 2
    ub = Rot([p.sb(f"ub{i}", [128, 16, JG, 2, 128], BF16) for i in range(2)])
    db = Rot([p.sb(f"db{i}", [128, JG, D], BF16) for i in range(2)])
    gT = Rot([p.sb(f"gT{i}", [128, JG, Tc], BF16) for i in range(2)])
    ha = Rot([p.sb(f"ha{i}", [128, BL]) for i in range(2)])
    hb = Rot([p.sb(f"hb{i}", [128, BL]) for i in range(2)])
    sa = Rot([p.sb(f"sa{i}", [128, BL]) for i in range(2)])
    psu = Rot([p.ps(f"psu{i}", [128, 512]) for i in range(4)])
    psd = Rot([p.ps(f"psd{i}", [128, 512]) for i in range(4)])

    def load_group(jg):
        u = ub.next()
        dd = db.next()
        uv = u[:, :, :, :, :].re("p k j h c -> p (k j h c)")
        p.dma(uv[:, 0:4096], wup[jg, :, 0:4096])
        p.dma(uv[:, 4096:8192], wup[jg, :, 4096:8192])
        p.dma(dd[:, :, :].re("p j n -> p (j n)"), wdn[jg, :, :])
        return u, dd

    nxt = load_group(0)
    for t in range(NT):
        p.dma(acc[:, t, :], x1[t * 128:(t + 1) * 128, :])
    for t in range(NT):
        p.ts(acc[:, t, :], acc[:, t, :], ALPHA, ALU.mult, e=("dve" if t % 2 == 0 else "pool"))
    for jg in range(NJ // JG):
        u, dd = nxt
        if jg + 1 < NJ // JG:
            nxt = load_group(jg + 1)
        g = gT.next()
        for ji in range(JG):
            j = jg * JG + ji
            for blk in range(NB):
                hs = []
                for half in range(2):
                    ps = psu.next()
                    for kc in range(16):
                        p.mm(ps[:, 0:BL + 2], u[:, kc, ji, half, :], xb[:, kc, blk * BL:blk * BL + BL + 2],
                             start=(kc == 0), stop=(kc == 15))
                    h_ = (ha if half == 0 else hb).next()
                    cj = j + half * NJ
                    p.act(h_, ps[:, 2:BL + 2], AF.Copy, scale=wc[:, cj, 2:3])
                    p.stt(h_, ps[:, 1:BL + 1], wc[:, cj, 1:2], h_, ALU.mult, ALU.add)
                    p.stt(h_, ps[:, 0:BL], wc[:, cj, 0:1], h_, ALU.mult, ALU.add)
                    hs.append(h_)
                s = sa.next()
                p.act(s, hs[0], AF.Silu)
                p.tt(g[:, ji, blk * BL:(blk + 1) * BL], s, hs[1], ALU.mult, e="pool")
        for t in range(NT):
            for cc in range(4):
                ps = psd.next()
                for ji in range(JG):
                    p.mm(ps, g[:, ji, t * 128:(t + 1) * 128], dd[:, ji, cc * 512:(cc + 1) * 512],
                         start=(ji == 0), stop=(ji == JG - 1))
                a = acc[:, t, cc * 512:(cc + 1) * 512]
                p.tt(a, a, ps, ALU.add)
    scr = ln_scratch(p, "2")
    for t in range(NT):
        o = acc[:, t, :]
        layer_norm_tile(p, acc[:, t, :], o, lng, lnb, scr)
        p.dma(x2[t * 128:(t + 1) * 128, :], o)
    return p


def build_B1(Tt):
    p = P()
    TB = min(512, Tt)
    NBK = Tt // TB
    uT = p.dram("uT", [128, Tt])
    zT = p.dram("zT", [128, Tt], BF16, kind="ExternalOutput")
    small = {}
    for nm, shp in (("LRr", [128, 64]), ("LIr", [128, 64]), ("LSr", [128, 64]), ("BreT", [128, 64]),
                    ("BimT", [128, 64]), ("LRc", [128, 8]), ("LIc", [128, 8]), ("LSc", [128, 8]),
                    ("CC", [128, 128]), ("CCs", [128, 128]), ("dsk", [128, 1]), ("identf", [128, 128]),
                    ("S12", [128, 128]), ("sgnA", [128, 1]), ("maskg", [128, 8]), ("iota", [128, TB])):
        dt_ = p.dram(nm, shp)
        st = p.sb("s_" + nm, shp)
        p.dma(st, dt_)
        small[nm] = st
    S = small
    n = [0]

    def tmp(shape, dt=F32):
        n[0] += 1
        return p.sb(f"t{n[0]}", shape, dt)

    sh = [128, 64]
    stp = tmp(sh); ang = tmp(sh); sn = tmp(sh); cs = tmp(sh); tf = tmp(sh); ti = tmp(sh, I32); tk = tmp(sh)
    p.act(stp, S["LSr"], AF.Exp)
    p.tt(ang, S["LIr"], stp, ALU.mult)
    sincos(p, ang, sn, cs, tf, ti, tk)
    mag = tmp(sh)
    p.tt(mag, S["LRr"], stp, ALU.mult)
    p.act(mag, mag, AF.Exp)
    abr1 = tmp(sh); abi = tmp(sh)
    p.tt(abr1, mag, cs, ALU.mult)
    p.ts(abr1, abr1, -1.0, ALU.add)
    p.tt(abi, mag, sn, ALU.mult)
    den = tmp(sh); t2 = tmp(sh)
    p.tt(den, S["LRr"], S["LRr"], ALU.mult)
    p.tt(t2, S["LIr"], S["LIr"], ALU.mult)
    p.tt(den, den, t2, ALU.add)
    p.recip(den, den)
    fre = tmp(sh); fim = tmp(sh)
    p.tt(fre, abr1, S["LRr"], ALU.mult)
    p.tt(t2, abi, S["LIr"], ALU.mult)
    p.tt(fre, fre, t2, ALU.add)
    p.tt(fre, fre, den, ALU.mult)
    p.tt(fim, abi, S["LRr"], ALU.mult)
    p.tt(t2, abr1, S["LIr"], ALU.mult)
    p.tt(fim, fim, t2, ALU.subtract)
    p.tt(fim, fim, den, ALU.mult)
    bbr = tmp(sh); bbi = tmp(sh); nbbr = tmp(sh)
    p.tt(bbr, fre, S["BreT"], ALU.mult)
    p.tt(t2, fim, S["BimT"], ALU.mult)
    p.tt(bbr, bbr, t2, ALU.subtract)
    p.tt(bbi, fre, S["BimT"], ALU.mult)
    p.tt(t2, fim, S["BreT"], ALU.mult)
    p.tt(bbi, bbi, t2, ALU.add)
    p.ts(nbbr, bbr, -1.0, ALU.mult)
    BT = p.sb("BT", [128, 8, 128], BF16)
    BTs = p.sb("BTs", [128, 8, 128], BF16)
    for g in range(8):
        mg = S["maskg"][:, g:g + 1]
        p.ts(BT[:, g, 0:64], bbr, mg, ALU.mult)
        p.ts(BT[:, g, 64:128], bbi, mg, ALU.mult)
        p.ts(BTs[:, g, 0:64], bbi, mg, ALU.mult)
        p.ts(BTs[:, g, 64:128], nbbr, mg, ALU.mult)
    C1a = p.sb("C1a", [128, 8, 128], BF16)
    C2a = p.sb("C2a", [128, 8, 128], BF16)
    p.memset(C1a, 0.0)
    p.memset(C2a, 0.0)
    for g in range(8):
        p.ts(C1a[:, g, 16 * g:16 * g + 16], S["CC"][:, 16 * g:16 * g + 16], S["sgnA"][:, 0:1], ALU.mult)
        p.ts(C2a[:, g, 16 * g:16 * g + 16], S["CCs"][:, 16 * g:16 * g + 16], -1.0, ALU.mult)
    sh8 = [128, 8]
    stc = tmp(sh8); rho = tmp(sh8); theta = tmp(sh8)
    p.act(stc, S["LSc"], AF.Exp)
    p.tt(rho, S["LRc"], stc, ALU.mult)
    p.act(rho, rho, AF.Exp)
    p.tt(theta, S["LIc"], stc, ALU.mult)
    phi = tmp(sh8); sph = tmp(sh8); cph = tmp(sh8); f8 = tmp(sh8); i8 = tmp(sh8, I32); k8 = tmp(sh8)
    p.ts(phi, theta, float(TB), ALU.mult)
    sincos(p, phi, sph, cph, f8, i8, k8)
    p.ts(sph, sph, S["sgnA"][:, 0:1], ALU.mult)
    ROT = p.sb("ROT", [128, 8, 128])
    for g in range(8):
        p.ts(ROT[:, g, :], S["identf"], cph[:, g:g + 1], ALU.mult)
        p.stt(ROT[:, g, :], S["S12"], sph[:, g:g + 1], ROT[:, g, :], ALU.mult, ALU.add)
    SIN = p.sb("SIN", [128, 8, TB])
    COS = p.sb("COS", [128, 8, TB])
    RHO = p.sb("RHO", [128, 8, TB])
    shb = [128, TB]
    angb = tmp(shb); fb = tmp(shb); ib = tmp(shb, I32); kb = tmp(shb)
    for g in range(8):
        p.ts(angb, S["iota"], theta[:, g:g + 1], ALU.mult)
        sincos(p, angb, SIN[:, g, :], COS[:, g, :], fb, ib, kb)
        p.ts(RHO[:, g, :], S["iota"], 0.0, ALU.mult, rho[:, g:g + 1], ALU.add)

    ust = Rot([p.sb(f"ust{i}", [128, TB]) for i in range(2)])
    ubf = Rot([p.sb(f"ubf{i}", [128, TB], BF16) for i in range(2)])
    psa = Rot([p.ps(f"psa{i}", [128, TB]) for i in range(2)])
    psb = Rot([p.ps(f"psb{i}", [128, TB]) for i in range(2)])
    psy = Rot([p.ps(f"psy{i}", [128, TB]) for i in range(2)])
    psr = p.ps("psr", [128, 8])
    v1 = Rot([p.sb(f"v1{i}", [128, TB]) for i in range(2)])
    v2 = Rot([p.sb(f"v2{i}", [128, TB]) for i in range(2)])
    vv = Rot([p.sb(f"vv{i}", [128, TB]) for i in range(2)])
    shat = Rot([p.sb(f"shat{i}", [128, TB]) for i in range(3)])
    w1 = Rot([p.sb(f"w1{i}", [128, TB], BF16) for i in range(2)])
    w2 = Rot([p.sb(f"w2{i}", [128, TB], BF16) for i in range(2)])
    last = p.sb("last", [128, 8])
    init = p.sb("init", [128, 8])
    yb = Rot([p.sb(f"yb{i}", [128, TB]) for i in range(2)])
    gt = Rot([p.sb(f"gt{i}", [128, TB]) for i in range(2)])
    zo = Rot([p.sb(f"zo{i}", [128, TB], BF16) for i in range(2)])
    vv3 = Rot([p.sb(f"vv3{i}", [128, TB]) for i in range(3)])
    w13 = Rot([p.sb(f"w13{i}", [128, TB], BF16) for i in range(3)])
    w23 = Rot([p.sb(f"w23{i}", [128, TB], BF16) for i in range(3)])
    items = [(b, g) for b in range(NBK) for g in range(8)]
    blk = {}
    stA = {}
    stB = {}

    def block_begin(b):
        uf = ust.next()
        p.dma(uf, uT[:, b * TB:(b + 1) * TB])
        ub = ubf.next()
        p.copy(ub, uf, e="act")
        blk[b] = {"uf": uf, "ub": ub, "py": psy.next()}

    def stage_A(b, g):
        if g == 0:
            block_begin(b)
        ub = blk[b]["ub"]
        pa = psa.next()
        pb = psb.next()
        p.mm(pa, BT[:, g, :], ub)
        p.mm(pb, BTs[:, g, :], ub)
        a1 = v1.next(); a2 = v2.next(); av = vv3.next()
        p.tt(a1, pa, COS[:, g, :], ALU.mult)
        p.tt(a2, pb, SIN[:, g, :], ALU.mult)
        p.tt(av, a1, a2, ALU.add, e="pool")
        stA[(b, g)] = av

    def stage_B(b, g):
        av = stA.pop((b, g))
        if b > 0 and g == 0:
            for g_ in range(8):
                p.mm(psr[:, g_:g_ + 1], ROT[:, g_, :], last[:, g_:g_ + 1])
            p.copy(init, psr)
        sh_ = shat.next()
        p.scan(sh_, RHO[:, g, :], av, (0.0 if b == 0 else init[:, g:g + 1]), ALU.mult, ALU.add)
        p.copy(last[:, g:g + 1], sh_[:, TB - 1:TB], e="act")
        b1 = w13.next(); b2 = w23.next()
        p.tt(b1, sh_, COS[:, g, :], ALU.mult, e="pool")
        p.tt(b2, sh_, SIN[:, g, :], ALU.mult)
        stB[(b, g)] = (b1, b2)

    def stage_C(b, g):
        b1, b2 = stB.pop((b, g))
        py = blk[b]["py"]
        p.mm(py, C1a[:, g, :], b1, start=(g == 0), stop=False)
        p.mm(py, C2a[:, g, :], b2, start=False, stop=(g == 7))
        if g == 7:
            uf = blk[b]["uf"]
            y = yb.next()
            p.stt(y, uf, S["dsk"][:, 0:1], py, ALU.mult, ALU.add)
            t_ = gt.next()
            p.tt(t_, y, y, ALU.mult, e="pool")
            p.ts(t_, t_, 0.044715, ALU.mult, 1.0, ALU.add, e="pool")
            p.tt(t_, t_, y, ALU.mult, e="pool")
            p.act(t_, t_, AF.Sigmoid, scale=1.5957691216057308)
            z = zo.next()
            p.tt(z, y, t_, ALU.mult)
            p.dma(zT[:, b * TB:(b + 1) * TB], z, q="act")
            del blk[b]

    n_it = len(items)
    for i in range(-1, n_it + 1):
        if 0 <= i + 1 < n_it:
            stage_A(*items[i + 1])
        if 0 <= i < n_it:
            stage_B(*items[i])
        if 0 <= i - 1 < n_it:
            stage_C(*items[i - 1])
    return p


def host_B1_inputs(inp, l, c, uT_c, TB):
    gs = slice(8 * c, 8 * c + 8)
    lr = inp["ssm_lambda_re"][l][gs]; li = inp["ssm_lambda_im"][l][gs]; ls = inp["ssm_log_step"][l][gs]
    bre = inp["ssm_b_re"][l][gs]; bim = inp["ssm_b_im"][l][gs]
    cre = inp["ssm_c_re"][l][gs]; cim = inp["ssm_c_im"][l][gs]
    f = np.float32
    rep16 = lambda a: np.ascontiguousarray(np.repeat(a, 16, axis=0)).astype(f)
    m = {"uT": uT_c}
    m["LRr"] = rep16(lr); m["LIr"] = rep16(li)
    m["LSr"] = rep16(np.broadcast_to(ls[:, None], (8, 64)))
    m["BreT"] = np.ascontiguousarray(bre.transpose(0, 2, 1).reshape(128, 64)).astype(f)
    m["BimT"] = np.ascontiguousarray(bim.transpose(0, 2, 1).reshape(128, 64)).astype(f)
    m["LRc"] = np.ascontiguousarray(np.concatenate([lr.T, lr.T], 0)).astype(f)
    m["LIc"] = np.ascontiguousarray(np.concatenate([li.T, li.T], 0)).astype(f)
    m["LSc"] = np.ascontiguousarray(np.broadcast_to(ls[None, :], (128, 8))).astype(f)
    creT = cre.transpose(2, 0, 1).reshape(64, 128); cimT = cim.transpose(2, 0, 1).reshape(64, 128)
    m["CC"] = np.ascontiguousarray(np.concatenate([creT, cimT], 0)).astype(f)
    m["CCs"] = np.ascontiguousarray(np.concatenate([cimT, creT], 0)).astype(f)
    m["dsk"] = np.ascontiguousarray(inp["ssm_d"][l][128 * c:128 * c + 128].reshape(128, 1)).astype(f)
    m["identf"] = np.eye(128, dtype=f)
    s12 = np.zeros((128, 128), f)
    for k in range(64):
        s12[k, k + 64] = 1.0
        s12[k + 64, k] = 1.0
    m["S12"] = s12
    sg = np.ones((128, 1), f); sg[64:] = -1.0
    m["sgnA"] = sg
    mg = np.zeros((128, 8), f)
    for g in range(8):
        mg[16 * g:16 * g + 16, g] = 1.0
    m["maskg"] = mg
    m["iota"] = np.ascontiguousarray(np.broadcast_to(np.arange(TB, dtype=f)[None, :], (128, TB)))
    return m


def build_B2(Tt):
    p = P()
    TB = min(512, Tt)
    NBK = Tt // TB
    NCH = TB // 128
    qT = p.dram("qT", [128, Tt])
    kT = p.dram("kT", [128, Tt])
    vv = p.dram("v", [Tt, 128])
    sg = p.dram("sg", [Tt, 128])
    pos = p.dram("pos", [128, Tt], I32)
    o_out = p.dram("o", [Tt, 128], BF16, kind="ExternalOutput")
    S = {}
    for nm, shp in (("invf", [128, 1]), ("PERM", [128, 128]), ("DmT", [128, 128]), ("qd", [128, 1]),
                    ("kd", [128, 1]), ("cdec", [128, 1]), ("NG", [128, 128]), ("NB", [128, 128]),
                    ("identf", [128, 128])):
        dt_ = p.dram(nm, shp)
        st = p.sb("s_" + nm, shp)
        p.dma(st, dt_)
        S[nm] = st
    identb = p.sb("identb", [128, 128], BF16)
    p.copy(identb, S["identf"])
    state = p.sb("state", [128, 128])
    state_bf = p.sb("state_bf", [128, 128], BF16)
    p.memset(state, 0.0)
    p.memset(state_bf, 0.0)
    shb = [128, TB]
    qf = Rot([p.sb(f"qf{i}", shb) for i in range(2)])
    kf = Rot([p.sb(f"kf{i}", shb) for i in range(2)])
    posi = p.sb("posi", shb, I32)
    ang = p.sb("ang", shb); fb = p.sb("fb", shb); ib = p.sb("ib", shb, I32); kb = p.sb("kb", shb)
    SINt = p.sb("SINt", shb); COSt = p.sb("COSt", shb)
    r1 = p.sb("r1", shb); r2 = p.sb("r2", shb)
    rqT = Rot([p.sb(f"rqT{i}", shb, BF16) for i in range(2)])
    rkT = Rot([p.sb(f"rkT{i}", shb, BF16) for i in range(2)])
    pP = Rot([p.ps(f"pP{i}", shb) for i in range(1)])
    psc = p.ps("psc", [128, 128]); po2 = p.ps("po2", [128, 128])
    po1 = Rot([p.ps(f"po1{i}", [128, 128]) for i in range(2)])
    ptr = p.ps("ptr", [128, 128], BF16)
    pkv = Rot([p.ps(f"pkv{i}", [128, 128]) for i in range(2)])
    vf = Rot([p.sb(f"vf{i}", [128, 128]) for i in range(2)])
    vb = Rot([p.sb(f"vb{i}", [128, 128], BF16) for i in range(2)])
    sgf = Rot([p.sb(f"sgf{i}", [128, 128]) for i in range(4)])
    scm = Rot([p.sb(f"scm{i}", [128, 128], BF16) for i in range(2)])
    insb = Rot([p.sb(f"insb{i}", [128, 128]) for i in range(2)])
    osb = Rot([p.sb(f"osb{i}", [128, 128]) for i in range(3)])
    kdb = Rot([p.sb(f"kdb{i}", [128, 128], BF16) for i in range(2)])
    st6r = Rot([p.sb(f"st6{i}", [128, 6]) for i in range(3)])
    mvr = Rot([p.sb(f"mv{i}", [128, 2]) for i in range(3)])
    rsr = Rot([p.sb(f"rs{i}", [128, 1]) for i in range(3)])
    onb = Rot([p.sb(f"onb{i}", [128, 128]) for i in range(2)])
    oo = Rot([p.sb(f"oo{i}", [128, 128], BF16) for i in range(2)])
    rot_blk = {}
    live = {}
    live3 = {}

    def rotary(b):
        cs_ = slice(b * TB, (b + 1) * TB)
        q_ = qf.next(); k_ = kf.next()
        p.dma(q_, qT[:, cs_])
        p.dma(k_, kT[:, cs_])
        p.dma(posi, pos[:, cs_])
        p.copy(ang, posi)
        p.ts(ang, ang, S["invf"][:, 0:1], ALU.mult)
        sincos(p, ang, SINt, COSt, fb, ib, kb)
        rots = []
        for src, dstrot in ((q_, rqT), (k_, rkT)):
            pp = pP.next()
            p.mm(pp, S["PERM"], src)
            p.tt(r1, src, COSt, ALU.mult, e="pool")
            p.tt(r2, pp, SINt, ALU.mult)
            dd = dstrot.next()
            p.tt(dd, r1, r2, ALU.add)
            rots.append(dd)
        rot_blk[b] = rots

    def s1(b, n_):
        if n_ == 0:
            rotary(b)
        rq, rk = rot_blk[b]
        c0 = n_ * 128
        rows = slice(b * TB + c0, b * TB + c0 + 128)
        v_ = vf.next(); s_ = sgf.next()
        p.dma(v_, vv[rows, :])
        p.dma(s_, sg[rows, :])
        vb_ = vb.next()
        p.copy(vb_, v_, e="act")
        p.mm(psc, rk[:, c0:c0 + 128], rq[:, c0:c0 + 128])
        sc = scm.next()
        p.tt(sc, psc, S["DmT"], ALU.mult)
        p1 = po1.next()
        p.mm(p1, sc, vb_)
        i_ = insb.next()
        p.copy(i_, p1, e="act")
        p.tr(ptr, rk[:, c0:c0 + 128], identb)
        kd_ = kdb.next()
        p.ts(kd_, ptr, S["kd"][:, 0:1], ALU.mult)
        pk = pkv.next()
        p.mm(pk, kd_, vb_)
        live[(b, n_)] = (rq, c0, rows, s_, i_, pk)

    def s2(b, n_):
        rq, c0, rows, s_, i_, pk = live.pop((b, n_))
        p.mm(po2, rq[:, c0:c0 + 128], state_bf)
        o_ = osb.next()
        p.stt(o_, po2, S["qd"][:, 0:1], i_, ALU.mult, ALU.add)
        p.stt(state, state, S["cdec"][:, 0:1], pk, ALU.mult, ALU.add)
        p.copy(state_bf, state, e="act")
        st6 = st6r.next(); mv = mvr.next(); rs = rsr.next()
        p.bn_stats(st6, o_)
        p.bn_aggr(mv, st6)
        p.ts(rs, mv[:, 1:2], 1e-5, ALU.add)
        p.act(rs, rs, AF.Sqrt)
        live3[(b, n_)] = (rows, s_, o_, mv, rs)
        if n_ == NCH - 1:
            del rot_blk[b]

    def s3(b, n_):
        rows, s_, o_, mv, rs = live3.pop((b, n_))
        p.recip(rs, rs)
        on = onb.next()
        p.ts(on, o_, mv[:, 0:1], ALU.subtract, rs[:, 0:1], ALU.mult)
        p.tt(on, on, S["NG"], ALU.mult)
        p.tt(on, on, S["NB"], ALU.add)
        ob = oo.next()
        p.tt(ob, on, s_, ALU.mult)
        p.dma(o_out[rows, :], ob, q="pool")

    items = [(b, n_) for b in range(NBK) for n_ in range(NCH)]
    s1(*items[0])
    for i in range(len(items)):
        if i + 1 < len(items):
            s1(*items[i + 1])
        s2(*items[i])
        if i >= 1:
            s3(*items[i - 1])
    s3(*items[-1])
    return p


def host_B2_inputs(inp, l, c, qT_c, kT_c, v_c, sg_c, pos_T):
    f = np.float32
    h = c
    lg = np.log1p(-np.exp2(-5.0 - h))
    idx = np.arange(128, dtype=np.float64)
    rel = idx[None, :] - idx[:, None]
    sc = 128 ** -0.5
    DmT = np.where(rel >= 0, np.exp(lg * np.maximum(rel, 0.0)), 0.0) * sc
    perm = np.zeros((128, 128), f)
    for m_ in range(64):
        perm[m_ + 64, m_] = -1.0
        perm[m_, m_ + 64] = 1.0
    invf = (10000.0 ** (-np.arange(64, dtype=f) / 64)).astype(f)
    rep = lambda v_: np.ascontiguousarray(np.broadcast_to(v_[None, :], (128, v_.shape[0]))).astype(f)
    return {
        "qT": qT_c, "kT": kT_c, "v": v_c, "sg": sg_c, "pos": pos_T,
        "invf": np.concatenate([invf, invf]).reshape(128, 1).astype(f),
        "PERM": perm, "DmT": DmT.astype(f),
        "qd": np.exp(lg * (idx + 1.0)).reshape(128, 1).astype(f),
        "kd": (np.exp(lg * (127.0 - idx)) * sc).reshape(128, 1).astype(f),
        "cdec": np.full((128, 1), np.exp(lg * 128.0), f),
        "NG": rep(inp["ret_norm_g"][l][128 * c:128 * c + 128]),
        "NB": rep(inp["ret_norm_b"][l][128 * c:128 * c + 128]),
        "identf": np.eye(128, dtype=f),
    }


def build_B3(Tt):
    p = P()
    TB = min(512, Tt)
    NBK = Tt // TB
    CH = 64
    NCH = TB // CH
    NG_ = NCH // 4
    zin = {nm: p.dram(nm, [128, Tt + 1]) for nm in ("zr", "zk", "zv", "zl1", "zl2")}
    o_out = p.dram("o", [Tt, 128], BF16, kind="ExternalOutput")
    S = {}
    for nm, shp in (("MU", [128, 5]), ("PV", [128, 5]), ("w2c", [128, 128]), ("g2c", [128, 128]),
                    ("NG", [64, 128]), ("NB", [64, 128]), ("bones", [128, 128]), ("hsel", [128, 2]),
                    ("cmask", [128, TB]), ("MK1", [64, 512]), ("MK2", [64, 512]), ("MK3", [64, 256]),
                    ("identf", [128, 128]), ("bdmask", [128, 128])):
        dt_ = p.dram(nm, shp)
        st = p.sb("s_" + nm, shp)
        p.dma(st, dt_)
        S[nm] = st
    w2b = p.sb("w2b", [128, 128], BF16); p.copy(w2b, S["w2c"])
    g2b = p.sb("g2b", [128, 128], BF16); p.copy(g2b, S["g2c"])
    hselb = p.sb("hselb", [128, 2], BF16); p.copy(hselb, S["hsel"])
    identb = p.sb("identb", [128, 128], BF16); p.copy(identb, S["identf"])
    bk = [p.ps(f"bank{i}", [128, 512]) for i in range(8)]
    H = p.sb("H", [128, 128]); Hb = p.sb("Hb", [128, 128], BF16)
    p.memset(H, 0.0); p.memset(Hb, 0.0)
    shb = [128, TB]
    n = [0]

    def tmp(shape=shb, dt=F32):
        n[0] += 1
        return p.sb(f"t{n[0]}", shape, dt)

    zst = {nm: Rot([p.sb(f"zst_{nm}{i}", [128, TB + 1]) for i in range(2)]) for nm in zin}
    dsh = tmp(); r_ = tmp(); k_ = tmp(); v_ = tmp(); l1 = tmp(); l2 = tmp()
    l1b = tmp(dt=BF16); sgb = tmp([128, TB + 64], BF16)
    p.memset(sgb, 0.0)
    ld = tmp(); a_ = tmp(); cs = tmp(); e0 = tmp(); e1 = tmp(); e2 = tmp(); tq = tmp()
    kk = tmp(); rn = tmp(); kmod = tmp(); bv = tmp()
    rkr = tmp([128, TB + 64], BF16); vbf = tmp([128, TB + 64], BF16)
    for t_ in (rkr, vbf):
        p.memset(t_, 0.0)
    Mm = p.sb("Mm", [64, NCH + 1, 2, 64], BF16)
    p.memset(Mm, 0.0)
    MMs = [Rot([p.sb(f"MM{g}{i}", [64, 9, 128], BF16) for i in range(2)]) for g in range(NG_)]
    Tts = [Rot([p.sb(f"Tt{g}{i}", [64, 10, 64], BF16) for i in range(2)]) for g in range(NG_)]
    D1 = []
    for i3 in range(3):
        d_ = {"KR": p.sb(f"KR{i3}", [128, NCH + 1, 2, CH], BF16),
              "Bh": p.sb(f"Bh{i3}", [128, TB + 64], BF16), "Kh": p.sb(f"Kh{i3}", [128, TB + 64], BF16),
              "Vb": p.sb(f"Vb{i3}", [64, NCH, 128], BF16), "Vf": p.sb(f"Vf{i3}", [64, NCH, 128]),
              "BhT": p.sb(f"BhT{i3}", [64, NCH, 128], BF16), "KhT": p.sb(f"KhT{i3}", [64, NCH, 128], BF16),
              "Gtm": p.sb(f"Gtm{i3}", [64, NCH, 128]), "BS": p.sb(f"BS{i3}", [64, NCH * 2]),
              "PC": p.sb(f"PC{i3}", [128, NCH])}
        for nm in ("KR", "Bh", "Kh"):
            p.memset(d_[nm], 0.0)
        D1.append(d_)
    D2 = []
    for par in range(2):
        d_ = {"A1": p.sb(f"A1{par}", [64, NCH + 1, 2, 128], BF16),
              "A2": p.sb(f"A2{par}", [64, NCH + 1, 2, 128], BF16),
              "Tf": [p.sb(f"Tf{par}{g}", [64, 10, 64], BF16) for g in range(NG_)]}
        for nm in ("A1", "A2"):
            p.memset(d_[nm], 0.0)
        for g in range(NG_):
            p.memset(d_["Tf"][g], 0.0)
        D2.append(d_)
    W0n = Rot([p.sb(f"W0n{i}", [64, 128], BF16) for i in range(2)])
    Ub = Rot([p.sb(f"Ub{i}", [64, 128], BF16) for i in range(2)])
    Ysb = p.sb("Ysb", [64, NCH, 128])
    tH = p.sb("tH", [128, 128])
    xc = p.sb("xc", [64, NCH, 128]); sq_ = p.sb("sq_", [64, NCH, 128])
    s16 = p.sb("s16", [64, NCH * 2]); v16 = p.sb("v16", [64, NCH * 2])
    osb = Rot([p.sb(f"osb{i}", [64, NCH, 128], BF16) for i in range(2)])
    mk1 = S["MK1"][:, :].re("p (c t) -> p c t", t=128)
    mk2 = S["MK2"][:, :].re("p (c t) -> p c t", t=128)
    mk3 = S["MK3"][:, :].re("p (c t) -> p c t", t=64)

    def c3(vw):
        return vw.re("p (c t) -> p c t", t=CH)

    def pre1(b):
        D_ = D1[b % 3]
        KR, Bh, Kh = D_["KR"], D_["Bh"], D_["Kh"]
        zs = {}
        for nm in zin:
            st = zst[nm].next()
            p.dma(st, zin[nm][:, b * TB:b * TB + TB + 1])
            zs[nm] = st
        for i, (nm, dst) in enumerate((("zr", r_), ("zk", k_), ("zv", v_), ("zl1", l1), ("zl2", l2))):
            st = zs[nm]
            p.tt(dsh, st[:, 0:TB], st[:, 1:TB + 1], ALU.subtract, e="pool")
            p.stt(dst, dsh, S["MU"][:, i:i + 1], st[:, 1:TB + 1], ALU.mult, ALU.add)
        yield
        p.act(l1b[0:64, :], l1[0:64, :], AF.Tanh)
        p.copy(l1b[64:128, :], l1[64:128, :], e="act")
        p.act(sgb[:, 0:TB], l2, AF.Sigmoid)
        p.mm(bk[4], w2b[0:64, :], l1b[0:64, :])
        p.mm(bk[5], w2b[64:128, :], l1b[64:128, :])
        p.act(ld, bk[4], AF.Sigmoid, bias=S["PV"][:, 0:1])
        p.ts(ld, ld, -EXPM05, ALU.mult)
        p.act(a_, bk[5], AF.Sigmoid, bias=S["PV"][:, 1:2])
        p.scan(cs, S["cmask"], ld, 0.0, ALU.mult, ALU.add)
        p.act(e1, cs, AF.Exp)
        p.act(e2, cs, AF.Exp, scale=-1.0)
        p.tt(tq, cs, ld, ALU.subtract, e="pool")
        p.act(e0, tq, AF.Exp)
        p.copy(D_["PC"], e1[:, :].re("p (c t) -> p c t", t=CH)[:, :, CH - 1], e="pool")
        yield
        p.ts(kk, k_, S["PV"][:, 2:3], ALU.mult)
        p.tt(tq, kk, kk, ALU.mult, e="pool")
        p.mm(bk[6], S["bones"], tq)
        p.ts(rn, bk[6], 1e-24, ALU.max)
        p.act(rn, rn, AF.Sqrt)
        p.recip(rn, rn)
        p.tt(kk, kk, rn, ALU.mult)
        p.ts(tq, a_, -1.0, ALU.add)
        p.ts(tq, tq, S["PV"][:, 3:4], ALU.mult)
        p.stt(kmod, tq, 1.0, k_, ALU.add, ALU.mult)
        p.tt(bv, a_, kk, ALU.mult, e="pool")
        yield
        p.tt(KR[:, 0:NCH, 0, :], c3(kk[:, :]), c3(e0[:, :]), ALU.mult)
        p.tt(KR[:, 0:NCH, 1, :], c3(r_[:, :]), c3(e1[:, :]), ALU.mult)
        p.tt(Bh[:, 0:TB], bv, e2, ALU.mult, e="pool")
        p.tt(Kh[:, 0:TB], kmod, e2, ALU.mult, e="pool")
        p.ts(tq, r_, S["PV"][:, 4:5], ALU.mult)
        p.tt(rkr[:, 0:TB], tq, kmod, ALU.mult)
        p.copy(vbf[:, 0:TB], v_, e="act")
        yield
        for g4 in range(NG_):
            sl4 = slice(g4 * 4, (g4 + 1) * 4)
            for i in range(4):
                c = g4 * 4 + i
                p.mm(bk[4][:, i * 128:(i + 1) * 128], vbf[:, c * CH:c * CH + 128], identb)
            p.copy(D_["Vf"][:, sl4, :], bk[4][0:64, 0:512].re("p (c e) -> p c e", e=128), e="act")
            p.copy(D_["Vb"][:, sl4, :], D_["Vf"][:, sl4, :])
            for i in range(4):
                c = g4 * 4 + i
                p.mm(bk[5][:, i * 128:(i + 1) * 128], sgb[:, c * CH:c * CH + 128], g2b)
            p.copy(D_["Gtm"][:, sl4, :], bk[5][0:64, :].re("p (c e) -> p c e", e=128), e="act")
            for bi_, (src, dstT) in enumerate(((Bh, D_["BhT"]), (Kh, D_["KhT"]))):
                bk_ = bk[6 - 2 * bi_]
                for i in range(4):
                    c = g4 * 4 + i
                    p.mm(bk_[:, i * 128:(i + 1) * 128], src[:, c * CH:c * CH + 128], identb)
                p.copy(dstT[:, sl4, :], bk_[0:64, 0:512].re("p (c e) -> p c e", e=128))
            yield
        for c in range(NCH):
            p.mm(bk[5][:, 2 * c:2 * c + 2], rkr[:, c * CH:c * CH + 128], hselb)
        p.copy(D_["BS"], bk[5][0:64, 0:2 * NCH], e="act")
        yield

    def pre2(b):
        D_ = D1[b % 3]
        KR, Bh, Kh = D_["KR"], D_["Bh"], D_["Kh"]
        E_ = D2[b % 2]
        A1, A2 = E_["A1"], E_["A2"]
        for g4 in range(NG_):
            sl4 = slice(g4 * 4, (g4 + 1) * 4)
            for ci in range(4):
                c = g4 * 4 + ci
                for h in range(2):
                    ho = 64 * h
                    krv = KR[ho:ho + 64, c, :, :].re("p a t -> p (a t)")
                    p.mm(bk[0 + h][:, ci * 128:(ci + 1) * 128], Bh[ho:ho + 64, c * CH:c * CH + 128], krv)
                    p.mm(bk[2 + h][:, ci * 128:(ci + 1) * 128], Kh[ho:ho + 64, c * CH:c * CH + 128], krv)
            for h in range(2):
                p.tt(A1[:, sl4, h, :], bk[0 + h][0:64, :].re("p (c t) -> p c t", t=128), mk1, ALU.mult)
                p.tt(A2[:, sl4, h, :], bk[2 + h][0:64, :].re("p (c t) -> p c t", t=128), mk2, ALU.mult)
            for ci in range(4):
                c = g4 * 4 + ci
                for h in range(2):
                    ho = 64 * h
                    krv = KR[ho:ho + 64, c, :, :].re("p a t -> p (a t)")
                    p.mm(bk[0 + h][:, ci * 64:(ci + 1) * 64], krv, Bh[ho:ho + 64, c * CH:(c + 1) * CH])
            for h in range(2):
                p.tt(Mm[:, sl4, h, :], bk[0 + h][0:64, 0:256].re("p (c t) -> p c t", t=64), mk3, ALU.mult)
            yield
        pairs_g = [[(G * 4 + ci, h) for ci in range(4) for h in range(2)] for G in range(NG_)]
        Tcur = []
        curM = []
        curMt = []
        Mflat = Mm[:, :, :, :].re("p c h t -> p (c h t)")
        for G in range(NG_):
            Tt = Tts[G].next()
            for pi, (c, h) in enumerate(pairs_g[G]):
                p.tt(Tt[:, pi, :], A1[:, c, h, 0:64], identb[0:64, 0:64], ALU.add, e="pool")
            Tcur.append(Tt)
            curM.append([Mflat[:, (2 * c + h) * 64:(2 * c + h) * 64 + 128] for (c, h) in pairs_g[G]])
            curMt.append([A1[:, c, h, 0:128] for (c, h) in pairs_g[G]])
        yield
        for lev in range(1, 6):
            last = (lev == 5)
            for G in range(NG_):
                sqb = [bk[(2 * G) % 4], bk[(2 * G + 1) % 4]]
                for pi in range(8):
                    b_ = sqb[pi // 4]
                    col = (pi % 4) * 128
                    p.mm(b_[:, col:col + 64], curMt[G][pi], curM[G][pi][:, 0:64])
                    if not last:
                        p.mm(b_[:, col + 64:col + 128], curM[G][pi], curMt[G][pi][:, 0:64])
            for G in range(NG_):
                sqb = [bk[(2 * G) % 4], bk[(2 * G + 1) % 4]]
                MMn = MMs[G].next()
                for hb in range(2):
                    p.copy(MMn[:, 4 * hb:4 * hb + 4, :], sqb[hb][0:64, :].re("p (a t) -> p a t", t=128),
                           e=("act" if hb == 0 else "dve"))
                MMf = MMn[:, :, :].re("p a t -> p (a t)")
                curM[G] = [MMf[:, pi * 128:pi * 128 + 128] for pi in range(8)]
                curMt[G] = [MMf[:, pi * 128 + 64:pi * 128 + 192] for pi in range(8)]
            for G in range(NG_):
                pacc = bk[(2 * G) % 4]
                for pi in range(8):
                    p.mm(pacc[:, pi * 64:(pi + 1) * 64], curM[G][pi], Tcur[G][:, pi, :])
            for G in range(NG_):
                pacc = bk[(2 * G) % 4]
                Tn = E_["Tf"][G] if last else Tts[G].next()
                p.tt(Tn[:, 0:8, :], Tcur[G][:, 0:8, :], pacc[0:64, :].re("p (a t) -> p a t", t=64), ALU.add)
                Tcur[G] = Tn
            yield

    def chain(b):
        D_ = D1[b % 3]
        E_ = D2[b % 2]
        KR, Vb, BhT, KhT = D_["KR"], D_["Vb"], D_["BhT"], D_["KhT"]
        A1, A2 = E_["A1"], E_["A2"]
        krf = KR[:, :, :, :].re("p c a t -> p (c a t)")
        a1f = A1[:, :, :, :].re("p c h t -> p (c h t)")
        a2f = A2[:, :, :, :].re("p c h t -> p (c h t)")
        pw, pu, py, ph = bk[7][:, 0:128], bk[7][:, 128:256], bk[7][:, 256:384], bk[7][:, 384:512]
        for c in range(NCH):
            Tt = E_["Tf"][c // 4]
            p.mm(pw[:, 0:128], krf[:, c * 128:c * 128 + 128], Hb, start=True, stop=False)
            for h in range(2):
                p.mm(pw[:, 64 * h:64 * h + 64], A2[:, c, h, 0:128], Vb[:, c, 64 * h:64 * h + 64],
                     start=False, stop=(h == 1))
            wn = W0n.next()
            p.act(wn, pw[0:64, 0:128], AF.Copy, scale=-1.0)
            for h in range(2):
                pi = (c % 4) * 2 + h
                p.mm(pu[:, 64 * h:64 * h + 64], Tt[:, pi:pi + 2, :].re("p a t -> p (a t)"), wn[:, 64 * h:64 * h + 64])
            ub = Ub.next()
            p.copy(ub, pu[0:64, 0:128])
            p.mm(py[:, 0:128], krf[:, c * 128 + 64:c * 128 + 192], Hb, start=True, stop=False)
            for h in range(2):
                sl = slice(64 * h, 64 * h + 64)
                o_ = (2 * c + h) * 128 + 64
                p.mm(py[:, sl], a1f[:, o_:o_ + 128], ub[:, sl], start=False, stop=False)
                p.mm(py[:, sl], a2f[:, o_:o_ + 128], Vb[:, c, sl], start=False, stop=(h == 1))
            p.copy(Ysb[:, c, :], py[0:64, 0:128], e="act")
            p.mm(ph[:, 0:128], BhT[:, c, :], ub, start=True, stop=False)
            p.mm(ph[:, 0:128], KhT[:, c, :], Vb[:, c, :], start=False, stop=True)
            p.tt(tH, ph[:, 0:128], S["bdmask"], ALU.mult)
            p.tt(tH, tH, H, ALU.add)
            pc = D_["PC"][:, c:c + 1]
            p.ts(H, tH, pc, ALU.mult)
            p.act(Hb, tH, AF.Copy, scale=pc)
            yield
        y4 = Ysb[:, :, :].re("p c (h e) -> p (c h) e", e=64)
        p.reduce(s16, y4, ALU.add)
        p.ts(s16, s16, 1.0 / 64, ALU.mult)
        x4 = xc[:, :, :].re("p c (h e) -> p (c h) e", e=64)
        q4 = sq_[:, :, :].re("p c (h e) -> p (c h) e", e=64)
        p.tt(x4, y4, s16[:, :].re("p (a o) -> p a o", o=1).bc([64, NCH * 2, 64]), ALU.subtract)
        p.tt(q4, x4, x4, ALU.mult, e="pool")
        p.reduce(v16, q4, ALU.add)
        p.ts(v16, v16, 1.0 / 64, ALU.mult, 64e-5, ALU.add)
        p.act(v16, v16, AF.Sqrt)
        p.recip(v16, v16)
        p.tt(x4, x4, v16[:, :].re("p (a o) -> p a o", o=1).bc([64, NCH * 2, 64]), ALU.mult)
        ngb = S["NG"][:, :].re("p (o e) -> p o e", o=1).bc([64, NCH, 128])
        nbb = S["NB"][:, :].re("p (o e) -> p o e", o=1).bc([64, NCH, 128])
        p.tt(xc, xc, ngb, ALU.mult)
        p.tt(xc, xc, nbb, ALU.add)
        vf4 = D_["Vf"][:, :, :].re("p c (h e) -> p (c h) e", e=64)
        p.tt(q4, vf4, D_["BS"][:, :].re("p (a o) -> p a o", o=1).bc([64, NCH * 2, 64]), ALU.mult)
        p.tt(xc, xc, sq_, ALU.add)
        ob = osb.next()
        p.tt(ob, xc, D_["Gtm"], ALU.mult)
        p.dma(V(o_out, o_out.ap[b * TB:(b + 1) * TB, :].rearrange("(c t) e -> t c e", t=CH)), ob)
        yield

    def drain(gens):
        gens = [g for g in gens if g is not None]
        while gens:
            for g in list(gens):
                try:
                    next(g)
                except StopIteration:
                    gens.remove(g)

    drain([pre1(0)])
    drain([pre2(0), pre1(1) if NBK > 1 else None])
    for b in range(NBK):
        drain([chain(b), pre2(b + 1) if b + 1 < NBK else None, pre1(b + 2) if b + 2 < NBK else None])
    return p


def host_B3_inputs(inp, l, c, zT_full, TB):
    f = np.float32
    Tt = zT_full.shape[1]
    ch = slice(128 * c, 128 * c + 128)

    def pad(a):
        return np.ascontiguousarray(np.concatenate([np.zeros((a.shape[0], 1), f), a], axis=1))

    mu = inp["rwkv_mu"][l]
    m = {"zr": pad(zT_full[0:1024][ch]), "zk": pad(zT_full[1024:2048][ch]), "zv": pad(zT_full[2048:3072][ch]),
         "zl1": pad(zT_full[3072:3200]), "zl2": pad(zT_full[3200:3328])}
    m["MU"] = np.ascontiguousarray(np.stack([mu[0:1024][ch], mu[1024:2048][ch], mu[2048:3072][ch],
                                             mu[3072:3200], mu[3200:3328]], axis=1)).astype(f)
    m["PV"] = np.ascontiguousarray(np.stack([inp["rwkv_w0"][l][ch], inp["rwkv_a0"][l][ch], inp["rwkv_k_k"][l][ch],
                                             inp["rwkv_k_a"][l][ch], inp["rwkv_r_k"][l][ch]], axis=1)).astype(f)
    m["w2c"] = np.ascontiguousarray(np.concatenate([inp["rwkv_w2"][l][:, ch], inp["rwkv_a2"][l][:, ch]], 0)).astype(f)
    m["g2c"] = np.ascontiguousarray(inp["rwkv_g2"][l][:, ch]).astype(f)
    rep = lambda v_: np.ascontiguousarray(np.broadcast_to(v_[None, :], (64, v_.shape[0]))).astype(f)
    m["NG"] = rep(inp["rwkv_norm_g"][l][ch]); m["NB"] = rep(inp["rwkv_norm_b"][l][ch])
    bo = np.zeros((128, 128), f); bo[0:64, 0:64] = 1.0; bo[64:, 64:] = 1.0
    m["bones"] = bo
    m["bdmask"] = bo.copy()
    hs = np.zeros((128, 2), f); hs[0:64, 0] = 1.0; hs[64:, 1] = 1.0
    m["hsel"] = hs
    cm = np.ones((128, TB), f); cm[:, ::64] = 0.0
    m["cmask"] = cm
    s = np.arange(64)[:, None]; t = np.arange(64)[None, :]
    strict_st = (t > s).astype(f); incl_st = (t >= s).astype(f)
    strict_ts = (t < s).astype(f)
    m["MK1"] = np.ascontiguousarray(np.tile(np.concatenate([-strict_st, incl_st], 1), (1, 4)))
    m["MK2"] = np.ascontiguousarray(np.tile(np.concatenate([strict_st, incl_st], 1), (1, 4)))
    m["MK3"] = np.ascontiguousarray(np.tile(-strict_ts, (1, 4)))
    m["identf"] = np.eye(128, dtype=f)
    return m


_PROGS = {}


def _prog(name, builder, *a):
    key = (name,) + a
    if key not in _PROGS:
        _PROGS[key] = builder(*a)
    return _PROGS[key]


def _run(p, maps):
    return p.run(maps).results


def kernel(**inp):
    inp = {k: np.asarray(v) for k, v in inp.items()}
    x = np.ascontiguousarray(inp["x"][0]).astype(np.float32)
    Tt = x.shape[0]
    Tc = Tt // NCORE
    TB = min(512, Tt)
    f = np.float32
    posT = np.ascontiguousarray(np.broadcast_to(inp["positions"][0][None, :], (128, Tt))).astype(np.int32)
    rep = lambda v_: np.ascontiguousarray(np.broadcast_to(v_[None, :], (128, v_.shape[0]))).astype(f)
    sl = lambda c: slice(c * Tc, (c + 1) * Tc)
    hs = lambda c: slice(128 * c, 128 * c + 128)
    eye = np.eye(128, dtype=f)
    flat = np.concatenate([np.concatenate([inp["ffn_up"][l].ravel(), inp["ffn_down"][l].ravel()])
                           for l in range(DEPTH)])
    Lp = flat.size // (NCORE * 128)
    rp = _run(build_PREP(Lp), [{"w": np.ascontiguousarray(flat[c * 128 * Lp:(c + 1) * 128 * Lp].reshape(128, Lp))}
                               for c in range(NCORE)])
    flat_b = np.concatenate([rp[c]["wb"].reshape(-1) for c in range(NCORE)])
    del flat
    nup, ndn = D * 2 * D_FF, D_FF * D
    wupT, wdnT = [], []
    for l in range(DEPTH):
        o_ = l * (nup + ndn)
        up = flat_b[o_:o_ + nup].reshape(16, 128, 2, D_FF // 256, 2, 128)
        wupT.append(np.ascontiguousarray(up.transpose(3, 1, 0, 4, 2, 5)).reshape(D_FF // 256, 128, 8192))
        dn = flat_b[o_ + nup:o_ + nup + ndn].reshape(D_FF // 256, 2, 128, D)
        wdnT.append(np.ascontiguousarray(dn.transpose(0, 2, 1, 3)).reshape(D_FF // 256, 128, 2 * D))
    for l in range(DEPTH):
        pa = build_A(Tc)
        ra = _run(pa, [{"xT": np.ascontiguousarray(x[sl(c)].T), "w": inp["w_in"][l]} for c in range(NCORE)])
        cat_fm = lambda nm: np.concatenate([ra[c][nm] for c in range(NCORE)], axis=1)
        cat_tm = lambda nm: np.concatenate([ra[c][nm] for c in range(NCORE)], axis=0)
        uT, qT, kT, zT = cat_fm("uT"), cat_fm("qT"), cat_fm("kT"), cat_fm("zT")
        v, sg = cat_tm("v"), cat_tm("sg")
        gates = [ra[c]["gates"] for c in range(NCORE)]
        r1 = _run(build_B1(Tt), [host_B1_inputs(inp, l, c, np.ascontiguousarray(uT[hs(c)]), TB) for c in range(NCORE)])
        zs = np.concatenate([r1[c]["zT"] for c in range(NCORE)], axis=0)
        r2 = _run(build_B2(Tt), [host_B2_inputs(inp, l, c, np.ascontiguousarray(qT[hs(c)]), np.ascontiguousarray(kT[hs(c)]),
                                                np.ascontiguousarray(v[:, hs(c)]), np.ascontiguousarray(sg[:, hs(c)]), posT)
                                 for c in range(NCORE)])
        orT = np.ascontiguousarray(np.concatenate([r2[c]["o"] for c in range(NCORE)], axis=1).T)
        r3 = _run(build_B3(Tt), [host_B3_inputs(inp, l, c, zT, TB) for c in range(NCORE)])
        owT = np.ascontiguousarray(np.concatenate([r3[c]["o"] for c in range(NCORE)], axis=1).T)
        rc1 = _run(build_C1(Tc), [{
            "zT": np.ascontiguousarray(zs[:, sl(c)]), "orT": np.ascontiguousarray(orT[:, sl(c)]),
            "owT": np.ascontiguousarray(owT[:, sl(c)]), "gates": gates[c], "x": np.ascontiguousarray(x[sl(c)]),
            "wglu": inp["ssm_glu"][l], "wret": inp["ret_out"][l], "wrw": inp["rwkv_out"][l], "wo": inp["w_o"][l],
            "lng": rep(inp["ln1_g"][l]), "lnb": rep(inp["ln1_b"][l]), "identf": eye} for c in range(NCORE)])
        x1 = np.concatenate([rc1[c]["x1"] for c in range(NCORE)], axis=0)
        x1p = np.concatenate([np.zeros((2, D), f), x1], axis=0)
        wcv = np.ascontiguousarray(inp["ffn_conv"][l].T.reshape(88, 128, 3).transpose(1, 0, 2)).astype(f)
        rc2 = _run(build_C2(Tc), [{
            "x1": np.ascontiguousarray(x1[sl(c)]), "x1T": np.ascontiguousarray(x1p[c * Tc:(c + 1) * Tc + 2].T),
            "wupT": wupT[l], "wcv": wcv, "wdnT": wdnT[l],
            "lng": rep(inp["ln2_g"][l]), "lnb": rep(inp["ln2_b"][l])} for c in range(NCORE)])
        x = np.concatenate([rc2[c]["x2"] for c in range(NCORE)], axis=0)
    return x[None].astype(np.float32)
```
